# Optimizing a Trainium2 kernel written in Bass

```python
import math
import jax, jax.numpy as jnp
from jax import lax
import numpy as np

D_MODEL = 1024
BATCH = 8
SEQ = 2048
DEPTH = 4
DEC_BATCH = 128
DEC_SEQ = 8
PAST_LEN = 16384
PAGE_SIZE = 128

N_META = 16
N_MIXERS = 2
N_LAYERS_A = (DEPTH + 1) // 2
N_LAYERS_B = DEPTH // 2
S5_WIDTH = D_MODEL
S5_GROUP = 16
S5_GROUPS = S5_WIDTH // S5_GROUP
S5_STATE = 64
RG_WIDTH = D_MODEL
RG_BLOCKS = 4
RG_BLOCK = RG_WIDTH // RG_BLOCKS
RG_CONV = 4
RG_C = 8.0
FF_WIDTH = 2816
FF_CONV = 3
DN_ALPHA = (2 * DEPTH) ** 0.25
DN_BETA = (8 * DEPTH) ** -0.25
LN_EPS = 1e-5

kernel_name = "s5_rglru_convffn_deepnorm_meta_step"


def layer_norm(x, g, b):
    xf = x.astype(jnp.float32)
    mu = jnp.mean(xf, -1, keepdims=True)
    var = jnp.mean(jnp.square(xf - mu), -1, keepdims=True)
    y = (xf - mu) * lax.rsqrt(var + LN_EPS) * g.astype(jnp.float32) + b.astype(jnp.float32)
    return y.astype(x.dtype)


def causal_dwconv(x, hist, w, b):
    K = w.shape[0]
    L = x.shape[1]
    xp = jnp.concatenate([hist.astype(x.dtype), x], axis=1)
    y = b + w[0] * xp[:, 0:L]
    for k in range(1, K):
        y = y + w[k] * xp[:, k:k + L]
    return y, xp[:, L:]


def _complex_combine(e1, e2):
    a1r, a1i, b1r, b1i = e1
    a2r, a2i, b2r, b2i = e2
    return (a1r * a2r - a1i * a2i,
            a1r * a2i + a1i * a2r,
            a2r * b1r - a2i * b1i + b2r,
            a2r * b1i + a2i * b1r + b2i)


def _real_combine(e1, e2):
    a1, b1 = e1
    a2, b2 = e2
    return a1 * a2, a2 * b1 + b2


def s5_mixer(x, h_re, h_im, w_in, lam_re, lam_im, log_step, b_re, b_im, c_re, c_im, d_skip, w_out):
    f32 = jnp.float32
    Bsz, L, _ = x.shape
    u = x @ w_in
    ug = u.astype(f32).reshape(Bsz, L, S5_GROUPS, S5_GROUP)
    step = jnp.exp(log_step.astype(f32))[:, None]
    lr, li = lam_re.astype(f32), lam_im.astype(f32)
    mag = jnp.exp(lr * step)
    ab_re, ab_im = mag * jnp.cos(li * step), mag * jnp.sin(li * step)
    den = lr * lr + li * li
    nr, ni = ab_re - 1.0, ab_im
    q_re = ((nr * lr + ni * li) / den)[..., None]
    q_im = ((ni * lr - nr * li) / den)[..., None]
    br, bi = b_re.astype(f32), b_im.astype(f32)
    bb_re = q_re * br - q_im * bi
    bb_im = q_re * bi + q_im * br
    bu_re = jnp.einsum('blgh,gph->blgp', ug, bb_re)
    bu_im = jnp.einsum('blgh,gph->blgp', ug, bb_im)
    hr, hi = h_re.astype(f32), h_im.astype(f32)
    bu_re = bu_re.at[:, 0].add(ab_re * hr - ab_im * hi)
    bu_im = bu_im.at[:, 0].add(ab_re * hi + ab_im * hr)
    a_re = jnp.broadcast_to(ab_re, bu_re.shape)
    a_im = jnp.broadcast_to(ab_im, bu_im.shape)
    _, _, s_re, s_im = lax.associative_scan(_complex_combine, (a_re, a_im, bu_re, bu_im), axis=1)
    y = (jnp.einsum('blgp,ghp->blgh', s_re, c_re.astype(f32))
         - jnp.einsum('blgp,ghp->blgh', s_im, c_im.astype(f32)))
    y = y.reshape(Bsz, L, S5_WIDTH) + d_skip.astype(f32) * u.astype(f32)
    gy = jax.nn.gelu(y).astype(x.dtype)
    vg = gy @ w_out
    out = vg[..., :D_MODEL] * jax.nn.sigmoid(vg[..., D_MODEL:])
    return out, s_re[:, -1], s_im[:, -1]


def rglru_mixer(x, h0, conv_hist, w_in, conv_w, conv_b, w_gates, b_gates, lam, w_out):
    f32 = jnp.float32
    Bsz, L, _ = x.shape
    z = x @ w_in
    gate = jax.nn.gelu(z[..., :RG_WIDTH])
    xc, new_hist = causal_dwconv(z[..., RG_WIDTH:], conv_hist, conv_w, conv_b)
    xg = xc.reshape(Bsz, L, RG_BLOCKS, RG_BLOCK)
    gts = jnp.einsum('blnc,ncd->blnd', xg, w_gates)
    r = jax.nn.sigmoid((gts[..., :RG_BLOCK].reshape(Bsz, L, RG_WIDTH) + b_gates[:RG_WIDTH]).astype(f32))
    i = jax.nn.sigmoid((gts[..., RG_BLOCK:].reshape(Bsz, L, RG_WIDTH) + b_gates[RG_WIDTH:]).astype(f32))
    log_a = -RG_C * r * jax.nn.softplus(-lam.astype(f32))
    a = jnp.exp(log_a)
    mult = jnp.sqrt(-jnp.expm1(2.0 * log_a))
    b = mult * (i * xc.astype(f32))
    b = b.at[:, 0].add(a[:, 0] * h0.astype(f32))
    _, h = lax.associative_scan(_real_combine, (a, b), axis=1)
    y = (h.astype(x.dtype) * gate) @ w_out
    return y, h[:, -1], new_hist


def conv_ffn(x, hist, w_up, conv_w, conv_b, w_down):
    up = x @ w_up
    upc, new_hist = causal_dwconv(up, hist, conv_w, conv_b)
    y = (jax.nn.gelu(upc[..., FF_WIDTH:]) * upc[..., :FF_WIDTH]) @ w_down
    return y, new_hist


def _trunk(h, s5_re0, s5_im0, rg_h0, rg_conv0, ff_conv0, p):
    s5r, s5i, rgh, rgc, ffc = [], [], [], [], []
    for i in range(DEPTH):
        j = i // N_MIXERS
        if i % N_MIXERS == 0:
            mix, sr, si = s5_mixer(h, s5_re0[j], s5_im0[j], p['s5_w_in'][j], p['s5_lam_re'][j],
                                   p['s5_lam_im'][j], p['s5_log_step'][j], p['s5_b_re'][j],
                                   p['s5_b_im'][j], p['s5_c_re'][j], p['s5_c_im'][j],
                                   p['s5_d'][j], p['s5_w_out'][j])
            s5r.append(sr)
            s5i.append(si)
        else:
            mix, hr, cb = rglru_mixer(h, rg_h0[j], rg_conv0[j], p['rg_w_in'][j], p['rg_conv_w'][j],
                                      p['rg_conv_b'][j], p['rg_w_gates'][j], p['rg_b_gates'][j],
                                      p['rg_lam'][j], p['rg_w_out'][j])
            rgh.append(hr)
            rgc.append(cb)
        h = layer_norm(DN_ALPHA * h + mix, p['ln_g'][i, 0], p['ln_b'][i, 0])
        f, fb = conv_ffn(h, ff_conv0[i], p['ffn_w_up'][i], p['ffn_conv_w'][i], p['ffn_conv_b'][i],
                         p['ffn_w_down'][i])
        ffc.append(fb)
        h = layer_norm(DN_ALPHA * h + f, p['ln_g'][i, 1], p['ln_b'][i, 1])
    return h, jnp.stack(s5r), jnp.stack(s5i), jnp.stack(rgh), jnp.stack(rgc), jnp.stack(ffc)


def setup_inputs(seed: int = 0) -> dict:
    key = jax.random.key(seed)
    ks = list(jax.random.split(key, 40))
    nrm = lambda k, s, sc: jax.random.normal(k, s, jnp.float32) * sc
    E, G, P, H = S5_WIDTH, S5_GROUPS, S5_STATE, S5_GROUP
    R, F = RG_WIDTH, FF_WIDTH
    lam_im = jnp.pi * jnp.arange(P, dtype=jnp.float32)
    a_c = jax.random.uniform(ks[20], (N_LAYERS_B, R), jnp.float32, 0.9, 0.999)
    s = a_c ** (1.0 / RG_C)
    s5_w_out = jnp.concatenate([nrm(ks[18], (N_LAYERS_A, E, D_MODEL), E ** -0.5 * DN_BETA),
                                nrm(ks[19], (N_LAYERS_A, E, D_MODEL), E ** -0.5)], axis=-1)
    return {
        "x_prompt": nrm(ks[0], (BATCH, SEQ, D_MODEL), 1.0),
        "x_sample": nrm(ks[1], (DEC_BATCH, DEC_SEQ, D_MODEL), 1.0),
        "state_s5_re": nrm(ks[2], (N_LAYERS_A, DEC_BATCH, G, P), 0.1),
        "state_s5_im": nrm(ks[3], (N_LAYERS_A, DEC_BATCH, G, P), 0.1),
        "state_rg_h": nrm(ks[4], (N_LAYERS_B, DEC_BATCH, R), 0.5),
        "state_rg_conv": nrm(ks[5], (N_LAYERS_B, DEC_BATCH, RG_CONV - 1, R), 1.0),
        "state_ffn_conv": nrm(ks[6], (DEPTH, DEC_BATCH, FF_CONV - 1, 2 * F), 1.0),
        "meta_tokens": nrm(ks[7], (N_META, D_MODEL), 1.0),
        "s5_w_in": nrm(ks[8], (N_LAYERS_A, D_MODEL, E), D_MODEL ** -0.5),
        "s5_lam_re": -0.5 + nrm(ks[9], (N_LAYERS_A, G, P), 0.01),
        "s5_lam_im": lam_im + nrm(ks[10], (N_LAYERS_A, G, P), 0.01),
        "s5_log_step": jax.random.uniform(ks[11], (N_LAYERS_A, G), jnp.float32,
                                          math.log(0.001), math.log(0.1)),
        "s5_b_re": nrm(ks[12], (N_LAYERS_A, G, P, H), (2 * H) ** -0.5),
        "s5_b_im": nrm(ks[13], (N_LAYERS_A, G, P, H), (2 * H) ** -0.5),
        "s5_c_re": nrm(ks[14], (N_LAYERS_A, G, H, P), 0.5),
        "s5_c_im": nrm(ks[15], (N_LAYERS_A, G, H, P), 0.5),
        "s5_d": nrm(ks[16], (N_LAYERS_A, E), 1.0),
        "s5_w_out": s5_w_out,
        "rg_w_in": nrm(ks[21], (N_LAYERS_B, D_MODEL, 2 * R), D_MODEL ** -0.5),
        "rg_conv_w": nrm(ks[22], (N_LAYERS_B, RG_CONV, R), RG_CONV ** -0.5),
        "rg_conv_b": nrm(ks[23], (N_LAYERS_B, R), 0.01),
        "rg_w_gates": nrm(ks[24], (N_LAYERS_B, RG_BLOCKS, RG_BLOCK, 2 * RG_BLOCK), RG_BLOCK ** -0.5),
        "rg_b_gates": nrm(ks[25], (N_LAYERS_B, 2 * R), 0.01),
        "rg_lam": jnp.log(s) - jnp.log1p(-s),
        "rg_w_out": nrm(ks[26], (N_LAYERS_B, R, D_MODEL), R ** -0.5 * DN_BETA),
        "ffn_w_up": nrm(ks[27], (DEPTH, D_MODEL, 2 * F), D_MODEL ** -0.5),
        "ffn_conv_w": nrm(ks[28], (DEPTH, FF_CONV, 2 * F), FF_CONV ** -0.5),
        "ffn_conv_b": nrm(ks[29], (DEPTH, 2 * F), 0.01),
        "ffn_w_down": nrm(ks[30], (DEPTH, F, D_MODEL), F ** -0.5 * DN_BETA),
        "ln_g": 1.0 + nrm(ks[31], (DEPTH, 2, D_MODEL), 0.02),
        "ln_b": nrm(ks[32], (DEPTH, 2, D_MODEL), 0.02),
    }


def reference(x_prompt, x_sample, state_s5_re, state_s5_im, state_rg_h, state_rg_conv, state_ffn_conv,
              meta_tokens, s5_w_in, s5_lam_re, s5_lam_im, s5_log_step, s5_b_re, s5_b_im, s5_c_re,
              s5_c_im, s5_d, s5_w_out, rg_w_in, rg_conv_w, rg_conv_b, rg_w_gates, rg_b_gates, rg_lam,
              rg_w_out, ffn_w_up, ffn_conv_w, ffn_conv_b, ffn_w_down, ln_g, ln_b):
    p = dict(s5_w_in=s5_w_in, s5_lam_re=s5_lam_re, s5_lam_im=s5_lam_im, s5_log_step=s5_log_step,
             s5_b_re=s5_b_re, s5_b_im=s5_b_im, s5_c_re=s5_c_re, s5_c_im=s5_c_im, s5_d=s5_d,
             s5_w_out=s5_w_out, rg_w_in=rg_w_in, rg_conv_w=rg_conv_w, rg_conv_b=rg_conv_b,
             rg_w_gates=rg_w_gates, rg_b_gates=rg_b_gates, rg_lam=rg_lam, rg_w_out=rg_w_out,
             ffn_w_up=ffn_w_up, ffn_conv_w=ffn_conv_w, ffn_conv_b=ffn_conv_b, ffn_w_down=ffn_w_down,
             ln_g=ln_g, ln_b=ln_b)
    bp = x_prompt.shape[0]
    dt = x_prompt.dtype
    h_p = jnp.concatenate([jnp.broadcast_to(meta_tokens[None].astype(dt), (bp, N_META, D_MODEL)),
                           x_prompt], axis=1)
    z_s5 = jnp.zeros((N_LAYERS_A, bp, S5_GROUPS, S5_STATE), jnp.float32)
    z_rgh = jnp.zeros((N_LAYERS_B, bp, RG_WIDTH), jnp.float32)
    z_rgc = jnp.zeros((N_LAYERS_B, bp, RG_CONV - 1, RG_WIDTH), dt)
    z_ffc = jnp.zeros((DEPTH, bp, FF_CONV - 1, 2 * FF_WIDTH), dt)
    yp, s5r_p, s5i_p, rgh_p, rgc_p, ffc_p = _trunk(h_p, z_s5, z_s5, z_rgh, z_rgc, z_ffc, p)
    ys, s5r_s, s5i_s, rgh_s, rgc_s, ffc_s = _trunk(x_sample, state_s5_re, state_s5_im, state_rg_h,
                                                    state_rg_conv, state_ffn_conv, p)
    return (yp[:, N_META:], ys, s5r_p, s5i_p, rgh_p, rgc_p, ffc_p, s5r_s, s5i_s, rgh_s, rgc_s, ffc_s)
```

```python
import math
from contextlib import ExitStack
import numpy as np
import concourse.bass as bass
import concourse.mybir as mybir
from concourse.bass_utils import run_bass_kernel_spmd

F32 = mybir.dt.float32
BF16 = mybir.dt.bfloat16
I32 = mybir.dt.int32
AF = mybir.ActivationFunctionType
ALU = mybir.AluOpType

D = 1024
NB = 8
FF = 2816
FF2 = 5632
FB = 44
FBH = 22
DEPTH = 4
SEQ = 2048
NMETA = 16
DN_ALPHA = (2 * DEPTH) ** 0.25
LN_EPS = 1e-5
RG_C = 8.0
TWO_PI = 2.0 * math.pi


class Res:
    __slots__ = ("name", "w", "r")

    def __init__(self, name=""):
        self.name = name
        self.w = None
        self.r = {}


class DSem:
    def __init__(self, sem):
        self.sem = sem
        self.val = 0


class Q:
    def __init__(self, name, eng, sem, eager):
        self.name = name
        self.eng = eng
        self.sem = sem
        self.eager = eager
        self.n = 0
        self.last = None
        self.last_ms = True
        self.ms = []
        self.semval = 0
        self.known = {}


class KB:
    def __init__(self, nc, es):
        self.nc = nc
        self.es = es
        self.q = {}
        for name, eng, eager in (("pe", nc.tensor, False), ("act", nc.scalar, False),
                                 ("dve", nc.vector, False), ("pool", nc.gpsimd, True),
                                 ("sp", nc.sync, True)):
            self.q[name] = Q(name, eng, self.sem("q_" + name), eager)
        self.dsems = []
        self.nsem = 0

    def sem(self, name):
        return self.es.enter_context(self.nc.semaphore(name))

    def dsem(self, name):
        d = DSem(self.sem("d_" + name))
        self.dsems.append(d)
        return d

    def sb(self, name, shape, dt, es=None):
        return (es or self.es).enter_context(self.nc.sbuf_tensor("sb_" + name, list(shape), dt))

    def ps(self, name, shape, dt=F32, es=None):
        return (es or self.es).enter_context(self.nc.psum_tensor("pt_" + name, list(shape), dt))

    def _milestone(self, A, k):
        lo, hi = 0, len(A.ms)
        while lo < hi:
            mid = (lo + hi) // 2
            if A.ms[mid][0] >= k:
                hi = mid
            else:
                lo = mid + 1
        if lo < len(A.ms):
            return A.ms[lo][1]
        assert A.last is not None and not A.last_ms and A.n >= k
        A.semval += 1
        A.last.then_inc(A.sem, 1)
        A.last_ms = True
        A.ms.append((A.n, A.semval))
        return A.semval

    def _wait(self, q, deps):
        need = {}
        for d in deps:
            if d is None:
                continue
            if d[0] == "q":
                A, k = d[1], d[2]
                if A is q and q.name == "pe":
                    continue
                v = self._milestone(A, k)
                key = A
                sem = A.sem
            else:
                key = d[1]
                sem = d[1].sem
                v = d[2]
            if q.known.get(key, 0) >= v:
                continue
            if need.get(key, (None, 0))[1] < v:
                need[key] = (sem, v)
        for key, (sem, v) in need.items():
            q.eng.wait_ge(sem, v)
            q.known[key] = v

    def _deps(self, reads, writes):
        deps = []
        for r in reads:
            if r.w is not None:
                deps.append(r.w)
        for w in writes:
            if w.w is not None:
                deps.append(w.w)
            deps.extend(w.r.values())
        return deps

    def op(self, qn, reads, writes, fn):
        q = self.q[qn]
        self._wait(q, self._deps(reads, writes))
        ins = fn(q.eng)
        q.n += 1
        q.last = ins
        q.last_ms = False
        if q.eager:
            q.semval += 1
            ins.then_inc(q.sem, 1)
            q.last_ms = True
            q.ms.append((q.n, q.semval))
        me = ("q", q, q.n)
        for r in reads:
            r.r[q] = me
        for w in writes:
            w.w = me
            w.r = {}
        return ins

    def dma(self, qn, out, in_, reads, writes, ds, **kw):
        q = self.q[qn]
        self._wait(q, self._deps(reads, writes))
        ins = q.eng.dma_start(out=out, in_=in_, **kw)
        ds.val += 16
        ins.then_inc(ds.sem, 16)
        me = ("d", ds, ds.val)
        for r in reads:
            r.r[ds] = me
        for w in writes:
            w.w = me
            w.r = {}
        return ins

    def group_final(self, ress, ds):
        for r in ress:
            r.w = ("d", ds, ds.val)

    def barrier(self, exclude=()):
        marks = []
        for A in self.q.values():
            if A.n > 0:
                marks.append(("q", A, A.n))
        for d in self.dsems:
            if d in exclude:
                continue
            if d.val > 0:
                marks.append(("d", d, d.val))
        for q in self.q.values():
            self._wait(q, [m for m in marks if not (m[0] == "q" and m[1] is q)])

    def finish(self):
        self.barrier()


def build(nc, dbg=None):
    es = ExitStack()
    kb = KB(nc, es)
    with es:
        _build(nc, kb, dbg)
    return nc


def _dram_in(nc, name, shape, dt=F32):
    return nc.dram_tensor(name, list(shape), dt, kind="ExternalInput").ap()


def _dram_out(nc, name, shape, dt=F32):
    return nc.dram_tensor(name, list(shape), dt, kind="ExternalOutput").ap()


def _build(nc, kb, dbg):
    op, dma = kb.op, kb.dma
    I = {}
    I["x_prompt"] = _dram_in(nc, "x_prompt", [SEQ, D])
    I["x_sample"] = _dram_in(nc, "x_sample", [128, D])
    I["state_s5_re"] = _dram_in(nc, "state_s5_re", [2, 16, 4096])
    I["state_s5_im"] = _dram_in(nc, "state_s5_im", [2, 16, 4096])
    I["state_rg_h"] = _dram_in(nc, "state_rg_h", [2, 16, D])
    I["state_rg_conv"] = _dram_in(nc, "state_rg_conv", [2, 48, D])
    I["state_ffn_conv"] = _dram_in(nc, "state_ffn_conv", [4, 32, FF2])
    I["meta_tokens"] = _dram_in(nc, "meta_tokens", [NMETA, D])
    I["s5_w_in"] = _dram_in(nc, "s5_w_in", [2, D, D])
    I["s5_lam_re"] = _dram_in(nc, "s5_lam_re", [2, 64, 64])
    I["s5_lam_im"] = _dram_in(nc, "s5_lam_im", [2, 64, 64])
    I["s5_log_step"] = _dram_in(nc, "s5_log_step", [2, 64])
    I["s5_b_re"] = _dram_in(nc, "s5_b_re", [2, 64, 64, 16])
    I["s5_b_im"] = _dram_in(nc, "s5_b_im", [2, 64, 64, 16])
    I["s5_c_re"] = _dram_in(nc, "s5_c_re", [2, 64, 16, 64])
    I["s5_c_im"] = _dram_in(nc, "s5_c_im", [2, 64, 16, 64])
    I["s5_d"] = _dram_in(nc, "s5_d", [2, D])
    I["s5_w_out"] = _dram_in(nc, "s5_w_out", [2, D, 2 * D])
    I["rg_w_in"] = _dram_in(nc, "rg_w_in", [2, D, 2 * D])
    I["rg_conv_w"] = _dram_in(nc, "rg_conv_w", [2, 4, D])
    I["rg_conv_b"] = _dram_in(nc, "rg_conv_b", [2, D])
    I["rg_w_gates"] = _dram_in(nc, "rg_w_gates", [2, 4, 256, 512])
    I["rg_b_gates"] = _dram_in(nc, "rg_b_gates", [2, 2 * D])
    I["rg_lam"] = _dram_in(nc, "rg_lam", [2, D])
    I["rg_w_out"] = _dram_in(nc, "rg_w_out", [2, D, D])
    I["ffn_w_up"] = _dram_in(nc, "ffn_w_up", [4, D, FF2])
    I["ffn_conv_w"] = _dram_in(nc, "ffn_conv_w", [4, 3, FF2])
    I["ffn_conv_b"] = _dram_in(nc, "ffn_conv_b", [4, FF2])
    I["ffn_w_down"] = _dram_in(nc, "ffn_w_down", [4, FF, D])
    I["ln_g"] = _dram_in(nc, "ln_g", [4, 2, D])
    I["ln_b"] = _dram_in(nc, "ln_b", [4, 2, D])
    I["ident"] = _dram_in(nc, "ident", [128, 128])
    I["bmask"] = _dram_in(nc, "bmask", [128, 128])
    I["sel2"] = _dram_in(nc, "sel2", [2, 128])

    O = {}
    O["y_prompt"] = _dram_out(nc, "y_prompt", [SEQ, D])
    O["y_sample"] = _dram_out(nc, "y_sample", [128, D])
    O["s5_re_p"] = _dram_out(nc, "s5_re_p", [2, 1, 4096])
    O["s5_im_p"] = _dram_out(nc, "s5_im_p", [2, 1, 4096])
    O["rg_h_p"] = _dram_out(nc, "rg_h_p", [2, 1, D])
    O["rg_conv_p"] = _dram_out(nc, "rg_conv_p", [2, 3, D])
    O["ffn_conv_p"] = _dram_out(nc, "ffn_conv_p", [4, 2, FF2])
    O["s5_re_s"] = _dram_out(nc, "s5_re_s", [2, 16, 4096])
    O["s5_im_s"] = _dram_out(nc, "s5_im_s", [2, 16, 4096])
    O["rg_h_s"] = _dram_out(nc, "rg_h_s", [2, 16, D])
    O["rg_conv_s"] = _dram_out(nc, "rg_conv_s", [2, 48, D])
    O["ffn_conv_s"] = _dram_out(nc, "ffn_conv_s", [4, 32, FF2])

    WXd = [nc.dram_tensor("WXd%d" % j, [128, 8, 8, 2, 128], BF16, kind="Internal").ap() for j in range(2)]
    WYd = [nc.dram_tensor("WYd%d" % j, [128, 32, 8, 2, 32], BF16, kind="Internal").ap() for j in range(2)]
    KId = [nc.dram_tensor("KId%d" % j, [128, 8, 8, 128], BF16, kind="Internal").ap() for j in range(2)]

    NBLK = 2 * NB + 2 * 2 * NB + 4 * FB
    ctx_w = dict(
        WBLK=nc.dram_tensor("WBLK", [NBLK, 128, NB * 128], BF16, kind="Internal").ap(),
        WDd=[nc.dram_tensor("WDd%d" % l, [128, FBH * D], BF16, kind="Internal").ap() for l in range(4)],
        S5WOd=[nc.dram_tensor("S5WOd%d" % j, [128, NB * 2 * D], BF16, kind="Internal").ap() for j in range(2)],
        RGWOd=[nc.dram_tensor("RGWOd%d" % j, [128, NB * D], BF16, kind="Internal").ap() for j in range(2)],
        RGWGd=[nc.dram_tensor("RGWGd%d" % j, [128, 4 * 2 * 512], BF16, kind="Internal").ap() for j in range(2)],
    )
    sb = kb.sb
    ident = sb("ident", [128, 128], F32)
    bmask = sb("bmask", [128, 128], F32)
    R_const = Res("const")
    ds_c = kb.dsem("const")
    dma("sp", ident[:], I["ident"][:, :], [], [R_const], ds_c)
    dma("sp", bmask[:], I["bmask"][:, :], [], [R_const], ds_c)
    sel2 = sb("sel2", [2, 128], F32)
    dma("sp", sel2[:], I["sel2"][:, :], [], [R_const], ds_c)

    R_par = Res("par")
    ds_p = kb.dsem("par")

    par_loads = []

    def load_cols(name, src_1d, nblk, ds):
        t = sb(name, [128, nblk], F32)
        par_loads.append((t, src_1d))
        return t

    def issue_par_loads():
        with nc.allow_non_contiguous_dma(reason="small param"):
            for t, src_1d in par_loads:
                dma("sp", t[:], src_1d.rearrange("(b p) -> p b", p=128), [], [R_par], ds_p)

    ffn_cw = [[load_cols("fcw%d_%d" % (l, k), I["ffn_conv_w"][l, k], FB, ds_c) for k in range(3)] for l in range(4)]
    ffn_cb = [load_cols("fcb%d" % l, I["ffn_conv_b"][l], FB, ds_c) for l in range(4)]
    rg_cw = [[load_cols("rcw%d_%d" % (j, k), I["rg_conv_w"][j, k], NB, ds_c) for k in range(4)] for j in range(2)]
    rg_cb = [load_cols("rcb%d" % j, I["rg_conv_b"][j], NB, ds_c) for j in range(2)]
    rg_bg = [load_cols("rbg%d" % j, I["rg_b_gates"][j], 16, ds_c) for j in range(2)]
    rg_lm = [load_cols("rlm%d" % j, I["rg_lam"][j], NB, ds_c) for j in range(2)]
    s5_dd = [load_cols("s5d%d" % j, I["s5_d"][j], NB, ds_c) for j in range(2)]
    rg_m8sp = [sb("m8sp%d" % j, [128, NB], F32) for j in range(2)]
    A8re = [sb("A8re%d" % j, [128, 32], F32) for j in range(2)]
    A8im = [sb("A8im%d" % j, [128, 32], F32) for j in range(2)]
    A8imn = [sb("A8imn%d" % j, [128, 32], F32) for j in range(2)]
    ffn_hist_p = [sb("fhp%d" % l, [128, FB, 1, 2], F32) for l in range(4)]
    rg_hist_p = [sb("rhp%d" % j, [128, NB, 1, 3], F32) for j in range(2)]
    rg_h_p = [sb("rgh%d" % j, [128, NB, 1], F32) for j in range(2)]
    s5st_p = [sb("s5p%d" % j, [128, 32, 2, 1], F32) for j in range(2)]
    R_state = Res("state")

    PS = [kb.ps("ps%d" % i, [128, 512]) for i in range(8)]
    RP = [Res("ps%d" % i) for i in range(8)]

    for j in range(2):
        pass
    for t in ffn_hist_p + rg_hist_p + rg_h_p + s5st_p:
        op("dve", [], [R_state], lambda e, t=t: e.memset(t[:], 0.0))

    ctx = dict(nc=nc, kb=kb, I=I, O=O, sel2=sel2, R_par=R_par, WXd=WXd, WYd=WYd, KId=KId, ident=ident, bmask=bmask,
               R_const=R_const, PS=PS, RP=RP, ffn_cw=ffn_cw, ffn_cb=ffn_cb, rg_cw=rg_cw, rg_cb=rg_cb,
               rg_bg=rg_bg, rg_m8sp=rg_m8sp, s5_dd=s5_dd, A8re=A8re, A8im=A8im, A8imn=A8imn,
               ffn_hist_p=ffn_hist_p, rg_hist_p=rg_hist_p, rg_h_p=rg_h_p, s5st_p=s5st_p,
               R_state=R_state, dbg=dbg, ds_out=kb.dsem("out"), ds_w=None, ds_wb=kb.dsem("wb"), **ctx_w)

    with ExitStack() as lds:
        with ExitStack() as nat:
            gens = [s5_prologue_loads(ctx, j, lds, nat) for j in range(2)]
            for g in gens:
                next(g)
            lds_ = [next(g) for g in gens]
            issue_par_loads()
            kb.barrier(exclude=(ds_p,))
        for j in range(2):
            with ExitStack() as pes:
                s5_prologue(ctx, j, pes, lds_[j])
                kb.barrier(exclude=(ds_p,) if j == 0 else ())
    kb.group_final([R_par], ds_p)
    for j in range(2):
        op("act", [R_par], [R_par], lambda e, j=j: e.activation(out=rg_m8sp[j][:], in_=rg_lm[j][:], func=AF.Exp, scale=-1.0))
        op("act", [R_par], [R_par], lambda e, j=j: e.activation(out=rg_m8sp[j][:], in_=rg_m8sp[j][:], func=AF.Ln, bias=1.0))
        op("dve", [R_par], [R_par], lambda e, j=j: e.tensor_scalar(out=rg_m8sp[j][:], in0=rg_m8sp[j][:], scalar1=-RG_C, scalar2=None, op0=ALU.mult))
    kb.barrier()
    passes = [("A", 1, 1024), ("B", 1, 1040), ("S", 16, 8)]
    for (pn, nseq, L) in passes:
        with ExitStack() as pes:
            run_pass(ctx, pn, nseq, L, pes)
            kb.barrier()
    kb.finish()


def colgroups(N, g=512):
    return [(c0, min(g, N - c0)) for c0 in range(0, N, g)]


def store_T(ctx, src_fn, nblk, m, dram_rows, stage, R_src, tag):
    kb, PS, RP, ident = ctx["kb"], ctx["PS"], ctx["RP"], ctx["ident"]
    R_stage = ctx.setdefault("stage_res", {}).setdefault(id(stage), Res("stage" + tag))
    GB = stage.shape[1] // 128
    for g0 in range(0, nblk, GB):
        ng = min(GB, nblk - g0)
        for b0 in range(g0, g0 + ng, 4):
            nb_ = min(4, g0 + ng - b0)
            bank = 6 + ((b0 // 4) % 2)
            for b in range(nb_):
                kb.op("pe", [R_src, ctx["R_const"]], [RP[bank]],
                      lambda e, b=b, b0=b0, bank=bank: e.transpose(out=PS[bank][0:m, b * 128:(b + 1) * 128], in_=src_fn(b0 + b), identity=ident[:, :]))
            kb.op("act", [RP[bank]], [R_stage],
                  lambda e, b0=b0, nb_=nb_, bank=bank, g0=g0: e.copy(out=stage[0:m, (b0 - g0) * 128:(b0 - g0 + nb_) * 128], in_=PS[bank][0:m, 0:nb_ * 128]))
        kb.dma("sp", dram_rows[:, g0 * 128:(g0 + ng) * 128], stage[0:m, 0:ng * 128], [R_stage], [], ctx["ds_out"])


def load_T(ctx, dram_rows, nblk, m, dst_fn, stage, R_dst, tag, ds):
    kb, PS, RP, ident = ctx["kb"], ctx["PS"], ctx["RP"], ctx["ident"]
    R_stage = ctx.setdefault("stage_res", {}).setdefault(id(stage), Res("lstage" + tag))
    ctx["uid"] = ctx.get("uid", 0) + 1
    ds = kb.dsem("lt%d" % ctx["uid"])
    kb.dma("sp", stage[0:m, 0:nblk * 128], dram_rows, [], [R_stage], ds)
    per = 512 // m
    gi = 0
    for b0 in range(0, nblk, per):
        nb_ = min(per, nblk - b0)
        bank = 6 + (gi % 2)
        gi += 1
        for b in range(nb_):
            kb.op("pe", [R_stage, ctx["R_const"]], [RP[bank]],
                  lambda e, b=b: e.transpose(out=PS[bank][:, b * m:(b + 1) * m], in_=stage[0:m, (b0 + b) * 128:(b0 + b + 1) * 128], identity=ident[0:m, 0:m]))
        kb.op("act", [RP[bank]], [R_dst],
              lambda e: e.copy(out=dst_fn(b0, nb_), in_=PS[bank][:, 0:nb_ * m].rearrange("p (b m) -> p b m", m=m)))


def s5_prologue_loads(ctx, j, pes, nat):
    nc, kb, I = ctx["nc"], ctx["kb"], ctx["I"]
    dma, op = kb.dma, kb.op
    PS, RP, ident, Rc = ctx["PS"], ctx["RP"], ctx["ident"], ctx["R_const"]
    sb = lambda n, s, d=F32: kb.sb("pl%d_%s" % (j, n), s, d, es=pes)
    ds = kb.dsem("pl%d" % j)
    R = Res("pl")
    Rn = Res("plnat")
    lr = sb("lr", [128, 32]); li = sb("li", [128, 32]); stp = sb("stp", [128, 32])
    Bre = sb("Bre", [128, 32, 16]); Bim = sb("Bim", [128, 32, 16])
    yield None
    sbn = lambda n, s, d=F32: kb.sb("pl%d_%s" % (j, n), s, d, es=nat)
    lrn = sbn("lrn", [32, 128]); lin = sbn("lin", [32, 128]); stn = sbn("stn", [2, 32])
    Bn = [sbn("Bn%d" % i, [32, 128, 16]) for i in range(2)]
    dma("sp", lrn[:], I["s5_lam_re"][j].rearrange("(pr g2) p -> pr (g2 p)", g2=2), [], [Rn], ds)
    dma("sp", lin[:], I["s5_lam_im"][j].rearrange("(pr g2) p -> pr (g2 p)", g2=2), [], [Rn], ds)
    with nc.allow_non_contiguous_dma(reason="tiny"):
        dma("sp", stn[:], I["s5_log_step"][j].rearrange("(pr g2) -> g2 pr", g2=2), [], [Rn], ds)
    dma("sp", Bn[0][:], I["s5_b_re"][j].rearrange("(pr g2) p h -> pr (g2 p) h", g2=2), [], [Rn], ds)
    dma("sp", Bn[1][:], I["s5_b_im"][j].rearrange("(pr g2) p h -> pr (g2 p) h", g2=2), [], [Rn], ds)
    kb.group_final([Rn], ds)
    bank = 4 + j
    op("pe", [Rn, Rc], [RP[bank]], lambda e: e.transpose(out=PS[bank][:, 0:32], in_=lrn[:, :], identity=ident[0:32, 0:32]))
    op("pe", [Rn, Rc], [RP[bank]], lambda e: e.transpose(out=PS[bank][:, 32:64], in_=lin[:, :], identity=ident[0:32, 0:32]))
    op("pe", [Rn, Rc], [RP[bank]], lambda e: e.matmul(PS[bank][:, 64:96], lhsT=ctx["sel2"][:, :], rhs=stn[:, :], start=True, stop=True))
    op("act", [RP[bank]], [R], lambda e: e.copy(out=lr[:], in_=PS[bank][:, 0:32]))
    op("act", [RP[bank]], [R], lambda e: e.copy(out=li[:], in_=PS[bank][:, 32:64]))
    op("act", [RP[bank]], [R], lambda e: e.copy(out=stp[:], in_=PS[bank][:, 64:96]))
    for i, Bd in enumerate((Bre, Bim)):
        for h in range(16):
            op("pe", [Rn, Rc], [RP[bank]], lambda e, h=h, i=i: e.transpose(out=PS[bank][:, h * 32:(h + 1) * 32], in_=Bn[i][:, :, h], identity=ident[0:32, 0:32]))
        op("act", [RP[bank]], [R], lambda e, Bd=Bd: e.copy(out=Bd[:, :, :].rearrange("p r h -> p h r"), in_=PS[bank][:, :].rearrange("p (h r) -> p h r", r=32)))
    yield dict(R=R, ds=ds, lr=lr, li=li, stp=stp, Bre=Bre, Bim=Bim)


def s5_prologue(ctx, j, pes, ld):
    nc, kb, I = ctx["nc"], ctx["kb"], ctx["I"]
    op, dma = kb.op, kb.dma
    PS, RP, ident, bmask = ctx["PS"], ctx["RP"], ctx["ident"], ctx["bmask"]
    Rc = ctx["R_const"]
    sb = lambda n, s, d=F32: kb.sb("pl%d_%s" % (j, n), s, d, es=pes)
    R, ds = ld["R"], ld["ds"]
    lr, li, stp, Bre, Bim = (ld[k] for k in ("lr", "li", "stp", "Bre", "Bim"))
    Cre = sb("Cre", [128, 32, 16]); Cim = sb("Cim", [128, 32, 16])
    Ch_re = sb("Chre", [16, 64, 64]); Ch_im = sb("Chim", [16, 64, 64])
    R_ch = Res("ch")
    ds_ch = kb.dsem("plc%d" % j)
    with nc.allow_non_contiguous_dma(reason="param layout"):
        dma("sp", Ch_re[:], I["s5_c_re"][j].rearrange("g h p -> h g p"), [], [R_ch], ds_ch)
        dma("sp", Ch_im[:], I["s5_c_im"][j].rearrange("g h p -> h g p"), [], [R_ch], ds_ch)
    kb.group_final([R_ch], ds_ch)
    for (Ch, Cd) in ((Ch_re, Cre), (Ch_im, Cim)):
        for g0 in range(0, 32, 16):
            bank = 4 + (g0 // 16)
            for pr in range(g0, g0 + 16):
                op("pe", [R_ch, Rc], [RP[bank]],
                   lambda e, pr=pr, Ch=Ch: e.transpose(out=PS[bank][:, (pr - g0) * 16:(pr - g0 + 1) * 16],
                                                       in_=Ch[0:16, 2 * pr:2 * pr + 2, :].rearrange("h g p -> h (g p)"),
                                                       identity=ident[0:16, 0:16]))
            op("act", [RP[bank]], [R], lambda e, Cd=Cd: e.copy(out=Cd[:, g0:g0 + 16, :], in_=PS[bank][:, 0:256].rearrange("p (a h) -> p a h", h=16)))

    t = lambda n: sb(n, [128, 32])
    ang = t("ang"); mag = t("mag"); sn = t("sn"); cs = t("cs"); tmp = t("tmp"); tmp2 = t("tmp2")
    abre = t("abre"); abim = t("abim"); qre = t("qre"); qim = t("qim"); ki = sb("ki", [128, 32], I32)
    V = lambda f: op("dve", [R], [R], f)
    A = lambda f: op("act", [R], [R], f)
    A(lambda e: e.activation(out=stp[:], in_=stp[:], func=AF.Exp))
    V(lambda e: e.tensor_tensor(out=ang[:], in0=li[:], in1=stp[:], op=ALU.mult))
    V(lambda e: e.tensor_tensor(out=mag[:], in0=lr[:], in1=stp[:], op=ALU.mult))
    A(lambda e: e.activation(out=mag[:], in_=mag[:], func=AF.Exp))

    def sin_of(dst, shift):
        V(lambda e: e.tensor_scalar(out=tmp[:], in0=ang[:], scalar1=shift, scalar2=1.0 / TWO_PI, op0=ALU.add, op1=ALU.mult))
        V(lambda e: e.tensor_copy(out=ki[:], in_=tmp[:]))
        V(lambda e: e.tensor_copy(out=tmp2[:], in_=ki[:]))
        V(lambda e: e.tensor_tensor(out=tmp[:], in0=tmp[:], in1=tmp2[:], op=ALU.subtract))
        V(lambda e: e.tensor_scalar(out=tmp2[:], in0=tmp[:], scalar1=0.5, scalar2=None, op0=ALU.is_gt))
        V(lambda e: e.tensor_tensor(out=tmp[:], in0=tmp[:], in1=tmp2[:], op=ALU.subtract))
        V(lambda e: e.tensor_scalar(out=tmp2[:], in0=tmp[:], scalar1=-0.5, scalar2=None, op0=ALU.is_lt))
        V(lambda e: e.tensor_tensor(out=tmp[:], in0=tmp[:], in1=tmp2[:], op=ALU.add))
        V(lambda e: e.tensor_scalar(out=tmp[:], in0=tmp[:], scalar1=TWO_PI, scalar2=math.pi, op0=ALU.mult, op1=ALU.min))
        V(lambda e: e.tensor_scalar(out=tmp[:], in0=tmp[:], scalar1=-math.pi, scalar2=None, op0=ALU.max))
        A(lambda e: e.activation(out=dst[:], in_=tmp[:], func=AF.Sin))

    sin_of(sn, 0.0)
    sin_of(cs, math.pi / 2)
    V(lambda e: e.tensor_tensor(out=abre[:], in0=mag[:], in1=cs[:], op=ALU.mult))
    V(lambda e: e.tensor_tensor(out=abim[:], in0=mag[:], in1=sn[:], op=ALU.mult))
    den = t("den"); nr = t("nr")
    V(lambda e: e.tensor_tensor(out=den[:], in0=lr[:], in1=lr[:], op=ALU.mult))
    V(lambda e: e.tensor_tensor(out=tmp[:], in0=li[:], in1=li[:], op=ALU.mult))
    V(lambda e: e.tensor_tensor(out=den[:], in0=den[:], in1=tmp[:], op=ALU.add))
    V(lambda e: e.reciprocal(out=den[:], in_=den[:]))
    V(lambda e: e.tensor_scalar(out=nr[:], in0=abre[:], scalar1=-1.0, scalar2=None, op0=ALU.add))
    V(lambda e: e.tensor_tensor(out=qre[:], in0=nr[:], in1=lr[:], op=ALU.mult))
    V(lambda e: e.tensor_tensor(out=tmp[:], in0=abim[:], in1=li[:], op=ALU.mult))
    V(lambda e: e.tensor_tensor(out=qre[:], in0=qre[:], in1=tmp[:], op=ALU.add))
    V(lambda e: e.tensor_tensor(out=qre[:], in0=qre[:], in1=den[:], op=ALU.mult))
    V(lambda e: e.tensor_tensor(out=qim[:], in0=abim[:], in1=lr[:], op=ALU.mult))
    V(lambda e: e.tensor_tensor(out=tmp[:], in0=nr[:], in1=li[:], op=ALU.mult))
    V(lambda e: e.tensor_tensor(out=qim[:], in0=qim[:], in1=tmp[:], op=ALU.subtract))
    V(lambda e: e.tensor_tensor(out=qim[:], in0=qim[:], in1=den[:], op=ALU.mult))
    pwr = sb("pwr", [128, 9, 32]); pwi = sb("pwi", [128, 9, 32])
    V(lambda e: e.memset(pwr[:, 0, :], 1.0))
    V(lambda e: e.memset(pwi[:, 0, :], 0.0))
    for k in range(1, 9):
        V(lambda e, k=k: e.tensor_tensor(out=pwr[:, k, :], in0=pwr[:, k - 1, :], in1=abre[:], op=ALU.mult))
        V(lambda e, k=k: e.tensor_tensor(out=tmp[:], in0=pwi[:, k - 1, :], in1=abim[:], op=ALU.mult))
        V(lambda e, k=k: e.tensor_tensor(out=pwr[:, k, :], in0=pwr[:, k, :], in1=tmp[:], op=ALU.subtract))
        V(lambda e, k=k: e.tensor_tensor(out=pwi[:, k, :], in0=pwr[:, k - 1, :], in1=abim[:], op=ALU.mult))
        V(lambda e, k=k: e.tensor_tensor(out=tmp[:], in0=pwi[:, k - 1, :], in1=abre[:], op=ALU.mult))
        V(lambda e, k=k: e.tensor_tensor(out=pwi[:, k, :], in0=pwi[:, k, :], in1=tmp[:], op=ALU.add))
    Rst = ctx["R_state"]
    op("dve", [R], [Rst], lambda e: e.tensor_copy(out=ctx["A8re"][j][:], in_=pwr[:, 8, :]))
    op("dve", [R], [Rst], lambda e: e.tensor_copy(out=ctx["A8im"][j][:], in_=pwi[:, 8, :]))
    op("dve", [R], [Rst], lambda e: e.tensor_scalar(out=ctx["A8imn"][j][:], in0=pwi[:, 8, :], scalar1=-1.0, scalar2=None, op0=ALU.mult))

    def bc(x2d):
        return x2d.unsqueeze(2).broadcast_to([128, 32, 16])

    T3 = lambda n: sb(n, [128, 32, 16])
    w1 = T3("w1"); w2 = T3("w2")

    def cmul(dre, dim, xre, xim, sre, sim_):
        V(lambda e: e.tensor_tensor(out=w1[:], in0=xre, in1=bc(sre), op=ALU.mult))
        V(lambda e: e.tensor_tensor(out=w2[:], in0=xim, in1=bc(sim_), op=ALU.mult))
        V(lambda e: e.tensor_tensor(out=dre, in0=w1[:], in1=w2[:], op=ALU.subtract))
        V(lambda e: e.tensor_tensor(out=w1[:], in0=xre, in1=bc(sim_), op=ALU.mult))
        V(lambda e: e.tensor_tensor(out=w2[:], in0=xim, in1=bc(sre), op=ALU.mult))
        V(lambda e: e.tensor_tensor(out=dim, in0=w1[:], in1=w2[:], op=ALU.add))

    Bbre = T3("Bbre"); Bbim = T3("Bbim")
    cmul(Bbre[:], Bbim[:], Bre[:], Bim[:], qre[:], qim[:])
    Bpre = sb("Bpre", [128, 32, 32]); Bpim = sb("Bpim", [128, 32, 32])
    V(lambda e: e.memset(Bpre[:], 0.0))
    V(lambda e: e.memset(Bpim[:], 0.0))
    for g2 in range(2):
        hp = slice(g2 * 64, (g2 + 1) * 64); hc = slice(g2 * 16, (g2 + 1) * 16)
        V(lambda e, hp=hp, hc=hc: e.tensor_copy(out=Bpre[hp, :, hc], in_=Bbre[hp, :, :]))
        V(lambda e, hp=hp, hc=hc: e.tensor_copy(out=Bpim[hp, :, hc], in_=Bbim[hp, :, :]))

    WY = sb("WY", [128, 32, 8, 2, 32], BF16)
    KI = sb("KI", [128, 8, 8, 128], BF16)
    WX = sb("WX", [128, 8, 8, 2, 128], BF16)
    V(lambda e: e.memset(WY[:].rearrange("p a b c d -> p (a b c d)"), 0.0))
    CAre = T3("CAre"); CAim = T3("CAim")
    CApre = sb("CApre", [128, 32, 32]); CApimn = sb("CApimn", [128, 32, 32])
    V(lambda e: e.memset(CApre[:], 0.0))
    V(lambda e: e.memset(CApimn[:], 0.0))
    kit = sb("kit", [128, 128])
    dcol = ctx["s5_dd"][j]
    R_o = Res("pl_out")
    R_kit = Res("kit")
    for k in range(9):
        cmul(CAre[:], CAim[:], Cre[:], Cim[:], pwr[:, k, :], pwi[:, k, :])
        for g2 in range(2):
            hp = slice(g2 * 64, (g2 + 1) * 64); hc = slice(g2 * 16, (g2 + 1) * 16)
            V(lambda e, hp=hp, hc=hc: e.tensor_copy(out=CApre[hp, :, hc], in_=CAre[hp, :, :]))
            V(lambda e, hp=hp, hc=hc: e.tensor_scalar(out=CApimn[hp, :, hc], in0=CAim[hp, :, :], scalar1=-1.0, scalar2=None, op0=ALU.mult))
            if k >= 1:
                V(lambda e, hp=hp, hc=hc, k=k: e.tensor_copy(out=WY[hp, :, k - 1, 0, hc], in_=CAre[hp, :, :]))
                V(lambda e, hp=hp, hc=hc, k=k: e.tensor_scalar(out=WY[hp, :, k - 1, 1, hc], in0=CAim[hp, :, :], scalar1=-1.0, scalar2=None, op0=ALU.mult))
        if k <= 7:
            for c in range(8):
                bank = 4 + (c % 2)
                cs4 = slice(4 * c, 4 * c + 4)
                op("pe", [R], [RP[bank]], lambda e, cs4=cs4: e.matmul(PS[bank][:, 0:128], lhsT=Bpre[:, cs4, :].rearrange("p a b -> p (a b)"),
                                                                 rhs=CApre[:, cs4, :].rearrange("p a b -> p (a b)"), start=True, stop=False))
                op("pe", [R], [RP[bank]], lambda e, cs4=cs4: e.matmul(PS[bank][:, 0:128], lhsT=Bpim[:, cs4, :].rearrange("p a b -> p (a b)"),
                                                                 rhs=CApimn[:, cs4, :].rearrange("p a b -> p (a b)"), start=False, stop=True))
                if k == 0:
                    op("dve", [RP[bank], Rc], [R_kit], lambda e, bank=bank: e.tensor_tensor(out=kit[:], in0=PS[bank][:, 0:128], in1=bmask[:], op=ALU.mult))
                    op("dve", [R_kit, Rc, ctx["R_par"]], [R_o], lambda e, c=c: e.scalar_tensor_tensor(out=KI[:, c, 0, :], in0=ident[:], scalar=dcol[:, c:c + 1], in1=kit[:], op0=ALU.mult, op1=ALU.add))
                else:
                    op("dve", [RP[bank], Rc], [R_o], lambda e, c=c, k=k, bank=bank: e.tensor_tensor(out=KI[:, c, k, :], in0=PS[bank][:, 0:128], in1=bmask[:], op=ALU.mult))
    XBre = sb("XBre", [128, 32, 32]); XBim = sb("XBim", [128, 32, 32])
    x1 = sb("x1", [128, 32, 32]); x2 = sb("x2", [128, 32, 32])
    bc32 = lambda x2d: x2d.unsqueeze(2).broadcast_to([128, 32, 32])
    for tau in range(8):
        k = 7 - tau
        V(lambda e, k=k: e.tensor_tensor(out=x1[:], in0=Bpre[:], in1=bc32(pwr[:, k, :]), op=ALU.mult))
        V(lambda e, k=k: e.tensor_tensor(out=x2[:], in0=Bpim[:], in1=bc32(pwi[:, k, :]), op=ALU.mult))
        V(lambda e: e.tensor_tensor(out=XBre[:], in0=x1[:], in1=x2[:], op=ALU.subtract))
        V(lambda e, k=k: e.tensor_tensor(out=x1[:], in0=Bpre[:], in1=bc32(pwi[:, k, :]), op=ALU.mult))
        V(lambda e, k=k: e.tensor_tensor(out=x2[:], in0=Bpim[:], in1=bc32(pwr[:, k, :]), op=ALU.mult))
        V(lambda e: e.tensor_tensor(out=XBim[:], in0=x1[:], in1=x2[:], op=ALU.add))
        for ri, XB in enumerate((XBre, XBim)):
            for c in range(8):
                bank = 4 + (c % 2)
                op("pe", [R, Rc], [RP[bank]], lambda e, c=c, XB=XB: e.transpose(out=PS[bank][:, 0:128], in_=XB[:, 4 * c:4 * c + 4, :].rearrange("p a b -> p (a b)"), identity=ident[:, :]))
                op("act", [RP[bank]], [R_o], lambda e, c=c, ri=ri, tau=tau, bank=bank: e.copy(out=WX[:, c, tau, ri, :], in_=PS[bank][:, 0:128]))
    dma("sp", ctx["WXd"][j].rearrange("p a b c d -> p (a b c d)"), WX[:].rearrange("p a b c d -> p (a b c d)"), [R, R_o], [], ds)
    dma("sp", ctx["WYd"][j].rearrange("p a b c d -> p (a b c d)"), WY[:].rearrange("p a b c d -> p (a b c d)"), [R], [], ds)
    dma("sp", ctx["KId"][j].rearrange("p a b c -> p (a b c)"), KI[:].rearrange("p a b c -> p (a b c)"), [R, R_o], [], ds)


def run_pass(ctx, pn, nseq, L, pes):
    nc, kb, I, O = ctx["nc"], ctx["kb"], ctx["I"], ctx["O"]
    op, dma = kb.op, kb.dma
    PS, RP, ident = ctx["PS"], ctx["RP"], ctx["ident"]
    Rc = ctx["R_const"]
    N = nseq * L
    NT = (N + 127) // 128
    rows = [min(128, N - t * 128) for t in range(NT)]
    NC = N // 8
    sb = lambda n, s, d=F32, es=None: kb.sb("p%s_%s" % (pn, n), s, d, es=es or pes)

    h_tok = sb("htok", [128, NT, D])
    hT = sb("hT", [128, NB, N], BF16)
    R_h = [Res("h%d" % t) for t in range(NT)]
    R_hT = [Res("hT%d" % t) for t in range(NT)]
    ds_in = kb.dsem("in" + pn)
    ds_w = kb.dsem("w" + pn)
    ds_ln = kb.dsem("ln" + pn)
    gbc = sb("gbc", [128, D]); bbc = sb("bbc", [128, D])
    R_ln = Res("ln")
    NZ = 4
    zt = [sb("zt%d" % i, [128, D]) for i in range(NZ)]
    R_zt = [Res("zt%d" % i) for i in range(NZ)]
    stat = [sb("stat%d" % i, [128, 2, 6]) for i in range(NZ)]
    mv = [sb("mv%d" % i, [128, 2]) for i in range(NZ)]
    rstd = [sb("rstd%d" % i, [128, 1]) for i in range(NZ)]
    NWS = 8 if pn == "S" else 4
    wslot = [sb("wslot%d" % i, [128, NB, 128], BF16) for i in range(NWS)]
    R_ws = [Res("ws%d" % i) for i in range(NWS)]
    ws_i = [0]
    ds_ws = [kb.dsem("ws%s%d" % (pn, i)) for i in range(NWS)]

    if pn == "A":
        dma("sp", h_tok[0:16, 0, :], I["meta_tokens"][:, :], [], [R_h[0]], ds_in)
        dma("sp", h_tok[16:128, 0, :], I["x_prompt"][0:112, :], [], [R_h[0]], ds_in)
        for t in range(1, NT):
            dma("sp", h_tok[:, t, :], I["x_prompt"][112 + 128 * (t - 1):112 + 128 * t, :], [], [R_h[t]], ds_in)
    elif pn == "B":
        for t in range(NT):
            dma("sp", h_tok[0:rows[t], t, :], I["x_prompt"][1008 + 128 * t:1008 + 128 * t + rows[t], :], [], [R_h[t]], ds_in)
    else:
        dma("sp", h_tok[:, 0, :], I["x_sample"][:, :], [], [R_h[0]], ds_in)

    kb.group_final(R_h, ds_in)

    def to_hT(t):
        r = rows[t]
        for half in range(2):
            bank = 4 + half
            for b in range(4):
                blk = half * 4 + b
                op("pe", [R_h[t], Rc], [RP[bank]], lambda e, b=b, blk=blk: e.transpose(out=PS[bank][:, b * 128:b * 128 + r], in_=h_tok[0:r, t, blk * 128:(blk + 1) * 128], identity=ident[0:r, 0:r]))
            op("act", [RP[bank]], [R_hT[t]], lambda e, half=half: e.copy(out=hT[:, half * 4:half * 4 + 4, t * 128:t * 128 + r],
                                                                     in_=PS[bank][:, :].rearrange("p (b n) -> p b n", n=128)[:, :, 0:r]))

    for t in range(NT):
        to_hT(t)

    nlayers = ctx["dbg"].get("nlayers", DEPTH) if ctx["dbg"] else DEPTH
    glist = []
    for l_ in range(nlayers):
        j_ = l_ // 2
        if l_ % 2 == 0:
            glist += [(I["s5_w_in"][j_], c * 128) for c in range(NB)]
        else:
            for c in range(NB):
                glist += [(I["rg_w_in"][j_], c * 128), (I["rg_w_in"][j_], D + c * 128)]
        for v in range(FBH):
            glist += [(I["ffn_w_up"][l_], v * 128), (I["ffn_w_up"][l_], (v + FBH) * 128)]
    wplan = {"list": glist, "issued": 0, "hooks": {}, "off": 0, "next_off": 0}

    def plan_w(blocks, hooks=None):
        wplan["off"] = wplan["next_off"]
        wplan["next_off"] = wplan["off"] + len(blocks)
        for k, f in (hooks or {}).items():
            gi = wplan["off"] + k
            if gi < wplan["issued"]:
                f()
            else:
                wplan["hooks"][gi] = f

    def write_back(k):
        s = k % NWS
        dma("sp", ctx["WBLK"][k], wslot[s][:].rearrange("p a b -> p (a b)"), [R_ws[s]], [], ctx["ds_wb"])

    def load_w_block(i, pf=NWS - 1):
        gi = wplan["off"] + i
        while wplan["issued"] < min(len(wplan["list"]), gi + pf + 1):
            k = wplan["issued"]
            src2d, c0 = wplan["list"][k]
            s = k % NWS
            if pn == "A":
                dma("pool", wslot[s][:], src2d[:, c0:c0 + 128].rearrange("(kc p) f -> p kc f", p=128), [], [R_ws[s]], ds_ws[s])
                if k >= 2:
                    write_back(k - 2)
            else:
                dma("pool", wslot[s][:].rearrange("p a b -> p (a b)"), ctx["WBLK"][k], [], [R_ws[s]], ds_ws[s])
            wplan["issued"] += 1
            if k in wplan["hooks"]:
                wplan["hooks"].pop(k)()
        return gi % NWS

    def up_matmul(s, ps_banks, extra_reads):
        for gi, (c0, cn) in enumerate(colgroups(N)):
            bank = ps_banks[gi]
            for kc in range(NB):
                op("pe", [R_ws[s]] + R_hT + extra_reads, [RP[bank]],
                   lambda e, kc=kc, bank=bank, c0=c0, cn=cn: e.matmul(PS[bank][:, 0:cn], lhsT=wslot[s][:, kc, :], rhs=hT[:, kc, c0:c0 + cn], start=(kc == 0), stop=(kc == NB - 1)))

    def layer_norm_tile(t, zi, li_, k, last):
        r = rows[t]
        z = zt[zi]
        for hh in range(2):
            op("dve", [R_zt[zi]], [R_zt[zi]], lambda e, hh=hh: e.bn_stats(out=stat[zi][0:r, hh, :], in_=z[0:r, hh * 512:(hh + 1) * 512]))
        op("dve", [R_zt[zi]], [R_zt[zi]], lambda e: e.bn_aggr(out=mv[zi][0:r, :], in_=stat[zi][0:r, :, :].rearrange("p a b -> p (a b)")))
        op("act", [R_zt[zi]], [R_zt[zi]], lambda e: e.activation(out=rstd[zi][0:r, :], in_=mv[zi][0:r, 1:2], func=AF.Sqrt, bias=LN_EPS_AP[0:r, :], scale=1.0))
        op("dve", [R_zt[zi]], [R_zt[zi]], lambda e: e.reciprocal(out=rstd[zi][0:r, :], in_=rstd[zi][0:r, :]))
        op("dve", [R_zt[zi]], [R_zt[zi]], lambda e: e.tensor_scalar(out=z[0:r, :], in0=z[0:r, :], scalar1=mv[zi][0:r, 0:1], scalar2=rstd[zi][0:r, 0:1], op0=ALU.subtract, op1=ALU.mult))
        op("pool", [R_zt[zi], R_ln], [R_zt[zi]], lambda e: e.tensor_tensor(out=z[0:r, :], in0=z[0:r, :], in1=gbc[0:r, :], op=ALU.mult))
        op("pool", [R_zt[zi], R_ln], [R_h[t]], lambda e: e.tensor_tensor(out=h_tok[0:r, t, :], in0=z[0:r, :], in1=bbc[0:r, :], op=ALU.add))
        if not last:
            return lambda: to_hT(t)
        else:
            if pn == "A":
                if t == 0:
                    dma("sp", O["y_prompt"][0:112, :], h_tok[16:128, 0, :], [R_h[t]], [], ctx["ds_out"])
                else:
                    dma("sp", O["y_prompt"][112 + 128 * (t - 1):112 + 128 * t, :], h_tok[:, t, :], [R_h[t]], [], ctx["ds_out"])
            elif pn == "B":
                dma("sp", O["y_prompt"][1008 + 128 * t:1008 + 128 * t + r, :], h_tok[0:r, t, :], [R_h[t]], [], ctx["ds_out"])
            else:
                dma("sp", O["y_sample"][:, :], h_tok[:, 0, :], [R_h[t]], [], ctx["ds_out"])
            return lambda: None

    LN_EPS_AP = sb("lneps", [128, 1])
    op("dve", [], [Rc], lambda e: e.memset(LN_EPS_AP[:], LN_EPS))
    one_ap = sb("one", [128, 1])
    op("dve", [], [Rc], lambda e: e.memset(one_ap[:], 1.0))

    def load_ln(li_, k):
        dma("sp", gbc[:], I["ln_g"][li_, k].partition_broadcast(128), [], [R_ln], ds_ln)
        dma("sp", bbc[:], I["ln_b"][li_, k].partition_broadcast(128), [], [R_ln], ds_ln)

    env = dict(ctx=ctx, pn=pn, nseq=nseq, L=L, N=N, NT=NT, rows=rows, NC=NC, sb=sb, h_tok=h_tok, hT=hT,
               R_h=R_h, R_hT=R_hT, ds_w=ds_w, ds_in=ds_in, zt=zt, R_zt=R_zt, load_w_block=load_w_block, plan_w=plan_w,
               up_matmul=up_matmul, layer_norm_tile=layer_norm_tile, one_ap=one_ap, load_ln=load_ln, wslot=wslot, R_ws=R_ws)

    for li_ in range(nlayers):
        j = li_ // 2
        with ExitStack() as les:
            if li_ % 2 == 0:
                s5_layer(env, li_, j, les)
            else:
                rg_layer(env, li_, j, les)
            kb.barrier()
        with ExitStack() as les:
            ffn_layer(env, li_, les, last=(li_ == nlayers - 1))
            if pn == "A" and li_ == nlayers - 1:
                for k in range(max(0, len(glist) - 2), len(glist)):
                    write_back(k)
            kb.barrier()


def conv_taps(kb, xs, acc, nseq, L, K, wcols, bcol, R_main, R_halo, R_acc):
    kb.op("act", [R_main], [R_acc], lambda e: e.activation(out=acc[:, :, :], in_=xs[:, :, K - 1:K - 1 + L], func=AF.Identity, scale=wcols[K - 1], bias=bcol))
    for k in range(K - 1):
        kb.op("dve", [R_main, R_halo, R_acc], [R_acc], lambda e, k=k: e.scalar_tensor_tensor(out=acc[:, :, :], in0=xs[:, :, k:k + L], scalar=wcols[k], in1=acc[:, :, :], op0=ALU.mult, op1=ALU.add))


def ffn_layer(env, li_, les, last):
    ctx = env["ctx"]; kb = ctx["kb"]; nc = ctx["nc"]; I = ctx["I"]; O = ctx["O"]
    op, dma = kb.op, kb.dma
    PS, RP = ctx["PS"], ctx["RP"]
    pn, nseq, L, N, NT, rows = env["pn"], env["nseq"], env["L"], env["N"], env["NT"], env["rows"]
    sb = lambda n, s, d=F32: env["sb"]("f%d_%s" % (li_, n), s, d, es=les)
    h_tok, hT = env["h_tok"], env["hT"]
    actT = sb("actT", [128, FBH, N], BF16)
    R_act = [Res("act%d" % v) for v in range(FBH)]
    wd = sb("wd", [128, FBH, D], BF16)
    R_wd = [Res("wd%d" % v) for v in range(FBH)]
    xs = [sb("xs%d" % i, [128, nseq, 2 + L]) for i in range(2)]
    acc4 = [sb("acc%d" % i, [128, nseq, L]) for i in range(4)]
    R_xs = [Res("xs%d" % i) for i in range(2)]
    R_xh = [Res("xh%d" % i) for i in range(2)]
    R_acc4 = [Res("acc%d" % i) for i in range(4)]
    cw, cb = ctx["ffn_cw"][li_], ctx["ffn_cb"][li_]
    Rst = ctx["R_state"]
    if nseq == 1:
        hist = ctx["ffn_hist_p"][li_]
        R_hist = Rst
    else:
        hist = sb("hist", [128, FB, nseq, 2])
        R_hist = Res("hist")
        stage = sb("hstage", [32, FF2])
        load_T(ctx, I["state_ffn_conv"][li_], FB, 32, lambda b0, nb_: hist[:, b0:b0 + nb_, :, :].rearrange("p b s k -> p b (s k)"), stage, R_hist, "fh", env["ds_in"])
    env["load_ln"](li_, 1)
    def wd_hook(v):
        def f():
            if pn == "A":
                dma("pool", wd[:, v, :], I["ffn_w_down"][li_][v * 128:(v + 1) * 128, :], [], [R_wd[v]], env["ds_w"])
            else:
                dma("pool", wd[:, v, :], ctx["WDd"][li_][:, v * D:(v + 1) * D], [], [R_wd[v]], env["ds_w"])
            if v == FBH - 1:
                kb.group_final(R_wd, env["ds_w"])
        return f
    blocks = []
    for v in range(FBH):
        blocks += [(I["ffn_w_up"][li_], v * 128), (I["ffn_w_up"][li_], (v + FBH) * 128)]
    env["plan_w"](blocks, {2 * v + 1: wd_hook(v) for v in range(FBH)})
    ngrp = len(colgroups(N))
    banksets = [[0, 1, 2][:ngrp], [3, 4, 5][:ngrp]]
    bi = 0
    for v in range(FBH):
        acc = acc4[2 * (v % 2):2 * (v % 2) + 2]
        R_acc = R_acc4[2 * (v % 2):2 * (v % 2) + 2]
        for which in range(2):
            blk = v + which * FBH
            s = env["load_w_block"](2 * v + which)
            banks = banksets[bi % 2]
            bi += 1
            env["up_matmul"](s, banks, [])
            x = xs[which]
            op("act", [R_hist], [R_xh[which]], lambda e, x=x, blk=blk: e.copy(out=x[:, :, 0:2], in_=hist[:, blk, :, :]))
            for gi, (c0, cn) in enumerate(colgroups(N)):
                if nseq == 1:
                    op("act", [RP[banks[gi]]], [R_xs[which]], lambda e, gi=gi, c0=c0, cn=cn, x=x: e.copy(out=x[:, 0, 2 + c0:2 + c0 + cn], in_=PS[banks[gi]][:, 0:cn]))
                else:
                    op("act", [RP[banks[gi]]], [R_xs[which]], lambda e, gi=gi, cn=cn, x=x: e.copy(out=x[:, :, 2:2 + L], in_=PS[banks[gi]][:, 0:cn].rearrange("p (s t) -> p s t", t=L)))
            op("act", [R_xs[which]], [R_hist], lambda e, x=x, blk=blk: e.copy(out=hist[:, blk, :, :], in_=x[:, :, L:L + 2]))
            conv_taps(kb, x, acc[which], nseq, L, 3, [cw[k][:, blk:blk + 1] for k in range(3)], cb[:, blk:blk + 1], R_xs[which], R_xh[which], R_acc[which])
        op("act", [R_acc[1]], [R_acc[1]], lambda e, acc=acc: e.activation(out=acc[1][:, :, :], in_=acc[1][:, :, :], func=AF.Gelu_apprx_tanh))
        op("dve", [R_acc[0], R_acc[1]], [R_act[v]], lambda e, v=v, acc=acc: e.tensor_tensor(out=actT[:, v, :], in0=acc[0][:, :, :].rearrange("p s t -> p (s t)"), in1=acc[1][:, :, :].rearrange("p s t -> p (s t)"), op=ALU.mult))
    if pn == "A":
        dma("sp", ctx["WDd"][li_], wd[:].rearrange("p a b -> p (a b)"), R_wd, [], ctx["ds_wb"])
    if pn != "A":
        stg = sb("ostage", [32, 256 if nseq == 1 else 2048])
        m = 2 * nseq
        store_T(ctx, lambda b: hist[:, b, :, :].rearrange("p s k -> p (s k)"), FB, m,
                (O["ffn_conv_p"] if nseq == 1 else O["ffn_conv_s"])[li_], stg, R_hist, "fo")
    pending = None
    for t in range(NT):
        r = rows[t]
        pb = [0, 1] if t % 2 == 0 else [2, 3]
        for v in range(FBH):
            for hh in range(2):
                op("pe", [R_act[v], R_wd[v]], [RP[pb[hh]]], lambda e, v=v, hh=hh: e.matmul(PS[pb[hh]][0:r, :], lhsT=actT[:, v, t * 128:t * 128 + r], rhs=wd[:, v, hh * 512:(hh + 1) * 512], start=(v == 0), stop=(v == FBH - 1)))
        zi = t % 4
        z = env["zt"][zi]
        for hh in range(2):
            op("dve", [env["R_h"][t], RP[pb[hh]]], [env["R_zt"][zi]], lambda e, hh=hh: e.scalar_tensor_tensor(out=z[0:r, hh * 512:(hh + 1) * 512], in0=h_tok[0:r, t, hh * 512:(hh + 1) * 512], scalar=DN_ALPHA, in1=PS[pb[hh]][0:r, :], op0=ALU.mult, op1=ALU.add))
        if pending:
            pending()
        pending = env["layer_norm_tile"](t, zi, li_, 1, last)
    pending()


def rg_layer(env, li_, j, les):
    ctx = env["ctx"]; kb = ctx["kb"]; nc = ctx["nc"]; I = ctx["I"]; O = ctx["O"]
    op, dma = kb.op, kb.dma
    PS, RP = ctx["PS"], ctx["RP"]
    pn, nseq, L, N, NT, rows = env["pn"], env["nseq"], env["L"], env["N"], env["NT"], env["rows"]
    sb = lambda n, s, d=F32: env["sb"]("r%d_%s" % (li_, n), s, d, es=les)
    h_tok, hT = env["h_tok"], env["hT"]
    Rst = ctx["R_state"]
    hgT = sb("hgT", [128, NB, N], BF16)
    R_hg = [Res("hg%d" % c) for c in range(NB)]
    wg = sb("wg", [128, 4, 2, 512], BF16)
    R_wg = Res("wg")
    wo = sb("wo", [128, NB, D], BF16)
    R_wo = Res("wo")
    if pn == "A":
        for n in range(4):
            dma("pool", wg[:, n, :, :], I["rg_w_gates"][j, n].rearrange("(kc p) f -> p kc f", p=128), [], [R_wg], env["ds_w"])
    else:
        dma("pool", wg[:].rearrange("p a b c -> p (a b c)"), ctx["RGWGd"][j], [], [R_wg], env["ds_w"])
    kb.group_final([R_wg], env["ds_w"])
    env["load_ln"](li_, 0)
    if nseq == 1:
        hist = ctx["rg_hist_p"][j]; hst = ctx["rg_h_p"][j]; R_hist = Rst
    else:
        hist = sb("hist", [128, NB, nseq, 3]); hst = sb("hst", [128, NB, nseq]); R_hist = Res("rhist")
        stage = sb("stage", [48, D])
        load_T(ctx, I["state_rg_conv"][j], NB, 48, lambda b0, nb_: hist[:, b0:b0 + nb_, :, :].rearrange("p b s k -> p b (s k)"), stage, R_hist, "rc", env["ds_in"])
        stage2 = sb("stage2", [16, D])
        load_T(ctx, I["state_rg_h"][j], NB, 16, lambda b0, nb_: hst[:, b0:b0 + nb_, :], stage2, R_hist, "rh", env["ds_in"])
    xs = [sb("xs%d" % i, [128, nseq, 3 + L]) for i in range(2)]
    xc = [[sb("xc%d_%d" % (b, i), [128, nseq, L]) for i in range(2)] for b in range(2)]
    xcb = [[sb("xcb%d_%d" % (b, i), [128, N], BF16) for i in range(2)] for b in range(2)]
    gl = [[sb("gl%d_%d" % (b, i), [128, N]) for i in range(2)] for b in range(2)]
    work = [[sb("wk%d_%d" % (b, i), [128, nseq, L]) for i in range(3)] for b in range(2)]
    R_xs = [Res() for _ in range(2)]; R_xh = [Res() for _ in range(2)]
    R_xc = [[Res() for _ in range(2)] for _ in range(2)]; R_gl = [[Res() for _ in range(2)] for _ in range(2)]
    R_wk = [Res("rgwork0"), Res("rgwork1")]
    cw, cb, bg, m8 = ctx["rg_cw"][j], ctx["rg_cb"][j], ctx["rg_bg"][j], ctx["rg_m8sp"][j]
    one_ap = env["one_ap"]
    ngrp = len(colgroups(N))
    banksets = [[0, 1, 2][:ngrp], [3, 4, 5][:ngrp]]
    bi = [0]
    blocks = []
    for c in range(NB):
        blocks += [(I["rg_w_in"][j], c * 128), (I["rg_w_in"][j], D + c * 128)]
    env["plan_w"](blocks)
    flat = lambda t: t[:, :, :].rearrange("p s t -> p (s t)")

    def stage_proj(n):
        pb_ = n % 2
        for q in range(2):
            c = 2 * n + q
            s = env["load_w_block"](2 * c)
            banks = banksets[bi[0] % 2]; bi[0] += 1
            env["up_matmul"](s, banks, [])
            for gi, (c0, cn) in enumerate(colgroups(N)):
                op("act", [RP[banks[gi]]], [R_gl[pb_][q]], lambda e, gi=gi, c0=c0, cn=cn, q=q, banks=banks: e.activation(out=gl[pb_][q][:, c0:c0 + cn], in_=PS[banks[gi]][:, 0:cn], func=AF.Gelu_apprx_tanh))
            s = env["load_w_block"](2 * c + 1)
            banks = banksets[bi[0] % 2]; bi[0] += 1
            env["up_matmul"](s, banks, [])
            x = xs[q]
            op("act", [R_hist], [R_xh[q]], lambda e, x=x, c=c: e.copy(out=x[:, :, 0:3], in_=hist[:, c, :, :]))
            for gi, (c0, cn) in enumerate(colgroups(N)):
                if nseq == 1:
                    op("act", [RP[banks[gi]]], [R_xs[q]], lambda e, gi=gi, c0=c0, cn=cn, x=x, banks=banks: e.copy(out=x[:, 0, 3 + c0:3 + c0 + cn], in_=PS[banks[gi]][:, 0:cn]))
                else:
                    op("act", [RP[banks[gi]]], [R_xs[q]], lambda e, gi=gi, cn=cn, x=x, banks=banks: e.copy(out=x[:, :, 3:3 + L], in_=PS[banks[gi]][:, 0:cn].rearrange("p (s t) -> p s t", t=L)))
            op("act", [R_xs[q]], [R_hist], lambda e, x=x, c=c: e.copy(out=hist[:, c, :, :], in_=x[:, :, L:L + 3]))
            conv_taps(kb, x, xc[pb_][q], nseq, L, 4, [cw[k][:, c:c + 1] for k in range(4)], cb[:, c:c + 1], R_xs[q], R_xh[q], R_xc[pb_][q])
            op("dve", [R_xc[pb_][q]], [R_xc[pb_][q]], lambda e, q=q: e.tensor_copy(out=xcb[pb_][q][:, :], in_=flat(xc[pb_][q])))

    def stage_gate(n):
        pb_ = n % 2
        for q in range(2):
            c = 2 * n + q
            T1, T2, T3 = work[c % 2]
            Rw = R_wk[c % 2]
            pr_ = banksets[0]; pi_ = banksets[1]
            for (dst_banks, off) in ((pr_, q * 128), (pi_, 256 + q * 128)):
                for gi, (c0, cn) in enumerate(colgroups(N)):
                    for kc in range(2):
                        op("pe", [R_wg, R_xc[pb_][kc]], [RP[dst_banks[gi]]], lambda e, kc=kc, gi=gi, c0=c0, cn=cn, off=off, dst_banks=dst_banks: e.matmul(PS[dst_banks[gi]][:, 0:cn], lhsT=wg[:, n, kc, off:off + 128], rhs=xcb[pb_][kc][:, c0:c0 + cn], start=(kc == 0), stop=(kc == 1)))
            for gi, (c0, cn) in enumerate(colgroups(N)):
                op("act", [RP[pr_[gi]]], [Rw], lambda e, gi=gi, c0=c0, cn=cn, c=c: e.activation(out=flat(T1)[:, c0:c0 + cn], in_=PS[pr_[gi]][:, 0:cn], func=AF.Sigmoid, bias=bg[:, c:c + 1], scale=1.0))
                op("act", [RP[pi_[gi]]], [Rw], lambda e, gi=gi, c0=c0, cn=cn, c=c: e.activation(out=flat(T2)[:, c0:c0 + cn], in_=PS[pi_[gi]][:, 0:cn], func=AF.Sigmoid, bias=bg[:, 8 + c:8 + c + 1], scale=1.0))
            op("act", [Rw], [Rw], lambda e, c=c: e.activation(out=flat(T1), in_=flat(T1), func=AF.Exp, scale=m8[:, c:c + 1]))
            op("act", [Rw], [Rw], lambda e: e.activation(out=flat(T3), in_=flat(T1), func=AF.Square))
            op("act", [Rw], [Rw], lambda e: e.activation(out=flat(T3), in_=flat(T3), func=AF.Sqrt, scale=-1.0, bias=one_ap[:, :]))
            op("dve", [Rw, R_xc[pb_][q]], [Rw], lambda e, q=q: e.tensor_tensor(out=flat(T2), in0=flat(T2), in1=flat(xc[pb_][q]), op=ALU.mult))
            op("dve", [Rw], [Rw], lambda e: e.tensor_tensor(out=flat(T2), in0=flat(T2), in1=flat(T3), op=ALU.mult))
            for s_ in range(nseq):
                op("dve", [Rw, R_hist], [Rw], lambda e, s_=s_, c=c: e.tensor_tensor_scan(out=T3[:, s_, :], data0=T1[:, s_, :], data1=T2[:, s_, :], initial=hst[:, c, s_:s_ + 1], op0=ALU.mult, op1=ALU.add))
            op("dve", [Rw], [R_hist], lambda e, c=c: e.tensor_copy(out=hst[:, c, :], in_=T3[:, :, L - 1]))
            op("dve", [Rw, R_gl[pb_][q]], [R_hg[c]], lambda e, c=c, q=q: e.tensor_tensor(out=hgT[:, c, :], in0=flat(T3), in1=gl[pb_][q][:, :], op=ALU.mult))

    for n in range(4):
        stage_proj(n)
        if n == 1:
            if pn == "A":
                dma("pool", wo[:], I["rg_w_out"][j].rearrange("(kc p) f -> p kc f", p=128), [], [R_wo], env["ds_w"])
            else:
                dma("pool", wo[:].rearrange("p a b -> p (a b)"), ctx["RGWOd"][j], [], [R_wo], env["ds_w"])
            kb.group_final([R_wo], env["ds_w"])
        if n > 0:
            stage_gate(n - 1)
    stage_gate(3)
    if pn == "A":
        dma("sp", ctx["RGWGd"][j], wg[:].rearrange("p a b c -> p (a b c)"), [R_wg], [], ctx["ds_wb"])
        dma("sp", ctx["RGWOd"][j], wo[:].rearrange("p a b -> p (a b)"), [R_wo], [], ctx["ds_wb"])
    if pn != "A":
        stg = sb("ostage", [48, 256])
        store_T(ctx, lambda b: hist[:, b, :, :].rearrange("p s k -> p (s k)"), NB, 3 * nseq,
                (O["rg_conv_p"] if nseq == 1 else O["rg_conv_s"])[j], stg, R_hist, "ro")
        store_T(ctx, lambda b: hst[:, b, :], NB, nseq, (O["rg_h_p"] if nseq == 1 else O["rg_h_s"])[j], stg, R_hist, "rho")
    pending = None
    for t in range(NT):
        r = rows[t]
        pb = [0, 1] if t % 2 == 0 else [2, 3]
        for kc in range(NB):
            for hh in range(2):
                op("pe", [R_hg[kc], R_wo], [RP[pb[hh]]], lambda e, kc=kc, hh=hh: e.matmul(PS[pb[hh]][0:r, :], lhsT=hgT[:, kc, t * 128:t * 128 + r], rhs=wo[:, kc, hh * 512:(hh + 1) * 512], start=(kc == 0), stop=(kc == NB - 1)))
        zi = t % 4
        z = env["zt"][zi]
        for hh in range(2):
            op("dve", [env["R_h"][t], RP[pb[hh]]], [env["R_zt"][zi]], lambda e, hh=hh: e.scalar_tensor_tensor(out=z[0:r, hh * 512:(hh + 1) * 512], in0=h_tok[0:r, t, hh * 512:(hh + 1) * 512], scalar=DN_ALPHA, in1=PS[pb[hh]][0:r, :], op0=ALU.mult, op1=ALU.add))
        if pending:
            pending()
        pending = env["layer_norm_tile"](t, zi, li_, 0, False)
    pending()


def ONE_AP(env):
    if "one_ap" not in env:
        kb = env["ctx"]["kb"]
        t = env["sb"]("one", [128, 1])
        kb.op("dve", [], [env["ctx"]["R_const"]], lambda e: e.memset(t[:], 1.0))
        env["one_ap"] = t
    return env["one_ap"]


def s5_layer(env, li_, j, les):
    ctx = env["ctx"]; kb = ctx["kb"]; nc = ctx["nc"]; I = ctx["I"]; O = ctx["O"]
    op, dma = kb.op, kb.dma
    PS, RP = ctx["PS"], ctx["RP"]
    pn, nseq, L, N, NT, rows, NC = env["pn"], env["nseq"], env["L"], env["N"], env["NT"], env["rows"], env["NC"]
    sb = lambda n, s, d=F32: env["sb"]("s%d_%s" % (li_, n), s, d, es=les)
    h_tok, hT = env["h_tok"], env["hT"]
    Rst = ctx["R_state"]
    uT = sb("uT", [128, NB, N], BF16)
    gyT = uT
    R_u = [Res() for _ in range(NB)]
    R_gy = R_u
    R_X = Res("Xs")
    Sinb = sb("Sinb", [128, 32, 2, NC], BF16)
    R_S = Res("Sinb")
    wo = sb("wo", [128, NB, 2 * D], BF16)
    R_wo = Res("wo")
    env["load_ln"](li_, 0)
    if nseq == 1:
        st = ctx["s5st_p"][j]; R_st = Rst
    else:
        st = sb("st", [128, 32, 2, nseq]); R_st = Res("s5st")
        stage = sb("stage", [16, 4096])
        for ri, nm in enumerate(("state_s5_re", "state_s5_im")):
            load_T(ctx, I[nm][j], 32, 16, lambda b0, nb_, ri=ri: st[:, b0:b0 + nb_, ri, :], stage, R_st, "s5" + str(ri), env["ds_in"])
    ngrp = len(colgroups(N))
    banksets = [[0, 1, 2][:ngrp], [3, 4, 5][:ngrp]]
    env["plan_w"]([(I["s5_w_in"][j], c * 128) for c in range(NB)])
    for c in range(NB):
        s = env["load_w_block"](c)
        banks = banksets[c % 2]
        env["up_matmul"](s, banks, [])
        for gi, (c0, cn) in enumerate(colgroups(N)):
            op("act", [RP[banks[gi]]], [R_u[c]], lambda e, gi=gi, c0=c0, cn=cn, c=c, banks=banks: e.copy(out=uT[:, c, c0:c0 + cn], in_=PS[banks[gi]][:, 0:cn]))
    if pn == "A":
        dma("pool", wo[:, 0:4, :], I["s5_w_out"][j][0:512, :].rearrange("(kc p) f -> p kc f", p=128), [], [R_wo], env["ds_w"])
        dma("pool", wo[:, 4:8, :], I["s5_w_out"][j][512:1024, :].rearrange("(kc p) f -> p kc f", p=128), [], [R_wo], env["ds_w"])
        kb.group_final([R_wo], env["ds_w"])
    else:
        dma("pool", wo[:].rearrange("p a b -> p (a b)"), ctx["S5WOd"][j], [], [R_wo], env["ds_w"])
        kb.group_final([R_wo], env["ds_w"])
    ies = ExitStack()
    sbo = sb
    sb = lambda n, s_, d=F32: env["sb"]("s%d_%s" % (li_, n), s_, d, es=ies)
    Xs = sb("Xs", [128, 32, 2, NC])
    WXs = [sb("WX%d" % i, [128, 8, 2, 128], BF16) for i in range(2)]
    R_WX = [Res() for _ in range(2)]
    ds_wx = [kb.dsem("s5x%s%d_%d" % (pn, li_, i)) for i in range(2)]
    ds_kw = [kb.dsem("s5k%s%d_%d" % (pn, li_, i)) for i in range(2)]
    for c in range(NB):
        wi = c % 2
        dma("sp", WXs[wi][:], ctx["WXd"][j][:, c], [], [R_WX[wi]], ds_wx[wi])
        for ri in range(2):
            for tau in range(8):
                for pr in range(4):
                    bank = pr
                    rs = slice(32 * pr, 32 * pr + 32)
                    op("pe", [R_WX[wi], R_u[c]], [RP[bank]], lambda e, ri=ri, tau=tau, rs=rs, bank=bank, wi=wi, c=c, pr=pr: e.matmul(
                        PS[bank][:, ri * 256:ri * 256 + NC], lhsT=WXs[wi][rs, tau, ri, :],
                        rhs=uT[rs, c, :].rearrange("p (n t) -> p n t", t=8)[:, :, tau], start=(tau == 0), stop=(tau == 7), tile_position=(32 * pr, 0)))
        for pr in range(4):
            bank = pr
            op("act", [RP[bank]], [R_X], lambda e, bank=bank, c=c, pr=pr: e.copy(out=Xs[:, 4 * c + pr, :, :], in_=PS[bank][:, :].rearrange("p (r n) -> p r n", n=256)[:, :, 0:NC]))
    a8r, a8i, a8in = ctx["A8re"][j], ctx["A8im"][j], ctx["A8imn"][j]
    Mrot = sb("Mrot", [128, 32, 2, 2]); prod = sb("prod", [128, 32, 2, 2]); t1 = sb("t1", [128, 32, 2])
    op("dve", [Rst], [R_X], lambda e: e.tensor_copy(out=Mrot[:, :, 0, 0], in_=a8r[:, :]))
    op("dve", [Rst], [R_X], lambda e: e.tensor_copy(out=Mrot[:, :, 1, 1], in_=a8r[:, :]))
    op("dve", [Rst], [R_X], lambda e: e.tensor_copy(out=Mrot[:, :, 0, 1], in_=a8in[:, :]))
    op("dve", [Rst], [R_X], lambda e: e.tensor_copy(out=Mrot[:, :, 1, 0], in_=a8i[:, :]))
    if nseq == 1:
        op("act", [R_st], [R_S], lambda e: e.copy(out=Sinb[:, :, :, 0], in_=st[:, :, :, 0]))
    else:
        op("act", [R_st], [R_S], lambda e: e.copy(out=Sinb[:, :, :, :], in_=st[:, :, :, :]))
    nsteps = NC if nseq == 1 else nseq
    for c in range(nsteps):
        if nseq == 1:
            prev = st[:, :, :, 0] if c == 0 else Xs[:, :, :, c - 1]
        else:
            prev = st[:, :, :, c]
        cur = Xs[:, :, :, c]
        rd = [R_X, R_st, Rst]
        pb_ = prev.unsqueeze(2).broadcast_to([128, 32, 2, 2])
        op("dve", rd, [R_X], lambda e, pb_=pb_: e.tensor_tensor(out=prod[:], in0=pb_, in1=Mrot[:], op=ALU.mult))
        op("dve", rd, [R_X], lambda e: e.tensor_tensor(out=t1[:], in0=prod[:, :, :, 0], in1=prod[:, :, :, 1], op=ALU.add))
        op("dve", rd, [R_X], lambda e, cur=cur: e.tensor_tensor(out=cur, in0=cur, in1=t1[:], op=ALU.add))
    if nseq == 1:
        op("act", [R_X], [R_S], lambda e: e.copy(out=Sinb[:, :, :, 1:NC], in_=Xs[:, :, :, 0:NC - 1]))
        op("dve", [R_X], [R_st], lambda e: e.tensor_copy(out=st[:, :, :, 0], in_=Xs[:, :, :, NC - 1]))
    else:
        op("dve", [R_X], [R_st], lambda e: e.tensor_copy(out=st[:, :, :, :], in_=Xs[:, :, :, :]))
    if pn != "A":
        stg = sb("ostage", [16, 2048])
        for ri, nm in enumerate(("s5_re", "s5_im")):
            store_T(ctx, lambda b, ri=ri: st[:, b, ri, :], 32, nseq, O[nm + ("_p" if nseq == 1 else "_s")][j], stg, R_st, "s5o%d" % ri)
    kb.barrier()
    ies.close()
    sb = sbo
    KIs = [sb("KI%d" % i, [128, 8, 128], BF16) for i in range(2)]
    WYs = [sb("WY%d" % i, [128, 4, 8, 2, 32], BF16) for i in range(2)]
    R_KI = [Res() for _ in range(2)]
    R_WY = [Res() for _ in range(2)]
    cg = colgroups(NC, 64)
    for c in range(NB):
        wi = c % 2
        dma("sp", KIs[wi][:], ctx["KId"][j][:, c], [], [R_KI[wi]], ds_kw[wi])
        dma("sp", WYs[wi][:], ctx["WYd"][j][:, 4 * c:4 * c + 4], [], [R_WY[wi]], ds_kw[wi])
        kb.group_final([R_KI[wi], R_WY[wi]], ds_kw[wi])
        banks = banksets[c % 2]
        for gi, (n0, nn) in enumerate(cg):
            bank = banks[gi]
            uv = uT[:, c, n0 * 8:(n0 + nn) * 8].rearrange("p (n t) -> p t n", t=8)
            for lag in range(8):
                op("pe", [R_KI[wi], R_u[c]], [RP[bank]], lambda e, lag=lag, bank=bank, nn=nn, uv=uv, wi=wi: e.matmul(
                    PS[bank][:, lag * nn:8 * nn], lhsT=KIs[wi][:, lag, :], rhs=uv[:, 0:8 - lag, :], start=(lag == 0), stop=False))
            for tau in range(8):
                for ri in range(2):
                    for pr in range(4):
                        lastmm = (tau == 7 and ri == 1)
                        op("pe", [R_WY[wi], R_S], [RP[bank]], lambda e, pr=pr, tau=tau, ri=ri, bank=bank, wi=wi, n0=n0, nn=nn, c=c, lastmm=lastmm: e.matmul(
                            PS[bank][32 * pr:32 * pr + 32, tau * nn:(tau + 1) * nn], lhsT=WYs[wi][:, pr, tau, ri, :], rhs=Sinb[:, 4 * c + pr, ri, n0:n0 + nn], start=False, stop=lastmm, tile_position=(0, 32 * pr)))
            op("act", [RP[bank]], [R_gy[c]], lambda e, bank=bank, n0=n0, nn=nn, c=c: e.activation(out=gyT[:, c, n0 * 8:(n0 + nn) * 8].rearrange("p (n t) -> p n t", t=8), in_=PS[bank][:, 0:nn * 8].rearrange("p (t n) -> p n t", t=8), func=AF.Gelu_apprx_tanh))
    if pn == "A":
        dma("sp", ctx["S5WOd"][j], wo[:].rearrange("p a b -> p (a b)"), [R_wo], [], ctx["ds_wb"])
    sgs = [sb("sg%d" % i, [128, D]) for i in range(2)]
    R_sgs = [Res("sg%d" % i) for i in range(2)]
    pending = None
    for t in range(NT):
        sg = sgs[t % 2]
        R_sg = R_sgs[t % 2]
        r = rows[t]
        pb = [2, 3, 0, 1]
        for qq in (2, 3, 0, 1):
            for kc in range(NB):
                op("pe", [R_gy[kc], R_wo], [RP[pb[qq]]], lambda e, kc=kc, qq=qq: e.matmul(PS[pb[qq]][0:r, :], lhsT=gyT[:, kc, t * 128:t * 128 + r], rhs=wo[:, kc, qq * 512:(qq + 1) * 512], start=(kc == 0), stop=(kc == NB - 1)))
        zi = t % 4
        z = env["zt"][zi]
        for hh in range(2):
            op("act", [RP[pb[2 + hh]]], [R_sg], lambda e, hh=hh: e.activation(out=sg[0:r, hh * 512:(hh + 1) * 512], in_=PS[pb[2 + hh]][0:r, :], func=AF.Sigmoid))
            op("dve", [R_sg, RP[pb[hh]]], [R_sg], lambda e, hh=hh: e.tensor_tensor(out=sg[0:r, hh * 512:(hh + 1) * 512], in0=sg[0:r, hh * 512:(hh + 1) * 512], in1=PS[pb[hh]][0:r, :], op=ALU.mult))
            op("dve", [env["R_h"][t], R_sg], [env["R_zt"][zi]], lambda e, hh=hh: e.scalar_tensor_tensor(out=z[0:r, hh * 512:(hh + 1) * 512], in0=h_tok[0:r, t, hh * 512:(hh + 1) * 512], scalar=DN_ALPHA, in1=sg[0:r, hh * 512:(hh + 1) * 512], op0=ALU.mult, op1=ALU.add))
        if pending:
            pending()
        pending = env["layer_norm_tile"](t, zi, li_, 0, False)
    pending()


_NC_CACHE = {}


def _get_nc(dbg=None):
    key = repr(dbg)
    if key not in _NC_CACHE:
        nc = bass.Bass("TRN2", target_bir_lowering=False)
        build(nc, dbg)
        _NC_CACHE[key] = nc
    return _NC_CACHE[key]


def kernel(dbg=None, **inp):
    f = lambda a: np.ascontiguousarray(np.asarray(a, dtype=np.float32))
    ident = np.eye(128, dtype=np.float32)
    bmask = np.kron(np.eye(4, dtype=np.float32), np.ones((32, 32), np.float32))
    wnames = ["meta_tokens", "s5_w_in", "s5_lam_re", "s5_lam_im", "s5_log_step", "s5_b_re", "s5_b_im", "s5_c_re",
              "s5_c_im", "s5_d", "s5_w_out", "rg_w_in", "rg_conv_w", "rg_conv_b", "rg_w_gates", "rg_b_gates",
              "rg_lam", "rg_w_out", "ffn_w_up", "ffn_conv_w", "ffn_conv_b", "ffn_w_down", "ln_g", "ln_b"]
    shared = {n: f(inp[n]) for n in wnames}
    shared["ident"] = ident
    shared["bmask"] = bmask
    shared["sel2"] = np.kron(np.eye(2, dtype=np.float32), np.ones((1, 64), np.float32))
    in_maps = []
    for c in range(8):
        m = dict(shared)
        sl = slice(16 * c, 16 * c + 16)
        m["x_prompt"] = f(inp["x_prompt"][c])
        m["x_sample"] = f(inp["x_sample"][sl]).reshape(128, D)
        m["state_s5_re"] = f(inp["state_s5_re"][:, sl]).reshape(2, 16, 4096)
        m["state_s5_im"] = f(inp["state_s5_im"][:, sl]).reshape(2, 16, 4096)
        m["state_rg_h"] = f(inp["state_rg_h"][:, sl])
        m["state_rg_conv"] = f(inp["state_rg_conv"][:, sl]).reshape(2, 48, D)
        m["state_ffn_conv"] = f(inp["state_ffn_conv"][:, sl]).reshape(4, 32, FF2)
        in_maps.append(m)
    nc = _get_nc(dbg)
    res = run_bass_kernel_spmd(nc, in_maps, core_ids=list(range(8)))
    R = res.results
    cat = lambda k, ax: np.concatenate([np.asarray(R[c][k]) for c in range(8)], axis=ax)
    y_prompt = np.stack([np.asarray(R[c]["y_prompt"]) for c in range(8)], 0)
    y_sample = cat("y_sample", 0).reshape(128, 8, D)
    s5_re_p = cat("s5_re_p", 1).reshape(2, 8, 64, 64)
    s5_im_p = cat("s5_im_p", 1).reshape(2, 8, 64, 64)
    rg_h_p = cat("rg_h_p", 1).reshape(2, 8, D)
    rg_conv_p = np.stack([np.asarray(R[c]["rg_conv_p"]) for c in range(8)], 1).reshape(2, 8, 3, D)
    ffn_conv_p = np.stack([np.asarray(R[c]["ffn_conv_p"]) for c in range(8)], 1).reshape(4, 8, 2, FF2)
    s5_re_s = cat("s5_re_s", 1).reshape(2, 128, 64, 64)
    s5_im_s = cat("s5_im_s", 1).reshape(2, 128, 64, 64)
    rg_h_s = cat("rg_h_s", 1).reshape(2, 128, D)
    rg_conv_s = cat("rg_conv_s", 1).reshape(2, 128, 3, D)
    ffn_conv_s = cat("ffn_conv_s", 1).reshape(4, 128, 2, FF2)
    outs = (y_prompt, y_sample, s5_re_p, s5_im_p, rg_h_p, rg_conv_p, ffn_conv_p,
            s5_re_s, s5_im_s, rg_h_s, rg_conv_s, ffn_conv_s)
    return tuple(np.ascontiguousarray(o, dtype=np.float32) for o in outs)
```

```python
import math
from contextlib import ExitStack
import numpy as np
import concourse.bass as bass
import concourse.mybir as mybir
from concourse.bass_utils import run_bass_kernel_spmd

F32 = mybir.dt.float32
BF16 = mybir.dt.bfloat16
I32 = mybir.dt.int32
AF = mybir.ActivationFunctionType
ALU = mybir.AluOpType

D = 1024
NB = 8
FF = 2816
FF2 = 5632
FB = 44
FBH = 22
DEPTH = 4
SEQ = 2048
NMETA = 16
DN_ALPHA = (2 * DEPTH) ** 0.25
LN_EPS = 1e-5
RG_C = 8.0
TWO_PI = 2.0 * math.pi


class Res:
    __slots__ = ("name", "w", "r")

    def __init__(self, name=""):
        self.name = name
        self.w = None
        self.r = {}


class DSem:
    def __init__(self, sem):
        self.sem = sem
        self.val = 0


class Q:
    def __init__(self, name, eng, sem, eager):
        self.name = name
        self.eng = eng
        self.sem = sem
        self.eager = eager
        self.n = 0
        self.last = None
        self.last_ms = True
        self.ms = []
        self.semval = 0
        self.known = {}


class KB:
    def __init__(self, nc, es):
        self.nc = nc
        self.es = es
        self.q = {}
        for name, eng, eager in (("pe", nc.tensor, False), ("act", nc.scalar, False),
                                 ("dve", nc.vector, False), ("pool", nc.gpsimd, True),
                                 ("sp", nc.sync, True)):
            self.q[name] = Q(name, eng, self.sem("q_" + name), eager)
        self.dsems = []
        self.nsem = 0

    def sem(self, name):
        return self.es.enter_context(self.nc.semaphore(name))

    def dsem(self, name):
        d = DSem(self.sem("d_" + name))
        self.dsems.append(d)
        return d

    def sb(self, name, shape, dt, es=None):
        return (es or self.es).enter_context(self.nc.sbuf_tensor("sb_" + name, list(shape), dt))

    def ps(self, name, shape, dt=F32, es=None):
        return (es or self.es).enter_context(self.nc.psum_tensor("pt_" + name, list(shape), dt))

    def _milestone(self, A, k):
        lo, hi = 0, len(A.ms)
        while lo < hi:
            mid = (lo + hi) // 2
            if A.ms[mid][0] >= k:
                hi = mid
            else:
                lo = mid + 1
        if lo < len(A.ms):
            return A.ms[lo][1]
        assert A.last is not None and not A.last_ms and A.n >= k
        A.semval += 1
        A.last.then_inc(A.sem, 1)
        A.last_ms = True
        A.ms.append((A.n, A.semval))
        return A.semval

    def _wait(self, q, deps):
        need = {}
        for d in deps:
            if d is None:
                continue
            if d[0] == "q":
                A, k = d[1], d[2]
                if A is q and q.name == "pe":
                    continue
                v = self._milestone(A, k)
                key = A
                sem = A.sem
            else:
                key = d[1]
                sem = d[1].sem
                v = d[2]
            if q.known.get(key, 0) >= v:
                continue
            if need.get(key, (None, 0))[1] < v:
                need[key] = (sem, v)
        for key, (sem, v) in need.items():
            q.eng.wait_ge(sem, v)
            q.known[key] = v

    def _deps(self, reads, writes):
        deps = []
        for r in reads:
            if r.w is not None:
                deps.append(r.w)
        for w in writes:
            if w.w is not None:
                deps.append(w.w)
            deps.extend(w.r.values())
        return deps

    def op(self, qn, reads, writes, fn):
        q = self.q[qn]
        self._wait(q, self._deps(reads, writes))
        ins = fn(q.eng)
        q.n += 1
        q.last = ins
        q.last_ms = False
        if q.eager:
            q.semval += 1
            ins.then_inc(q.sem, 1)
            q.last_ms = True
            q.ms.append((q.n, q.semval))
        me = ("q", q, q.n)
        for r in reads:
            r.r[q] = me
        for w in writes:
            w.w = me
            w.r = {}
        return ins

    def dma(self, qn, out, in_, reads, writes, ds, **kw):
        q = self.q[qn]
        self._wait(q, self._deps(reads, writes))
        ins = q.eng.dma_start(out=out, in_=in_, **kw)
        ds.val += 16
        ins.then_inc(ds.sem, 16)
        me = ("d", ds, ds.val)
        for r in reads:
            r.r[ds] = me
        for w in writes:
            w.w = me
            w.r = {}
        return ins

    def group_final(self, ress, ds):
        for r in ress:
            r.w = ("d", ds, ds.val)

    def barrier(self, exclude=()):
        marks = []
        for A in self.q.values():
            if A.n > 0:
                marks.append(("q", A, A.n))
        for d in self.dsems:
            if d in exclude:
                continue
            if d.val > 0:
                marks.append(("d", d, d.val))
        for q in self.q.values():
            self._wait(q, [m for m in marks if not (m[0] == "q" and m[1] is q)])

    def finish(self):
        self.barrier()


def build(nc, dbg=None):
    es = ExitStack()
    kb = KB(nc, es)
    with es:
        _build(nc, kb, dbg)
    return nc


def _dram_in(nc, name, shape, dt=F32):
    return nc.dram_tensor(name, list(shape), dt, kind="ExternalInput").ap()


def _dram_out(nc, name, shape, dt=F32):
    return nc.dram_tensor(name, list(shape), dt, kind="ExternalOutput").ap()


def _build(nc, kb, dbg):
    op, dma = kb.op, kb.dma
    I = {}
    I["x_prompt"] = _dram_in(nc, "x_prompt", [SEQ, D])
    I["x_sample"] = _dram_in(nc, "x_sample", [128, D])
    I["state_s5_re"] = _dram_in(nc, "state_s5_re", [2, 16, 4096])
    I["state_s5_im"] = _dram_in(nc, "state_s5_im", [2, 16, 4096])
    I["state_rg_h"] = _dram_in(nc, "state_rg_h", [2, 16, D])
    I["state_rg_conv"] = _dram_in(nc, "state_rg_conv", [2, 48, D])
    I["state_ffn_conv"] = _dram_in(nc, "state_ffn_conv", [4, 32, FF2])
    I["meta_tokens"] = _dram_in(nc, "meta_tokens", [NMETA, D])
    I["s5_w_in"] = _dram_in(nc, "s5_w_in", [2, D, D])
    I["s5_lam_re"] = _dram_in(nc, "s5_lam_re", [2, 64, 64])
    I["s5_lam_im"] = _dram_in(nc, "s5_lam_im", [2, 64, 64])
    I["s5_log_step"] = _dram_in(nc, "s5_log_step", [2, 64])
    I["s5_b_re"] = _dram_in(nc, "s5_b_re", [2, 64, 64, 16])
    I["s5_b_im"] = _dram_in(nc, "s5_b_im", [2, 64, 64, 16])
    I["s5_c_re"] = _dram_in(nc, "s5_c_re", [2, 64, 16, 64])
    I["s5_c_im"] = _dram_in(nc, "s5_c_im", [2, 64, 16, 64])
    I["s5_d"] = _dram_in(nc, "s5_d", [2, D])
    I["s5_w_out"] = _dram_in(nc, "s5_w_out", [2, D, 2 * D])
    I["rg_w_in"] = _dram_in(nc, "rg_w_in", [2, D, 2 * D])
    I["rg_conv_w"] = _dram_in(nc, "rg_conv_w", [2, 4, D])
    I["rg_conv_b"] = _dram_in(nc, "rg_conv_b", [2, D])
    I["rg_w_gates"] = _dram_in(nc, "rg_w_gates", [2, 4, 256, 512])
    I["rg_b_gates"] = _dram_in(nc, "rg_b_gates", [2, 2 * D])
    I["rg_lam"] = _dram_in(nc, "rg_lam", [2, D])
    I["rg_w_out"] = _dram_in(nc, "rg_w_out", [2, D, D])
    I["ffn_w_up"] = _dram_in(nc, "ffn_w_up", [4, D, FF2])
    I["ffn_conv_w"] = _dram_in(nc, "ffn_conv_w", [4, 3, FF2])
    I["ffn_conv_b"] = _dram_in(nc, "ffn_conv_b", [4, FF2])
    I["ffn_w_down"] = _dram_in(nc, "ffn_w_down", [4, FF, D])
    I["ln_g"] = _dram_in(nc, "ln_g", [4, 2, D])
    I["ln_b"] = _dram_in(nc, "ln_b", [4, 2, D])
    I["ident"] = _dram_in(nc, "ident", [128, 128])
    I["bmask"] = _dram_in(nc, "bmask", [128, 128])
    I["sel2"] = _dram_in(nc, "sel2", [2, 128])

    O = {}
    O["y_prompt"] = _dram_out(nc, "y_prompt", [SEQ, D])
    O["y_sample"] = _dram_out(nc, "y_sample", [128, D])
    O["s5_re_p"] = _dram_out(nc, "s5_re_p", [2, 1, 4096])
    O["s5_im_p"] = _dram_out(nc, "s5_im_p", [2, 1, 4096])
    O["rg_h_p"] = _dram_out(nc, "rg_h_p", [2, 1, D])
    O["rg_conv_p"] = _dram_out(nc, "rg_conv_p", [2, 3, D])
    O["ffn_conv_p"] = _dram_out(nc, "ffn_conv_p", [4, 2, FF2])
    O["s5_re_s"] = _dram_out(nc, "s5_re_s", [2, 16, 4096])
    O["s5_im_s"] = _dram_out(nc, "s5_im_s", [2, 16, 4096])
    O["rg_h_s"] = _dram_out(nc, "rg_h_s", [2, 16, D])
    O["rg_conv_s"] = _dram_out(nc, "rg_conv_s", [2, 48, D])
    O["ffn_conv_s"] = _dram_out(nc, "ffn_conv_s", [4, 32, FF2])

    WXd = [nc.dram_tensor("WXd%d" % j, [128, 8, 8, 2, 128], BF16, kind="Internal").ap() for j in range(2)]
    WYd = [nc.dram_tensor("WYd%d" % j, [128, 32, 8, 2, 32], BF16, kind="Internal").ap() for j in range(2)]
    KId = [nc.dram_tensor("KId%d" % j, [128, 8, 8, 128], BF16, kind="Internal").ap() for j in range(2)]

    NBLK = 2 * NB + 2 * 2 * NB + 4 * FB
    ctx_w = dict(
        WBLK=nc.dram_tensor("WBLK", [NBLK, 128, NB * 128], BF16, kind="Internal").ap(),
        WDd=[nc.dram_tensor("WDd%d" % l, [128, FBH * D], BF16, kind="Internal").ap() for l in range(4)],
        S5WOd=[nc.dram_tensor("S5WOd%d" % j, [128, NB * 2 * D], BF16, kind="Internal").ap() for j in range(2)],
        RGWOd=[nc.dram_tensor("RGWOd%d" % j, [128, NB * D], BF16, kind="Internal").ap() for j in range(2)],
        RGWGd=[nc.dram_tensor("RGWGd%d" % j, [128, 4 * 2 * 512], BF16, kind="Internal").ap() for j in range(2)],
    )
    sb = kb.sb
    ident = sb("ident", [128, 128], F32)
    bmask = sb("bmask", [128, 128], F32)
    R_const = Res("const")
    ds_c = kb.dsem("const")
    dma("sp", ident[:], I["ident"][:, :], [], [R_const], ds_c)
    dma("sp", bmask[:], I["bmask"][:, :], [], [R_const], ds_c)
    sel2 = sb("sel2", [2, 128], F32)
    dma("sp", sel2[:], I["sel2"][:, :], [], [R_const], ds_c)

    R_par = Res("par")
    ds_p = kb.dsem("par")

    par_loads = []

    def load_cols(name, src_1d, nblk, ds):
        t = sb(name, [128, nblk], F32)
        par_loads.append((t, src_1d))
        return t

    def issue_par_loads():
        with nc.allow_non_contiguous_dma(reason="small param"):
            for t, src_1d in par_loads:
                dma("sp", t[:], src_1d.rearrange("(b p) -> p b", p=128), [], [R_par], ds_p)

    ffn_cw = [[load_cols("fcw%d_%d" % (l, k), I["ffn_conv_w"][l, k], FB, ds_c) for k in range(3)] for l in range(4)]
    ffn_cb = [load_cols("fcb%d" % l, I["ffn_conv_b"][l], FB, ds_c) for l in range(4)]
    rg_cw = [[load_cols("rcw%d_%d" % (j, k), I["rg_conv_w"][j, k], NB, ds_c) for k in range(4)] for j in range(2)]
    rg_cb = [load_cols("rcb%d" % j, I["rg_conv_b"][j], NB, ds_c) for j in range(2)]
    rg_bg = [load_cols("rbg%d" % j, I["rg_b_gates"][j], 16, ds_c) for j in range(2)]
    rg_lm = [load_cols("rlm%d" % j, I["rg_lam"][j], NB, ds_c) for j in range(2)]
    s5_dd = [sb("s5d%d" % j, [128, NB], F32) for j in range(2)]
    with nc.allow_non_contiguous_dma(reason="small param"):
        for j in range(2):
            dma("sp", s5_dd[j][:], I["s5_d"][j].rearrange("(b p) -> p b", p=128), [], [R_const], ds_c)
    rg_m8sp = [sb("m8sp%d" % j, [128, NB], F32) for j in range(2)]
    A8re = [sb("A8re%d" % j, [128, 32], F32) for j in range(2)]
    A8im = [sb("A8im%d" % j, [128, 32], F32) for j in range(2)]
    A8imn = [sb("A8imn%d" % j, [128, 32], F32) for j in range(2)]
    ffn_hist_p = [sb("fhp%d" % l, [128, FB, 1, 2], F32) for l in range(4)]
    rg_hist_p = [sb("rhp%d" % j, [128, NB, 1, 3], F32) for j in range(2)]
    rg_h_p = [sb("rgh%d" % j, [128, NB, 1], F32) for j in range(2)]
    s5st_p = [sb("s5p%d" % j, [128, 32, 2, 1], F32) for j in range(2)]
    R_state = Res("state")

    PS = [kb.ps("ps%d" % i, [128, 512]) for i in range(8)]
    RP = [Res("ps%d" % i) for i in range(8)]

    for j in range(2):
        pass
    for t in ffn_hist_p + rg_hist_p + rg_h_p + s5st_p:
        op("dve", [], [R_state], lambda e, t=t: e.memset(t[:], 0.0))

    ctx = dict(nc=nc, kb=kb, I=I, O=O, sel2=sel2, R_par=R_par, WXd=WXd, WYd=WYd, KId=KId, ident=ident, bmask=bmask,
               R_const=R_const, PS=PS, RP=RP, ffn_cw=ffn_cw, ffn_cb=ffn_cb, rg_cw=rg_cw, rg_cb=rg_cb,
               rg_bg=rg_bg, rg_m8sp=rg_m8sp, s5_dd=s5_dd, A8re=A8re, A8im=A8im, A8imn=A8imn,
               ffn_hist_p=ffn_hist_p, rg_hist_p=rg_hist_p, rg_h_p=rg_h_p, s5st_p=s5st_p,
               R_state=R_state, dbg=dbg, ds_out=kb.dsem("out"), ds_w=None, ds_wb=kb.dsem("wb"), **ctx_w)

    with ExitStack() as lds:
        with ExitStack() as nat:
            gens = [s5_prologue_loads(ctx, j, lds, nat) for j in range(2)]
            for g in gens:
                next(g)
            lds_ = [next(g) for g in gens]
            kb.barrier()
        for j in range(2):
            with ExitStack() as pes:
                ctx["issue_par_loads"] = issue_par_loads
                s5_prologue(ctx, j, pes, lds_[j])
                kb.barrier()
    kb.group_final([R_par], ds_p)
    for j in range(2):
        op("act", [R_par], [R_par], lambda e, j=j: e.activation(out=rg_m8sp[j][:], in_=rg_lm[j][:], func=AF.Exp, scale=-1.0))
        op("act", [R_par], [R_par], lambda e, j=j: e.activation(out=rg_m8sp[j][:], in_=rg_m8sp[j][:], func=AF.Ln, bias=1.0))
        op("dve", [R_par], [R_par], lambda e, j=j: e.tensor_scalar(out=rg_m8sp[j][:], in0=rg_m8sp[j][:], scalar1=-RG_C, scalar2=None, op0=ALU.mult))
    kb.barrier()
    passes = [("A", 1, 1024), ("B", 1, 1040), ("S", 16, 8)]
    for (pn, nseq, L) in passes:
        with ExitStack() as pes:
            run_pass(ctx, pn, nseq, L, pes)
            kb.barrier()
    kb.finish()


def colgroups(N, g=512):
    return [(c0, min(g, N - c0)) for c0 in range(0, N, g)]


def store_T(ctx, src_fn, nblk, m, dram_rows, stage, R_src, tag):
    kb, PS, RP, ident = ctx["kb"], ctx["PS"], ctx["RP"], ctx["ident"]
    R_stage = ctx.setdefault("stage_res", {}).setdefault(id(stage), Res("stage" + tag))
    GB = stage.shape[1] // 128
    for g0 in range(0, nblk, GB):
        ng = min(GB, nblk - g0)
        for b0 in range(g0, g0 + ng, 4):
            nb_ = min(4, g0 + ng - b0)
            bank = 6 + ((b0 // 4) % 2)
            for b in range(nb_):
                kb.op("pe", [R_src, ctx["R_const"]], [RP[bank]],
                      lambda e, b=b, b0=b0, bank=bank: e.transpose(out=PS[bank][0:m, b * 128:(b + 1) * 128], in_=src_fn(b0 + b), identity=ident[:, :]))
            kb.op("act", [RP[bank]], [R_stage],
                  lambda e, b0=b0, nb_=nb_, bank=bank, g0=g0: e.copy(out=stage[0:m, (b0 - g0) * 128:(b0 - g0 + nb_) * 128], in_=PS[bank][0:m, 0:nb_ * 128]))
        kb.dma("sp", dram_rows[:, g0 * 128:(g0 + ng) * 128], stage[0:m, 0:ng * 128], [R_stage], [], ctx["ds_out"])


def load_T(ctx, dram_rows, nblk, m, dst_fn, stage, R_dst, tag, ds):
    kb, PS, RP, ident = ctx["kb"], ctx["PS"], ctx["RP"], ctx["ident"]
    R_stage = ctx.setdefault("stage_res", {}).setdefault(id(stage), Res("lstage" + tag))
    ctx["uid"] = ctx.get("uid", 0) + 1
    ds = kb.dsem("lt%d" % ctx["uid"])
    kb.dma("sp", stage[0:m, 0:nblk * 128], dram_rows, [], [R_stage], ds)
    per = 512 // m
    gi = 0
    for b0 in range(0, nblk, per):
        nb_ = min(per, nblk - b0)
        bank = 6 + (gi % 2)
        gi += 1
        for b in range(nb_):
            kb.op("pe", [R_stage, ctx["R_const"]], [RP[bank]],
                  lambda e, b=b: e.transpose(out=PS[bank][:, b * m:(b + 1) * m], in_=stage[0:m, (b0 + b) * 128:(b0 + b + 1) * 128], identity=ident[0:m, 0:m]))
        kb.op("act", [RP[bank]], [R_dst],
              lambda e: e.copy(out=dst_fn(b0, nb_), in_=PS[bank][:, 0:nb_ * m].rearrange("p (b m) -> p b m", m=m)))


def s5_prologue_loads(ctx, j, pes, nat):
    nc, kb, I = ctx["nc"], ctx["kb"], ctx["I"]
    dma, op = kb.dma, kb.op
    PS, RP, ident, Rc = ctx["PS"], ctx["RP"], ctx["ident"], ctx["R_const"]
    sb = lambda n, s, d=F32: kb.sb("pl%d_%s" % (j, n), s, d, es=pes)
    ds = kb.dsem("pl%d" % j)
    R = Res("pl")
    Rn = Res("plnat")
    lr = sb("lr", [128, 32]); li = sb("li", [128, 32]); stp = sb("stp", [128, 32])
    Bre = sb("Bre", [128, 32, 16]); Bim = sb("Bim", [128, 32, 16])
    yield None
    sbn = lambda n, s, d=F32: kb.sb("pl%d_%s" % (j, n), s, d, es=nat)
    lrn = sbn("lrn", [32, 128]); lin = sbn("lin", [32, 128]); stn = sbn("stn", [2, 32])
    Bn = [sbn("Bn%d" % i, [32, 128, 16]) for i in range(2)]
    dma("sp", lrn[:], I["s5_lam_re"][j].rearrange("(pr g2) p -> pr (g2 p)", g2=2), [], [Rn], ds)
    dma("sp", lin[:], I["s5_lam_im"][j].rearrange("(pr g2) p -> pr (g2 p)", g2=2), [], [Rn], ds)
    with nc.allow_non_contiguous_dma(reason="tiny"):
        dma("sp", stn[:], I["s5_log_step"][j].rearrange("(pr g2) -> g2 pr", g2=2), [], [Rn], ds)
    dma("sp", Bn[0][:], I["s5_b_re"][j].rearrange("(pr g2) p h -> pr (g2 p) h", g2=2), [], [Rn], ds)
    dma("sp", Bn[1][:], I["s5_b_im"][j].rearrange("(pr g2) p h -> pr (g2 p) h", g2=2), [], [Rn], ds)
    kb.group_final([Rn], ds)
    bank = 4 + j
    op("pe", [Rn, Rc], [RP[bank]], lambda e: e.transpose(out=PS[bank][:, 0:32], in_=lrn[:, :], identity=ident[0:32, 0:32]))
    op("pe", [Rn, Rc], [RP[bank]], lambda e: e.transpose(out=PS[bank][:, 32:64], in_=lin[:, :], identity=ident[0:32, 0:32]))
    op("pe", [Rn, Rc], [RP[bank]], lambda e: e.matmul(PS[bank][:, 64:96], lhsT=ctx["sel2"][:, :], rhs=stn[:, :], start=True, stop=True))
    op("act", [RP[bank]], [R], lambda e: e.copy(out=lr[:], in_=PS[bank][:, 0:32]))
    op("act", [RP[bank]], [R], lambda e: e.copy(out=li[:], in_=PS[bank][:, 32:64]))
    op("act", [RP[bank]], [R], lambda e: e.copy(out=stp[:], in_=PS[bank][:, 64:96]))
    for i, Bd in enumerate((Bre, Bim)):
        for h in range(16):
            op("pe", [Rn, Rc], [RP[bank]], lambda e, h=h, i=i: e.transpose(out=PS[bank][:, h * 32:(h + 1) * 32], in_=Bn[i][:, :, h], identity=ident[0:32, 0:32]))
        op("act", [RP[bank]], [R], lambda e, Bd=Bd: e.copy(out=Bd[:, :, :].rearrange("p r h -> p h r"), in_=PS[bank][:, :].rearrange("p (h r) -> p h r", r=32)))
    yield dict(R=R, ds=ds, lr=lr, li=li, stp=stp, Bre=Bre, Bim=Bim)


def s5_prologue(ctx, j, pes, ld):
    nc, kb, I = ctx["nc"], ctx["kb"], ctx["I"]
    op, dma = kb.op, kb.dma
    PS, RP, ident, bmask = ctx["PS"], ctx["RP"], ctx["ident"], ctx["bmask"]
    Rc = ctx["R_const"]
    sb = lambda n, s, d=F32: kb.sb("pl%d_%s" % (j, n), s, d, es=pes)
    R, ds = ld["R"], ld["ds"]
    lr, li, stp, Bre, Bim = (ld[k] for k in ("lr", "li", "stp", "Bre", "Bim"))
    Cre = sb("Cre", [128, 32, 16]); Cim = sb("Cim", [128, 32, 16])
    Ch_re = sb("Chre", [16, 64, 64]); Ch_im = sb("Chim", [16, 64, 64])
    R_ch = Res("ch")
    ds_ch = kb.dsem("plc%d" % j)
    with nc.allow_non_contiguous_dma(reason="param layout"):
        dma("sp", Ch_re[:], I["s5_c_re"][j].rearrange("g h p -> h g p"), [], [R_ch], ds_ch)
        dma("sp", Ch_im[:], I["s5_c_im"][j].rearrange("g h p -> h g p"), [], [R_ch], ds_ch)
    kb.group_final([R_ch], ds_ch)
    if j == 1:
        ctx["issue_par_loads"]()
    for (Ch, Cd) in ((Ch_re, Cre), (Ch_im, Cim)):
        for g0 in range(0, 32, 16):
            bank = 4 + (g0 // 16)
            for pr in range(g0, g0 + 16):
                op("pe", [R_ch, Rc], [RP[bank]],
                   lambda e, pr=pr, Ch=Ch: e.transpose(out=PS[bank][:, (pr - g0) * 16:(pr - g0 + 1) * 16],
                                                       in_=Ch[0:16, 2 * pr:2 * pr + 2, :].rearrange("h g p -> h (g p)"),
                                                       identity=ident[0:16, 0:16]))
            op("act", [RP[bank]], [R], lambda e, Cd=Cd: e.copy(out=Cd[:, g0:g0 + 16, :], in_=PS[bank][:, 0:256].rearrange("p (a h) -> p a h", h=16)))

    t = lambda n: sb(n, [128, 32])
    ang = t("ang"); mag = t("mag"); sn = t("sn"); cs = t("cs"); tmp = t("tmp"); tmp2 = t("tmp2")
    abre = t("abre"); abim = t("abim"); qre = t("qre"); qim = t("qim"); ki = sb("ki", [128, 32], I32)
    V = lambda f: op("dve", [R], [R], f)
    A = lambda f: op("act", [R], [R], f)
    A(lambda e: e.activation(out=stp[:], in_=stp[:], func=AF.Exp))
    V(lambda e: e.tensor_tensor(out=ang[:], in0=li[:], in1=stp[:], op=ALU.mult))
    V(lambda e: e.tensor_tensor(out=mag[:], in0=lr[:], in1=stp[:], op=ALU.mult))
    A(lambda e: e.activation(out=mag[:], in_=mag[:], func=AF.Exp))

    def sin_of(dst, shift):
        V(lambda e: e.tensor_scalar(out=tmp[:], in0=ang[:], scalar1=shift, scalar2=1.0 / TWO_PI, op0=ALU.add, op1=ALU.mult))
        V(lambda e: e.tensor_copy(out=ki[:], in_=tmp[:]))
        V(lambda e: e.tensor_copy(out=tmp2[:], in_=ki[:]))
        V(lambda e: e.tensor_tensor(out=tmp[:], in0=tmp[:], in1=tmp2[:], op=ALU.subtract))
        V(lambda e: e.tensor_scalar(out=tmp2[:], in0=tmp[:], scalar1=0.5, scalar2=None, op0=ALU.is_gt))
        V(lambda e: e.tensor_tensor(out=tmp[:], in0=tmp[:], in1=tmp2[:], op=ALU.subtract))
        V(lambda e: e.tensor_scalar(out=tmp2[:], in0=tmp[:], scalar1=-0.5, scalar2=None, op0=ALU.is_lt))
        V(lambda e: e.tensor_tensor(out=tmp[:], in0=tmp[:], in1=tmp2[:], op=ALU.add))
        V(lambda e: e.tensor_scalar(out=tmp[:], in0=tmp[:], scalar1=TWO_PI, scalar2=math.pi, op0=ALU.mult, op1=ALU.min))
        V(lambda e: e.tensor_scalar(out=tmp[:], in0=tmp[:], scalar1=-math.pi, scalar2=None, op0=ALU.max))
        A(lambda e: e.activation(out=dst[:], in_=tmp[:], func=AF.Sin))

    sin_of(sn, 0.0)
    sin_of(cs, math.pi / 2)
    V(lambda e: e.tensor_tensor(out=abre[:], in0=mag[:], in1=cs[:], op=ALU.mult))
    V(lambda e: e.tensor_tensor(out=abim[:], in0=mag[:], in1=sn[:], op=ALU.mult))
    den = t("den"); nr = t("nr")
    V(lambda e: e.tensor_tensor(out=den[:], in0=lr[:], in1=lr[:], op=ALU.mult))
    V(lambda e: e.tensor_tensor(out=tmp[:], in0=li[:], in1=li[:], op=ALU.mult))
    V(lambda e: e.tensor_tensor(out=den[:], in0=den[:], in1=tmp[:], op=ALU.add))
    V(lambda e: e.reciprocal(out=den[:], in_=den[:]))
    V(lambda e: e.tensor_scalar(out=nr[:], in0=abre[:], scalar1=-1.0, scalar2=None, op0=ALU.add))
    V(lambda e: e.tensor_tensor(out=qre[:], in0=nr[:], in1=lr[:], op=ALU.mult))
    V(lambda e: e.tensor_tensor(out=tmp[:], in0=abim[:], in1=li[:], op=ALU.mult))
    V(lambda e: e.tensor_tensor(out=qre[:], in0=qre[:], in1=tmp[:], op=ALU.add))
    V(lambda e: e.tensor_tensor(out=qre[:], in0=qre[:], in1=den[:], op=ALU.mult))
    V(lambda e: e.tensor_tensor(out=qim[:], in0=abim[:], in1=lr[:], op=ALU.mult))
    V(lambda e: e.tensor_tensor(out=tmp[:], in0=nr[:], in1=li[:], op=ALU.mult))
    V(lambda e: e.tensor_tensor(out=qim[:], in0=qim[:], in1=tmp[:], op=ALU.subtract))
    V(lambda e: e.tensor_tensor(out=qim[:], in0=qim[:], in1=den[:], op=ALU.mult))
    pwr = sb("pwr", [128, 9, 32]); pwi = sb("pwi", [128, 9, 32])
    V(lambda e: e.memset(pwr[:, 0, :], 1.0))
    V(lambda e: e.memset(pwi[:, 0, :], 0.0))
    for k in range(1, 9):
        V(lambda e, k=k: e.tensor_tensor(out=pwr[:, k, :], in0=pwr[:, k - 1, :], in1=abre[:], op=ALU.mult))
        V(lambda e, k=k: e.tensor_tensor(out=tmp[:], in0=pwi[:, k - 1, :], in1=abim[:], op=ALU.mult))
        V(lambda e, k=k: e.tensor_tensor(out=pwr[:, k, :], in0=pwr[:, k, :], in1=tmp[:], op=ALU.subtract))
        V(lambda e, k=k: e.tensor_tensor(out=pwi[:, k, :], in0=pwr[:, k - 1, :], in1=abim[:], op=ALU.mult))
        V(lambda e, k=k: e.tensor_tensor(out=tmp[:], in0=pwi[:, k - 1, :], in1=abre[:], op=ALU.mult))
        V(lambda e, k=k: e.tensor_tensor(out=pwi[:, k, :], in0=pwi[:, k, :], in1=tmp[:], op=ALU.add))
    Rst = ctx["R_state"]
    op("dve", [R], [Rst], lambda e: e.tensor_copy(out=ctx["A8re"][j][:], in_=pwr[:, 8, :]))
    op("dve", [R], [Rst], lambda e: e.tensor_copy(out=ctx["A8im"][j][:], in_=pwi[:, 8, :]))
    op("dve", [R], [Rst], lambda e: e.tensor_scalar(out=ctx["A8imn"][j][:], in0=pwi[:, 8, :], scalar1=-1.0, scalar2=None, op0=ALU.mult))

    def bc(x2d):
        return x2d.unsqueeze(2).broadcast_to([128, 32, 16])

    T3 = lambda n: sb(n, [128, 32, 16])
    w1 = T3("w1"); w2 = T3("w2")

    def cmul(dre, dim, xre, xim, sre, sim_):
        V(lambda e: e.tensor_tensor(out=w1[:], in0=xre, in1=bc(sre), op=ALU.mult))
        V(lambda e: e.tensor_tensor(out=w2[:], in0=xim, in1=bc(sim_), op=ALU.mult))
        V(lambda e: e.tensor_tensor(out=dre, in0=w1[:], in1=w2[:], op=ALU.subtract))
        V(lambda e: e.tensor_tensor(out=w1[:], in0=xre, in1=bc(sim_), op=ALU.mult))
        V(lambda e: e.tensor_tensor(out=w2[:], in0=xim, in1=bc(sre), op=ALU.mult))
        V(lambda e: e.tensor_tensor(out=dim, in0=w1[:], in1=w2[:], op=ALU.add))

    Bbre = T3("Bbre"); Bbim = T3("Bbim")
    cmul(Bbre[:], Bbim[:], Bre[:], Bim[:], qre[:], qim[:])
    Bpre = sb("Bpre", [128, 32, 32]); Bpim = sb("Bpim", [128, 32, 32])
    V(lambda e: e.memset(Bpre[:], 0.0))
    V(lambda e: e.memset(Bpim[:], 0.0))
    for g2 in range(2):
        hp = slice(g2 * 64, (g2 + 1) * 64); hc = slice(g2 * 16, (g2 + 1) * 16)
        V(lambda e, hp=hp, hc=hc: e.tensor_copy(out=Bpre[hp, :, hc], in_=Bbre[hp, :, :]))
        V(lambda e, hp=hp, hc=hc: e.tensor_copy(out=Bpim[hp, :, hc], in_=Bbim[hp, :, :]))

    WY = sb("WY", [128, 32, 8, 2, 32], BF16)
    KI = sb("KI", [128, 8, 8, 128], BF16)
    WX = sb("WX", [128, 8, 8, 2, 128], BF16)
    V(lambda e: e.memset(WY[:].rearrange("p a b c d -> p (a b c d)"), 0.0))
    CAre = T3("CAre"); CAim = T3("CAim")
    CApre = sb("CApre", [128, 32, 32]); CApimn = sb("CApimn", [128, 32, 32])
    V(lambda e: e.memset(CApre[:], 0.0))
    V(lambda e: e.memset(CApimn[:], 0.0))
    kit = sb("kit", [128, 128])
    dcol = ctx["s5_dd"][j]
    R_o = Res("pl_out")
    R_kit = Res("kit")
    for k in range(9):
        cmul(CAre[:], CAim[:], Cre[:], Cim[:], pwr[:, k, :], pwi[:, k, :])
        for g2 in range(2):
            hp = slice(g2 * 64, (g2 + 1) * 64); hc = slice(g2 * 16, (g2 + 1) * 16)
            V(lambda e, hp=hp, hc=hc: e.tensor_copy(out=CApre[hp, :, hc], in_=CAre[hp, :, :]))
            V(lambda e, hp=hp, hc=hc: e.tensor_scalar(out=CApimn[hp, :, hc], in0=CAim[hp, :, :], scalar1=-1.0, scalar2=None, op0=ALU.mult))
            if k >= 1:
                V(lambda e, hp=hp, hc=hc, k=k: e.tensor_copy(out=WY[hp, :, k - 1, 0, hc], in_=CAre[hp, :, :]))
                V(lambda e, hp=hp, hc=hc, k=k: e.tensor_scalar(out=WY[hp, :, k - 1, 1, hc], in0=CAim[hp, :, :], scalar1=-1.0, scalar2=None, op0=ALU.mult))
        if k <= 7:
            for c in range(8):
                bank = 4 + (c % 2)
                cs4 = slice(4 * c, 4 * c + 4)
                op("pe", [R], [RP[bank]], lambda e, cs4=cs4: e.matmul(PS[bank][:, 0:128], lhsT=Bpre[:, cs4, :].rearrange("p a b -> p (a b)"),
                                                                 rhs=CApre[:, cs4, :].rearrange("p a b -> p (a b)"), start=True, stop=False))
                op("pe", [R], [RP[bank]], lambda e, cs4=cs4: e.matmul(PS[bank][:, 0:128], lhsT=Bpim[:, cs4, :].rearrange("p a b -> p (a b)"),
                                                                 rhs=CApimn[:, cs4, :].rearrange("p a b -> p (a b)"), start=False, stop=True))
                if k == 0:
                    op("dve", [RP[bank], Rc], [R_kit], lambda e, bank=bank: e.tensor_tensor(out=kit[:], in0=PS[bank][:, 0:128], in1=bmask[:], op=ALU.mult))
                    op("dve", [R_kit, Rc, ctx["R_par"]], [R_o], lambda e, c=c: e.scalar_tensor_tensor(out=KI[:, c, 0, :], in0=ident[:], scalar=dcol[:, c:c + 1], in1=kit[:], op0=ALU.mult, op1=ALU.add))
                else:
                    op("dve", [RP[bank], Rc], [R_o], lambda e, c=c, k=k, bank=bank: e.tensor_tensor(out=KI[:, c, k, :], in0=PS[bank][:, 0:128], in1=bmask[:], op=ALU.mult))
    XBre = sb("XBre", [128, 32, 32]); XBim = sb("XBim", [128, 32, 32])
    x1 = sb("x1", [128, 32, 32]); x2 = sb("x2", [128, 32, 32])
    bc32 = lambda x2d: x2d.unsqueeze(2).broadcast_to([128, 32, 32])
    for tau in range(8):
        k = 7 - tau
        V(lambda e, k=k: e.tensor_tensor(out=x1[:], in0=Bpre[:], in1=bc32(pwr[:, k, :]), op=ALU.mult))
        V(lambda e, k=k: e.tensor_tensor(out=x2[:], in0=Bpim[:], in1=bc32(pwi[:, k, :]), op=ALU.mult))
        V(lambda e: e.tensor_tensor(out=XBre[:], in0=x1[:], in1=x2[:], op=ALU.subtract))
        V(lambda e, k=k: e.tensor_tensor(out=x1[:], in0=Bpre[:], in1=bc32(pwi[:, k, :]), op=ALU.mult))
        V(lambda e, k=k: e.tensor_tensor(out=x2[:], in0=Bpim[:], in1=bc32(pwr[:, k, :]), op=ALU.mult))
        V(lambda e: e.tensor_tensor(out=XBim[:], in0=x1[:], in1=x2[:], op=ALU.add))
        for ri, XB in enumerate((XBre, XBim)):
            for c in range(8):
                bank = 4 + (c % 2)
                op("pe", [R, Rc], [RP[bank]], lambda e, c=c, XB=XB: e.transpose(out=PS[bank][:, 0:128], in_=XB[:, 4 * c:4 * c + 4, :].rearrange("p a b -> p (a b)"), identity=ident[:, :]))
                op("act", [RP[bank]], [R_o], lambda e, c=c, ri=ri, tau=tau, bank=bank: e.copy(out=WX[:, c, tau, ri, :], in_=PS[bank][:, 0:128]))
    dma("sp", ctx["WXd"][j].rearrange("p a b c d -> p (a b c d)"), WX[:].rearrange("p a b c d -> p (a b c d)"), [R, R_o], [], ds)
    dma("sp", ctx["WYd"][j].rearrange("p a b c d -> p (a b c d)"), WY[:].rearrange("p a b c d -> p (a b c d)"), [R], [], ds)
    dma("sp", ctx["KId"][j].rearrange("p a b c -> p (a b c)"), KI[:].rearrange("p a b c -> p (a b c)"), [R, R_o], [], ds)


def run_pass(ctx, pn, nseq, L, pes):
    nc, kb, I, O = ctx["nc"], ctx["kb"], ctx["I"], ctx["O"]
    op, dma = kb.op, kb.dma
    PS, RP, ident = ctx["PS"], ctx["RP"], ctx["ident"]
    Rc = ctx["R_const"]
    N = nseq * L
    NT = (N + 127) // 128
    rows = [min(128, N - t * 128) for t in range(NT)]
    NC = N // 8
    sb = lambda n, s, d=F32, es=None: kb.sb("p%s_%s" % (pn, n), s, d, es=es or pes)

    h_tok = sb("htok", [128, NT, D])
    hT = sb("hT", [128, NB, N], BF16)
    R_h = [Res("h%d" % t) for t in range(NT)]
    R_hT = [Res("hT%d" % t) for t in range(NT)]
    ds_in = kb.dsem("in" + pn)
    ds_w = kb.dsem("w" + pn)
    ds_ln = kb.dsem("ln" + pn)
    gbc = sb("gbc", [128, D]); bbc = sb("bbc", [128, D])
    R_ln = Res("ln")
    NZ = 4
    zt = [sb("zt%d" % i, [128, D]) for i in range(NZ)]
    R_zt = [Res("zt%d" % i) for i in range(NZ)]
    stat = [sb("stat%d" % i, [128, 2, 6]) for i in range(NZ)]
    mv = [sb("mv%d" % i, [128, 2]) for i in range(NZ)]
    rstd = [sb("rstd%d" % i, [128, 1]) for i in range(NZ)]
    NWS = 8 if pn == "S" else 4
    wslot = [sb("wslot%d" % i, [128, NB, 128], BF16) for i in range(NWS)]
    R_ws = [Res("ws%d" % i) for i in range(NWS)]
    ws_i = [0]
    ds_ws = [kb.dsem("ws%s%d" % (pn, i)) for i in range(NWS)]

    if pn == "A":
        dma("sp", h_tok[0:16, 0, :], I["meta_tokens"][:, :], [], [R_h[0]], ds_in)
        dma("sp", h_tok[16:128, 0, :], I["x_prompt"][0:112, :], [], [R_h[0]], ds_in)
        for t in range(1, NT):
            dma("sp", h_tok[:, t, :], I["x_prompt"][112 + 128 * (t - 1):112 + 128 * t, :], [], [R_h[t]], ds_in)
    elif pn == "B":
        for t in range(NT):
            dma("sp", h_tok[0:rows[t], t, :], I["x_prompt"][1008 + 128 * t:1008 + 128 * t + rows[t], :], [], [R_h[t]], ds_in)
    else:
        dma("sp", h_tok[:, 0, :], I["x_sample"][:, :], [], [R_h[0]], ds_in)

    kb.group_final(R_h, ds_in)

    def to_hT(t):
        r = rows[t]
        for half in range(2):
            bank = 4 + half
            for b in range(4):
                blk = half * 4 + b
                op("pe", [R_h[t], Rc], [RP[bank]], lambda e, b=b, blk=blk: e.transpose(out=PS[bank][:, b * 128:b * 128 + r], in_=h_tok[0:r, t, blk * 128:(blk + 1) * 128], identity=ident[0:r, 0:r]))
            op("act", [RP[bank]], [R_hT[t]], lambda e, half=half: e.copy(out=hT[:, half * 4:half * 4 + 4, t * 128:t * 128 + r],
                                                                     in_=PS[bank][:, :].rearrange("p (b n) -> p b n", n=128)[:, :, 0:r]))

    for t in range(NT):
        to_hT(t)

    nlayers = ctx["dbg"].get("nlayers", DEPTH) if ctx["dbg"] else DEPTH
    glist = []
    for l_ in range(nlayers):
        j_ = l_ // 2
        if l_ % 2 == 0:
            glist += [(I["s5_w_in"][j_], c * 128) for c in range(NB)]
        else:
            for c in range(NB):
                glist += [(I["rg_w_in"][j_], c * 128), (I["rg_w_in"][j_], D + c * 128)]
        for v in range(FBH):
            glist += [(I["ffn_w_up"][l_], v * 128), (I["ffn_w_up"][l_], (v + FBH) * 128)]
    wplan = {"list": glist, "issued": 0, "hooks": {}, "off": 0, "next_off": 0}

    def plan_w(blocks, hooks=None):
        wplan["off"] = wplan["next_off"]
        wplan["next_off"] = wplan["off"] + len(blocks)
        for k, f in (hooks or {}).items():
            gi = wplan["off"] + k
            if gi < wplan["issued"]:
                f()
            else:
                wplan["hooks"][gi] = f

    def write_back(k):
        s = k % NWS
        dma("sp", ctx["WBLK"][k], wslot[s][:].rearrange("p a b -> p (a b)"), [R_ws[s]], [], ctx["ds_wb"])

    def load_w_block(i, pf=NWS - 1):
        gi = wplan["off"] + i
        while wplan["issued"] < min(len(wplan["list"]), gi + pf + 1):
            k = wplan["issued"]
            src2d, c0 = wplan["list"][k]
            s = k % NWS
            if pn == "A":
                dma("pool", wslot[s][:], src2d[:, c0:c0 + 128].rearrange("(kc p) f -> p kc f", p=128), [], [R_ws[s]], ds_ws[s])
                if k >= 2:
                    write_back(k - 2)
            else:
                dma("pool", wslot[s][:].rearrange("p a b -> p (a b)"), ctx["WBLK"][k], [], [R_ws[s]], ds_ws[s])
            wplan["issued"] += 1
            if k in wplan["hooks"]:
                wplan["hooks"].pop(k)()
        return gi % NWS

    def up_matmul(s, ps_banks, extra_reads):
        for gi, (c0, cn) in enumerate(colgroups(N)):
            bank = ps_banks[gi]
            for kc in range(NB):
                op("pe", [R_ws[s]] + R_hT + extra_reads, [RP[bank]],
                   lambda e, kc=kc, bank=bank, c0=c0, cn=cn: e.matmul(PS[bank][:, 0:cn], lhsT=wslot[s][:, kc, :], rhs=hT[:, kc, c0:c0 + cn], start=(kc == 0), stop=(kc == NB - 1)))

    def layer_norm_tile(t, zi, li_, k, last):
        r = rows[t]
        z = zt[zi]
        for hh in range(2):
            op("dve", [R_zt[zi]], [R_zt[zi]], lambda e, hh=hh: e.bn_stats(out=stat[zi][0:r, hh, :], in_=z[0:r, hh * 512:(hh + 1) * 512]))
        op("dve", [R_zt[zi]], [R_zt[zi]], lambda e: e.bn_aggr(out=mv[zi][0:r, :], in_=stat[zi][0:r, :, :].rearrange("p a b -> p (a b)")))
        op("act", [R_zt[zi]], [R_zt[zi]], lambda e: e.activation(out=rstd[zi][0:r, :], in_=mv[zi][0:r, 1:2], func=AF.Sqrt, bias=LN_EPS_AP[0:r, :], scale=1.0))
        op("dve", [R_zt[zi]], [R_zt[zi]], lambda e: e.reciprocal(out=rstd[zi][0:r, :], in_=rstd[zi][0:r, :]))
        op("dve", [R_zt[zi]], [R_zt[zi]], lambda e: e.tensor_scalar(out=z[0:r, :], in0=z[0:r, :], scalar1=mv[zi][0:r, 0:1], scalar2=rstd[zi][0:r, 0:1], op0=ALU.subtract, op1=ALU.mult))
        op("pool", [R_zt[zi], R_ln], [R_zt[zi]], lambda e: e.tensor_tensor(out=z[0:r, :], in0=z[0:r, :], in1=gbc[0:r, :], op=ALU.mult))
        op("pool", [R_zt[zi], R_ln], [R_h[t]], lambda e: e.tensor_tensor(out=h_tok[0:r, t, :], in0=z[0:r, :], in1=bbc[0:r, :], op=ALU.add))
        if not last:
            return lambda: to_hT(t)
        else:
            if pn == "A":
                if t == 0:
                    dma("sp", O["y_prompt"][0:112, :], h_tok[16:128, 0, :], [R_h[t]], [], ctx["ds_out"])
                else:
                    dma("sp", O["y_prompt"][112 + 128 * (t - 1):112 + 128 * t, :], h_tok[:, t, :], [R_h[t]], [], ctx["ds_out"])
            elif pn == "B":
                dma("sp", O["y_prompt"][1008 + 128 * t:1008 + 128 * t + r, :], h_tok[0:r, t, :], [R_h[t]], [], ctx["ds_out"])
            else:
                dma("sp", O["y_sample"][:, :], h_tok[:, 0, :], [R_h[t]], [], ctx["ds_out"])
            return lambda: None

    LN_EPS_AP = sb("lneps", [128, 1])
    op("dve", [], [Rc], lambda e: e.memset(LN_EPS_AP[:], LN_EPS))
    one_ap = sb("one", [128, 1])
    op("dve", [], [Rc], lambda e: e.memset(one_ap[:], 1.0))

    def load_ln(li_, k):
        dma("sp", gbc[:], I["ln_g"][li_, k].partition_broadcast(128), [], [R_ln], ds_ln)
        dma("sp", bbc[:], I["ln_b"][li_, k].partition_broadcast(128), [], [R_ln], ds_ln)

    env = dict(ctx=ctx, pn=pn, nseq=nseq, L=L, N=N, NT=NT, rows=rows, NC=NC, sb=sb, h_tok=h_tok, hT=hT,
               R_h=R_h, R_hT=R_hT, ds_w=ds_w, ds_in=ds_in, zt=zt, R_zt=R_zt, load_w_block=load_w_block, plan_w=plan_w,
               up_matmul=up_matmul, layer_norm_tile=layer_norm_tile, one_ap=one_ap, load_ln=load_ln, wslot=wslot, R_ws=R_ws)

    for li_ in range(nlayers):
        j = li_ // 2
        with ExitStack() as les:
            if li_ % 2 == 0:
                s5_layer(env, li_, j, les)
            else:
                rg_layer(env, li_, j, les)
            kb.barrier()
        with ExitStack() as les:
            ffn_layer(env, li_, les, last=(li_ == nlayers - 1))
            if pn == "A" and li_ == nlayers - 1:
                for k in range(max(0, len(glist) - 2), len(glist)):
                    write_back(k)
            kb.barrier()


def conv_taps(kb, xs, acc, nseq, L, K, wcols, bcol, R_main, R_halo, R_acc):
    kb.op("act", [R_main], [R_acc], lambda e: e.activation(out=acc[:, :, :], in_=xs[:, :, K - 1:K - 1 + L], func=AF.Identity, scale=wcols[K - 1], bias=bcol))
    for k in range(K - 1):
        kb.op("dve", [R_main, R_halo, R_acc], [R_acc], lambda e, k=k: e.scalar_tensor_tensor(out=acc[:, :, :], in0=xs[:, :, k:k + L], scalar=wcols[k], in1=acc[:, :, :], op0=ALU.mult, op1=ALU.add))


def ffn_layer(env, li_, les, last):
    ctx = env["ctx"]; kb = ctx["kb"]; nc = ctx["nc"]; I = ctx["I"]; O = ctx["O"]
    op, dma = kb.op, kb.dma
    PS, RP = ctx["PS"], ctx["RP"]
    pn, nseq, L, N, NT, rows = env["pn"], env["nseq"], env["L"], env["N"], env["NT"], env["rows"]
    sb = lambda n, s, d=F32: env["sb"]("f%d_%s" % (li_, n), s, d, es=les)
    h_tok, hT = env["h_tok"], env["hT"]
    actT = sb("actT", [128, FBH, N], BF16)
    R_act = [Res("act%d" % v) for v in range(FBH)]
    wd = sb("wd", [128, FBH, D], BF16)
    R_wd = [Res("wd%d" % v) for v in range(FBH)]
    xs = [sb("xs%d" % i, [128, nseq, 2 + L]) for i in range(2)]
    acc4 = [sb("acc%d" % i, [128, nseq, L]) for i in range(4)]
    R_xs = [Res("xs%d" % i) for i in range(2)]
    R_xh = [Res("xh%d" % i) for i in range(2)]
    R_acc4 = [Res("acc%d" % i) for i in range(4)]
    cw, cb = ctx["ffn_cw"][li_], ctx["ffn_cb"][li_]
    Rst = ctx["R_state"]
    if nseq == 1:
        hist = ctx["ffn_hist_p"][li_]
        R_hist = Rst
    else:
        hist = sb("hist", [128, FB, nseq, 2])
        R_hist = Res("hist")
        stage = sb("hstage", [32, FF2])
        load_T(ctx, I["state_ffn_conv"][li_], FB, 32, lambda b0, nb_: hist[:, b0:b0 + nb_, :, :].rearrange("p b s k -> p b (s k)"), stage, R_hist, "fh", env["ds_in"])
    env["load_ln"](li_, 1)
    def wd_hook(v):
        def f():
            if pn == "A":
                dma("pool", wd[:, v, :], I["ffn_w_down"][li_][v * 128:(v + 1) * 128, :], [], [R_wd[v]], env["ds_w"])
            else:
                dma("pool", wd[:, v, :], ctx["WDd"][li_][:, v * D:(v + 1) * D], [], [R_wd[v]], env["ds_w"])
            if v == FBH - 1:
                kb.group_final(R_wd, env["ds_w"])
        return f
    blocks = []
    for v in range(FBH):
        blocks += [(I["ffn_w_up"][li_], v * 128), (I["ffn_w_up"][li_], (v + FBH) * 128)]
    env["plan_w"](blocks, {2 * v + 1: wd_hook(v) for v in range(FBH)})
    ngrp = len(colgroups(N))
    banksets = [[0, 1, 2][:ngrp], [3, 4, 5][:ngrp]]
    bi = 0
    for v in range(FBH):
        acc = acc4[2 * (v % 2):2 * (v % 2) + 2]
        R_acc = R_acc4[2 * (v % 2):2 * (v % 2) + 2]
        for which in range(2):
            blk = v + which * FBH
            s = env["load_w_block"](2 * v + which)
            banks = banksets[bi % 2]
            bi += 1
            env["up_matmul"](s, banks, [])
            x = xs[which]
            op("act", [R_hist], [R_xh[which]], lambda e, x=x, blk=blk: e.copy(out=x[:, :, 0:2], in_=hist[:, blk, :, :]))
            for gi, (c0, cn) in enumerate(colgroups(N)):
                if nseq == 1:
                    op("act", [RP[banks[gi]]], [R_xs[which]], lambda e, gi=gi, c0=c0, cn=cn, x=x: e.copy(out=x[:, 0, 2 + c0:2 + c0 + cn], in_=PS[banks[gi]][:, 0:cn]))
                else:
                    op("act", [RP[banks[gi]]], [R_xs[which]], lambda e, gi=gi, cn=cn, x=x: e.copy(out=x[:, :, 2:2 + L], in_=PS[banks[gi]][:, 0:cn].rearrange("p (s t) -> p s t", t=L)))
            op("act", [R_xs[which]], [R_hist], lambda e, x=x, blk=blk: e.copy(out=hist[:, blk, :, :], in_=x[:, :, L:L + 2]))
            conv_taps(kb, x, acc[which], nseq, L, 3, [cw[k][:, blk:blk + 1] for k in range(3)], cb[:, blk:blk + 1], R_xs[which], R_xh[which], R_acc[which])
        op("act", [R_acc[1]], [R_acc[1]], lambda e, acc=acc: e.activation(out=acc[1][:, :, :], in_=acc[1][:, :, :], func=AF.Gelu_apprx_tanh))
        op("dve", [R_acc[0], R_acc[1]], [R_act[v]], lambda e, v=v, acc=acc: e.tensor_tensor(out=actT[:, v, :], in0=acc[0][:, :, :].rearrange("p s t -> p (s t)"), in1=acc[1][:, :, :].rearrange("p s t -> p (s t)"), op=ALU.mult))
    if pn == "A":
        dma("sp", ctx["WDd"][li_], wd[:].rearrange("p a b -> p (a b)"), R_wd, [], ctx["ds_wb"])
    if pn != "A":
        stg = sb("ostage", [32, 256 if nseq == 1 else 2048])
        m = 2 * nseq
        store_T(ctx, lambda b: hist[:, b, :, :].rearrange("p s k -> p (s k)"), FB, m,
                (O["ffn_conv_p"] if nseq == 1 else O["ffn_conv_s"])[li_], stg, R_hist, "fo")
    pending = None
    for t in range(NT):
        r = rows[t]
        pb = [0, 1] if t % 2 == 0 else [2, 3]
        for v in range(FBH):
            for hh in range(2):
                op("pe", [R_act[v], R_wd[v]], [RP[pb[hh]]], lambda e, v=v, hh=hh: e.matmul(PS[pb[hh]][0:r, :], lhsT=actT[:, v, t * 128:t * 128 + r], rhs=wd[:, v, hh * 512:(hh + 1) * 512], start=(v == 0), stop=(v == FBH - 1)))
        zi = t % 4
        z = env["zt"][zi]
        for hh in range(2):
            op("dve", [env["R_h"][t], RP[pb[hh]]], [env["R_zt"][zi]], lambda e, hh=hh: e.scalar_tensor_tensor(out=z[0:r, hh * 512:(hh + 1) * 512], in0=h_tok[0:r, t, hh * 512:(hh + 1) * 512], scalar=DN_ALPHA, in1=PS[pb[hh]][0:r, :], op0=ALU.mult, op1=ALU.add))
        if pending:
            pending()
        pending = env["layer_norm_tile"](t, zi, li_, 1, last)
    pending()


def rg_layer(env, li_, j, les):
    ctx = env["ctx"]; kb = ctx["kb"]; nc = ctx["nc"]; I = ctx["I"]; O = ctx["O"]
    op, dma = kb.op, kb.dma
    PS, RP = ctx["PS"], ctx["RP"]
    pn, nseq, L, N, NT, rows = env["pn"], env["nseq"], env["L"], env["N"], env["NT"], env["rows"]
    sb = lambda n, s, d=F32: env["sb"]("r%d_%s" % (li_, n), s, d, es=les)
    h_tok, hT = env["h_tok"], env["hT"]
    Rst = ctx["R_state"]
    hgT = sb("hgT", [128, NB, N], BF16)
    R_hg = [Res("hg%d" % c) for c in range(NB)]
    wg = sb("wg", [128, 4, 2, 512], BF16)
    R_wg = Res("wg")
    wo = sb("wo", [128, NB, D], BF16)
    R_wo = Res("wo")
    if pn == "A":
        for n in range(4):
            dma("pool", wg[:, n, :, :], I["rg_w_gates"][j, n].rearrange("(kc p) f -> p kc f", p=128), [], [R_wg], env["ds_w"])
    else:
        dma("pool", wg[:].rearrange("p a b c -> p (a b c)"), ctx["RGWGd"][j], [], [R_wg], env["ds_w"])
    kb.group_final([R_wg], env["ds_w"])
    env["load_ln"](li_, 0)
    if nseq == 1:
        hist = ctx["rg_hist_p"][j]; hst = ctx["rg_h_p"][j]; R_hist = Rst
    else:
        hist = sb("hist", [128, NB, nseq, 3]); hst = sb("hst", [128, NB, nseq]); R_hist = Res("rhist")
        stage = sb("stage", [48, D])
        load_T(ctx, I["state_rg_conv"][j], NB, 48, lambda b0, nb_: hist[:, b0:b0 + nb_, :, :].rearrange("p b s k -> p b (s k)"), stage, R_hist, "rc", env["ds_in"])
        stage2 = sb("stage2", [16, D])
        load_T(ctx, I["state_rg_h"][j], NB, 16, lambda b0, nb_: hst[:, b0:b0 + nb_, :], stage2, R_hist, "rh", env["ds_in"])
    xs = [sb("xs%d" % i, [128, nseq, 3 + L]) for i in range(2)]
    xc = [[sb("xc%d_%d" % (b, i), [128, nseq, L]) for i in range(2)] for b in range(2)]
    xcb = [[sb("xcb%d_%d" % (b, i), [128, N], BF16) for i in range(2)] for b in range(2)]
    gl = [[sb("gl%d_%d" % (b, i), [128, N]) for i in range(2)] for b in range(2)]
    work = [[sb("wk%d_%d" % (b, i), [128, nseq, L]) for i in range(3)] for b in range(2)]
    R_xs = [Res() for _ in range(2)]; R_xh = [Res() for _ in range(2)]
    R_xc = [[Res() for _ in range(2)] for _ in range(2)]; R_gl = [[Res() for _ in range(2)] for _ in range(2)]
    R_wk = [Res("rgwork0"), Res("rgwork1")]
    cw, cb, bg, m8 = ctx["rg_cw"][j], ctx["rg_cb"][j], ctx["rg_bg"][j], ctx["rg_m8sp"][j]
    one_ap = env["one_ap"]
    ngrp = len(colgroups(N))
    banksets = [[0, 1, 2][:ngrp], [3, 4, 5][:ngrp]]
    bi = [0]
    blocks = []
    for c in range(NB):
        blocks += [(I["rg_w_in"][j], c * 128), (I["rg_w_in"][j], D + c * 128)]
    env["plan_w"](blocks)
    flat = lambda t: t[:, :, :].rearrange("p s t -> p (s t)")

    def stage_proj(n):
        pb_ = n % 2
        for q in range(2):
            c = 2 * n + q
            s = env["load_w_block"](2 * c)
            banks = banksets[bi[0] % 2]; bi[0] += 1
            env["up_matmul"](s, banks, [])
            for gi, (c0, cn) in enumerate(colgroups(N)):
                op("act", [RP[banks[gi]]], [R_gl[pb_][q]], lambda e, gi=gi, c0=c0, cn=cn, q=q, banks=banks: e.activation(out=gl[pb_][q][:, c0:c0 + cn], in_=PS[banks[gi]][:, 0:cn], func=AF.Gelu_apprx_tanh))
            s = env["load_w_block"](2 * c + 1)
            banks = banksets[bi[0] % 2]; bi[0] += 1
            env["up_matmul"](s, banks, [])
            x = xs[q]
            op("act", [R_hist], [R_xh[q]], lambda e, x=x, c=c: e.copy(out=x[:, :, 0:3], in_=hist[:, c, :, :]))
            for gi, (c0, cn) in enumerate(colgroups(N)):
                if nseq == 1:
                    op("act", [RP[banks[gi]]], [R_xs[q]], lambda e, gi=gi, c0=c0, cn=cn, x=x, banks=banks: e.copy(out=x[:, 0, 3 + c0:3 + c0 + cn], in_=PS[banks[gi]][:, 0:cn]))
                else:
                    op("act", [RP[banks[gi]]], [R_xs[q]], lambda e, gi=gi, cn=cn, x=x, banks=banks: e.copy(out=x[:, :, 3:3 + L], in_=PS[banks[gi]][:, 0:cn].rearrange("p (s t) -> p s t", t=L)))
            op("act", [R_xs[q]], [R_hist], lambda e, x=x, c=c: e.copy(out=hist[:, c, :, :], in_=x[:, :, L:L + 3]))
            conv_taps(kb, x, xc[pb_][q], nseq, L, 4, [cw[k][:, c:c + 1] for k in range(4)], cb[:, c:c + 1], R_xs[q], R_xh[q], R_xc[pb_][q])
            op("dve", [R_xc[pb_][q]], [R_xc[pb_][q]], lambda e, q=q: e.tensor_copy(out=xcb[pb_][q][:, :], in_=flat(xc[pb_][q])))

    def stage_gate(n):
        pb_ = n % 2
        for q in range(2):
            c = 2 * n + q
            T1, T2, T3 = work[c % 2]
            Rw = R_wk[c % 2]
            pr_ = banksets[0]; pi_ = banksets[1]
            for (dst_banks, off) in ((pr_, q * 128), (pi_, 256 + q * 128)):
                for gi, (c0, cn) in enumerate(colgroups(N)):
                    for kc in range(2):
                        op("pe", [R_wg, R_xc[pb_][kc]], [RP[dst_banks[gi]]], lambda e, kc=kc, gi=gi, c0=c0, cn=cn, off=off, dst_banks=dst_banks: e.matmul(PS[dst_banks[gi]][:, 0:cn], lhsT=wg[:, n, kc, off:off + 128], rhs=xcb[pb_][kc][:, c0:c0 + cn], start=(kc == 0), stop=(kc == 1)))
            for gi, (c0, cn) in enumerate(colgroups(N)):
                op("act", [RP[pr_[gi]]], [Rw], lambda e, gi=gi, c0=c0, cn=cn, c=c: e.activation(out=flat(T1)[:, c0:c0 + cn], in_=PS[pr_[gi]][:, 0:cn], func=AF.Sigmoid, bias=bg[:, c:c + 1], scale=1.0))
                op("act", [RP[pi_[gi]]], [Rw], lambda e, gi=gi, c0=c0, cn=cn, c=c: e.activation(out=flat(T2)[:, c0:c0 + cn], in_=PS[pi_[gi]][:, 0:cn], func=AF.Sigmoid, bias=bg[:, 8 + c:8 + c + 1], scale=1.0))
            op("act", [Rw], [Rw], lambda e, c=c: e.activation(out=flat(T1), in_=flat(T1), func=AF.Exp, scale=m8[:, c:c + 1]))
            op("act", [Rw], [Rw], lambda e: e.activation(out=flat(T3), in_=flat(T1), func=AF.Square))
            op("act", [Rw], [Rw], lambda e: e.activation(out=flat(T3), in_=flat(T3), func=AF.Sqrt, scale=-1.0, bias=one_ap[:, :]))
            op("dve", [Rw, R_xc[pb_][q]], [Rw], lambda e, q=q: e.tensor_tensor(out=flat(T2), in0=flat(T2), in1=flat(xc[pb_][q]), op=ALU.mult))
            op("dve", [Rw], [Rw], lambda e: e.tensor_tensor(out=flat(T2), in0=flat(T2), in1=flat(T3), op=ALU.mult))
            for s_ in range(nseq):
                op("dve", [Rw, R_hist], [Rw], lambda e, s_=s_, c=c: e.tensor_tensor_scan(out=T3[:, s_, :], data0=T1[:, s_, :], data1=T2[:, s_, :], initial=hst[:, c, s_:s_ + 1], op0=ALU.mult, op1=ALU.add))
            op("dve", [Rw], [R_hist], lambda e, c=c: e.tensor_copy(out=hst[:, c, :], in_=T3[:, :, L - 1]))
            op("dve", [Rw, R_gl[pb_][q]], [R_hg[c]], lambda e, c=c, q=q: e.tensor_tensor(out=hgT[:, c, :], in0=flat(T3), in1=gl[pb_][q][:, :], op=ALU.mult))

    for n in range(4):
        stage_proj(n)
        if n == 1:
            if pn == "A":
                dma("pool", wo[:], I["rg_w_out"][j].rearrange("(kc p) f -> p kc f", p=128), [], [R_wo], env["ds_w"])
            else:
                dma("pool", wo[:].rearrange("p a b -> p (a b)"), ctx["RGWOd"][j], [], [R_wo], env["ds_w"])
            kb.group_final([R_wo], env["ds_w"])
        if n > 0:
            stage_gate(n - 1)
    stage_gate(3)
    if pn == "A":
        dma("sp", ctx["RGWGd"][j], wg[:].rearrange("p a b c -> p (a b c)"), [R_wg], [], ctx["ds_wb"])
        dma("sp", ctx["RGWOd"][j], wo[:].rearrange("p a b -> p (a b)"), [R_wo], [], ctx["ds_wb"])
    if pn != "A":
        stg = sb("ostage", [48, 256])
        store_T(ctx, lambda b: hist[:, b, :, :].rearrange("p s k -> p (s k)"), NB, 3 * nseq,
                (O["rg_conv_p"] if nseq == 1 else O["rg_conv_s"])[j], stg, R_hist, "ro")
        store_T(ctx, lambda b: hst[:, b, :], NB, nseq, (O["rg_h_p"] if nseq == 1 else O["rg_h_s"])[j], stg, R_hist, "rho")
    pending = None
    for t in range(NT):
        r = rows[t]
        pb = [0, 1] if t % 2 == 0 else [2, 3]
        for kc in range(NB):
            for hh in range(2):
                op("pe", [R_hg[kc], R_wo], [RP[pb[hh]]], lambda e, kc=kc, hh=hh: e.matmul(PS[pb[hh]][0:r, :], lhsT=hgT[:, kc, t * 128:t * 128 + r], rhs=wo[:, kc, hh * 512:(hh + 1) * 512], start=(kc == 0), stop=(kc == NB - 1)))
        zi = t % 4
        z = env["zt"][zi]
        for hh in range(2):
            op("dve", [env["R_h"][t], RP[pb[hh]]], [env["R_zt"][zi]], lambda e, hh=hh: e.scalar_tensor_tensor(out=z[0:r, hh * 512:(hh + 1) * 512], in0=h_tok[0:r, t, hh * 512:(hh + 1) * 512], scalar=DN_ALPHA, in1=PS[pb[hh]][0:r, :], op0=ALU.mult, op1=ALU.add))
        if pending:
            pending()
        pending = env["layer_norm_tile"](t, zi, li_, 0, False)
    pending()


def ONE_AP(env):
    if "one_ap" not in env:
        kb = env["ctx"]["kb"]
        t = env["sb"]("one", [128, 1])
        kb.op("dve", [], [env["ctx"]["R_const"]], lambda e: e.memset(t[:], 1.0))
        env["one_ap"] = t
    return env["one_ap"]


def s5_layer(env, li_, j, les):
    ctx = env["ctx"]; kb = ctx["kb"]; nc = ctx["nc"]; I = ctx["I"]; O = ctx["O"]
    op, dma = kb.op, kb.dma
    PS, RP = ctx["PS"], ctx["RP"]
    pn, nseq, L, N, NT, rows, NC = env["pn"], env["nseq"], env["L"], env["N"], env["NT"], env["rows"], env["NC"]
    sb = lambda n, s, d=F32: env["sb"]("s%d_%s" % (li_, n), s, d, es=les)
    h_tok, hT = env["h_tok"], env["hT"]
    Rst = ctx["R_state"]
    uT = sb("uT", [128, NB, N], BF16)
    gyT = uT
    R_u = [Res() for _ in range(NB)]
    R_gy = R_u
    R_X = Res("Xs")
    Sinb = sb("Sinb", [128, 32, 2, NC], BF16)
    R_S = Res("Sinb")
    wo = sb("wo", [128, NB, 2 * D], BF16)
    R_wo = Res("wo")
    env["load_ln"](li_, 0)
    if nseq == 1:
        st = ctx["s5st_p"][j]; R_st = Rst
    else:
        st = sb("st", [128, 32, 2, nseq]); R_st = Res("s5st")
        stage = sb("stage", [16, 4096])
        for ri, nm in enumerate(("state_s5_re", "state_s5_im")):
            load_T(ctx, I[nm][j], 32, 16, lambda b0, nb_, ri=ri: st[:, b0:b0 + nb_, ri, :], stage, R_st, "s5" + str(ri), env["ds_in"])
    ngrp = len(colgroups(N))
    banksets = [[0, 1, 2][:ngrp], [3, 4, 5][:ngrp]]
    env["plan_w"]([(I["s5_w_in"][j], c * 128) for c in range(NB)])
    for c in range(NB):
        s = env["load_w_block"](c)
        banks = banksets[c % 2]
        env["up_matmul"](s, banks, [])
        for gi, (c0, cn) in enumerate(colgroups(N)):
            op("act", [RP[banks[gi]]], [R_u[c]], lambda e, gi=gi, c0=c0, cn=cn, c=c, banks=banks: e.copy(out=uT[:, c, c0:c0 + cn], in_=PS[banks[gi]][:, 0:cn]))
    if pn == "A":
        dma("pool", wo[:, 0:4, :], I["s5_w_out"][j][0:512, :].rearrange("(kc p) f -> p kc f", p=128), [], [R_wo], env["ds_w"])
        dma("pool", wo[:, 4:8, :], I["s5_w_out"][j][512:1024, :].rearrange("(kc p) f -> p kc f", p=128), [], [R_wo], env["ds_w"])
        kb.group_final([R_wo], env["ds_w"])
    else:
        dma("pool", wo[:].rearrange("p a b -> p (a b)"), ctx["S5WOd"][j], [], [R_wo], env["ds_w"])
        kb.group_final([R_wo], env["ds_w"])
    ies = ExitStack()
    sbo = sb
    sb = lambda n, s_, d=F32: env["sb"]("s%d_%s" % (li_, n), s_, d, es=ies)
    Xs = sb("Xs", [128, 32, 2, NC])
    WXs = [sb("WX%d" % i, [128, 8, 2, 128], BF16) for i in range(2)]
    R_WX = [Res() for _ in range(2)]
    ds_wx = [kb.dsem("s5x%s%d_%d" % (pn, li_, i)) for i in range(2)]
    ds_kw = [kb.dsem("s5k%s%d_%d" % (pn, li_, i)) for i in range(2)]
    for c in range(NB):
        wi = c % 2
        dma("sp", WXs[wi][:], ctx["WXd"][j][:, c], [], [R_WX[wi]], ds_wx[wi])
        for ri in range(2):
            for tau in range(8):
                for pr in range(4):
                    bank = pr
                    rs = slice(32 * pr, 32 * pr + 32)
                    op("pe", [R_WX[wi], R_u[c]], [RP[bank]], lambda e, ri=ri, tau=tau, rs=rs, bank=bank, wi=wi, c=c, pr=pr: e.matmul(
                        PS[bank][:, ri * 256:ri * 256 + NC], lhsT=WXs[wi][rs, tau, ri, :],
                        rhs=uT[rs, c, :].rearrange("p (n t) -> p n t", t=8)[:, :, tau], start=(tau == 0), stop=(tau == 7), tile_position=(32 * pr, 0)))
        for pr in range(4):
            bank = pr
            op("act", [RP[bank]], [R_X], lambda e, bank=bank, c=c, pr=pr: e.copy(out=Xs[:, 4 * c + pr, :, :], in_=PS[bank][:, :].rearrange("p (r n) -> p r n", n=256)[:, :, 0:NC]))
    a8r, a8i, a8in = ctx["A8re"][j], ctx["A8im"][j], ctx["A8imn"][j]
    Mrot = sb("Mrot", [128, 32, 2, 2]); prod = sb("prod", [128, 32, 2, 2]); t1 = sb("t1", [128, 32, 2])
    op("dve", [Rst], [R_X], lambda e: e.tensor_copy(out=Mrot[:, :, 0, 0], in_=a8r[:, :]))
    op("dve", [Rst], [R_X], lambda e: e.tensor_copy(out=Mrot[:, :, 1, 1], in_=a8r[:, :]))
    op("dve", [Rst], [R_X], lambda e: e.tensor_copy(out=Mrot[:, :, 0, 1], in_=a8in[:, :]))
    op("dve", [Rst], [R_X], lambda e: e.tensor_copy(out=Mrot[:, :, 1, 0], in_=a8i[:, :]))
    if nseq == 1:
        op("act", [R_st], [R_S], lambda e: e.copy(out=Sinb[:, :, :, 0], in_=st[:, :, :, 0]))
    else:
        op("act", [R_st], [R_S], lambda e: e.copy(out=Sinb[:, :, :, :], in_=st[:, :, :, :]))
    nsteps = NC if nseq == 1 else nseq
    for c in range(nsteps):
        if nseq == 1:
            prev = st[:, :, :, 0] if c == 0 else Xs[:, :, :, c - 1]
        else:
            prev = st[:, :, :, c]
        cur = Xs[:, :, :, c]
        rd = [R_X, R_st, Rst]
        pb_ = prev.unsqueeze(2).broadcast_to([128, 32, 2, 2])
        op("dve", rd, [R_X], lambda e, pb_=pb_: e.tensor_tensor(out=prod[:], in0=pb_, in1=Mrot[:], op=ALU.mult))
        op("dve", rd, [R_X], lambda e: e.tensor_tensor(out=t1[:], in0=prod[:, :, :, 0], in1=prod[:, :, :, 1], op=ALU.add))
        op("dve", rd, [R_X], lambda e, cur=cur: e.tensor_tensor(out=cur, in0=cur, in1=t1[:], op=ALU.add))
    if nseq == 1:
        op("act", [R_X], [R_S], lambda e: e.copy(out=Sinb[:, :, :, 1:NC], in_=Xs[:, :, :, 0:NC - 1]))
        op("dve", [R_X], [R_st], lambda e: e.tensor_copy(out=st[:, :, :, 0], in_=Xs[:, :, :, NC - 1]))
    else:
        op("dve", [R_X], [R_st], lambda e: e.tensor_copy(out=st[:, :, :, :], in_=Xs[:, :, :, :]))
    if pn != "A":
        stg = sb("ostage", [16, 2048])
        for ri, nm in enumerate(("s5_re", "s5_im")):
            store_T(ctx, lambda b, ri=ri: st[:, b, ri, :], 32, nseq, O[nm + ("_p" if nseq == 1 else "_s")][j], stg, R_st, "s5o%d" % ri)
    kb.barrier()
    ies.close()
    sb = sbo
    KIs = [sb("KI%d" % i, [128, 8, 128], BF16) for i in range(2)]
    WYs = [sb("WY%d" % i, [128, 4, 8, 2, 32], BF16) for i in range(2)]
    R_KI = [Res() for _ in range(2)]
    R_WY = [Res() for _ in range(2)]
    cg = colgroups(NC, 64)
    for c in range(NB):
        wi = c % 2
        dma("sp", KIs[wi][:], ctx["KId"][j][:, c], [], [R_KI[wi]], ds_kw[wi])
        dma("sp", WYs[wi][:], ctx["WYd"][j][:, 4 * c:4 * c + 4], [], [R_WY[wi]], ds_kw[wi])
        kb.group_final([R_KI[wi], R_WY[wi]], ds_kw[wi])
        banks = banksets[c % 2]
        for gi, (n0, nn) in enumerate(cg):
            bank = banks[gi]
            uv = uT[:, c, n0 * 8:(n0 + nn) * 8].rearrange("p (n t) -> p t n", t=8)
            for lag in range(8):
                op("pe", [R_KI[wi], R_u[c]], [RP[bank]], lambda e, lag=lag, bank=bank, nn=nn, uv=uv, wi=wi: e.matmul(
                    PS[bank][:, lag * nn:8 * nn], lhsT=KIs[wi][:, lag, :], rhs=uv[:, 0:8 - lag, :], start=(lag == 0), stop=False))
            for tau in range(8):
                for ri in range(2):
                    for pr in range(4):
                        lastmm = (tau == 7 and ri == 1)
                        op("pe", [R_WY[wi], R_S], [RP[bank]], lambda e, pr=pr, tau=tau, ri=ri, bank=bank, wi=wi, n0=n0, nn=nn, c=c, lastmm=lastmm: e.matmul(
                            PS[bank][32 * pr:32 * pr + 32, tau * nn:(tau + 1) * nn], lhsT=WYs[wi][:, pr, tau, ri, :], rhs=Sinb[:, 4 * c + pr, ri, n0:n0 + nn], start=False, stop=lastmm, tile_position=(0, 32 * pr)))
            op("act", [RP[bank]], [R_gy[c]], lambda e, bank=bank, n0=n0, nn=nn, c=c: e.activation(out=gyT[:, c, n0 * 8:(n0 + nn) * 8].rearrange("p (n t) -> p n t", t=8), in_=PS[bank][:, 0:nn * 8].rearrange("p (t n) -> p n t", t=8), func=AF.Gelu_apprx_tanh))
    if pn == "A":
        dma("sp", ctx["S5WOd"][j], wo[:].rearrange("p a b -> p (a b)"), [R_wo], [], ctx["ds_wb"])
    sgs = [sb("sg%d" % i, [128, D]) for i in range(2)]
    R_sgs = [Res("sg%d" % i) for i in range(2)]
    pending = None
    for t in range(NT):
        sg = sgs[t % 2]
        R_sg = R_sgs[t % 2]
        r = rows[t]
        pb = [2, 3, 0, 1]
        for qq in (2, 3, 0, 1):
            for kc in range(NB):
                op("pe", [R_gy[kc], R_wo], [RP[pb[qq]]], lambda e, kc=kc, qq=qq: e.matmul(PS[pb[qq]][0:r, :], lhsT=gyT[:, kc, t * 128:t * 128 + r], rhs=wo[:, kc, qq * 512:(qq + 1) * 512], start=(kc == 0), stop=(kc == NB - 1)))
        zi = t % 4
        z = env["zt"][zi]
        for hh in range(2):
            op("act", [RP[pb[2 + hh]]], [R_sg], lambda e, hh=hh: e.activation(out=sg[0:r, hh * 512:(hh + 1) * 512], in_=PS[pb[2 + hh]][0:r, :], func=AF.Sigmoid))
            op("dve", [R_sg, RP[pb[hh]]], [R_sg], lambda e, hh=hh: e.tensor_tensor(out=sg[0:r, hh * 512:(hh + 1) * 512], in0=sg[0:r, hh * 512:(hh + 1) * 512], in1=PS[pb[hh]][0:r, :], op=ALU.mult))
            op("dve", [env["R_h"][t], R_sg], [env["R_zt"][zi]], lambda e, hh=hh: e.scalar_tensor_tensor(out=z[0:r, hh * 512:(hh + 1) * 512], in0=h_tok[0:r, t, hh * 512:(hh + 1) * 512], scalar=DN_ALPHA, in1=sg[0:r, hh * 512:(hh + 1) * 512], op0=ALU.mult, op1=ALU.add))
        if pending:
            pending()
        pending = env["layer_norm_tile"](t, zi, li_, 0, False)
    pending()


_NC_CACHE = {}


def _get_nc(dbg=None):
    key = repr(dbg)
    if key not in _NC_CACHE:
        nc = bass.Bass("TRN2", target_bir_lowering=False)
        build(nc, dbg)
        _NC_CACHE[key] = nc
    return _NC_CACHE[key]


def kernel(dbg=None, **inp):
    f = lambda a: np.ascontiguousarray(np.asarray(a, dtype=np.float32))
    ident = np.eye(128, dtype=np.float32)
    bmask = np.kron(np.eye(4, dtype=np.float32), np.ones((32, 32), np.float32))
    wnames = ["meta_tokens", "s5_w_in", "s5_lam_re", "s5_lam_im", "s5_log_step", "s5_b_re", "s5_b_im", "s5_c_re",
              "s5_c_im", "s5_d", "s5_w_out", "rg_w_in", "rg_conv_w", "rg_conv_b", "rg_w_gates", "rg_b_gates",
              "rg_lam", "rg_w_out", "ffn_w_up", "ffn_conv_w", "ffn_conv_b", "ffn_w_down", "ln_g", "ln_b"]
    shared = {n: f(inp[n]) for n in wnames}
    shared["ident"] = ident
    shared["bmask"] = bmask
    shared["sel2"] = np.kron(np.eye(2, dtype=np.float32), np.ones((1, 64), np.float32))
    in_maps = []
    for c in range(8):
        m = dict(shared)
        sl = slice(16 * c, 16 * c + 16)
        m["x_prompt"] = f(inp["x_prompt"][c])
        m["x_sample"] = f(inp["x_sample"][sl]).reshape(128, D)
        m["state_s5_re"] = f(inp["state_s5_re"][:, sl]).reshape(2, 16, 4096)
        m["state_s5_im"] = f(inp["state_s5_im"][:, sl]).reshape(2, 16, 4096)
        m["state_rg_h"] = f(inp["state_rg_h"][:, sl])
        m["state_rg_conv"] = f(inp["state_rg_conv"][:, sl]).reshape(2, 48, D)
        m["state_ffn_conv"] = f(inp["state_ffn_conv"][:, sl]).reshape(4, 32, FF2)
        in_maps.append(m)
    nc = _get_nc(dbg)
    res = run_bass_kernel_spmd(nc, in_maps, core_ids=list(range(8)))
    R = res.results
    cat = lambda k, ax: np.concatenate([np.asarray(R[c][k]) for c in range(8)], axis=ax)
    y_prompt = np.stack([np.asarray(R[c]["y_prompt"]) for c in range(8)], 0)
    y_sample = cat("y_sample", 0).reshape(128, 8, D)
    s5_re_p = cat("s5_re_p", 1).reshape(2, 8, 64, 64)
    s5_im_p = cat("s5_im_p", 1).reshape(2, 8, 64, 64)
    rg_h_p = cat("rg_h_p", 1).reshape(2, 8, D)
    rg_conv_p = np.stack([np.asarray(R[c]["rg_conv_p"]) for c in range(8)], 1).reshape(2, 8, 3, D)
    ffn_conv_p = np.stack([np.asarray(R[c]["ffn_conv_p"]) for c in range(8)], 1).reshape(4, 8, 2, FF2)
    s5_re_s = cat("s5_re_s", 1).reshape(2, 128, 64, 64)
    s5_im_s = cat("s5_im_s", 1).reshape(2, 128, 64, 64)
    rg_h_s = cat("rg_h_s", 1).reshape(2, 128, D)
    rg_conv_s = cat("rg_conv_s", 1).reshape(2, 128, 3, D)
    ffn_conv_s = cat("ffn_conv_s", 1).reshape(4, 128, 2, FF2)
    outs = (y_prompt, y_sample, s5_re_p, s5_im_p, rg_h_p, rg_conv_p, ffn_conv_p,
            s5_re_s, s5_im_s, rg_h_s, rg_conv_s, ffn_conv_s)
    return tuple(np.ascontiguousarray(o, dtype=np.float32) for o in outs)
```

```python
import math
from contextlib import ExitStack
import numpy as np
import concourse.bass as bass
import concourse.mybir as mybir
from concourse.bass_utils import run_bass_kernel_spmd

F32 = mybir.dt.float32
BF16 = mybir.dt.bfloat16
I32 = mybir.dt.int32
AF = mybir.ActivationFunctionType
ALU = mybir.AluOpType

D = 1024
NB = 8
FF = 2816
FF2 = 5632
FB = 44
FBH = 22
DEPTH = 4
SEQ = 2048
NMETA = 16
DN_ALPHA = (2 * DEPTH) ** 0.25
LN_EPS = 1e-5
RG_C = 8.0
TWO_PI = 2.0 * math.pi
ATTACH_WAITS = True


class Res:
    __slots__ = ("name", "w", "r")

    def __init__(self, name=""):
        self.name = name
        self.w = None
        self.r = {}


class DSem:
    def __init__(self, sem):
        self.sem = sem
        self.val = 0


class Q:
    def __init__(self, name, eng, sem, eager):
        self.name = name
        self.eng = eng
        self.sem = sem
        self.eager = eager
        self.n = 0
        self.last = None
        self.last_ms = True
        self.ms = []
        self.semval = 0
        self.known = {}


class KB:
    def __init__(self, nc, es):
        self.nc = nc
        self.es = es
        self.q = {}
        for name, eng, eager in (("pe", nc.tensor, False), ("act", nc.scalar, False),
                                 ("dve", nc.vector, False), ("pool", nc.gpsimd, True),
                                 ("sp", nc.sync, True)):
            self.q[name] = Q(name, eng, self.sem("q_" + name), eager)
        self.dsems = []
        self.nsem = 0

    def sem(self, name):
        return self.es.enter_context(self.nc.semaphore(name))

    def dsem(self, name):
        d = DSem(self.sem("d_" + name))
        self.dsems.append(d)
        return d

    def sb(self, name, shape, dt, es=None):
        return (es or self.es).enter_context(self.nc.sbuf_tensor("sb_" + name, list(shape), dt))

    def ps(self, name, shape, dt=F32, es=None):
        return (es or self.es).enter_context(self.nc.psum_tensor("pt_" + name, list(shape), dt))

    def _milestone(self, A, k):
        lo, hi = 0, len(A.ms)
        while lo < hi:
            mid = (lo + hi) // 2
            if A.ms[mid][0] >= k:
                hi = mid
            else:
                lo = mid + 1
        if lo < len(A.ms):
            return A.ms[lo][1]
        assert A.last is not None and not A.last_ms and A.n >= k
        A.semval += 1
        A.last.then_inc(A.sem, 1)
        A.last_ms = True
        A.ms.append((A.n, A.semval))
        return A.semval

    def _wait(self, q, deps):
        need = {}
        for d in deps:
            if d is None:
                continue
            if d[0] == "q":
                A, k = d[1], d[2]
                if A is q and q.name == "pe":
                    continue
                v = self._milestone(A, k)
                key = A
                sem = A.sem
            else:
                key = d[1]
                sem = d[1].sem
                v = d[2]
            if q.known.get(key, 0) >= v:
                continue
            if need.get(key, (None, 0))[1] < v:
                need[key] = (sem, v)
        items = list(need.items())
        attach = None
        if ATTACH_WAITS and items:
            key, (sem, v) = items.pop()
            q.known[key] = v
            attach = (sem, v)
        for key, (sem, v) in items:
            q.eng.wait_ge(sem, v)
            q.known[key] = v
        return attach

    def _deps(self, reads, writes):
        deps = []
        for r in reads:
            if r.w is not None:
                deps.append(r.w)
        for w in writes:
            if w.w is not None:
                deps.append(w.w)
            deps.extend(w.r.values())
        return deps

    def op(self, qn, reads, writes, fn):
        q = self.q[qn]
        att = self._wait(q, self._deps(reads, writes))
        ins = fn(q.eng)
        if att is not None:
            ins._wait_ge(att[0], att[1])
        q.n += 1
        q.last = ins
        q.last_ms = False
        if q.eager:
            q.semval += 1
            ins.then_inc(q.sem, 1)
            q.last_ms = True
            q.ms.append((q.n, q.semval))
        me = ("q", q, q.n)
        for r in reads:
            r.r[q] = me
        for w in writes:
            w.w = me
            w.r = {}
        return ins

    def dma(self, qn, out, in_, reads, writes, ds, **kw):
        q = self.q[qn]
        att = self._wait(q, self._deps(reads, writes))
        ins = q.eng.dma_start(out=out, in_=in_, **kw)
        if att is not None:
            ins._wait_ge(att[0], att[1])
        ds.val += 16
        ins.then_inc(ds.sem, 16)
        me = ("d", ds, ds.val)
        for r in reads:
            r.r[ds] = me
        for w in writes:
            w.w = me
            w.r = {}
        return ins

    def group_final(self, ress, ds):
        for r in ress:
            r.w = ("d", ds, ds.val)

    def barrier(self, exclude=()):
        marks = []
        for A in self.q.values():
            if A.n > 0:
                marks.append(("q", A, A.n))
        for d in self.dsems:
            if d in exclude:
                continue
            if d.val > 0:
                marks.append(("d", d, d.val))
        global ATTACH_WAITS
        sv = ATTACH_WAITS
        ATTACH_WAITS = False
        for q in self.q.values():
            self._wait(q, [m for m in marks if not (m[0] == "q" and m[1] is q)])
        ATTACH_WAITS = sv

    def finish(self):
        self.barrier()


def build(nc, dbg=None):
    es = ExitStack()
    kb = KB(nc, es)
    with es:
        _build(nc, kb, dbg)
    return nc


def _dram_in(nc, name, shape, dt=F32):
    return nc.dram_tensor(name, list(shape), dt, kind="ExternalInput").ap()


def _dram_out(nc, name, shape, dt=F32):
    return nc.dram_tensor(name, list(shape), dt, kind="ExternalOutput").ap()


def _build(nc, kb, dbg):
    op, dma = kb.op, kb.dma
    I = {}
    I["x_prompt"] = _dram_in(nc, "x_prompt", [SEQ, D])
    I["x_sample"] = _dram_in(nc, "x_sample", [128, D])
    I["state_s5_re"] = _dram_in(nc, "state_s5_re", [2, 16, 4096])
    I["state_s5_im"] = _dram_in(nc, "state_s5_im", [2, 16, 4096])
    I["state_rg_h"] = _dram_in(nc, "state_rg_h", [2, 16, D])
    I["state_rg_conv"] = _dram_in(nc, "state_rg_conv", [2, 48, D])
    I["state_ffn_conv"] = _dram_in(nc, "state_ffn_conv", [4, 32, FF2])
    I["meta_tokens"] = _dram_in(nc, "meta_tokens", [NMETA, D])
    I["s5_w_in"] = _dram_in(nc, "s5_w_in", [2, D, D])
    I["s5_lam_re"] = _dram_in(nc, "s5_lam_re", [2, 64, 64])
    I["s5_lam_im"] = _dram_in(nc, "s5_lam_im", [2, 64, 64])
    I["s5_log_step"] = _dram_in(nc, "s5_log_step", [2, 64])
    I["s5_b_re"] = _dram_in(nc, "s5_b_re", [2, 64, 64, 16])
    I["s5_b_im"] = _dram_in(nc, "s5_b_im", [2, 64, 64, 16])
    I["s5_c_re"] = _dram_in(nc, "s5_c_re", [2, 64, 16, 64])
    I["s5_c_im"] = _dram_in(nc, "s5_c_im", [2, 64, 16, 64])
    I["s5_d"] = _dram_in(nc, "s5_d", [2, D])
    I["s5_w_out"] = _dram_in(nc, "s5_w_out", [2, D, 2 * D])
    I["rg_w_in"] = _dram_in(nc, "rg_w_in", [2, D, 2 * D])
    I["rg_conv_w"] = _dram_in(nc, "rg_conv_w", [2, 4, D])
    I["rg_conv_b"] = _dram_in(nc, "rg_conv_b", [2, D])
    I["rg_w_gates"] = _dram_in(nc, "rg_w_gates", [2, 4, 256, 512])
    I["rg_b_gates"] = _dram_in(nc, "rg_b_gates", [2, 2 * D])
    I["rg_lam"] = _dram_in(nc, "rg_lam", [2, D])
    I["rg_w_out"] = _dram_in(nc, "rg_w_out", [2, D, D])
    I["ffn_w_up"] = _dram_in(nc, "ffn_w_up", [4, D, FF2])
    I["ffn_conv_w"] = _dram_in(nc, "ffn_conv_w", [4, 3, FF2])
    I["ffn_conv_b"] = _dram_in(nc, "ffn_conv_b", [4, FF2])
    I["ffn_w_down"] = _dram_in(nc, "ffn_w_down", [4, FF, D])
    I["ln_g"] = _dram_in(nc, "ln_g", [4, 2, D])
    I["ln_b"] = _dram_in(nc, "ln_b", [4, 2, D])
    I["ident"] = _dram_in(nc, "ident", [128, 128])
    I["bmask"] = _dram_in(nc, "bmask", [128, 128])
    I["sel2"] = _dram_in(nc, "sel2", [2, 128])

    O = {}
    O["y_prompt"] = _dram_out(nc, "y_prompt", [SEQ, D])
    O["y_sample"] = _dram_out(nc, "y_sample", [128, D])
    O["s5_re_p"] = _dram_out(nc, "s5_re_p", [2, 1, 4096])
    O["s5_im_p"] = _dram_out(nc, "s5_im_p", [2, 1, 4096])
    O["rg_h_p"] = _dram_out(nc, "rg_h_p", [2, 1, D])
    O["rg_conv_p"] = _dram_out(nc, "rg_conv_p", [2, 3, D])
    O["ffn_conv_p"] = _dram_out(nc, "ffn_conv_p", [4, 2, FF2])
    O["s5_re_s"] = _dram_out(nc, "s5_re_s", [2, 16, 4096])
    O["s5_im_s"] = _dram_out(nc, "s5_im_s", [2, 16, 4096])
    O["rg_h_s"] = _dram_out(nc, "rg_h_s", [2, 16, D])
    O["rg_conv_s"] = _dram_out(nc, "rg_conv_s", [2, 48, D])
    O["ffn_conv_s"] = _dram_out(nc, "ffn_conv_s", [4, 32, FF2])

    WXd = [nc.dram_tensor("WXd%d" % j, [128, 8, 8, 2, 128], BF16, kind="Internal").ap() for j in range(2)]
    WYd = [nc.dram_tensor("WYd%d" % j, [128, 32, 8, 2, 32], BF16, kind="Internal").ap() for j in range(2)]
    KId = [nc.dram_tensor("KId%d" % j, [128, 8, 8, 128], BF16, kind="Internal").ap() for j in range(2)]

    NBLK = 2 * NB + 2 * 2 * NB + 4 * FB
    ctx_w = dict(
        WBLK=nc.dram_tensor("WBLK", [NBLK, 128, NB * 128], BF16, kind="Internal").ap(),
        WDd=[nc.dram_tensor("WDd%d" % l, [128, FBH * D], BF16, kind="Internal").ap() for l in range(4)],
        S5WOd=[nc.dram_tensor("S5WOd%d" % j, [128, NB * 2 * D], BF16, kind="Internal").ap() for j in range(2)],
        RGWOd=[nc.dram_tensor("RGWOd%d" % j, [128, NB * D], BF16, kind="Internal").ap() for j in range(2)],
        RGWGd=[nc.dram_tensor("RGWGd%d" % j, [128, 4 * 2 * 512], BF16, kind="Internal").ap() for j in range(2)],
    )
    sb = kb.sb
    ident = sb("ident", [128, 128], F32)
    bmask = sb("bmask", [128, 128], F32)
    R_const = Res("const")
    ds_c = kb.dsem("const")
    dma("sp", ident[:], I["ident"][:, :], [], [R_const], ds_c)
    dma("sp", bmask[:], I["bmask"][:, :], [], [R_const], ds_c)
    sel2 = sb("sel2", [2, 128], F32)
    dma("sp", sel2[:], I["sel2"][:, :], [], [R_const], ds_c)

    R_par = Res("par")
    ds_p = kb.dsem("par")

    par_loads = []

    def load_cols(name, src_1d, nblk, ds):
        t = sb(name, [128, nblk], F32)
        par_loads.append((t, src_1d))
        return t

    def issue_par_loads():
        with nc.allow_non_contiguous_dma(reason="small param"):
            for t, src_1d in par_loads:
                dma("sp", t[:], src_1d.rearrange("(b p) -> p b", p=128), [], [R_par], ds_p)

    ffn_cw = [[load_cols("fcw%d_%d" % (l, k), I["ffn_conv_w"][l, k], FB, ds_c) for k in range(3)] for l in range(4)]
    ffn_cb = [load_cols("fcb%d" % l, I["ffn_conv_b"][l], FB, ds_c) for l in range(4)]
    rg_cw = [[load_cols("rcw%d_%d" % (j, k), I["rg_conv_w"][j, k], NB, ds_c) for k in range(4)] for j in range(2)]
    rg_cb = [load_cols("rcb%d" % j, I["rg_conv_b"][j], NB, ds_c) for j in range(2)]
    rg_bg = [load_cols("rbg%d" % j, I["rg_b_gates"][j], 16, ds_c) for j in range(2)]
    rg_lm = [load_cols("rlm%d" % j, I["rg_lam"][j], NB, ds_c) for j in range(2)]
    s5_dd = [sb("s5d%d" % j, [128, NB], F32) for j in range(2)]
    with nc.allow_non_contiguous_dma(reason="small param"):
        for j in range(2):
            dma("sp", s5_dd[j][:], I["s5_d"][j].rearrange("(b p) -> p b", p=128), [], [R_const], ds_c)
    rg_m8sp = [sb("m8sp%d" % j, [128, NB], F32) for j in range(2)]
    A8re = [sb("A8re%d" % j, [128, 32], F32) for j in range(2)]
    A8im = [sb("A8im%d" % j, [128, 32], F32) for j in range(2)]
    A8imn = [sb("A8imn%d" % j, [128, 32], F32) for j in range(2)]
    ffn_hist_p = [sb("fhp%d" % l, [128, FB, 1, 2], F32) for l in range(4)]
    rg_hist_p = [sb("rhp%d" % j, [128, NB, 1, 3], F32) for j in range(2)]
    rg_h_p = [sb("rgh%d" % j, [128, NB, 1], F32) for j in range(2)]
    s5st_p = [sb("s5p%d" % j, [128, 32, 2, 1], F32) for j in range(2)]
    R_state = Res("state")

    PS = [kb.ps("ps%d" % i, [128, 512]) for i in range(8)]
    RP = [Res("ps%d" % i) for i in range(8)]

    for j in range(2):
        pass
    for t in ffn_hist_p + rg_hist_p + rg_h_p + s5st_p:
        op("dve", [], [R_state], lambda e, t=t: e.memset(t[:], 0.0))

    ctx = dict(nc=nc, kb=kb, I=I, O=O, sel2=sel2, R_par=R_par, WXd=WXd, WYd=WYd, KId=KId, ident=ident, bmask=bmask,
               R_const=R_const, PS=PS, RP=RP, ffn_cw=ffn_cw, ffn_cb=ffn_cb, rg_cw=rg_cw, rg_cb=rg_cb,
               rg_bg=rg_bg, rg_m8sp=rg_m8sp, s5_dd=s5_dd, A8re=A8re, A8im=A8im, A8imn=A8imn,
               ffn_hist_p=ffn_hist_p, rg_hist_p=rg_hist_p, rg_h_p=rg_h_p, s5st_p=s5st_p,
               R_state=R_state, dbg=dbg, ds_out=kb.dsem("out"), ds_w=None, ds_wb=kb.dsem("wb"), **ctx_w)

    with ExitStack() as lds:
        with ExitStack() as nat:
            gens = [s5_prologue_loads(ctx, j, lds, nat) for j in range(2)]
            for g in gens:
                next(g)
            lds_ = [next(g) for g in gens]
            kb.barrier()
        for j in range(2):
            with ExitStack() as pes:
                ctx["issue_par_loads"] = issue_par_loads
                s5_prologue(ctx, j, pes, lds_[j])
                kb.barrier()
    kb.group_final([R_par], ds_p)
    for j in range(2):
        op("act", [R_par], [R_par], lambda e, j=j: e.activation(out=rg_m8sp[j][:], in_=rg_lm[j][:], func=AF.Exp, scale=-1.0))
        op("act", [R_par], [R_par], lambda e, j=j: e.activation(out=rg_m8sp[j][:], in_=rg_m8sp[j][:], func=AF.Ln, bias=1.0))
        op("dve", [R_par], [R_par], lambda e, j=j: e.tensor_scalar(out=rg_m8sp[j][:], in0=rg_m8sp[j][:], scalar1=-RG_C, scalar2=None, op0=ALU.mult))
    kb.barrier()
    passes = [("A", 1, 1024), ("B", 1, 1040), ("S", 16, 8)]
    for (pn, nseq, L) in passes:
        with ExitStack() as pes:
            run_pass(ctx, pn, nseq, L, pes)
            kb.barrier()
    kb.finish()


def colgroups(N, g=512):
    return [(c0, min(g, N - c0)) for c0 in range(0, N, g)]


def store_T(ctx, src_fn, nblk, m, dram_rows, stage, R_src, tag):
    kb, PS, RP, ident = ctx["kb"], ctx["PS"], ctx["RP"], ctx["ident"]
    R_stage = ctx.setdefault("stage_res", {}).setdefault(id(stage), Res("stage" + tag))
    GB = stage.shape[1] // 128
    for g0 in range(0, nblk, GB):
        ng = min(GB, nblk - g0)
        for b0 in range(g0, g0 + ng, 4):
            nb_ = min(4, g0 + ng - b0)
            bank = 6 + ((b0 // 4) % 2)
            for b in range(nb_):
                kb.op("pe", [R_src, ctx["R_const"]], [RP[bank]],
                      lambda e, b=b, b0=b0, bank=bank: e.transpose(out=PS[bank][0:m, b * 128:(b + 1) * 128], in_=src_fn(b0 + b), identity=ident[:, :]))
            kb.op("act", [RP[bank]], [R_stage],
                  lambda e, b0=b0, nb_=nb_, bank=bank, g0=g0: e.copy(out=stage[0:m, (b0 - g0) * 128:(b0 - g0 + nb_) * 128], in_=PS[bank][0:m, 0:nb_ * 128]))
        kb.dma("sp", dram_rows[:, g0 * 128:(g0 + ng) * 128], stage[0:m, 0:ng * 128], [R_stage], [], ctx["ds_out"])


def load_T(ctx, dram_rows, nblk, m, dst_fn, stage, R_dst, tag, ds):
    kb, PS, RP, ident = ctx["kb"], ctx["PS"], ctx["RP"], ctx["ident"]
    R_stage = ctx.setdefault("stage_res", {}).setdefault(id(stage), Res("lstage" + tag))
    ctx["uid"] = ctx.get("uid", 0) + 1
    ds = kb.dsem("lt%d" % ctx["uid"])
    kb.dma("sp", stage[0:m, 0:nblk * 128], dram_rows, [], [R_stage], ds)
    per = 512 // m
    gi = 0
    for b0 in range(0, nblk, per):
        nb_ = min(per, nblk - b0)
        bank = 6 + (gi % 2)
        gi += 1
        for b in range(nb_):
            kb.op("pe", [R_stage, ctx["R_const"]], [RP[bank]],
                  lambda e, b=b: e.transpose(out=PS[bank][:, b * m:(b + 1) * m], in_=stage[0:m, (b0 + b) * 128:(b0 + b + 1) * 128], identity=ident[0:m, 0:m]))
        kb.op("act", [RP[bank]], [R_dst],
              lambda e: e.copy(out=dst_fn(b0, nb_), in_=PS[bank][:, 0:nb_ * m].rearrange("p (b m) -> p b m", m=m)))


def s5_prologue_loads(ctx, j, pes, nat):
    nc, kb, I = ctx["nc"], ctx["kb"], ctx["I"]
    dma, op = kb.dma, kb.op
    PS, RP, ident, Rc = ctx["PS"], ctx["RP"], ctx["ident"], ctx["R_const"]
    sb = lambda n, s, d=F32: kb.sb("pl%d_%s" % (j, n), s, d, es=pes)
    ds = kb.dsem("pl%d" % j)
    R = Res("pl")
    Rn = Res("plnat")
    lr = sb("lr", [128, 32]); li = sb("li", [128, 32]); stp = sb("stp", [128, 32])
    Bre = sb("Bre", [128, 32, 16]); Bim = sb("Bim", [128, 32, 16])
    yield None
    sbn = lambda n, s, d=F32: kb.sb("pl%d_%s" % (j, n), s, d, es=nat)
    lrn = sbn("lrn", [32, 128]); lin = sbn("lin", [32, 128]); stn = sbn("stn", [2, 32])
    Bn = [sbn("Bn%d" % i, [32, 128, 16]) for i in range(2)]
    dma("sp", lrn[:], I["s5_lam_re"][j].rearrange("(pr g2) p -> pr (g2 p)", g2=2), [], [Rn], ds)
    dma("sp", lin[:], I["s5_lam_im"][j].rearrange("(pr g2) p -> pr (g2 p)", g2=2), [], [Rn], ds)
    with nc.allow_non_contiguous_dma(reason="tiny"):
        dma("sp", stn[:], I["s5_log_step"][j].rearrange("(pr g2) -> g2 pr", g2=2), [], [Rn], ds)
    dma("sp", Bn[0][:], I["s5_b_re"][j].rearrange("(pr g2) p h -> pr (g2 p) h", g2=2), [], [Rn], ds)
    dma("sp", Bn[1][:], I["s5_b_im"][j].rearrange("(pr g2) p h -> pr (g2 p) h", g2=2), [], [Rn], ds)
    kb.group_final([Rn], ds)
    bank = 4 + j
    op("pe", [Rn, Rc], [RP[bank]], lambda e: e.transpose(out=PS[bank][:, 0:32], in_=lrn[:, :], identity=ident[0:32, 0:32]))
    op("pe", [Rn, Rc], [RP[bank]], lambda e: e.transpose(out=PS[bank][:, 32:64], in_=lin[:, :], identity=ident[0:32, 0:32]))
    op("pe", [Rn, Rc], [RP[bank]], lambda e: e.matmul(PS[bank][:, 64:96], lhsT=ctx["sel2"][:, :], rhs=stn[:, :], start=True, stop=True))
    op("act", [RP[bank]], [R], lambda e: e.copy(out=lr[:], in_=PS[bank][:, 0:32]))
    op("act", [RP[bank]], [R], lambda e: e.copy(out=li[:], in_=PS[bank][:, 32:64]))
    op("act", [RP[bank]], [R], lambda e: e.copy(out=stp[:], in_=PS[bank][:, 64:96]))
    for i, Bd in enumerate((Bre, Bim)):
        for h in range(16):
            op("pe", [Rn, Rc], [RP[bank]], lambda e, h=h, i=i: e.transpose(out=PS[bank][:, h * 32:(h + 1) * 32], in_=Bn[i][:, :, h], identity=ident[0:32, 0:32]))
        op("act", [RP[bank]], [R], lambda e, Bd=Bd: e.copy(out=Bd[:, :, :].rearrange("p r h -> p h r"), in_=PS[bank][:, :].rearrange("p (h r) -> p h r", r=32)))
    yield dict(R=R, ds=ds, lr=lr, li=li, stp=stp, Bre=Bre, Bim=Bim)


def s5_prologue(ctx, j, pes, ld):
    nc, kb, I = ctx["nc"], ctx["kb"], ctx["I"]
    op, dma = kb.op, kb.dma
    PS, RP, ident, bmask = ctx["PS"], ctx["RP"], ctx["ident"], ctx["bmask"]
    Rc = ctx["R_const"]
    sb = lambda n, s, d=F32: kb.sb("pl%d_%s" % (j, n), s, d, es=pes)
    R, ds = ld["R"], ld["ds"]
    lr, li, stp, Bre, Bim = (ld[k] for k in ("lr", "li", "stp", "Bre", "Bim"))
    Cre = sb("Cre", [128, 32, 16]); Cim = sb("Cim", [128, 32, 16])
    Ch_re = sb("Chre", [16, 64, 64]); Ch_im = sb("Chim", [16, 64, 64])
    R_ch = Res("ch")
    ds_ch = kb.dsem("plc%d" % j)
    with nc.allow_non_contiguous_dma(reason="param layout"):
        dma("sp", Ch_re[:], I["s5_c_re"][j].rearrange("g h p -> h g p"), [], [R_ch], ds_ch)
        dma("sp", Ch_im[:], I["s5_c_im"][j].rearrange("g h p -> h g p"), [], [R_ch], ds_ch)
    kb.group_final([R_ch], ds_ch)
    if j == 1:
        ctx["issue_par_loads"]()
    for (Ch, Cd) in ((Ch_re, Cre), (Ch_im, Cim)):
        for g0 in range(0, 32, 16):
            bank = 4 + (g0 // 16)
            for pr in range(g0, g0 + 16):
                op("pe", [R_ch, Rc], [RP[bank]],
                   lambda e, pr=pr, Ch=Ch: e.transpose(out=PS[bank][:, (pr - g0) * 16:(pr - g0 + 1) * 16],
                                                       in_=Ch[0:16, 2 * pr:2 * pr + 2, :].rearrange("h g p -> h (g p)"),
                                                       identity=ident[0:16, 0:16]))
            op("act", [RP[bank]], [R], lambda e, Cd=Cd: e.copy(out=Cd[:, g0:g0 + 16, :], in_=PS[bank][:, 0:256].rearrange("p (a h) -> p a h", h=16)))

    t = lambda n: sb(n, [128, 32])
    ang = t("ang"); mag = t("mag"); sn = t("sn"); cs = t("cs"); tmp = t("tmp"); tmp2 = t("tmp2")
    abre = t("abre"); abim = t("abim"); qre = t("qre"); qim = t("qim"); ki = sb("ki", [128, 32], I32)
    V = lambda f: op("dve", [R], [R], f)
    A = lambda f: op("act", [R], [R], f)
    A(lambda e: e.activation(out=stp[:], in_=stp[:], func=AF.Exp))
    V(lambda e: e.tensor_tensor(out=ang[:], in0=li[:], in1=stp[:], op=ALU.mult))
    V(lambda e: e.tensor_tensor(out=mag[:], in0=lr[:], in1=stp[:], op=ALU.mult))
    A(lambda e: e.activation(out=mag[:], in_=mag[:], func=AF.Exp))

    def sin_of(dst, shift):
        V(lambda e: e.tensor_scalar(out=tmp[:], in0=ang[:], scalar1=shift, scalar2=1.0 / TWO_PI, op0=ALU.add, op1=ALU.mult))
        V(lambda e: e.tensor_copy(out=ki[:], in_=tmp[:]))
        V(lambda e: e.tensor_copy(out=tmp2[:], in_=ki[:]))
        V(lambda e: e.tensor_tensor(out=tmp[:], in0=tmp[:], in1=tmp2[:], op=ALU.subtract))
        V(lambda e: e.tensor_scalar(out=tmp2[:], in0=tmp[:], scalar1=0.5, scalar2=None, op0=ALU.is_gt))
        V(lambda e: e.tensor_tensor(out=tmp[:], in0=tmp[:], in1=tmp2[:], op=ALU.subtract))
        V(lambda e: e.tensor_scalar(out=tmp2[:], in0=tmp[:], scalar1=-0.5, scalar2=None, op0=ALU.is_lt))
        V(lambda e: e.tensor_tensor(out=tmp[:], in0=tmp[:], in1=tmp2[:], op=ALU.add))
        V(lambda e: e.tensor_scalar(out=tmp[:], in0=tmp[:], scalar1=TWO_PI, scalar2=math.pi, op0=ALU.mult, op1=ALU.min))
        V(lambda e: e.tensor_scalar(out=tmp[:], in0=tmp[:], scalar1=-math.pi, scalar2=None, op0=ALU.max))
        A(lambda e: e.activation(out=dst[:], in_=tmp[:], func=AF.Sin))

    sin_of(sn, 0.0)
    sin_of(cs, math.pi / 2)
    V(lambda e: e.tensor_tensor(out=abre[:], in0=mag[:], in1=cs[:], op=ALU.mult))
    V(lambda e: e.tensor_tensor(out=abim[:], in0=mag[:], in1=sn[:], op=ALU.mult))
    den = t("den"); nr = t("nr")
    V(lambda e: e.tensor_tensor(out=den[:], in0=lr[:], in1=lr[:], op=ALU.mult))
    V(lambda e: e.tensor_tensor(out=tmp[:], in0=li[:], in1=li[:], op=ALU.mult))
    V(lambda e: e.tensor_tensor(out=den[:], in0=den[:], in1=tmp[:], op=ALU.add))
    V(lambda e: e.reciprocal(out=den[:], in_=den[:]))
    V(lambda e: e.tensor_scalar(out=nr[:], in0=abre[:], scalar1=-1.0, scalar2=None, op0=ALU.add))
    V(lambda e: e.tensor_tensor(out=qre[:], in0=nr[:], in1=lr[:], op=ALU.mult))
    V(lambda e: e.tensor_tensor(out=tmp[:], in0=abim[:], in1=li[:], op=ALU.mult))
    V(lambda e: e.tensor_tensor(out=qre[:], in0=qre[:], in1=tmp[:], op=ALU.add))
    V(lambda e: e.tensor_tensor(out=qre[:], in0=qre[:], in1=den[:], op=ALU.mult))
    V(lambda e: e.tensor_tensor(out=qim[:], in0=abim[:], in1=lr[:], op=ALU.mult))
    V(lambda e: e.tensor_tensor(out=tmp[:], in0=nr[:], in1=li[:], op=ALU.mult))
    V(lambda e: e.tensor_tensor(out=qim[:], in0=qim[:], in1=tmp[:], op=ALU.subtract))
    V(lambda e: e.tensor_tensor(out=qim[:], in0=qim[:], in1=den[:], op=ALU.mult))
    pwr = sb("pwr", [128, 9, 32]); pwi = sb("pwi", [128, 9, 32])
    V(lambda e: e.memset(pwr[:, 0, :], 1.0))
    V(lambda e: e.memset(pwi[:, 0, :], 0.0))
    for k in range(1, 9):
        V(lambda e, k=k: e.tensor_tensor(out=pwr[:, k, :], in0=pwr[:, k - 1, :], in1=abre[:], op=ALU.mult))
        V(lambda e, k=k: e.tensor_tensor(out=tmp[:], in0=pwi[:, k - 1, :], in1=abim[:], op=ALU.mult))
        V(lambda e, k=k: e.tensor_tensor(out=pwr[:, k, :], in0=pwr[:, k, :], in1=tmp[:], op=ALU.subtract))
        V(lambda e, k=k: e.tensor_tensor(out=pwi[:, k, :], in0=pwr[:, k - 1, :], in1=abim[:], op=ALU.mult))
        V(lambda e, k=k: e.tensor_tensor(out=tmp[:], in0=pwi[:, k - 1, :], in1=abre[:], op=ALU.mult))
        V(lambda e, k=k: e.tensor_tensor(out=pwi[:, k, :], in0=pwi[:, k, :], in1=tmp[:], op=ALU.add))
    Rst = ctx["R_state"]
    op("dve", [R], [Rst], lambda e: e.tensor_copy(out=ctx["A8re"][j][:], in_=pwr[:, 8, :]))
    op("dve", [R], [Rst], lambda e: e.tensor_copy(out=ctx["A8im"][j][:], in_=pwi[:, 8, :]))
    op("dve", [R], [Rst], lambda e: e.tensor_scalar(out=ctx["A8imn"][j][:], in0=pwi[:, 8, :], scalar1=-1.0, scalar2=None, op0=ALU.mult))

    def bc(x2d):
        return x2d.unsqueeze(2).broadcast_to([128, 32, 16])

    T3 = lambda n: sb(n, [128, 32, 16])
    w1 = T3("w1"); w2 = T3("w2")

    def cmul(dre, dim, xre, xim, sre, sim_):
        V(lambda e: e.tensor_tensor(out=w1[:], in0=xre, in1=bc(sre), op=ALU.mult))
        V(lambda e: e.tensor_tensor(out=w2[:], in0=xim, in1=bc(sim_), op=ALU.mult))
        V(lambda e: e.tensor_tensor(out=dre, in0=w1[:], in1=w2[:], op=ALU.subtract))
        V(lambda e: e.tensor_tensor(out=w1[:], in0=xre, in1=bc(sim_), op=ALU.mult))
        V(lambda e: e.tensor_tensor(out=w2[:], in0=xim, in1=bc(sre), op=ALU.mult))
        V(lambda e: e.tensor_tensor(out=dim, in0=w1[:], in1=w2[:], op=ALU.add))

    Bbre = T3("Bbre"); Bbim = T3("Bbim")
    cmul(Bbre[:], Bbim[:], Bre[:], Bim[:], qre[:], qim[:])
    Bpre = sb("Bpre", [128, 32, 32]); Bpim = sb("Bpim", [128, 32, 32])
    V(lambda e: e.memset(Bpre[:], 0.0))
    V(lambda e: e.memset(Bpim[:], 0.0))
    for g2 in range(2):
        hp = slice(g2 * 64, (g2 + 1) * 64); hc = slice(g2 * 16, (g2 + 1) * 16)
        V(lambda e, hp=hp, hc=hc: e.tensor_copy(out=Bpre[hp, :, hc], in_=Bbre[hp, :, :]))
        V(lambda e, hp=hp, hc=hc: e.tensor_copy(out=Bpim[hp, :, hc], in_=Bbim[hp, :, :]))

    WY = sb("WY", [128, 32, 8, 2, 32], BF16)
    KI = sb("KI", [128, 8, 8, 128], BF16)
    WX = sb("WX", [128, 8, 8, 2, 128], BF16)
    V(lambda e: e.memset(WY[:].rearrange("p a b c d -> p (a b c d)"), 0.0))
    CAre = T3("CAre"); CAim = T3("CAim")
    CApre = sb("CApre", [128, 32, 32]); CApimn = sb("CApimn", [128, 32, 32])
    V(lambda e: e.memset(CApre[:], 0.0))
    V(lambda e: e.memset(CApimn[:], 0.0))
    kit = sb("kit", [128, 128])
    dcol = ctx["s5_dd"][j]
    R_o = Res("pl_out")
    R_kit = Res("kit")
    for k in range(9):
        cmul(CAre[:], CAim[:], Cre[:], Cim[:], pwr[:, k, :], pwi[:, k, :])
        for g2 in range(2):
            hp = slice(g2 * 64, (g2 + 1) * 64); hc = slice(g2 * 16, (g2 + 1) * 16)
            V(lambda e, hp=hp, hc=hc: e.tensor_copy(out=CApre[hp, :, hc], in_=CAre[hp, :, :]))
            V(lambda e, hp=hp, hc=hc: e.tensor_scalar(out=CApimn[hp, :, hc], in0=CAim[hp, :, :], scalar1=-1.0, scalar2=None, op0=ALU.mult))
            if k >= 1:
                V(lambda e, hp=hp, hc=hc, k=k: e.tensor_copy(out=WY[hp, :, k - 1, 0, hc], in_=CAre[hp, :, :]))
                V(lambda e, hp=hp, hc=hc, k=k: e.tensor_scalar(out=WY[hp, :, k - 1, 1, hc], in0=CAim[hp, :, :], scalar1=-1.0, scalar2=None, op0=ALU.mult))
        if k <= 7:
            for c in range(8):
                bank = 4 + (c % 2)
                cs4 = slice(4 * c, 4 * c + 4)
                op("pe", [R], [RP[bank]], lambda e, cs4=cs4: e.matmul(PS[bank][:, 0:128], lhsT=Bpre[:, cs4, :].rearrange("p a b -> p (a b)"),
                                                                 rhs=CApre[:, cs4, :].rearrange("p a b -> p (a b)"), start=True, stop=False))
                op("pe", [R], [RP[bank]], lambda e, cs4=cs4: e.matmul(PS[bank][:, 0:128], lhsT=Bpim[:, cs4, :].rearrange("p a b -> p (a b)"),
                                                                 rhs=CApimn[:, cs4, :].rearrange("p a b -> p (a b)"), start=False, stop=True))
                if k == 0:
                    op("dve", [RP[bank], Rc], [R_kit], lambda e, bank=bank: e.tensor_tensor(out=kit[:], in0=PS[bank][:, 0:128], in1=bmask[:], op=ALU.mult))
                    op("dve", [R_kit, Rc, ctx["R_par"]], [R_o], lambda e, c=c: e.scalar_tensor_tensor(out=KI[:, c, 0, :], in0=ident[:], scalar=dcol[:, c:c + 1], in1=kit[:], op0=ALU.mult, op1=ALU.add))
                else:
                    op("dve", [RP[bank], Rc], [R_o], lambda e, c=c, k=k, bank=bank: e.tensor_tensor(out=KI[:, c, k, :], in0=PS[bank][:, 0:128], in1=bmask[:], op=ALU.mult))
    XBre = sb("XBre", [128, 32, 32]); XBim = sb("XBim", [128, 32, 32])
    x1 = sb("x1", [128, 32, 32]); x2 = sb("x2", [128, 32, 32])
    bc32 = lambda x2d: x2d.unsqueeze(2).broadcast_to([128, 32, 32])
    for tau in range(8):
        k = 7 - tau
        V(lambda e, k=k: e.tensor_tensor(out=x1[:], in0=Bpre[:], in1=bc32(pwr[:, k, :]), op=ALU.mult))
        V(lambda e, k=k: e.tensor_tensor(out=x2[:], in0=Bpim[:], in1=bc32(pwi[:, k, :]), op=ALU.mult))
        V(lambda e: e.tensor_tensor(out=XBre[:], in0=x1[:], in1=x2[:], op=ALU.subtract))
        V(lambda e, k=k: e.tensor_tensor(out=x1[:], in0=Bpre[:], in1=bc32(pwi[:, k, :]), op=ALU.mult))
        V(lambda e, k=k: e.tensor_tensor(out=x2[:], in0=Bpim[:], in1=bc32(pwr[:, k, :]), op=ALU.mult))
        V(lambda e: e.tensor_tensor(out=XBim[:], in0=x1[:], in1=x2[:], op=ALU.add))
        for ri, XB in enumerate((XBre, XBim)):
            for c in range(8):
                bank = 4 + (c % 2)
                op("pe", [R, Rc], [RP[bank]], lambda e, c=c, XB=XB: e.transpose(out=PS[bank][:, 0:128], in_=XB[:, 4 * c:4 * c + 4, :].rearrange("p a b -> p (a b)"), identity=ident[:, :]))
                op("act", [RP[bank]], [R_o], lambda e, c=c, ri=ri, tau=tau, bank=bank: e.copy(out=WX[:, c, tau, ri, :], in_=PS[bank][:, 0:128]))
    dma("sp", ctx["WXd"][j].rearrange("p a b c d -> p (a b c d)"), WX[:].rearrange("p a b c d -> p (a b c d)"), [R, R_o], [], ds)
    dma("sp", ctx["WYd"][j].rearrange("p a b c d -> p (a b c d)"), WY[:].rearrange("p a b c d -> p (a b c d)"), [R], [], ds)
    dma("sp", ctx["KId"][j].rearrange("p a b c -> p (a b c)"), KI[:].rearrange("p a b c -> p (a b c)"), [R, R_o], [], ds)


def run_pass(ctx, pn, nseq, L, pes):
    nc, kb, I, O = ctx["nc"], ctx["kb"], ctx["I"], ctx["O"]
    op, dma = kb.op, kb.dma
    PS, RP, ident = ctx["PS"], ctx["RP"], ctx["ident"]
    Rc = ctx["R_const"]
    N = nseq * L
    NT = (N + 127) // 128
    rows = [min(128, N - t * 128) for t in range(NT)]
    NC = N // 8
    sb = lambda n, s, d=F32, es=None: kb.sb("p%s_%s" % (pn, n), s, d, es=es or pes)

    h_tok = sb("htok", [128, NT, D])
    hT = sb("hT", [128, NB, N], BF16)
    R_h = [Res("h%d" % t) for t in range(NT)]
    R_hT = [Res("hT%d" % t) for t in range(NT)]
    ds_in = kb.dsem("in" + pn)
    ds_w = kb.dsem("w" + pn)
    ds_ln = kb.dsem("ln" + pn)
    gbc = sb("gbc", [128, D]); bbc = sb("bbc", [128, D])
    R_ln = Res("ln")
    NZ = 4
    zt = [sb("zt%d" % i, [128, D]) for i in range(NZ)]
    R_zt = [Res("zt%d" % i) for i in range(NZ)]
    stat = [sb("stat%d" % i, [128, 2, 6]) for i in range(NZ)]
    mv = [sb("mv%d" % i, [128, 2]) for i in range(NZ)]
    rstd = [sb("rstd%d" % i, [128, 1]) for i in range(NZ)]
    NWS = 8 if pn == "S" else 4
    wslot = [sb("wslot%d" % i, [128, NB, 128], BF16) for i in range(NWS)]
    R_ws = [Res("ws%d" % i) for i in range(NWS)]
    ws_i = [0]
    ds_ws = [kb.dsem("ws%s%d" % (pn, i)) for i in range(NWS)]

    if pn == "A":
        dma("sp", h_tok[0:16, 0, :], I["meta_tokens"][:, :], [], [R_h[0]], ds_in)
        dma("sp", h_tok[16:128, 0, :], I["x_prompt"][0:112, :], [], [R_h[0]], ds_in)
        for t in range(1, NT):
            dma("sp", h_tok[:, t, :], I["x_prompt"][112 + 128 * (t - 1):112 + 128 * t, :], [], [R_h[t]], ds_in)
    elif pn == "B":
        for t in range(NT):
            dma("sp", h_tok[0:rows[t], t, :], I["x_prompt"][1008 + 128 * t:1008 + 128 * t + rows[t], :], [], [R_h[t]], ds_in)
    else:
        dma("sp", h_tok[:, 0, :], I["x_sample"][:, :], [], [R_h[0]], ds_in)

    kb.group_final(R_h, ds_in)

    def to_hT(t):
        r = rows[t]
        for half in range(2):
            bank = 4 + half
            for b in range(4):
                blk = half * 4 + b
                op("pe", [R_h[t], Rc], [RP[bank]], lambda e, b=b, blk=blk: e.transpose(out=PS[bank][:, b * 128:b * 128 + r], in_=h_tok[0:r, t, blk * 128:(blk + 1) * 128], identity=ident[0:r, 0:r]))
            op("act", [RP[bank]], [R_hT[t]], lambda e, half=half: e.copy(out=hT[:, half * 4:half * 4 + 4, t * 128:t * 128 + r],
                                                                     in_=PS[bank][:, :].rearrange("p (b n) -> p b n", n=128)[:, :, 0:r]))

    for t in range(NT):
        to_hT(t)

    nlayers = ctx["dbg"].get("nlayers", DEPTH) if ctx["dbg"] else DEPTH
    glist = []
    for l_ in range(nlayers):
        j_ = l_ // 2
        if l_ % 2 == 0:
            glist += [(I["s5_w_in"][j_], c * 128) for c in range(NB)]
        else:
            for c in range(NB):
                glist += [(I["rg_w_in"][j_], c * 128), (I["rg_w_in"][j_], D + c * 128)]
        for v in range(FBH):
            glist += [(I["ffn_w_up"][l_], v * 128), (I["ffn_w_up"][l_], (v + FBH) * 128)]
    wplan = {"list": glist, "issued": 0, "hooks": {}, "off": 0, "next_off": 0}

    def plan_w(blocks, hooks=None):
        wplan["off"] = wplan["next_off"]
        wplan["next_off"] = wplan["off"] + len(blocks)
        for k, f in (hooks or {}).items():
            gi = wplan["off"] + k
            if gi < wplan["issued"]:
                f()
            else:
                wplan["hooks"][gi] = f

    def write_back(k):
        s = k % NWS
        dma("sp", ctx["WBLK"][k], wslot[s][:].rearrange("p a b -> p (a b)"), [R_ws[s]], [], ctx["ds_wb"])

    def load_w_block(i, pf=NWS - 1):
        gi = wplan["off"] + i
        while wplan["issued"] < min(len(wplan["list"]), gi + pf + 1):
            k = wplan["issued"]
            src2d, c0 = wplan["list"][k]
            s = k % NWS
            if pn == "A":
                dma("pool", wslot[s][:], src2d[:, c0:c0 + 128].rearrange("(kc p) f -> p kc f", p=128), [], [R_ws[s]], ds_ws[s])
                if k >= 2:
                    write_back(k - 2)
            else:
                dma("pool", wslot[s][:].rearrange("p a b -> p (a b)"), ctx["WBLK"][k], [], [R_ws[s]], ds_ws[s])
            wplan["issued"] += 1
            if k in wplan["hooks"]:
                wplan["hooks"].pop(k)()
        return gi % NWS

    def up_matmul(s, ps_banks, extra_reads):
        for gi, (c0, cn) in enumerate(colgroups(N)):
            bank = ps_banks[gi]
            for kc in range(NB):
                op("pe", [R_ws[s]] + R_hT + extra_reads, [RP[bank]],
                   lambda e, kc=kc, bank=bank, c0=c0, cn=cn: e.matmul(PS[bank][:, 0:cn], lhsT=wslot[s][:, kc, :], rhs=hT[:, kc, c0:c0 + cn], start=(kc == 0), stop=(kc == NB - 1)))

    def layer_norm_tile(t, zi, li_, k, last):
        r = rows[t]
        z = zt[zi]
        for hh in range(2):
            op("dve", [R_zt[zi]], [R_zt[zi]], lambda e, hh=hh: e.bn_stats(out=stat[zi][0:r, hh, :], in_=z[0:r, hh * 512:(hh + 1) * 512]))
        op("dve", [R_zt[zi]], [R_zt[zi]], lambda e: e.bn_aggr(out=mv[zi][0:r, :], in_=stat[zi][0:r, :, :].rearrange("p a b -> p (a b)")))
        op("act", [R_zt[zi]], [R_zt[zi]], lambda e: e.activation(out=rstd[zi][0:r, :], in_=mv[zi][0:r, 1:2], func=AF.Sqrt, bias=LN_EPS_AP[0:r, :], scale=1.0))
        op("dve", [R_zt[zi]], [R_zt[zi]], lambda e: e.reciprocal(out=rstd[zi][0:r, :], in_=rstd[zi][0:r, :]))
        op("dve", [R_zt[zi]], [R_zt[zi]], lambda e: e.tensor_scalar(out=z[0:r, :], in0=z[0:r, :], scalar1=mv[zi][0:r, 0:1], scalar2=rstd[zi][0:r, 0:1], op0=ALU.subtract, op1=ALU.mult))
        op("pool", [R_zt[zi], R_ln], [R_zt[zi]], lambda e: e.tensor_tensor(out=z[0:r, :], in0=z[0:r, :], in1=gbc[0:r, :], op=ALU.mult))
        op("pool", [R_zt[zi], R_ln], [R_h[t]], lambda e: e.tensor_tensor(out=h_tok[0:r, t, :], in0=z[0:r, :], in1=bbc[0:r, :], op=ALU.add))
        if not last:
            return lambda: to_hT(t)
        else:
            if pn == "A":
                if t == 0:
                    dma("sp", O["y_prompt"][0:112, :], h_tok[16:128, 0, :], [R_h[t]], [], ctx["ds_out"])
                else:
                    dma("sp", O["y_prompt"][112 + 128 * (t - 1):112 + 128 * t, :], h_tok[:, t, :], [R_h[t]], [], ctx["ds_out"])
            elif pn == "B":
                dma("sp", O["y_prompt"][1008 + 128 * t:1008 + 128 * t + r, :], h_tok[0:r, t, :], [R_h[t]], [], ctx["ds_out"])
            else:
                dma("sp", O["y_sample"][:, :], h_tok[:, 0, :], [R_h[t]], [], ctx["ds_out"])
            return lambda: None

    LN_EPS_AP = sb("lneps", [128, 1])
    op("dve", [], [Rc], lambda e: e.memset(LN_EPS_AP[:], LN_EPS))
    one_ap = sb("one", [128, 1])
    op("dve", [], [Rc], lambda e: e.memset(one_ap[:], 1.0))

    def load_ln(li_, k):
        dma("sp", gbc[:], I["ln_g"][li_, k].partition_broadcast(128), [], [R_ln], ds_ln)
        dma("sp", bbc[:], I["ln_b"][li_, k].partition_broadcast(128), [], [R_ln], ds_ln)

    env = dict(ctx=ctx, pn=pn, nseq=nseq, L=L, N=N, NT=NT, rows=rows, NC=NC, sb=sb, h_tok=h_tok, hT=hT,
               R_h=R_h, R_hT=R_hT, ds_w=ds_w, ds_in=ds_in, zt=zt, R_zt=R_zt, load_w_block=load_w_block, plan_w=plan_w,
               up_matmul=up_matmul, layer_norm_tile=layer_norm_tile, one_ap=one_ap, load_ln=load_ln, wslot=wslot, R_ws=R_ws)

    for li_ in range(nlayers):
        j = li_ // 2
        with ExitStack() as les:
            if li_ % 2 == 0:
                s5_layer(env, li_, j, les)
            else:
                rg_layer(env, li_, j, les)
            kb.barrier()
        with ExitStack() as les:
            ffn_layer(env, li_, les, last=(li_ == nlayers - 1))
            if pn == "A" and li_ == nlayers - 1:
                for k in range(max(0, len(glist) - 2), len(glist)):
                    write_back(k)
            kb.barrier()


def conv_taps(kb, xs, acc, nseq, L, K, wcols, bcol, R_main, R_halo, R_acc):
    kb.op("act", [R_main], [R_acc], lambda e: e.activation(out=acc[:, :, :], in_=xs[:, :, K - 1:K - 1 + L], func=AF.Identity, scale=wcols[K - 1], bias=bcol))
    for k in range(K - 1):
        kb.op("dve", [R_main, R_halo, R_acc], [R_acc], lambda e, k=k: e.scalar_tensor_tensor(out=acc[:, :, :], in0=xs[:, :, k:k + L], scalar=wcols[k], in1=acc[:, :, :], op0=ALU.mult, op1=ALU.add))


def ffn_layer(env, li_, les, last):
    ctx = env["ctx"]; kb = ctx["kb"]; nc = ctx["nc"]; I = ctx["I"]; O = ctx["O"]
    op, dma = kb.op, kb.dma
    PS, RP = ctx["PS"], ctx["RP"]
    pn, nseq, L, N, NT, rows = env["pn"], env["nseq"], env["L"], env["N"], env["NT"], env["rows"]
    sb = lambda n, s, d=F32: env["sb"]("f%d_%s" % (li_, n), s, d, es=les)
    h_tok, hT = env["h_tok"], env["hT"]
    actT = sb("actT", [128, FBH, N], BF16)
    R_act = [Res("act%d" % v) for v in range(FBH)]
    wd = sb("wd", [128, FBH, D], BF16)
    R_wd = [Res("wd%d" % v) for v in range(FBH)]
    xs = [sb("xs%d" % i, [128, nseq, 2 + L]) for i in range(2)]
    acc4 = [sb("acc%d" % i, [128, nseq, L]) for i in range(4)]
    R_xs = [Res("xs%d" % i) for i in range(2)]
    R_xh = [Res("xh%d" % i) for i in range(2)]
    R_acc4 = [Res("acc%d" % i) for i in range(4)]
    cw, cb = ctx["ffn_cw"][li_], ctx["ffn_cb"][li_]
    Rst = ctx["R_state"]
    if nseq == 1:
        hist = ctx["ffn_hist_p"][li_]
        R_hist = Rst
    else:
        hist = sb("hist", [128, FB, nseq, 2])
        R_hist = Res("hist")
        stage = sb("hstage", [32, FF2])
        load_T(ctx, I["state_ffn_conv"][li_], FB, 32, lambda b0, nb_: hist[:, b0:b0 + nb_, :, :].rearrange("p b s k -> p b (s k)"), stage, R_hist, "fh", env["ds_in"])
    env["load_ln"](li_, 1)
    def wd_hook(v):
        def f():
            if pn == "A":
                dma("pool", wd[:, v, :], I["ffn_w_down"][li_][v * 128:(v + 1) * 128, :], [], [R_wd[v]], env["ds_w"])
            else:
                dma("pool", wd[:, v, :], ctx["WDd"][li_][:, v * D:(v + 1) * D], [], [R_wd[v]], env["ds_w"])
            if v == FBH - 1:
                kb.group_final(R_wd, env["ds_w"])
        return f
    blocks = []
    for v in range(FBH):
        blocks += [(I["ffn_w_up"][li_], v * 128), (I["ffn_w_up"][li_], (v + FBH) * 128)]
    env["plan_w"](blocks, {2 * v + 1: wd_hook(v) for v in range(FBH)})
    ngrp = len(colgroups(N))
    banksets = [[0, 1, 2][:ngrp], [3, 4, 5][:ngrp]]
    bi = 0
    for v in range(FBH):
        acc = acc4[2 * (v % 2):2 * (v % 2) + 2]
        R_acc = R_acc4[2 * (v % 2):2 * (v % 2) + 2]
        for which in range(2):
            blk = v + which * FBH
            s = env["load_w_block"](2 * v + which)
            banks = banksets[bi % 2]
            bi += 1
            env["up_matmul"](s, banks, [])
            x = xs[which]
            op("act", [R_hist], [R_xh[which]], lambda e, x=x, blk=blk: e.copy(out=x[:, :, 0:2], in_=hist[:, blk, :, :]))
            for gi, (c0, cn) in enumerate(colgroups(N)):
                if nseq == 1:
                    op("act", [RP[banks[gi]]], [R_xs[which]], lambda e, gi=gi, c0=c0, cn=cn, x=x: e.copy(out=x[:, 0, 2 + c0:2 + c0 + cn], in_=PS[banks[gi]][:, 0:cn]))
                else:
                    op("act", [RP[banks[gi]]], [R_xs[which]], lambda e, gi=gi, cn=cn, x=x: e.copy(out=x[:, :, 2:2 + L], in_=PS[banks[gi]][:, 0:cn].rearrange("p (s t) -> p s t", t=L)))
            op("act", [R_xs[which]], [R_hist], lambda e, x=x, blk=blk: e.copy(out=hist[:, blk, :, :], in_=x[:, :, L:L + 2]))
            conv_taps(kb, x, acc[which], nseq, L, 3, [cw[k][:, blk:blk + 1] for k in range(3)], cb[:, blk:blk + 1], R_xs[which], R_xh[which], R_acc[which])
        op("act", [R_acc[1]], [R_acc[1]], lambda e, acc=acc: e.activation(out=acc[1][:, :, :], in_=acc[1][:, :, :], func=AF.Gelu_apprx_tanh))
        op("dve", [R_acc[0], R_acc[1]], [R_act[v]], lambda e, v=v, acc=acc: e.tensor_tensor(out=actT[:, v, :], in0=acc[0][:, :, :].rearrange("p s t -> p (s t)"), in1=acc[1][:, :, :].rearrange("p s t -> p (s t)"), op=ALU.mult))
    if pn == "A":
        dma("sp", ctx["WDd"][li_], wd[:].rearrange("p a b -> p (a b)"), R_wd, [], ctx["ds_wb"])
    if pn != "A":
        stg = sb("ostage", [32, 256 if nseq == 1 else 2048])
        m = 2 * nseq
        store_T(ctx, lambda b: hist[:, b, :, :].rearrange("p s k -> p (s k)"), FB, m,
                (O["ffn_conv_p"] if nseq == 1 else O["ffn_conv_s"])[li_], stg, R_hist, "fo")
    pending = None
    for t in range(NT):
        r = rows[t]
        pb = [0, 1] if t % 2 == 0 else [2, 3]
        for v in range(FBH):
            for hh in range(2):
                op("pe", [R_act[v], R_wd[v]], [RP[pb[hh]]], lambda e, v=v, hh=hh: e.matmul(PS[pb[hh]][0:r, :], lhsT=actT[:, v, t * 128:t * 128 + r], rhs=wd[:, v, hh * 512:(hh + 1) * 512], start=(v == 0), stop=(v == FBH - 1)))
        zi = t % 4
        z = env["zt"][zi]
        for hh in range(2):
            op("dve", [env["R_h"][t], RP[pb[hh]]], [env["R_zt"][zi]], lambda e, hh=hh: e.scalar_tensor_tensor(out=z[0:r, hh * 512:(hh + 1) * 512], in0=h_tok[0:r, t, hh * 512:(hh + 1) * 512], scalar=DN_ALPHA, in1=PS[pb[hh]][0:r, :], op0=ALU.mult, op1=ALU.add))
        if pending:
            pending()
        pending = env["layer_norm_tile"](t, zi, li_, 1, last)
    pending()


def rg_layer(env, li_, j, les):
    ctx = env["ctx"]; kb = ctx["kb"]; nc = ctx["nc"]; I = ctx["I"]; O = ctx["O"]
    op, dma = kb.op, kb.dma
    PS, RP = ctx["PS"], ctx["RP"]
    pn, nseq, L, N, NT, rows = env["pn"], env["nseq"], env["L"], env["N"], env["NT"], env["rows"]
    sb = lambda n, s, d=F32: env["sb"]("r%d_%s" % (li_, n), s, d, es=les)
    h_tok, hT = env["h_tok"], env["hT"]
    Rst = ctx["R_state"]
    hgT = sb("hgT", [128, NB, N], BF16)
    R_hg = [Res("hg%d" % c) for c in range(NB)]
    wg = sb("wg", [128, 4, 2, 512], BF16)
    R_wg = Res("wg")
    wo = sb("wo", [128, NB, D], BF16)
    R_wo = Res("wo")
    if pn == "A":
        for n in range(4):
            dma("pool", wg[:, n, :, :], I["rg_w_gates"][j, n].rearrange("(kc p) f -> p kc f", p=128), [], [R_wg], env["ds_w"])
    else:
        dma("pool", wg[:].rearrange("p a b c -> p (a b c)"), ctx["RGWGd"][j], [], [R_wg], env["ds_w"])
    kb.group_final([R_wg], env["ds_w"])
    env["load_ln"](li_, 0)
    if nseq == 1:
        hist = ctx["rg_hist_p"][j]; hst = ctx["rg_h_p"][j]; R_hist = Rst
    else:
        hist = sb("hist", [128, NB, nseq, 3]); hst = sb("hst", [128, NB, nseq]); R_hist = Res("rhist")
        stage = sb("stage", [48, D])
        load_T(ctx, I["state_rg_conv"][j], NB, 48, lambda b0, nb_: hist[:, b0:b0 + nb_, :, :].rearrange("p b s k -> p b (s k)"), stage, R_hist, "rc", env["ds_in"])
        stage2 = sb("stage2", [16, D])
        load_T(ctx, I["state_rg_h"][j], NB, 16, lambda b0, nb_: hst[:, b0:b0 + nb_, :], stage2, R_hist, "rh", env["ds_in"])
    xs = [sb("xs%d" % i, [128, nseq, 3 + L]) for i in range(2)]
    xc = [[sb("xc%d_%d" % (b, i), [128, nseq, L]) for i in range(2)] for b in range(2)]
    xcb = [[sb("xcb%d_%d" % (b, i), [128, N], BF16) for i in range(2)] for b in range(2)]
    gl = [[sb("gl%d_%d" % (b, i), [128, N]) for i in range(2)] for b in range(2)]
    work = [[sb("wk%d_%d" % (b, i), [128, nseq, L]) for i in range(3)] for b in range(2)]
    R_xs = [Res() for _ in range(2)]; R_xh = [Res() for _ in range(2)]
    R_xc = [[Res() for _ in range(2)] for _ in range(2)]; R_gl = [[Res() for _ in range(2)] for _ in range(2)]
    R_wk = [Res("rgwork0"), Res("rgwork1")]
    cw, cb, bg, m8 = ctx["rg_cw"][j], ctx["rg_cb"][j], ctx["rg_bg"][j], ctx["rg_m8sp"][j]
    one_ap = env["one_ap"]
    ngrp = len(colgroups(N))
    banksets = [[0, 1, 2][:ngrp], [3, 4, 5][:ngrp]]
    bi = [0]
    blocks = []
    for c in range(NB):
        blocks += [(I["rg_w_in"][j], c * 128), (I["rg_w_in"][j], D + c * 128)]
    env["plan_w"](blocks)
    flat = lambda t: t[:, :, :].rearrange("p s t -> p (s t)")

    def stage_proj(n):
        pb_ = n % 2
        for q in range(2):
            c = 2 * n + q
            s = env["load_w_block"](2 * c)
            banks = banksets[bi[0] % 2]; bi[0] += 1
            env["up_matmul"](s, banks, [])
            for gi, (c0, cn) in enumerate(colgroups(N)):
                op("act", [RP[banks[gi]]], [R_gl[pb_][q]], lambda e, gi=gi, c0=c0, cn=cn, q=q, banks=banks: e.activation(out=gl[pb_][q][:, c0:c0 + cn], in_=PS[banks[gi]][:, 0:cn], func=AF.Gelu_apprx_tanh))
            s = env["load_w_block"](2 * c + 1)
            banks = banksets[bi[0] % 2]; bi[0] += 1
            env["up_matmul"](s, banks, [])
            x = xs[q]
            op("act", [R_hist], [R_xh[q]], lambda e, x=x, c=c: e.copy(out=x[:, :, 0:3], in_=hist[:, c, :, :]))
            for gi, (c0, cn) in enumerate(colgroups(N)):
                if nseq == 1:
                    op("act", [RP[banks[gi]]], [R_xs[q]], lambda e, gi=gi, c0=c0, cn=cn, x=x, banks=banks: e.copy(out=x[:, 0, 3 + c0:3 + c0 + cn], in_=PS[banks[gi]][:, 0:cn]))
                else:
                    op("act", [RP[banks[gi]]], [R_xs[q]], lambda e, gi=gi, cn=cn, x=x, banks=banks: e.copy(out=x[:, :, 3:3 + L], in_=PS[banks[gi]][:, 0:cn].rearrange("p (s t) -> p s t", t=L)))
            op("act", [R_xs[q]], [R_hist], lambda e, x=x, c=c: e.copy(out=hist[:, c, :, :], in_=x[:, :, L:L + 3]))
            conv_taps(kb, x, xc[pb_][q], nseq, L, 4, [cw[k][:, c:c + 1] for k in range(4)], cb[:, c:c + 1], R_xs[q], R_xh[q], R_xc[pb_][q])
            op("dve", [R_xc[pb_][q]], [R_xc[pb_][q]], lambda e, q=q: e.tensor_copy(out=xcb[pb_][q][:, :], in_=flat(xc[pb_][q])))

    def stage_gate(n):
        pb_ = n % 2
        for q in range(2):
            c = 2 * n + q
            T1, T2, T3 = work[c % 2]
            Rw = R_wk[c % 2]
            pr_ = banksets[0]; pi_ = banksets[1]
            for (dst_banks, off) in ((pr_, q * 128), (pi_, 256 + q * 128)):
                for gi, (c0, cn) in enumerate(colgroups(N)):
                    for kc in range(2):
                        op("pe", [R_wg, R_xc[pb_][kc]], [RP[dst_banks[gi]]], lambda e, kc=kc, gi=gi, c0=c0, cn=cn, off=off, dst_banks=dst_banks: e.matmul(PS[dst_banks[gi]][:, 0:cn], lhsT=wg[:, n, kc, off:off + 128], rhs=xcb[pb_][kc][:, c0:c0 + cn], start=(kc == 0), stop=(kc == 1)))
            for gi, (c0, cn) in enumerate(colgroups(N)):
                op("act", [RP[pr_[gi]]], [Rw], lambda e, gi=gi, c0=c0, cn=cn, c=c: e.activation(out=flat(T1)[:, c0:c0 + cn], in_=PS[pr_[gi]][:, 0:cn], func=AF.Sigmoid, bias=bg[:, c:c + 1], scale=1.0))
                op("act", [RP[pi_[gi]]], [Rw], lambda e, gi=gi, c0=c0, cn=cn, c=c: e.activation(out=flat(T2)[:, c0:c0 + cn], in_=PS[pi_[gi]][:, 0:cn], func=AF.Sigmoid, bias=bg[:, 8 + c:8 + c + 1], scale=1.0))
            op("act", [Rw], [Rw], lambda e, c=c: e.activation(out=flat(T1), in_=flat(T1), func=AF.Exp, scale=m8[:, c:c + 1]))
            op("act", [Rw], [Rw], lambda e: e.activation(out=flat(T3), in_=flat(T1), func=AF.Square))
            op("act", [Rw], [Rw], lambda e: e.activation(out=flat(T3), in_=flat(T3), func=AF.Sqrt, scale=-1.0, bias=one_ap[:, :]))
            op("dve", [Rw, R_xc[pb_][q]], [Rw], lambda e, q=q: e.tensor_tensor(out=flat(T2), in0=flat(T2), in1=flat(xc[pb_][q]), op=ALU.mult))
            op("dve", [Rw], [Rw], lambda e: e.tensor_tensor(out=flat(T2), in0=flat(T2), in1=flat(T3), op=ALU.mult))
            for s_ in range(nseq):
                op("dve", [Rw, R_hist], [Rw], lambda e, s_=s_, c=c: e.tensor_tensor_scan(out=T3[:, s_, :], data0=T1[:, s_, :], data1=T2[:, s_, :], initial=hst[:, c, s_:s_ + 1], op0=ALU.mult, op1=ALU.add))
            op("dve", [Rw], [R_hist], lambda e, c=c: e.tensor_copy(out=hst[:, c, :], in_=T3[:, :, L - 1]))
            op("dve", [Rw, R_gl[pb_][q]], [R_hg[c]], lambda e, c=c, q=q: e.tensor_tensor(out=hgT[:, c, :], in0=flat(T3), in1=gl[pb_][q][:, :], op=ALU.mult))

    for n in range(4):
        stage_proj(n)
        if n == 1:
            if pn == "A":
                dma("pool", wo[:], I["rg_w_out"][j].rearrange("(kc p) f -> p kc f", p=128), [], [R_wo], env["ds_w"])
            else:
                dma("pool", wo[:].rearrange("p a b -> p (a b)"), ctx["RGWOd"][j], [], [R_wo], env["ds_w"])
            kb.group_final([R_wo], env["ds_w"])
        if n > 0:
            stage_gate(n - 1)
    stage_gate(3)
    if pn == "A":
        dma("sp", ctx["RGWGd"][j], wg[:].rearrange("p a b c -> p (a b c)"), [R_wg], [], ctx["ds_wb"])
        dma("sp", ctx["RGWOd"][j], wo[:].rearrange("p a b -> p (a b)"), [R_wo], [], ctx["ds_wb"])
    if pn != "A":
        stg = sb("ostage", [48, 256])
        store_T(ctx, lambda b: hist[:, b, :, :].rearrange("p s k -> p (s k)"), NB, 3 * nseq,
                (O["rg_conv_p"] if nseq == 1 else O["rg_conv_s"])[j], stg, R_hist, "ro")
        store_T(ctx, lambda b: hst[:, b, :], NB, nseq, (O["rg_h_p"] if nseq == 1 else O["rg_h_s"])[j], stg, R_hist, "rho")
    pending = None
    for t in range(NT):
        r = rows[t]
        pb = [0, 1] if t % 2 == 0 else [2, 3]
        for kc in range(NB):
            for hh in range(2):
                op("pe", [R_hg[kc], R_wo], [RP[pb[hh]]], lambda e, kc=kc, hh=hh: e.matmul(PS[pb[hh]][0:r, :], lhsT=hgT[:, kc, t * 128:t * 128 + r], rhs=wo[:, kc, hh * 512:(hh + 1) * 512], start=(kc == 0), stop=(kc == NB - 1)))
        zi = t % 4
        z = env["zt"][zi]
        for hh in range(2):
            op("dve", [env["R_h"][t], RP[pb[hh]]], [env["R_zt"][zi]], lambda e, hh=hh: e.scalar_tensor_tensor(out=z[0:r, hh * 512:(hh + 1) * 512], in0=h_tok[0:r, t, hh * 512:(hh + 1) * 512], scalar=DN_ALPHA, in1=PS[pb[hh]][0:r, :], op0=ALU.mult, op1=ALU.add))
        if pending:
            pending()
        pending = env["layer_norm_tile"](t, zi, li_, 0, False)
    pending()


def ONE_AP(env):
    if "one_ap" not in env:
        kb = env["ctx"]["kb"]
        t = env["sb"]("one", [128, 1])
        kb.op("dve", [], [env["ctx"]["R_const"]], lambda e: e.memset(t[:], 1.0))
        env["one_ap"] = t
    return env["one_ap"]


def s5_layer(env, li_, j, les):
    ctx = env["ctx"]; kb = ctx["kb"]; nc = ctx["nc"]; I = ctx["I"]; O = ctx["O"]
    op, dma = kb.op, kb.dma
    PS, RP = ctx["PS"], ctx["RP"]
    pn, nseq, L, N, NT, rows, NC = env["pn"], env["nseq"], env["L"], env["N"], env["NT"], env["rows"], env["NC"]
    sb = lambda n, s, d=F32: env["sb"]("s%d_%s" % (li_, n), s, d, es=les)
    h_tok, hT = env["h_tok"], env["hT"]
    Rst = ctx["R_state"]
    uT = sb("uT", [128, NB, N], BF16)
    gyT = uT
    R_u = [Res() for _ in range(NB)]
    R_gy = R_u
    R_X = Res("Xs")
    Sinb = sb("Sinb", [128, 32, 2, NC], BF16)
    R_S = Res("Sinb")
    wo = sb("wo", [128, NB, 2 * D], BF16)
    R_wo = Res("wo")
    env["load_ln"](li_, 0)
    if nseq == 1:
        st = ctx["s5st_p"][j]; R_st = Rst
    else:
        st = sb("st", [128, 32, 2, nseq]); R_st = Res("s5st")
        stage = sb("stage", [16, 4096])
        for ri, nm in enumerate(("state_s5_re", "state_s5_im")):
            load_T(ctx, I[nm][j], 32, 16, lambda b0, nb_, ri=ri: st[:, b0:b0 + nb_, ri, :], stage, R_st, "s5" + str(ri), env["ds_in"])
    ngrp = len(colgroups(N))
    banksets = [[0, 1, 2][:ngrp], [3, 4, 5][:ngrp]]
    env["plan_w"]([(I["s5_w_in"][j], c * 128) for c in range(NB)])
    for c in range(NB):
        s = env["load_w_block"](c)
        banks = banksets[c % 2]
        env["up_matmul"](s, banks, [])
        for gi, (c0, cn) in enumerate(colgroups(N)):
            op("act", [RP[banks[gi]]], [R_u[c]], lambda e, gi=gi, c0=c0, cn=cn, c=c, banks=banks: e.copy(out=uT[:, c, c0:c0 + cn], in_=PS[banks[gi]][:, 0:cn]))
    if pn == "A":
        dma("pool", wo[:, 0:4, :], I["s5_w_out"][j][0:512, :].rearrange("(kc p) f -> p kc f", p=128), [], [R_wo], env["ds_w"])
        dma("pool", wo[:, 4:8, :], I["s5_w_out"][j][512:1024, :].rearrange("(kc p) f -> p kc f", p=128), [], [R_wo], env["ds_w"])
        kb.group_final([R_wo], env["ds_w"])
    else:
        dma("pool", wo[:].rearrange("p a b -> p (a b)"), ctx["S5WOd"][j], [], [R_wo], env["ds_w"])
        kb.group_final([R_wo], env["ds_w"])
    ies = ExitStack()
    sbo = sb
    sb = lambda n, s_, d=F32: env["sb"]("s%d_%s" % (li_, n), s_, d, es=ies)
    Xs = sb("Xs", [128, 32, 2, NC])
    WXs = [sb("WX%d" % i, [128, 8, 2, 128], BF16) for i in range(2)]
    R_WX = [Res() for _ in range(2)]
    ds_wx = [kb.dsem("s5x%s%d_%d" % (pn, li_, i)) for i in range(2)]
    ds_kw = [kb.dsem("s5k%s%d_%d" % (pn, li_, i)) for i in range(2)]
    for c in range(NB):
        wi = c % 2
        dma("sp", WXs[wi][:], ctx["WXd"][j][:, c], [], [R_WX[wi]], ds_wx[wi])
        for ri in range(2):
            for tau in range(8):
                for pr in range(4):
                    bank = pr
                    rs = slice(32 * pr, 32 * pr + 32)
                    op("pe", [R_WX[wi], R_u[c]], [RP[bank]], lambda e, ri=ri, tau=tau, rs=rs, bank=bank, wi=wi, c=c, pr=pr: e.matmul(
                        PS[bank][:, ri * 256:ri * 256 + NC], lhsT=WXs[wi][rs, tau, ri, :],
                        rhs=uT[rs, c, :].rearrange("p (n t) -> p n t", t=8)[:, :, tau], start=(tau == 0), stop=(tau == 7), tile_position=(32 * pr, 0)))
        for pr in range(4):
            bank = pr
            op("act", [RP[bank]], [R_X], lambda e, bank=bank, c=c, pr=pr: e.copy(out=Xs[:, 4 * c + pr, :, :], in_=PS[bank][:, :].rearrange("p (r n) -> p r n", n=256)[:, :, 0:NC]))
    a8r, a8i, a8in = ctx["A8re"][j], ctx["A8im"][j], ctx["A8imn"][j]
    Mrot = sb("Mrot", [128, 32, 2, 2]); prod = sb("prod", [128, 32, 2, 2]); t1 = sb("t1", [128, 32, 2])
    op("dve", [Rst], [R_X], lambda e: e.tensor_copy(out=Mrot[:, :, 0, 0], in_=a8r[:, :]))
    op("dve", [Rst], [R_X], lambda e: e.tensor_copy(out=Mrot[:, :, 1, 1], in_=a8r[:, :]))
    op("dve", [Rst], [R_X], lambda e: e.tensor_copy(out=Mrot[:, :, 0, 1], in_=a8in[:, :]))
    op("dve", [Rst], [R_X], lambda e: e.tensor_copy(out=Mrot[:, :, 1, 0], in_=a8i[:, :]))
    if nseq == 1:
        op("act", [R_st], [R_S], lambda e: e.copy(out=Sinb[:, :, :, 0], in_=st[:, :, :, 0]))
    else:
        op("act", [R_st], [R_S], lambda e: e.copy(out=Sinb[:, :, :, :], in_=st[:, :, :, :]))
    nsteps = NC if nseq == 1 else nseq
    for c in range(nsteps):
        if nseq == 1:
            prev = st[:, :, :, 0] if c == 0 else Xs[:, :, :, c - 1]
        else:
            prev = st[:, :, :, c]
        cur = Xs[:, :, :, c]
        rd = [R_X, R_st, Rst]
        pb_ = prev.unsqueeze(2).broadcast_to([128, 32, 2, 2])
        op("dve", rd, [R_X], lambda e, pb_=pb_: e.tensor_tensor(out=prod[:], in0=pb_, in1=Mrot[:], op=ALU.mult))
        op("dve", rd, [R_X], lambda e: e.tensor_tensor(out=t1[:], in0=prod[:, :, :, 0], in1=prod[:, :, :, 1], op=ALU.add))
        op("dve", rd, [R_X], lambda e, cur=cur: e.tensor_tensor(out=cur, in0=cur, in1=t1[:], op=ALU.add))
    if nseq == 1:
        op("act", [R_X], [R_S], lambda e: e.copy(out=Sinb[:, :, :, 1:NC], in_=Xs[:, :, :, 0:NC - 1]))
        op("dve", [R_X], [R_st], lambda e: e.tensor_copy(out=st[:, :, :, 0], in_=Xs[:, :, :, NC - 1]))
    else:
        op("dve", [R_X], [R_st], lambda e: e.tensor_copy(out=st[:, :, :, :], in_=Xs[:, :, :, :]))
    if pn != "A":
        stg = sb("ostage", [16, 2048])
        for ri, nm in enumerate(("s5_re", "s5_im")):
            store_T(ctx, lambda b, ri=ri: st[:, b, ri, :], 32, nseq, O[nm + ("_p" if nseq == 1 else "_s")][j], stg, R_st, "s5o%d" % ri)
    kb.barrier()
    ies.close()
    sb = sbo
    KIs = [sb("KI%d" % i, [128, 8, 128], BF16) for i in range(2)]
    WYs = [sb("WY%d" % i, [128, 4, 8, 2, 32], BF16) for i in range(2)]
    R_KI = [Res() for _ in range(2)]
    R_WY = [Res() for _ in range(2)]
    cg = colgroups(NC, 64)
    for c in range(NB):
        wi = c % 2
        dma("sp", KIs[wi][:], ctx["KId"][j][:, c], [], [R_KI[wi]], ds_kw[wi])
        dma("sp", WYs[wi][:], ctx["WYd"][j][:, 4 * c:4 * c + 4], [], [R_WY[wi]], ds_kw[wi])
        kb.group_final([R_KI[wi], R_WY[wi]], ds_kw[wi])
        banks = banksets[c % 2]
        for gi, (n0, nn) in enumerate(cg):
            bank = banks[gi]
            uv = uT[:, c, n0 * 8:(n0 + nn) * 8].rearrange("p (n t) -> p t n", t=8)
            for lag in range(8):
                op("pe", [R_KI[wi], R_u[c]], [RP[bank]], lambda e, lag=lag, bank=bank, nn=nn, uv=uv, wi=wi: e.matmul(
                    PS[bank][:, lag * nn:8 * nn], lhsT=KIs[wi][:, lag, :], rhs=uv[:, 0:8 - lag, :], start=(lag == 0), stop=False))
            for tau in range(8):
                for ri in range(2):
                    for pr in range(4):
                        lastmm = (tau == 7 and ri == 1)
                        op("pe", [R_WY[wi], R_S], [RP[bank]], lambda e, pr=pr, tau=tau, ri=ri, bank=bank, wi=wi, n0=n0, nn=nn, c=c, lastmm=lastmm: e.matmul(
                            PS[bank][32 * pr:32 * pr + 32, tau * nn:(tau + 1) * nn], lhsT=WYs[wi][:, pr, tau, ri, :], rhs=Sinb[:, 4 * c + pr, ri, n0:n0 + nn], start=False, stop=lastmm, tile_position=(0, 32 * pr)))
            op("act", [RP[bank]], [R_gy[c]], lambda e, bank=bank, n0=n0, nn=nn, c=c: e.activation(out=gyT[:, c, n0 * 8:(n0 + nn) * 8].rearrange("p (n t) -> p n t", t=8), in_=PS[bank][:, 0:nn * 8].rearrange("p (t n) -> p n t", t=8), func=AF.Gelu_apprx_tanh))
    if pn == "A":
        dma("sp", ctx["S5WOd"][j], wo[:].rearrange("p a b -> p (a b)"), [R_wo], [], ctx["ds_wb"])
    sgs = [sb("sg%d" % i, [128, D]) for i in range(2)]
    R_sgs = [Res("sg%d" % i) for i in range(2)]
    pending = None
    for t in range(NT):
        sg = sgs[t % 2]
        R_sg = R_sgs[t % 2]
        r = rows[t]
        pb = [2, 3, 0, 1]
        for qq in (2, 3, 0, 1):
            for kc in range(NB):
                op("pe", [R_gy[kc], R_wo], [RP[pb[qq]]], lambda e, kc=kc, qq=qq: e.matmul(PS[pb[qq]][0:r, :], lhsT=gyT[:, kc, t * 128:t * 128 + r], rhs=wo[:, kc, qq * 512:(qq + 1) * 512], start=(kc == 0), stop=(kc == NB - 1)))
        zi = t % 4
        z = env["zt"][zi]
        for hh in range(2):
            op("act", [RP[pb[2 + hh]]], [R_sg], lambda e, hh=hh: e.activation(out=sg[0:r, hh * 512:(hh + 1) * 512], in_=PS[pb[2 + hh]][0:r, :], func=AF.Sigmoid))
            op("dve", [R_sg, RP[pb[hh]]], [R_sg], lambda e, hh=hh: e.tensor_tensor(out=sg[0:r, hh * 512:(hh + 1) * 512], in0=sg[0:r, hh * 512:(hh + 1) * 512], in1=PS[pb[hh]][0:r, :], op=ALU.mult))
            op("dve", [env["R_h"][t], R_sg], [env["R_zt"][zi]], lambda e, hh=hh: e.scalar_tensor_tensor(out=z[0:r, hh * 512:(hh + 1) * 512], in0=h_tok[0:r, t, hh * 512:(hh + 1) * 512], scalar=DN_ALPHA, in1=sg[0:r, hh * 512:(hh + 1) * 512], op0=ALU.mult, op1=ALU.add))
        if pending:
            pending()
        pending = env["layer_norm_tile"](t, zi, li_, 0, False)
    pending()


_NC_CACHE = {}


def _get_nc(dbg=None):
    key = repr(dbg)
    if key not in _NC_CACHE:
        nc = bass.Bass("TRN2", target_bir_lowering=False)
        build(nc, dbg)
        _NC_CACHE[key] = nc
    return _NC_CACHE[key]


def kernel(dbg=None, **inp):
    f = lambda a: np.ascontiguousarray(np.asarray(a, dtype=np.float32))
    ident = np.eye(128, dtype=np.float32)
    bmask = np.kron(np.eye(4, dtype=np.float32), np.ones((32, 32), np.float32))
    wnames = ["meta_tokens", "s5_w_in", "s5_lam_re", "s5_lam_im", "s5_log_step", "s5_b_re", "s5_b_im", "s5_c_re",
              "s5_c_im", "s5_d", "s5_w_out", "rg_w_in", "rg_conv_w", "rg_conv_b", "rg_w_gates", "rg_b_gates",
              "rg_lam", "rg_w_out", "ffn_w_up", "ffn_conv_w", "ffn_conv_b", "ffn_w_down", "ln_g", "ln_b"]
    shared = {n: f(inp[n]) for n in wnames}
    shared["ident"] = ident
    shared["bmask"] = bmask
    shared["sel2"] = np.kron(np.eye(2, dtype=np.float32), np.ones((1, 64), np.float32))
    in_maps = []
    for c in range(8):
        m = dict(shared)
        sl = slice(16 * c, 16 * c + 16)
        m["x_prompt"] = f(inp["x_prompt"][c])
        m["x_sample"] = f(inp["x_sample"][sl]).reshape(128, D)
        m["state_s5_re"] = f(inp["state_s5_re"][:, sl]).reshape(2, 16, 4096)
        m["state_s5_im"] = f(inp["state_s5_im"][:, sl]).reshape(2, 16, 4096)
        m["state_rg_h"] = f(inp["state_rg_h"][:, sl])
        m["state_rg_conv"] = f(inp["state_rg_conv"][:, sl]).reshape(2, 48, D)
        m["state_ffn_conv"] = f(inp["state_ffn_conv"][:, sl]).reshape(4, 32, FF2)
        in_maps.append(m)
    nc = _get_nc(dbg)
    res = run_bass_kernel_spmd(nc, in_maps, core_ids=list(range(8)))
    R = res.results
    cat = lambda k, ax: np.concatenate([np.asarray(R[c][k]) for c in range(8)], axis=ax)
    y_prompt = np.stack([np.asarray(R[c]["y_prompt"]) for c in range(8)], 0)
    y_sample = cat("y_sample", 0).reshape(128, 8, D)
    s5_re_p = cat("s5_re_p", 1).reshape(2, 8, 64, 64)
    s5_im_p = cat("s5_im_p", 1).reshape(2, 8, 64, 64)
    rg_h_p = cat("rg_h_p", 1).reshape(2, 8, D)
    rg_conv_p = np.stack([np.asarray(R[c]["rg_conv_p"]) for c in range(8)], 1).reshape(2, 8, 3, D)
    ffn_conv_p = np.stack([np.asarray(R[c]["ffn_conv_p"]) for c in range(8)], 1).reshape(4, 8, 2, FF2)
    s5_re_s = cat("s5_re_s", 1).reshape(2, 128, 64, 64)
    s5_im_s = cat("s5_im_s", 1).reshape(2, 128, 64, 64)
    rg_h_s = cat("rg_h_s", 1).reshape(2, 128, D)
    rg_conv_s = cat("rg_conv_s", 1).reshape(2, 128, 3, D)
    ffn_conv_s = cat("ffn_conv_s", 1).reshape(4, 128, 2, FF2)
    outs = (y_prompt, y_sample, s5_re_p, s5_im_p, rg_h_p, rg_conv_p, ffn_conv_p,
            s5_re_s, s5_im_s, rg_h_s, rg_conv_s, ffn_conv_s)
    return tuple(np.ascontiguousarray(o, dtype=np.float32) for o in outs)
```

```python
import math
from contextlib import ExitStack
import numpy as np
import concourse.bass as bass
import concourse.mybir as mybir
from concourse.bass_utils import run_bass_kernel_spmd

F32 = mybir.dt.float32
BF16 = mybir.dt.bfloat16
I32 = mybir.dt.int32
AF = mybir.ActivationFunctionType
ALU = mybir.AluOpType

D = 1024
NB = 8
FF = 2816
FF2 = 5632
FB = 44
FBH = 22
DEPTH = 4
SEQ = 2048
NMETA = 16
DN_ALPHA = (2 * DEPTH) ** 0.25
LN_EPS = 1e-5
RG_C = 8.0
TWO_PI = 2.0 * math.pi
ATTACH_WAITS = True
STATS = {}


class Res:
    __slots__ = ("name", "w", "r")

    def __init__(self, name=""):
        self.name = name
        self.w = None
        self.r = {}


class DSem:
    def __init__(self, sem):
        self.sem = sem
        self.val = 0


class Q:
    def __init__(self, name, eng, sem, eager):
        self.name = name
        self.eng = eng
        self.sem = sem
        self.eager = eager
        self.n = 0
        self.last = None
        self.last_ms = True
        self.ms = []
        self.semval = 0
        self.known = {}


class KB:
    def __init__(self, nc, es):
        self.nc = nc
        self.es = es
        self.q = {}
        for name, eng, eager in (("pe", nc.tensor, False), ("act", nc.scalar, False),
                                 ("dve", nc.vector, False), ("pool", nc.gpsimd, True),
                                 ("sp", nc.sync, True)):
            self.q[name] = Q(name, eng, self.sem("q_" + name), eager)
        self.dsems = []
        self.nsem = 0

    def sem(self, name):
        return self.es.enter_context(self.nc.semaphore(name))

    def dsem(self, name):
        d = DSem(self.sem("d_" + name))
        self.dsems.append(d)
        return d

    def sb(self, name, shape, dt, es=None):
        return (es or self.es).enter_context(self.nc.sbuf_tensor("sb_" + name, list(shape), dt))

    def ps(self, name, shape, dt=F32, es=None):
        return (es or self.es).enter_context(self.nc.psum_tensor("pt_" + name, list(shape), dt))

    def _milestone(self, A, k):
        lo, hi = 0, len(A.ms)
        while lo < hi:
            mid = (lo + hi) // 2
            if A.ms[mid][0] >= k:
                hi = mid
            else:
                lo = mid + 1
        if lo < len(A.ms):
            return A.ms[lo][1]
        assert A.last is not None and not A.last_ms and A.n >= k
        A.semval += 1
        A.last.then_inc(A.sem, 1)
        A.last_ms = True
        A.ms.append((A.n, A.semval))
        return A.semval

    def _wait(self, q, deps):
        need = {}
        for d in deps:
            if d is None:
                continue
            if d[0] == "q":
                A, k = d[1], d[2]
                if A is q and q.name == "pe":
                    continue
                v = self._milestone(A, k)
                key = A
                sem = A.sem
            else:
                key = d[1]
                sem = d[1].sem
                v = d[2]
            if q.known.get(key, 0) >= v:
                continue
            if need.get(key, (None, 0))[1] < v:
                need[key] = (sem, v)
        items = list(need.items())
        attach = None
        if ATTACH_WAITS and items:
            key, (sem, v) = items.pop()
            q.known[key] = v
            attach = (sem, v)
        for key, (sem, v) in items:
            q.eng.wait_ge(sem, v)
            q.known[key] = v
            STATS[q.name] = STATS.get(q.name, 0) + 1
        if attach:
            STATS["att_" + q.name] = STATS.get("att_" + q.name, 0) + 1
        return attach

    def _deps(self, reads, writes):
        deps = []
        for r in reads:
            if r.w is not None:
                deps.append(r.w)
        for w in writes:
            if w.w is not None:
                deps.append(w.w)
            deps.extend(w.r.values())
        return deps

    def op(self, qn, reads, writes, fn):
        q = self.q[qn]
        att = self._wait(q, self._deps(reads, writes))
        ins = fn(q.eng)
        if att is not None:
            ins._wait_ge(att[0], att[1])
        q.n += 1
        q.last = ins
        q.last_ms = False
        if q.eager:
            q.semval += 1
            ins.then_inc(q.sem, 1)
            q.last_ms = True
            q.ms.append((q.n, q.semval))
        me = ("q", q, q.n)
        for r in reads:
            r.r[q] = me
        for w in writes:
            w.w = me
            w.r = {}
        return ins

    def dma(self, qn, out, in_, reads, writes, ds, **kw):
        q = self.q[qn]
        att = self._wait(q, self._deps(reads, writes))
        ins = q.eng.dma_start(out=out, in_=in_, **kw)
        if att is not None:
            ins._wait_ge(att[0], att[1])
        ds.val += 16
        ins.then_inc(ds.sem, 16)
        me = ("d", ds, ds.val)
        for r in reads:
            r.r[ds] = me
        for w in writes:
            w.w = me
            w.r = {}
        return ins

    def group_final(self, ress, ds):
        for r in ress:
            r.w = ("d", ds, ds.val)

    def barrier(self, exclude=(), full=False):
        marks = []
        for A in self.q.values():
            if A.n > 0:
                marks.append(("q", A, A.n))
        for d in self.dsems:
            if d in exclude:
                continue
            if d.val > 0:
                marks.append(("d", d, d.val))
        global ATTACH_WAITS
        sv = ATTACH_WAITS
        ATTACH_WAITS = False
        for q in self.q.values():
            if q.name == "pe" and not full:
                continue
            self._wait(q, [m for m in marks if not (m[0] == "q" and m[1] is q)])
        ATTACH_WAITS = sv

    def finish(self):
        self.barrier(full=True)


def build(nc, dbg=None):
    es = ExitStack()
    kb = KB(nc, es)
    with es:
        _build(nc, kb, dbg)
    return nc


def _dram_in(nc, name, shape, dt=F32):
    return nc.dram_tensor(name, list(shape), dt, kind="ExternalInput").ap()


def _dram_out(nc, name, shape, dt=F32):
    return nc.dram_tensor(name, list(shape), dt, kind="ExternalOutput").ap()


def _build(nc, kb, dbg):
    op, dma = kb.op, kb.dma
    I = {}
    I["x_prompt"] = _dram_in(nc, "x_prompt", [SEQ, D])
    I["x_sample"] = _dram_in(nc, "x_sample", [128, D])
    I["state_s5_re"] = _dram_in(nc, "state_s5_re", [2, 16, 4096])
    I["state_s5_im"] = _dram_in(nc, "state_s5_im", [2, 16, 4096])
    I["state_rg_h"] = _dram_in(nc, "state_rg_h", [2, 16, D])
    I["state_rg_conv"] = _dram_in(nc, "state_rg_conv", [2, 48, D])
    I["state_ffn_conv"] = _dram_in(nc, "state_ffn_conv", [4, 32, FF2])
    I["meta_tokens"] = _dram_in(nc, "meta_tokens", [NMETA, D])
    I["s5_w_in"] = _dram_in(nc, "s5_w_in", [2, D, D])
    I["s5_lam_re"] = _dram_in(nc, "s5_lam_re", [2, 64, 64])
    I["s5_lam_im"] = _dram_in(nc, "s5_lam_im", [2, 64, 64])
    I["s5_log_step"] = _dram_in(nc, "s5_log_step", [2, 64])
    I["s5_b_re"] = _dram_in(nc, "s5_b_re", [2, 64, 64, 16])
    I["s5_b_im"] = _dram_in(nc, "s5_b_im", [2, 64, 64, 16])
    I["s5_c_re"] = _dram_in(nc, "s5_c_re", [2, 64, 16, 64])
    I["s5_c_im"] = _dram_in(nc, "s5_c_im", [2, 64, 16, 64])
    I["s5_d"] = _dram_in(nc, "s5_d", [2, D])
    I["s5_w_out"] = _dram_in(nc, "s5_w_out", [2, D, 2 * D])
    I["rg_w_in"] = _dram_in(nc, "rg_w_in", [2, D, 2 * D])
    I["rg_conv_w"] = _dram_in(nc, "rg_conv_w", [2, 4, D])
    I["rg_conv_b"] = _dram_in(nc, "rg_conv_b", [2, D])
    I["rg_w_gates"] = _dram_in(nc, "rg_w_gates", [2, 4, 256, 512])
    I["rg_b_gates"] = _dram_in(nc, "rg_b_gates", [2, 2 * D])
    I["rg_lam"] = _dram_in(nc, "rg_lam", [2, D])
    I["rg_w_out"] = _dram_in(nc, "rg_w_out", [2, D, D])
    I["ffn_w_up"] = _dram_in(nc, "ffn_w_up", [4, D, FF2])
    I["ffn_conv_w"] = _dram_in(nc, "ffn_conv_w", [4, 3, FF2])
    I["ffn_conv_b"] = _dram_in(nc, "ffn_conv_b", [4, FF2])
    I["ffn_w_down"] = _dram_in(nc, "ffn_w_down", [4, FF, D])
    I["ln_g"] = _dram_in(nc, "ln_g", [4, 2, D])
    I["ln_b"] = _dram_in(nc, "ln_b", [4, 2, D])
    I["ident"] = _dram_in(nc, "ident", [128, 128])
    I["bmask"] = _dram_in(nc, "bmask", [128, 128])
    I["sel2"] = _dram_in(nc, "sel2", [2, 128])

    O = {}
    O["y_prompt"] = _dram_out(nc, "y_prompt", [SEQ, D])
    O["y_sample"] = _dram_out(nc, "y_sample", [128, D])
    O["s5_re_p"] = _dram_out(nc, "s5_re_p", [2, 1, 4096])
    O["s5_im_p"] = _dram_out(nc, "s5_im_p", [2, 1, 4096])
    O["rg_h_p"] = _dram_out(nc, "rg_h_p", [2, 1, D])
    O["rg_conv_p"] = _dram_out(nc, "rg_conv_p", [2, 3, D])
    O["ffn_conv_p"] = _dram_out(nc, "ffn_conv_p", [4, 2, FF2])
    O["s5_re_s"] = _dram_out(nc, "s5_re_s", [2, 16, 4096])
    O["s5_im_s"] = _dram_out(nc, "s5_im_s", [2, 16, 4096])
    O["rg_h_s"] = _dram_out(nc, "rg_h_s", [2, 16, D])
    O["rg_conv_s"] = _dram_out(nc, "rg_conv_s", [2, 48, D])
    O["ffn_conv_s"] = _dram_out(nc, "ffn_conv_s", [4, 32, FF2])

    WXd = [nc.dram_tensor("WXd%d" % j, [128, 8, 8, 2, 128], BF16, kind="Internal").ap() for j in range(2)]
    WYd = [nc.dram_tensor("WYd%d" % j, [128, 32, 8, 2, 32], BF16, kind="Internal").ap() for j in range(2)]
    KId = [nc.dram_tensor("KId%d" % j, [128, 8, 8, 128], BF16, kind="Internal").ap() for j in range(2)]

    NBLK = 2 * NB + 2 * 2 * NB + 4 * FB
    ctx_w = dict(
        WBLK=nc.dram_tensor("WBLK", [NBLK, 128, NB * 128], BF16, kind="Internal").ap(),
        WDd=[nc.dram_tensor("WDd%d" % l, [128, FBH * D], BF16, kind="Internal").ap() for l in range(4)],
        S5WOd=[nc.dram_tensor("S5WOd%d" % j, [128, NB * 2 * D], BF16, kind="Internal").ap() for j in range(2)],
        RGWOd=[nc.dram_tensor("RGWOd%d" % j, [128, NB * D], BF16, kind="Internal").ap() for j in range(2)],
        RGWGd=[nc.dram_tensor("RGWGd%d" % j, [128, 4 * 2 * 512], BF16, kind="Internal").ap() for j in range(2)],
    )
    sb = kb.sb
    ident = sb("ident", [128, 128], F32)
    bmask = sb("bmask", [128, 128], F32)
    R_const = Res("const")
    ds_c = kb.dsem("const")
    dma("sp", ident[:], I["ident"][:, :], [], [R_const], ds_c)
    dma("sp", bmask[:], I["bmask"][:, :], [], [R_const], ds_c)
    sel2 = sb("sel2", [2, 128], F32)
    dma("sp", sel2[:], I["sel2"][:, :], [], [R_const], ds_c)

    R_par = Res("par")
    ds_p = kb.dsem("par")

    par_loads = []

    def load_cols(name, src_1d, nblk, ds):
        t = sb(name, [128, nblk], F32)
        par_loads.append((t, src_1d))
        return t

    def issue_par_loads():
        with nc.allow_non_contiguous_dma(reason="small param"):
            for t, src_1d in par_loads:
                dma("sp", t[:], src_1d.rearrange("(b p) -> p b", p=128), [], [R_par], ds_p)

    ffn_cw = [[load_cols("fcw%d_%d" % (l, k), I["ffn_conv_w"][l, k], FB, ds_c) for k in range(3)] for l in range(4)]
    ffn_cb = [load_cols("fcb%d" % l, I["ffn_conv_b"][l], FB, ds_c) for l in range(4)]
    rg_cw = [[load_cols("rcw%d_%d" % (j, k), I["rg_conv_w"][j, k], NB, ds_c) for k in range(4)] for j in range(2)]
    rg_cb = [load_cols("rcb%d" % j, I["rg_conv_b"][j], NB, ds_c) for j in range(2)]
    rg_bg = [load_cols("rbg%d" % j, I["rg_b_gates"][j], 16, ds_c) for j in range(2)]
    rg_lm = [load_cols("rlm%d" % j, I["rg_lam"][j], NB, ds_c) for j in range(2)]
    s5_dd = [sb("s5d%d" % j, [128, NB], F32) for j in range(2)]
    with nc.allow_non_contiguous_dma(reason="small param"):
        for j in range(2):
            dma("sp", s5_dd[j][:], I["s5_d"][j].rearrange("(b p) -> p b", p=128), [], [R_const], ds_c)
    rg_m8sp = [sb("m8sp%d" % j, [128, NB], F32) for j in range(2)]
    A8re = [sb("A8re%d" % j, [128, 32], F32) for j in range(2)]
    A8im = [sb("A8im%d" % j, [128, 32], F32) for j in range(2)]
    A8imn = [sb("A8imn%d" % j, [128, 32], F32) for j in range(2)]
    ffn_hist_p = [sb("fhp%d" % l, [128, FB, 1, 2], F32) for l in range(4)]
    rg_hist_p = [sb("rhp%d" % j, [128, NB, 1, 3], F32) for j in range(2)]
    rg_h_p = [sb("rgh%d" % j, [128, NB, 1], F32) for j in range(2)]
    s5st_p = [sb("s5p%d" % j, [128, 32, 2, 1], F32) for j in range(2)]
    R_state = Res("state")

    PS = [kb.ps("ps%d" % i, [128, 512]) for i in range(8)]
    RP = [Res("ps%d" % i) for i in range(8)]

    for j in range(2):
        pass
    for t in ffn_hist_p + rg_hist_p + rg_h_p + s5st_p:
        op("dve", [], [R_state], lambda e, t=t: e.memset(t[:], 0.0))

    ctx = dict(nc=nc, kb=kb, I=I, O=O, sel2=sel2, R_par=R_par, WXd=WXd, WYd=WYd, KId=KId, ident=ident, bmask=bmask,
               R_const=R_const, PS=PS, RP=RP, ffn_cw=ffn_cw, ffn_cb=ffn_cb, rg_cw=rg_cw, rg_cb=rg_cb,
               rg_bg=rg_bg, rg_m8sp=rg_m8sp, s5_dd=s5_dd, A8re=A8re, A8im=A8im, A8imn=A8imn,
               ffn_hist_p=ffn_hist_p, rg_hist_p=rg_hist_p, rg_h_p=rg_h_p, s5st_p=s5st_p,
               R_state=R_state, dbg=dbg, ds_out=kb.dsem("out"), ds_w=None, ds_wb=kb.dsem("wb"), **ctx_w)

    with ExitStack() as lds:
        with ExitStack() as nat:
            gens = [s5_prologue_loads(ctx, j, lds, nat) for j in range(2)]
            for g in gens:
                next(g)
            lds_ = [next(g) for g in gens]
            kb.barrier()
        for j in range(2):
            with ExitStack() as pes:
                ctx["issue_par_loads"] = issue_par_loads
                s5_prologue(ctx, j, pes, lds_[j])
                kb.barrier()
    kb.group_final([R_par], ds_p)
    for j in range(2):
        op("act", [R_par], [R_par], lambda e, j=j: e.activation(out=rg_m8sp[j][:], in_=rg_lm[j][:], func=AF.Exp, scale=-1.0))
        op("act", [R_par], [R_par], lambda e, j=j: e.activation(out=rg_m8sp[j][:], in_=rg_m8sp[j][:], func=AF.Ln, bias=1.0))
        op("dve", [R_par], [R_par], lambda e, j=j: e.tensor_scalar(out=rg_m8sp[j][:], in0=rg_m8sp[j][:], scalar1=-RG_C, scalar2=None, op0=ALU.mult))
    kb.barrier()
    passes = [("A", 1, 1024), ("B", 1, 1040), ("S", 16, 8)]
    for (pn, nseq, L) in passes:
        with ExitStack() as pes:
            run_pass(ctx, pn, nseq, L, pes)
            kb.barrier()
    kb.finish()


def colgroups(N, g=512):
    return [(c0, min(g, N - c0)) for c0 in range(0, N, g)]


def store_T(ctx, src_fn, nblk, m, dram_rows, stage, R_src, tag):
    kb, PS, RP, ident = ctx["kb"], ctx["PS"], ctx["RP"], ctx["ident"]
    R_stage = ctx.setdefault("stage_res", {}).setdefault(id(stage), Res("stage" + tag))
    GB = stage.shape[1] // 128
    for g0 in range(0, nblk, GB):
        ng = min(GB, nblk - g0)
        for b0 in range(g0, g0 + ng, 4):
            nb_ = min(4, g0 + ng - b0)
            bank = 6 + ((b0 // 4) % 2)
            for b in range(nb_):
                kb.op("pe", [R_src, ctx["R_const"]], [RP[bank]],
                      lambda e, b=b, b0=b0, bank=bank: e.transpose(out=PS[bank][0:m, b * 128:(b + 1) * 128], in_=src_fn(b0 + b), identity=ident[:, :]))
            kb.op("act", [RP[bank]], [R_stage],
                  lambda e, b0=b0, nb_=nb_, bank=bank, g0=g0: e.copy(out=stage[0:m, (b0 - g0) * 128:(b0 - g0 + nb_) * 128], in_=PS[bank][0:m, 0:nb_ * 128]))
        kb.dma("sp", dram_rows[:, g0 * 128:(g0 + ng) * 128], stage[0:m, 0:ng * 128], [R_stage], [], ctx["ds_out"])


def load_T(ctx, dram_rows, nblk, m, dst_fn, stage, R_dst, tag, ds):
    kb, PS, RP, ident = ctx["kb"], ctx["PS"], ctx["RP"], ctx["ident"]
    R_stage = ctx.setdefault("stage_res", {}).setdefault(id(stage), Res("lstage" + tag))
    ctx["uid"] = ctx.get("uid", 0) + 1
    ds = kb.dsem("lt%d" % ctx["uid"])
    kb.dma("sp", stage[0:m, 0:nblk * 128], dram_rows, [], [R_stage], ds)
    per = 512 // m
    gi = 0
    for b0 in range(0, nblk, per):
        nb_ = min(per, nblk - b0)
        bank = 6 + (gi % 2)
        gi += 1
        for b in range(nb_):
            kb.op("pe", [R_stage, ctx["R_const"]], [RP[bank]],
                  lambda e, b=b: e.transpose(out=PS[bank][:, b * m:(b + 1) * m], in_=stage[0:m, (b0 + b) * 128:(b0 + b + 1) * 128], identity=ident[0:m, 0:m]))
        kb.op("act", [RP[bank]], [R_dst],
              lambda e: e.copy(out=dst_fn(b0, nb_), in_=PS[bank][:, 0:nb_ * m].rearrange("p (b m) -> p b m", m=m)))


def s5_prologue_loads(ctx, j, pes, nat):
    nc, kb, I = ctx["nc"], ctx["kb"], ctx["I"]
    dma, op = kb.dma, kb.op
    PS, RP, ident, Rc = ctx["PS"], ctx["RP"], ctx["ident"], ctx["R_const"]
    sb = lambda n, s, d=F32: kb.sb("pl%d_%s" % (j, n), s, d, es=pes)
    ds = kb.dsem("pl%d" % j)
    R = Res("pl")
    Rn = Res("plnat")
    lr = sb("lr", [128, 32]); li = sb("li", [128, 32]); stp = sb("stp", [128, 32])
    Bre = sb("Bre", [128, 32, 16]); Bim = sb("Bim", [128, 32, 16])
    yield None
    sbn = lambda n, s, d=F32: kb.sb("pl%d_%s" % (j, n), s, d, es=nat)
    lrn = sbn("lrn", [32, 128]); lin = sbn("lin", [32, 128]); stn = sbn("stn", [2, 32])
    Bn = [sbn("Bn%d" % i, [32, 128, 16]) for i in range(2)]
    dma("sp", lrn[:], I["s5_lam_re"][j].rearrange("(pr g2) p -> pr (g2 p)", g2=2), [], [Rn], ds)
    dma("sp", lin[:], I["s5_lam_im"][j].rearrange("(pr g2) p -> pr (g2 p)", g2=2), [], [Rn], ds)
    with nc.allow_non_contiguous_dma(reason="tiny"):
        dma("sp", stn[:], I["s5_log_step"][j].rearrange("(pr g2) -> g2 pr", g2=2), [], [Rn], ds)
    dma("sp", Bn[0][:], I["s5_b_re"][j].rearrange("(pr g2) p h -> pr (g2 p) h", g2=2), [], [Rn], ds)
    dma("sp", Bn[1][:], I["s5_b_im"][j].rearrange("(pr g2) p h -> pr (g2 p) h", g2=2), [], [Rn], ds)
    kb.group_final([Rn], ds)
    bank = 4 + j
    op("pe", [Rn, Rc], [RP[bank]], lambda e: e.transpose(out=PS[bank][:, 0:32], in_=lrn[:, :], identity=ident[0:32, 0:32]))
    op("pe", [Rn, Rc], [RP[bank]], lambda e: e.transpose(out=PS[bank][:, 32:64], in_=lin[:, :], identity=ident[0:32, 0:32]))
    op("pe", [Rn, Rc], [RP[bank]], lambda e: e.matmul(PS[bank][:, 64:96], lhsT=ctx["sel2"][:, :], rhs=stn[:, :], start=True, stop=True))
    op("act", [RP[bank]], [R], lambda e: e.copy(out=lr[:], in_=PS[bank][:, 0:32]))
    op("act", [RP[bank]], [R], lambda e: e.copy(out=li[:], in_=PS[bank][:, 32:64]))
    op("act", [RP[bank]], [R], lambda e: e.copy(out=stp[:], in_=PS[bank][:, 64:96]))
    for i, Bd in enumerate((Bre, Bim)):
        for h in range(16):
            op("pe", [Rn, Rc], [RP[bank]], lambda e, h=h, i=i: e.transpose(out=PS[bank][:, h * 32:(h + 1) * 32], in_=Bn[i][:, :, h], identity=ident[0:32, 0:32]))
        op("act", [RP[bank]], [R], lambda e, Bd=Bd: e.copy(out=Bd[:, :, :].rearrange("p r h -> p h r"), in_=PS[bank][:, :].rearrange("p (h r) -> p h r", r=32)))
    yield dict(R=R, ds=ds, lr=lr, li=li, stp=stp, Bre=Bre, Bim=Bim)


def s5_prologue(ctx, j, pes, ld):
    nc, kb, I = ctx["nc"], ctx["kb"], ctx["I"]
    op, dma = kb.op, kb.dma
    PS, RP, ident, bmask = ctx["PS"], ctx["RP"], ctx["ident"], ctx["bmask"]
    Rc = ctx["R_const"]
    sb = lambda n, s, d=F32: kb.sb("pl%d_%s" % (j, n), s, d, es=pes)
    R, ds = ld["R"], ld["ds"]
    lr, li, stp, Bre, Bim = (ld[k] for k in ("lr", "li", "stp", "Bre", "Bim"))
    Cre = sb("Cre", [128, 32, 16]); Cim = sb("Cim", [128, 32, 16])
    Ch_re = sb("Chre", [16, 64, 64]); Ch_im = sb("Chim", [16, 64, 64])
    R_ch = Res("ch")
    ds_ch = kb.dsem("plc%d" % j)
    with nc.allow_non_contiguous_dma(reason="param layout"):
        dma("sp", Ch_re[:], I["s5_c_re"][j].rearrange("g h p -> h g p"), [], [R_ch], ds_ch)
        dma("sp", Ch_im[:], I["s5_c_im"][j].rearrange("g h p -> h g p"), [], [R_ch], ds_ch)
    kb.group_final([R_ch], ds_ch)
    if j == 1:
        ctx["issue_par_loads"]()
    for (Ch, Cd) in ((Ch_re, Cre), (Ch_im, Cim)):
        for g0 in range(0, 32, 16):
            bank = 4 + (g0 // 16)
            for pr in range(g0, g0 + 16):
                op("pe", [R_ch, Rc], [RP[bank]],
                   lambda e, pr=pr, Ch=Ch: e.transpose(out=PS[bank][:, (pr - g0) * 16:(pr - g0 + 1) * 16],
                                                       in_=Ch[0:16, 2 * pr:2 * pr + 2, :].rearrange("h g p -> h (g p)"),
                                                       identity=ident[0:16, 0:16]))
            op("act", [RP[bank]], [R], lambda e, Cd=Cd: e.copy(out=Cd[:, g0:g0 + 16, :], in_=PS[bank][:, 0:256].rearrange("p (a h) -> p a h", h=16)))

    t = lambda n: sb(n, [128, 32])
    ang = t("ang"); mag = t("mag"); sn = t("sn"); cs = t("cs"); tmp = t("tmp"); tmp2 = t("tmp2")
    abre = t("abre"); abim = t("abim"); qre = t("qre"); qim = t("qim"); ki = sb("ki", [128, 32], I32)
    V = lambda f: op("dve", [R], [R], f)
    A = lambda f: op("act", [R], [R], f)
    A(lambda e: e.activation(out=stp[:], in_=stp[:], func=AF.Exp))
    V(lambda e: e.tensor_tensor(out=ang[:], in0=li[:], in1=stp[:], op=ALU.mult))
    V(lambda e: e.tensor_tensor(out=mag[:], in0=lr[:], in1=stp[:], op=ALU.mult))
    A(lambda e: e.activation(out=mag[:], in_=mag[:], func=AF.Exp))

    def sin_of(dst, shift):
        V(lambda e: e.tensor_scalar(out=tmp[:], in0=ang[:], scalar1=shift, scalar2=1.0 / TWO_PI, op0=ALU.add, op1=ALU.mult))
        V(lambda e: e.tensor_copy(out=ki[:], in_=tmp[:]))
        V(lambda e: e.tensor_copy(out=tmp2[:], in_=ki[:]))
        V(lambda e: e.tensor_tensor(out=tmp[:], in0=tmp[:], in1=tmp2[:], op=ALU.subtract))
        V(lambda e: e.tensor_scalar(out=tmp2[:], in0=tmp[:], scalar1=0.5, scalar2=None, op0=ALU.is_gt))
        V(lambda e: e.tensor_tensor(out=tmp[:], in0=tmp[:], in1=tmp2[:], op=ALU.subtract))
        V(lambda e: e.tensor_scalar(out=tmp2[:], in0=tmp[:], scalar1=-0.5, scalar2=None, op0=ALU.is_lt))
        V(lambda e: e.tensor_tensor(out=tmp[:], in0=tmp[:], in1=tmp2[:], op=ALU.add))
        V(lambda e: e.tensor_scalar(out=tmp[:], in0=tmp[:], scalar1=TWO_PI, scalar2=math.pi, op0=ALU.mult, op1=ALU.min))
        V(lambda e: e.tensor_scalar(out=tmp[:], in0=tmp[:], scalar1=-math.pi, scalar2=None, op0=ALU.max))
        A(lambda e: e.activation(out=dst[:], in_=tmp[:], func=AF.Sin))

    sin_of(sn, 0.0)
    sin_of(cs, math.pi / 2)
    V(lambda e: e.tensor_tensor(out=abre[:], in0=mag[:], in1=cs[:], op=ALU.mult))
    V(lambda e: e.tensor_tensor(out=abim[:], in0=mag[:], in1=sn[:], op=ALU.mult))
    den = t("den"); nr = t("nr")
    V(lambda e: e.tensor_tensor(out=den[:], in0=lr[:], in1=lr[:], op=ALU.mult))
    V(lambda e: e.tensor_tensor(out=tmp[:], in0=li[:], in1=li[:], op=ALU.mult))
    V(lambda e: e.tensor_tensor(out=den[:], in0=den[:], in1=tmp[:], op=ALU.add))
    V(lambda e: e.reciprocal(out=den[:], in_=den[:]))
    V(lambda e: e.tensor_scalar(out=nr[:], in0=abre[:], scalar1=-1.0, scalar2=None, op0=ALU.add))
    V(lambda e: e.tensor_tensor(out=qre[:], in0=nr[:], in1=lr[:], op=ALU.mult))
    V(lambda e: e.tensor_tensor(out=tmp[:], in0=abim[:], in1=li[:], op=ALU.mult))
    V(lambda e: e.tensor_tensor(out=qre[:], in0=qre[:], in1=tmp[:], op=ALU.add))
    V(lambda e: e.tensor_tensor(out=qre[:], in0=qre[:], in1=den[:], op=ALU.mult))
    V(lambda e: e.tensor_tensor(out=qim[:], in0=abim[:], in1=lr[:], op=ALU.mult))
    V(lambda e: e.tensor_tensor(out=tmp[:], in0=nr[:], in1=li[:], op=ALU.mult))
    V(lambda e: e.tensor_tensor(out=qim[:], in0=qim[:], in1=tmp[:], op=ALU.subtract))
    V(lambda e: e.tensor_tensor(out=qim[:], in0=qim[:], in1=den[:], op=ALU.mult))
    pwr = sb("pwr", [128, 9, 32]); pwi = sb("pwi", [128, 9, 32])
    V(lambda e: e.memset(pwr[:, 0, :], 1.0))
    V(lambda e: e.memset(pwi[:, 0, :], 0.0))
    for k in range(1, 9):
        V(lambda e, k=k: e.tensor_tensor(out=pwr[:, k, :], in0=pwr[:, k - 1, :], in1=abre[:], op=ALU.mult))
        V(lambda e, k=k: e.tensor_tensor(out=tmp[:], in0=pwi[:, k - 1, :], in1=abim[:], op=ALU.mult))
        V(lambda e, k=k: e.tensor_tensor(out=pwr[:, k, :], in0=pwr[:, k, :], in1=tmp[:], op=ALU.subtract))
        V(lambda e, k=k: e.tensor_tensor(out=pwi[:, k, :], in0=pwr[:, k - 1, :], in1=abim[:], op=ALU.mult))
        V(lambda e, k=k: e.tensor_tensor(out=tmp[:], in0=pwi[:, k - 1, :], in1=abre[:], op=ALU.mult))
        V(lambda e, k=k: e.tensor_tensor(out=pwi[:, k, :], in0=pwi[:, k, :], in1=tmp[:], op=ALU.add))
    Rst = ctx["R_state"]
    op("dve", [R], [Rst], lambda e: e.tensor_copy(out=ctx["A8re"][j][:], in_=pwr[:, 8, :]))
    op("dve", [R], [Rst], lambda e: e.tensor_copy(out=ctx["A8im"][j][:], in_=pwi[:, 8, :]))
    op("dve", [R], [Rst], lambda e: e.tensor_scalar(out=ctx["A8imn"][j][:], in0=pwi[:, 8, :], scalar1=-1.0, scalar2=None, op0=ALU.mult))

    def bc(x2d):
        return x2d.unsqueeze(2).broadcast_to([128, 32, 16])

    T3 = lambda n: sb(n, [128, 32, 16])
    w1 = T3("w1"); w2 = T3("w2")

    def cmul(dre, dim, xre, xim, sre, sim_):
        V(lambda e: e.tensor_tensor(out=w1[:], in0=xre, in1=bc(sre), op=ALU.mult))
        V(lambda e: e.tensor_tensor(out=w2[:], in0=xim, in1=bc(sim_), op=ALU.mult))
        V(lambda e: e.tensor_tensor(out=dre, in0=w1[:], in1=w2[:], op=ALU.subtract))
        V(lambda e: e.tensor_tensor(out=w1[:], in0=xre, in1=bc(sim_), op=ALU.mult))
        V(lambda e: e.tensor_tensor(out=w2[:], in0=xim, in1=bc(sre), op=ALU.mult))
        V(lambda e: e.tensor_tensor(out=dim, in0=w1[:], in1=w2[:], op=ALU.add))

    Bbre = T3("Bbre"); Bbim = T3("Bbim")
    cmul(Bbre[:], Bbim[:], Bre[:], Bim[:], qre[:], qim[:])
    Bpre = sb("Bpre", [128, 32, 32]); Bpim = sb("Bpim", [128, 32, 32])
    V(lambda e: e.memset(Bpre[:], 0.0))
    V(lambda e: e.memset(Bpim[:], 0.0))
    for g2 in range(2):
        hp = slice(g2 * 64, (g2 + 1) * 64); hc = slice(g2 * 16, (g2 + 1) * 16)
        V(lambda e, hp=hp, hc=hc: e.tensor_copy(out=Bpre[hp, :, hc], in_=Bbre[hp, :, :]))
        V(lambda e, hp=hp, hc=hc: e.tensor_copy(out=Bpim[hp, :, hc], in_=Bbim[hp, :, :]))

    WY = sb("WY", [128, 32, 8, 2, 32], BF16)
    KI = sb("KI", [128, 8, 8, 128], BF16)
    WX = sb("WX", [128, 8, 8, 2, 128], BF16)
    V(lambda e: e.memset(WY[:].rearrange("p a b c d -> p (a b c d)"), 0.0))
    CAre = T3("CAre"); CAim = T3("CAim")
    CApre = sb("CApre", [128, 32, 32]); CApimn = sb("CApimn", [128, 32, 32])
    V(lambda e: e.memset(CApre[:], 0.0))
    V(lambda e: e.memset(CApimn[:], 0.0))
    kit = sb("kit", [128, 128])
    dcol = ctx["s5_dd"][j]
    R_o = Res("pl_out")
    R_kit = Res("kit")
    for k in range(9):
        cmul(CAre[:], CAim[:], Cre[:], Cim[:], pwr[:, k, :], pwi[:, k, :])
        for g2 in range(2):
            hp = slice(g2 * 64, (g2 + 1) * 64); hc = slice(g2 * 16, (g2 + 1) * 16)
            V(lambda e, hp=hp, hc=hc: e.tensor_copy(out=CApre[hp, :, hc], in_=CAre[hp, :, :]))
            V(lambda e, hp=hp, hc=hc: e.tensor_scalar(out=CApimn[hp, :, hc], in0=CAim[hp, :, :], scalar1=-1.0, scalar2=None, op0=ALU.mult))
            if k >= 1:
                V(lambda e, hp=hp, hc=hc, k=k: e.tensor_copy(out=WY[hp, :, k - 1, 0, hc], in_=CAre[hp, :, :]))
                V(lambda e, hp=hp, hc=hc, k=k: e.tensor_scalar(out=WY[hp, :, k - 1, 1, hc], in0=CAim[hp, :, :], scalar1=-1.0, scalar2=None, op0=ALU.mult))
        if k <= 7:
            for c in range(8):
                bank = 4 + (c % 2)
                cs4 = slice(4 * c, 4 * c + 4)
                op("pe", [R], [RP[bank]], lambda e, cs4=cs4: e.matmul(PS[bank][:, 0:128], lhsT=Bpre[:, cs4, :].rearrange("p a b -> p (a b)"),
                                                                 rhs=CApre[:, cs4, :].rearrange("p a b -> p (a b)"), start=True, stop=False))
                op("pe", [R], [RP[bank]], lambda e, cs4=cs4: e.matmul(PS[bank][:, 0:128], lhsT=Bpim[:, cs4, :].rearrange("p a b -> p (a b)"),
                                                                 rhs=CApimn[:, cs4, :].rearrange("p a b -> p (a b)"), start=False, stop=True))
                if k == 0:
                    op("dve", [RP[bank], Rc], [R_kit], lambda e, bank=bank: e.tensor_tensor(out=kit[:], in0=PS[bank][:, 0:128], in1=bmask[:], op=ALU.mult))
                    op("dve", [R_kit, Rc, ctx["R_par"]], [R_o], lambda e, c=c: e.scalar_tensor_tensor(out=KI[:, c, 0, :], in0=ident[:], scalar=dcol[:, c:c + 1], in1=kit[:], op0=ALU.mult, op1=ALU.add))
                else:
                    op("dve", [RP[bank], Rc], [R_o], lambda e, c=c, k=k, bank=bank: e.tensor_tensor(out=KI[:, c, k, :], in0=PS[bank][:, 0:128], in1=bmask[:], op=ALU.mult))
    XBre = sb("XBre", [128, 32, 32]); XBim = sb("XBim", [128, 32, 32])
    x1 = sb("x1", [128, 32, 32]); x2 = sb("x2", [128, 32, 32])
    bc32 = lambda x2d: x2d.unsqueeze(2).broadcast_to([128, 32, 32])
    for tau in range(8):
        k = 7 - tau
        V(lambda e, k=k: e.tensor_tensor(out=x1[:], in0=Bpre[:], in1=bc32(pwr[:, k, :]), op=ALU.mult))
        V(lambda e, k=k: e.tensor_tensor(out=x2[:], in0=Bpim[:], in1=bc32(pwi[:, k, :]), op=ALU.mult))
        V(lambda e: e.tensor_tensor(out=XBre[:], in0=x1[:], in1=x2[:], op=ALU.subtract))
        V(lambda e, k=k: e.tensor_tensor(out=x1[:], in0=Bpre[:], in1=bc32(pwi[:, k, :]), op=ALU.mult))
        V(lambda e, k=k: e.tensor_tensor(out=x2[:], in0=Bpim[:], in1=bc32(pwr[:, k, :]), op=ALU.mult))
        V(lambda e: e.tensor_tensor(out=XBim[:], in0=x1[:], in1=x2[:], op=ALU.add))
        for ri, XB in enumerate((XBre, XBim)):
            for c in range(8):
                bank = 4 + (c % 2)
                op("pe", [R, Rc], [RP[bank]], lambda e, c=c, XB=XB: e.transpose(out=PS[bank][:, 0:128], in_=XB[:, 4 * c:4 * c + 4, :].rearrange("p a b -> p (a b)"), identity=ident[:, :]))
                op("act", [RP[bank]], [R_o], lambda e, c=c, ri=ri, tau=tau, bank=bank: e.copy(out=WX[:, c, tau, ri, :], in_=PS[bank][:, 0:128]))
    dma("sp", ctx["WXd"][j].rearrange("p a b c d -> p (a b c d)"), WX[:].rearrange("p a b c d -> p (a b c d)"), [R, R_o], [], ds)
    dma("sp", ctx["WYd"][j].rearrange("p a b c d -> p (a b c d)"), WY[:].rearrange("p a b c d -> p (a b c d)"), [R], [], ds)
    dma("sp", ctx["KId"][j].rearrange("p a b c -> p (a b c)"), KI[:].rearrange("p a b c -> p (a b c)"), [R, R_o], [], ds)


def run_pass(ctx, pn, nseq, L, pes):
    nc, kb, I, O = ctx["nc"], ctx["kb"], ctx["I"], ctx["O"]
    op, dma = kb.op, kb.dma
    PS, RP, ident = ctx["PS"], ctx["RP"], ctx["ident"]
    Rc = ctx["R_const"]
    N = nseq * L
    NT = (N + 127) // 128
    rows = [min(128, N - t * 128) for t in range(NT)]
    NC = N // 8
    sb = lambda n, s, d=F32, es=None: kb.sb("p%s_%s" % (pn, n), s, d, es=es or pes)

    h_tok = sb("htok", [128, NT, D])
    hT = sb("hT", [128, NB, N], BF16)
    R_h = [Res("h%d" % t) for t in range(NT)]
    R_hT = [Res("hT%d" % t) for t in range(NT)]
    ds_in = kb.dsem("in" + pn)
    ds_w = kb.dsem("w" + pn)
    ds_ln = kb.dsem("ln" + pn)
    gbc = sb("gbc", [128, D]); bbc = sb("bbc", [128, D])
    R_ln = Res("ln")
    NZ = 4
    zt = [sb("zt%d" % i, [128, D]) for i in range(NZ)]
    R_zt = [Res("zt%d" % i) for i in range(NZ)]
    stat = [sb("stat%d" % i, [128, 2, 6]) for i in range(NZ)]
    mv = [sb("mv%d" % i, [128, 2]) for i in range(NZ)]
    rstd = [sb("rstd%d" % i, [128, 1]) for i in range(NZ)]
    NWS = 8 if pn == "S" else 4
    wslot = [sb("wslot%d" % i, [128, NB, 128], BF16) for i in range(NWS)]
    R_ws = [Res("ws%d" % i) for i in range(NWS)]
    ws_i = [0]
    ds_ws = [kb.dsem("ws%s%d" % (pn, i)) for i in range(NWS)]

    if pn == "A":
        dma("sp", h_tok[0:16, 0, :], I["meta_tokens"][:, :], [], [R_h[0]], ds_in)
        dma("sp", h_tok[16:128, 0, :], I["x_prompt"][0:112, :], [], [R_h[0]], ds_in)
        for t in range(1, NT):
            dma("sp", h_tok[:, t, :], I["x_prompt"][112 + 128 * (t - 1):112 + 128 * t, :], [], [R_h[t]], ds_in)
    elif pn == "B":
        for t in range(NT):
            dma("sp", h_tok[0:rows[t], t, :], I["x_prompt"][1008 + 128 * t:1008 + 128 * t + rows[t], :], [], [R_h[t]], ds_in)
    else:
        dma("sp", h_tok[:, 0, :], I["x_sample"][:, :], [], [R_h[0]], ds_in)

    kb.group_final(R_h, ds_in)

    def to_hT(t):
        r = rows[t]
        for half in range(2):
            bank = 4 + half
            for b in range(4):
                blk = half * 4 + b
                op("pe", [R_h[t], Rc], [RP[bank]], lambda e, b=b, blk=blk: e.transpose(out=PS[bank][:, b * 128:b * 128 + r], in_=h_tok[0:r, t, blk * 128:(blk + 1) * 128], identity=ident[0:r, 0:r]))
            op("act", [RP[bank]], [R_hT[t]], lambda e, half=half: e.copy(out=hT[:, half * 4:half * 4 + 4, t * 128:t * 128 + r],
                                                                     in_=PS[bank][:, :].rearrange("p (b n) -> p b n", n=128)[:, :, 0:r]))

    for t in range(NT):
        to_hT(t)

    nlayers = ctx["dbg"].get("nlayers", DEPTH) if ctx["dbg"] else DEPTH
    glist = []
    for l_ in range(nlayers):
        j_ = l_ // 2
        if l_ % 2 == 0:
            glist += [(I["s5_w_in"][j_], c * 128) for c in range(NB)]
        else:
            for c in range(NB):
                glist += [(I["rg_w_in"][j_], c * 128), (I["rg_w_in"][j_], D + c * 128)]
        for v in range(FBH):
            glist += [(I["ffn_w_up"][l_], v * 128), (I["ffn_w_up"][l_], (v + FBH) * 128)]
    wplan = {"list": glist, "issued": 0, "hooks": {}, "off": 0, "next_off": 0}

    def plan_w(blocks, hooks=None):
        wplan["off"] = wplan["next_off"]
        wplan["next_off"] = wplan["off"] + len(blocks)
        for k, f in (hooks or {}).items():
            gi = wplan["off"] + k
            if gi < wplan["issued"]:
                f()
            else:
                wplan["hooks"][gi] = f

    def write_back(k):
        s = k % NWS
        dma("sp", ctx["WBLK"][k], wslot[s][:].rearrange("p a b -> p (a b)"), [R_ws[s]], [], ctx["ds_wb"])

    def load_w_block(i, pf=NWS - 1):
        gi = wplan["off"] + i
        while wplan["issued"] < min(len(wplan["list"]), gi + pf + 1):
            k = wplan["issued"]
            src2d, c0 = wplan["list"][k]
            s = k % NWS
            if pn == "A":
                dma("pool", wslot[s][:], src2d[:, c0:c0 + 128].rearrange("(kc p) f -> p kc f", p=128), [], [R_ws[s]], ds_ws[s])
                if k >= 2:
                    write_back(k - 2)
            else:
                dma("pool", wslot[s][:].rearrange("p a b -> p (a b)"), ctx["WBLK"][k], [], [R_ws[s]], ds_ws[s])
            wplan["issued"] += 1
            if k in wplan["hooks"]:
                wplan["hooks"].pop(k)()
        return gi % NWS

    def up_matmul(s, ps_banks, extra_reads):
        for gi, (c0, cn) in enumerate(colgroups(N)):
            bank = ps_banks[gi]
            for kc in range(NB):
                op("pe", [R_ws[s]] + R_hT + extra_reads, [RP[bank]],
                   lambda e, kc=kc, bank=bank, c0=c0, cn=cn: e.matmul(PS[bank][:, 0:cn], lhsT=wslot[s][:, kc, :], rhs=hT[:, kc, c0:c0 + cn], start=(kc == 0), stop=(kc == NB - 1)))

    def layer_norm_tile(t, zi, li_, k, last):
        r = rows[t]
        z = zt[zi]
        for hh in range(2):
            op("dve", [R_zt[zi]], [R_zt[zi]], lambda e, hh=hh: e.bn_stats(out=stat[zi][0:r, hh, :], in_=z[0:r, hh * 512:(hh + 1) * 512]))
        op("dve", [R_zt[zi]], [R_zt[zi]], lambda e: e.bn_aggr(out=mv[zi][0:r, :], in_=stat[zi][0:r, :, :].rearrange("p a b -> p (a b)")))
        op("act", [R_zt[zi]], [R_zt[zi]], lambda e: e.activation(out=rstd[zi][0:r, :], in_=mv[zi][0:r, 1:2], func=AF.Sqrt, bias=LN_EPS_AP[0:r, :], scale=1.0))
        op("dve", [R_zt[zi]], [R_zt[zi]], lambda e: e.reciprocal(out=rstd[zi][0:r, :], in_=rstd[zi][0:r, :]))
        op("dve", [R_zt[zi]], [R_zt[zi]], lambda e: e.tensor_scalar(out=z[0:r, :], in0=z[0:r, :], scalar1=mv[zi][0:r, 0:1], scalar2=rstd[zi][0:r, 0:1], op0=ALU.subtract, op1=ALU.mult))
        op("pool", [R_zt[zi], R_ln], [R_zt[zi]], lambda e: e.tensor_tensor(out=z[0:r, :], in0=z[0:r, :], in1=gbc[0:r, :], op=ALU.mult))
        op("pool", [R_zt[zi], R_ln], [R_h[t]], lambda e: e.tensor_tensor(out=h_tok[0:r, t, :], in0=z[0:r, :], in1=bbc[0:r, :], op=ALU.add))
        if not last:
            return lambda: to_hT(t)
        else:
            if pn == "A":
                if t == 0:
                    dma("sp", O["y_prompt"][0:112, :], h_tok[16:128, 0, :], [R_h[t]], [], ctx["ds_out"])
                else:
                    dma("sp", O["y_prompt"][112 + 128 * (t - 1):112 + 128 * t, :], h_tok[:, t, :], [R_h[t]], [], ctx["ds_out"])
            elif pn == "B":
                dma("sp", O["y_prompt"][1008 + 128 * t:1008 + 128 * t + r, :], h_tok[0:r, t, :], [R_h[t]], [], ctx["ds_out"])
            else:
                dma("sp", O["y_sample"][:, :], h_tok[:, 0, :], [R_h[t]], [], ctx["ds_out"])
            return lambda: None

    LN_EPS_AP = sb("lneps", [128, 1])
    op("dve", [], [Rc], lambda e: e.memset(LN_EPS_AP[:], LN_EPS))
    one_ap = sb("one", [128, 1])
    op("dve", [], [Rc], lambda e: e.memset(one_ap[:], 1.0))

    def load_ln(li_, k):
        dma("sp", gbc[:], I["ln_g"][li_, k].partition_broadcast(128), [], [R_ln], ds_ln)
        dma("sp", bbc[:], I["ln_b"][li_, k].partition_broadcast(128), [], [R_ln], ds_ln)

    env = dict(ctx=ctx, pn=pn, nseq=nseq, L=L, N=N, NT=NT, rows=rows, NC=NC, sb=sb, h_tok=h_tok, hT=hT,
               R_h=R_h, R_hT=R_hT, ds_w=ds_w, ds_in=ds_in, zt=zt, R_zt=R_zt, load_w_block=load_w_block, plan_w=plan_w,
               up_matmul=up_matmul, layer_norm_tile=layer_norm_tile, one_ap=one_ap, load_ln=load_ln, wslot=wslot, R_ws=R_ws)

    for li_ in range(nlayers):
        j = li_ // 2
        with ExitStack() as les:
            if li_ % 2 == 0:
                s5_layer(env, li_, j, les)
            else:
                rg_layer(env, li_, j, les)
            kb.barrier()
        with ExitStack() as les:
            ffn_layer(env, li_, les, last=(li_ == nlayers - 1))
            if pn == "A" and li_ == nlayers - 1:
                for k in range(max(0, len(glist) - 2), len(glist)):
                    write_back(k)
            kb.barrier()


def conv_taps(kb, xs, acc, nseq, L, K, wcols, bcol, R_main, R_halo, R_acc):
    kb.op("act", [R_main], [R_acc], lambda e: e.activation(out=acc[:, :, :], in_=xs[:, :, K - 1:K - 1 + L], func=AF.Identity, scale=wcols[K - 1], bias=bcol))
    for k in range(K - 1):
        kb.op("dve", [R_main, R_halo, R_acc], [R_acc], lambda e, k=k: e.scalar_tensor_tensor(out=acc[:, :, :], in0=xs[:, :, k:k + L], scalar=wcols[k], in1=acc[:, :, :], op0=ALU.mult, op1=ALU.add))


def ffn_layer(env, li_, les, last):
    ctx = env["ctx"]; kb = ctx["kb"]; nc = ctx["nc"]; I = ctx["I"]; O = ctx["O"]
    op, dma = kb.op, kb.dma
    PS, RP = ctx["PS"], ctx["RP"]
    pn, nseq, L, N, NT, rows = env["pn"], env["nseq"], env["L"], env["N"], env["NT"], env["rows"]
    sb = lambda n, s, d=F32: env["sb"]("f%d_%s" % (li_, n), s, d, es=les)
    h_tok, hT = env["h_tok"], env["hT"]
    actT = sb("actT", [128, FBH, N], BF16)
    R_act = [Res("act%d" % v) for v in range(FBH)]
    wd = sb("wd", [128, FBH, D], BF16)
    R_wd = [Res("wd%d" % v) for v in range(FBH)]
    xs = [sb("xs%d" % i, [128, nseq, 2 + L]) for i in range(2)]
    acc4 = [sb("acc%d" % i, [128, nseq, L]) for i in range(4)]
    R_xs = [Res("xs%d" % i) for i in range(2)]
    R_xh = [Res("xh%d" % i) for i in range(2)]
    R_acc4 = [Res("acc%d" % i) for i in range(4)]
    cw, cb = ctx["ffn_cw"][li_], ctx["ffn_cb"][li_]
    Rst = ctx["R_state"]
    if nseq == 1:
        hist = ctx["ffn_hist_p"][li_]
        R_hist = Rst
    else:
        hist = sb("hist", [128, FB, nseq, 2])
        R_hist = Res("hist")
        stage = sb("hstage", [32, FF2])
        load_T(ctx, I["state_ffn_conv"][li_], FB, 32, lambda b0, nb_: hist[:, b0:b0 + nb_, :, :].rearrange("p b s k -> p b (s k)"), stage, R_hist, "fh", env["ds_in"])
    env["load_ln"](li_, 1)
    def wd_hook(v):
        def f():
            if pn == "A":
                dma("pool", wd[:, v, :], I["ffn_w_down"][li_][v * 128:(v + 1) * 128, :], [], [R_wd[v]], env["ds_w"])
            else:
                dma("pool", wd[:, v, :], ctx["WDd"][li_][:, v * D:(v + 1) * D], [], [R_wd[v]], env["ds_w"])
            if v == FBH - 1:
                kb.group_final(R_wd, env["ds_w"])
        return f
    blocks = []
    for v in range(FBH):
        blocks += [(I["ffn_w_up"][li_], v * 128), (I["ffn_w_up"][li_], (v + FBH) * 128)]
    env["plan_w"](blocks, {2 * v + 1: wd_hook(v) for v in range(FBH)})
    ngrp = len(colgroups(N))
    banksets = [[0, 1, 2][:ngrp], [3, 4, 5][:ngrp]]
    bi = 0
    for v in range(FBH):
        acc = acc4[2 * (v % 2):2 * (v % 2) + 2]
        R_acc = R_acc4[2 * (v % 2):2 * (v % 2) + 2]
        for which in range(2):
            blk = v + which * FBH
            s = env["load_w_block"](2 * v + which)
            banks = banksets[bi % 2]
            bi += 1
            env["up_matmul"](s, banks, [])
            x = xs[which]
            op("act", [R_hist], [R_xh[which]], lambda e, x=x, blk=blk: e.copy(out=x[:, :, 0:2], in_=hist[:, blk, :, :]))
            for gi, (c0, cn) in enumerate(colgroups(N)):
                if nseq == 1:
                    op("act", [RP[banks[gi]]], [R_xs[which]], lambda e, gi=gi, c0=c0, cn=cn, x=x: e.copy(out=x[:, 0, 2 + c0:2 + c0 + cn], in_=PS[banks[gi]][:, 0:cn]))
                else:
                    op("act", [RP[banks[gi]]], [R_xs[which]], lambda e, gi=gi, cn=cn, x=x: e.copy(out=x[:, :, 2:2 + L], in_=PS[banks[gi]][:, 0:cn].rearrange("p (s t) -> p s t", t=L)))
            op("act", [R_xs[which]], [R_hist], lambda e, x=x, blk=blk: e.copy(out=hist[:, blk, :, :], in_=x[:, :, L:L + 2]))
            conv_taps(kb, x, acc[which], nseq, L, 3, [cw[k][:, blk:blk + 1] for k in range(3)], cb[:, blk:blk + 1], R_xs[which], R_xh[which], R_acc[which])
        op("act", [R_acc[1]], [R_acc[1]], lambda e, acc=acc: e.activation(out=acc[1][:, :, :], in_=acc[1][:, :, :], func=AF.Gelu_apprx_tanh))
        op("dve", [R_acc[0], R_acc[1]], [R_act[v]], lambda e, v=v, acc=acc: e.tensor_tensor(out=actT[:, v, :], in0=acc[0][:, :, :].rearrange("p s t -> p (s t)"), in1=acc[1][:, :, :].rearrange("p s t -> p (s t)"), op=ALU.mult))
    if pn == "A":
        dma("sp", ctx["WDd"][li_], wd[:].rearrange("p a b -> p (a b)"), R_wd, [], ctx["ds_wb"])
    if pn != "A":
        stg = sb("ostage", [32, 256 if nseq == 1 else 2048])
        m = 2 * nseq
        store_T(ctx, lambda b: hist[:, b, :, :].rearrange("p s k -> p (s k)"), FB, m,
                (O["ffn_conv_p"] if nseq == 1 else O["ffn_conv_s"])[li_], stg, R_hist, "fo")
    pending = None
    for t in range(NT):
        r = rows[t]
        pb = [0, 1] if t % 2 == 0 else [2, 3]
        for v in range(FBH):
            for hh in range(2):
                op("pe", [R_act[v], R_wd[v]], [RP[pb[hh]]], lambda e, v=v, hh=hh: e.matmul(PS[pb[hh]][0:r, :], lhsT=actT[:, v, t * 128:t * 128 + r], rhs=wd[:, v, hh * 512:(hh + 1) * 512], start=(v == 0), stop=(v == FBH - 1)))
        zi = t % 4
        z = env["zt"][zi]
        for hh in range(2):
            op("dve", [env["R_h"][t], RP[pb[hh]]], [env["R_zt"][zi]], lambda e, hh=hh: e.scalar_tensor_tensor(out=z[0:r, hh * 512:(hh + 1) * 512], in0=h_tok[0:r, t, hh * 512:(hh + 1) * 512], scalar=DN_ALPHA, in1=PS[pb[hh]][0:r, :], op0=ALU.mult, op1=ALU.add))
        if pending:
            pending()
        pending = env["layer_norm_tile"](t, zi, li_, 1, last)
    pending()


def rg_layer(env, li_, j, les):
    ctx = env["ctx"]; kb = ctx["kb"]; nc = ctx["nc"]; I = ctx["I"]; O = ctx["O"]
    op, dma = kb.op, kb.dma
    PS, RP = ctx["PS"], ctx["RP"]
    pn, nseq, L, N, NT, rows = env["pn"], env["nseq"], env["L"], env["N"], env["NT"], env["rows"]
    sb = lambda n, s, d=F32: env["sb"]("r%d_%s" % (li_, n), s, d, es=les)
    h_tok, hT = env["h_tok"], env["hT"]
    Rst = ctx["R_state"]
    hgT = sb("hgT", [128, NB, N], BF16)
    R_hg = [Res("hg%d" % c) for c in range(NB)]
    wg = sb("wg", [128, 4, 2, 512], BF16)
    R_wg = Res("wg")
    wo = sb("wo", [128, NB, D], BF16)
    R_wo = Res("wo")
    if pn == "A":
        for n in range(4):
            dma("pool", wg[:, n, :, :], I["rg_w_gates"][j, n].rearrange("(kc p) f -> p kc f", p=128), [], [R_wg], env["ds_w"])
    else:
        dma("pool", wg[:].rearrange("p a b c -> p (a b c)"), ctx["RGWGd"][j], [], [R_wg], env["ds_w"])
    kb.group_final([R_wg], env["ds_w"])
    env["load_ln"](li_, 0)
    if nseq == 1:
        hist = ctx["rg_hist_p"][j]; hst = ctx["rg_h_p"][j]; R_hist = Rst
    else:
        hist = sb("hist", [128, NB, nseq, 3]); hst = sb("hst", [128, NB, nseq]); R_hist = Res("rhist")
        stage = sb("stage", [48, D])
        load_T(ctx, I["state_rg_conv"][j], NB, 48, lambda b0, nb_: hist[:, b0:b0 + nb_, :, :].rearrange("p b s k -> p b (s k)"), stage, R_hist, "rc", env["ds_in"])
        stage2 = sb("stage2", [16, D])
        load_T(ctx, I["state_rg_h"][j], NB, 16, lambda b0, nb_: hst[:, b0:b0 + nb_, :], stage2, R_hist, "rh", env["ds_in"])
    xs = [sb("xs%d" % i, [128, nseq, 3 + L]) for i in range(2)]
    xc = [[sb("xc%d_%d" % (b, i), [128, nseq, L]) for i in range(2)] for b in range(2)]
    xcb = [[sb("xcb%d_%d" % (b, i), [128, N], BF16) for i in range(2)] for b in range(2)]
    gl = [[sb("gl%d_%d" % (b, i), [128, N]) for i in range(2)] for b in range(2)]
    work = [[sb("wk%d_%d" % (b, i), [128, nseq, L]) for i in range(3)] for b in range(2)]
    R_xs = [Res() for _ in range(2)]; R_xh = [Res() for _ in range(2)]
    R_xc = [[Res() for _ in range(2)] for _ in range(2)]; R_gl = [[Res() for _ in range(2)] for _ in range(2)]
    R_wk = [Res("rgwork0"), Res("rgwork1")]
    cw, cb, bg, m8 = ctx["rg_cw"][j], ctx["rg_cb"][j], ctx["rg_bg"][j], ctx["rg_m8sp"][j]
    one_ap = env["one_ap"]
    ngrp = len(colgroups(N))
    banksets = [[0, 1, 2][:ngrp], [3, 4, 5][:ngrp]]
    bi = [0]
    blocks = []
    for c in range(NB):
        blocks += [(I["rg_w_in"][j], c * 128), (I["rg_w_in"][j], D + c * 128)]
    env["plan_w"](blocks)
    flat = lambda t: t[:, :, :].rearrange("p s t -> p (s t)")

    def stage_proj(n):
        pb_ = n % 2
        for q in range(2):
            c = 2 * n + q
            s = env["load_w_block"](2 * c)
            banks = banksets[bi[0] % 2]; bi[0] += 1
            env["up_matmul"](s, banks, [])
            for gi, (c0, cn) in enumerate(colgroups(N)):
                op("act", [RP[banks[gi]]], [R_gl[pb_][q]], lambda e, gi=gi, c0=c0, cn=cn, q=q, banks=banks: e.activation(out=gl[pb_][q][:, c0:c0 + cn], in_=PS[banks[gi]][:, 0:cn], func=AF.Gelu_apprx_tanh))
            s = env["load_w_block"](2 * c + 1)
            banks = banksets[bi[0] % 2]; bi[0] += 1
            env["up_matmul"](s, banks, [])
            x = xs[q]
            op("act", [R_hist], [R_xh[q]], lambda e, x=x, c=c: e.copy(out=x[:, :, 0:3], in_=hist[:, c, :, :]))
            for gi, (c0, cn) in enumerate(colgroups(N)):
                if nseq == 1:
                    op("act", [RP[banks[gi]]], [R_xs[q]], lambda e, gi=gi, c0=c0, cn=cn, x=x, banks=banks: e.copy(out=x[:, 0, 3 + c0:3 + c0 + cn], in_=PS[banks[gi]][:, 0:cn]))
                else:
                    op("act", [RP[banks[gi]]], [R_xs[q]], lambda e, gi=gi, cn=cn, x=x, banks=banks: e.copy(out=x[:, :, 3:3 + L], in_=PS[banks[gi]][:, 0:cn].rearrange("p (s t) -> p s t", t=L)))
            op("act", [R_xs[q]], [R_hist], lambda e, x=x, c=c: e.copy(out=hist[:, c, :, :], in_=x[:, :, L:L + 3]))
            conv_taps(kb, x, xc[pb_][q], nseq, L, 4, [cw[k][:, c:c + 1] for k in range(4)], cb[:, c:c + 1], R_xs[q], R_xh[q], R_xc[pb_][q])
            op("dve", [R_xc[pb_][q]], [R_xc[pb_][q]], lambda e, q=q: e.tensor_copy(out=xcb[pb_][q][:, :], in_=flat(xc[pb_][q])))

    def stage_gate(n):
        pb_ = n % 2
        for q in range(2):
            c = 2 * n + q
            T1, T2, T3 = work[c % 2]
            Rw = R_wk[c % 2]
            pr_ = banksets[0]; pi_ = banksets[1]
            for (dst_banks, off) in ((pr_, q * 128), (pi_, 256 + q * 128)):
                for gi, (c0, cn) in enumerate(colgroups(N)):
                    for kc in range(2):
                        op("pe", [R_wg, R_xc[pb_][kc]], [RP[dst_banks[gi]]], lambda e, kc=kc, gi=gi, c0=c0, cn=cn, off=off, dst_banks=dst_banks: e.matmul(PS[dst_banks[gi]][:, 0:cn], lhsT=wg[:, n, kc, off:off + 128], rhs=xcb[pb_][kc][:, c0:c0 + cn], start=(kc == 0), stop=(kc == 1)))
            for gi, (c0, cn) in enumerate(colgroups(N)):
                op("act", [RP[pr_[gi]]], [Rw], lambda e, gi=gi, c0=c0, cn=cn, c=c: e.activation(out=flat(T1)[:, c0:c0 + cn], in_=PS[pr_[gi]][:, 0:cn], func=AF.Sigmoid, bias=bg[:, c:c + 1], scale=1.0))
                op("act", [RP[pi_[gi]]], [Rw], lambda e, gi=gi, c0=c0, cn=cn, c=c: e.activation(out=flat(T2)[:, c0:c0 + cn], in_=PS[pi_[gi]][:, 0:cn], func=AF.Sigmoid, bias=bg[:, 8 + c:8 + c + 1], scale=1.0))
            op("act", [Rw], [Rw], lambda e, c=c: e.activation(out=flat(T1), in_=flat(T1), func=AF.Exp, scale=m8[:, c:c + 1]))
            op("act", [Rw], [Rw], lambda e: e.activation(out=flat(T3), in_=flat(T1), func=AF.Square))
            op("act", [Rw], [Rw], lambda e: e.activation(out=flat(T3), in_=flat(T3), func=AF.Sqrt, scale=-1.0, bias=one_ap[:, :]))
            op("dve", [Rw, R_xc[pb_][q]], [Rw], lambda e, q=q: e.tensor_tensor(out=flat(T2), in0=flat(T2), in1=flat(xc[pb_][q]), op=ALU.mult))
            op("dve", [Rw], [Rw], lambda e: e.tensor_tensor(out=flat(T2), in0=flat(T2), in1=flat(T3), op=ALU.mult))
            for s_ in range(nseq):
                op("dve", [Rw, R_hist], [Rw], lambda e, s_=s_, c=c: e.tensor_tensor_scan(out=T3[:, s_, :], data0=T1[:, s_, :], data1=T2[:, s_, :], initial=hst[:, c, s_:s_ + 1], op0=ALU.mult, op1=ALU.add))
            op("dve", [Rw], [R_hist], lambda e, c=c: e.tensor_copy(out=hst[:, c, :], in_=T3[:, :, L - 1]))
            op("dve", [Rw, R_gl[pb_][q]], [R_hg[c]], lambda e, c=c, q=q: e.tensor_tensor(out=hgT[:, c, :], in0=flat(T3), in1=gl[pb_][q][:, :], op=ALU.mult))

    for n in range(4):
        stage_proj(n)
        if n == 1:
            if pn == "A":
                dma("pool", wo[:], I["rg_w_out"][j].rearrange("(kc p) f -> p kc f", p=128), [], [R_wo], env["ds_w"])
            else:
                dma("pool", wo[:].rearrange("p a b -> p (a b)"), ctx["RGWOd"][j], [], [R_wo], env["ds_w"])
            kb.group_final([R_wo], env["ds_w"])
        if n > 0:
            stage_gate(n - 1)
    stage_gate(3)
    if pn == "A":
        dma("sp", ctx["RGWGd"][j], wg[:].rearrange("p a b c -> p (a b c)"), [R_wg], [], ctx["ds_wb"])
        dma("sp", ctx["RGWOd"][j], wo[:].rearrange("p a b -> p (a b)"), [R_wo], [], ctx["ds_wb"])
    if pn != "A":
        stg = sb("ostage", [48, 256])
        store_T(ctx, lambda b: hist[:, b, :, :].rearrange("p s k -> p (s k)"), NB, 3 * nseq,
                (O["rg_conv_p"] if nseq == 1 else O["rg_conv_s"])[j], stg, R_hist, "ro")
        store_T(ctx, lambda b: hst[:, b, :], NB, nseq, (O["rg_h_p"] if nseq == 1 else O["rg_h_s"])[j], stg, R_hist, "rho")
    pending = None
    for t in range(NT):
        r = rows[t]
        pb = [0, 1] if t % 2 == 0 else [2, 3]
        for kc in range(NB):
            for hh in range(2):
                op("pe", [R_hg[kc], R_wo], [RP[pb[hh]]], lambda e, kc=kc, hh=hh: e.matmul(PS[pb[hh]][0:r, :], lhsT=hgT[:, kc, t * 128:t * 128 + r], rhs=wo[:, kc, hh * 512:(hh + 1) * 512], start=(kc == 0), stop=(kc == NB - 1)))
        zi = t % 4
        z = env["zt"][zi]
        for hh in range(2):
            op("dve", [env["R_h"][t], RP[pb[hh]]], [env["R_zt"][zi]], lambda e, hh=hh: e.scalar_tensor_tensor(out=z[0:r, hh * 512:(hh + 1) * 512], in0=h_tok[0:r, t, hh * 512:(hh + 1) * 512], scalar=DN_ALPHA, in1=PS[pb[hh]][0:r, :], op0=ALU.mult, op1=ALU.add))
        if pending:
            pending()
        pending = env["layer_norm_tile"](t, zi, li_, 0, False)
    pending()


def ONE_AP(env):
    if "one_ap" not in env:
        kb = env["ctx"]["kb"]
        t = env["sb"]("one", [128, 1])
        kb.op("dve", [], [env["ctx"]["R_const"]], lambda e: e.memset(t[:], 1.0))
        env["one_ap"] = t
    return env["one_ap"]


def s5_layer(env, li_, j, les):
    ctx = env["ctx"]; kb = ctx["kb"]; nc = ctx["nc"]; I = ctx["I"]; O = ctx["O"]
    op, dma = kb.op, kb.dma
    PS, RP = ctx["PS"], ctx["RP"]
    pn, nseq, L, N, NT, rows, NC = env["pn"], env["nseq"], env["L"], env["N"], env["NT"], env["rows"], env["NC"]
    sb = lambda n, s, d=F32: env["sb"]("s%d_%s" % (li_, n), s, d, es=les)
    h_tok, hT = env["h_tok"], env["hT"]
    Rst = ctx["R_state"]
    uT = sb("uT", [128, NB, N], BF16)
    gyT = uT
    R_u = [Res() for _ in range(NB)]
    R_gy = R_u
    R_X = Res("Xs")
    Sinb = sb("Sinb", [128, 32, 2, NC], BF16)
    R_S = Res("Sinb")
    wo = sb("wo", [128, NB, 2 * D], BF16)
    R_wo = Res("wo")
    env["load_ln"](li_, 0)
    if nseq == 1:
        st = ctx["s5st_p"][j]; R_st = Rst
    else:
        st = sb("st", [128, 32, 2, nseq]); R_st = Res("s5st")
        stage = sb("stage", [16, 4096])
        for ri, nm in enumerate(("state_s5_re", "state_s5_im")):
            load_T(ctx, I[nm][j], 32, 16, lambda b0, nb_, ri=ri: st[:, b0:b0 + nb_, ri, :], stage, R_st, "s5" + str(ri), env["ds_in"])
    ngrp = len(colgroups(N))
    banksets = [[0, 1, 2][:ngrp], [3, 4, 5][:ngrp]]
    env["plan_w"]([(I["s5_w_in"][j], c * 128) for c in range(NB)])
    for c in range(NB):
        s = env["load_w_block"](c)
        banks = banksets[c % 2]
        env["up_matmul"](s, banks, [])
        for gi, (c0, cn) in enumerate(colgroups(N)):
            op("act", [RP[banks[gi]]], [R_u[c]], lambda e, gi=gi, c0=c0, cn=cn, c=c, banks=banks: e.copy(out=uT[:, c, c0:c0 + cn], in_=PS[banks[gi]][:, 0:cn]))
    if pn == "A":
        dma("pool", wo[:, 0:4, :], I["s5_w_out"][j][0:512, :].rearrange("(kc p) f -> p kc f", p=128), [], [R_wo], env["ds_w"])
        dma("pool", wo[:, 4:8, :], I["s5_w_out"][j][512:1024, :].rearrange("(kc p) f -> p kc f", p=128), [], [R_wo], env["ds_w"])
        kb.group_final([R_wo], env["ds_w"])
    else:
        dma("pool", wo[:].rearrange("p a b -> p (a b)"), ctx["S5WOd"][j], [], [R_wo], env["ds_w"])
        kb.group_final([R_wo], env["ds_w"])
    ies = ExitStack()
    sbo = sb
    sb = lambda n, s_, d=F32: env["sb"]("s%d_%s" % (li_, n), s_, d, es=ies)
    Xs = sb("Xs", [128, 32, 2, NC])
    WXs = [sb("WX%d" % i, [128, 8, 2, 128], BF16) for i in range(2)]
    R_WX = [Res() for _ in range(2)]
    ds_wx = [kb.dsem("s5x%s%d_%d" % (pn, li_, i)) for i in range(2)]
    ds_kw = [kb.dsem("s5k%s%d_%d" % (pn, li_, i)) for i in range(2)]
    for c in range(NB):
        wi = c % 2
        dma("sp", WXs[wi][:], ctx["WXd"][j][:, c], [], [R_WX[wi]], ds_wx[wi])
        for ri in range(2):
            for tau in range(8):
                for pr in range(4):
                    bank = pr
                    rs = slice(32 * pr, 32 * pr + 32)
                    op("pe", [R_WX[wi], R_u[c]], [RP[bank]], lambda e, ri=ri, tau=tau, rs=rs, bank=bank, wi=wi, c=c, pr=pr: e.matmul(
                        PS[bank][:, ri * 256:ri * 256 + NC], lhsT=WXs[wi][rs, tau, ri, :],
                        rhs=uT[rs, c, :].rearrange("p (n t) -> p n t", t=8)[:, :, tau], start=(tau == 0), stop=(tau == 7), tile_position=(32 * pr, 0)))
        for pr in range(4):
            bank = pr
            op("act", [RP[bank]], [R_X], lambda e, bank=bank, c=c, pr=pr: e.copy(out=Xs[:, 4 * c + pr, :, :], in_=PS[bank][:, :].rearrange("p (r n) -> p r n", n=256)[:, :, 0:NC]))
    a8r, a8i, a8in = ctx["A8re"][j], ctx["A8im"][j], ctx["A8imn"][j]
    Mrot = sb("Mrot", [128, 32, 2, 2]); prod = sb("prod", [128, 32, 2, 2]); t1 = sb("t1", [128, 32, 2])
    op("dve", [Rst], [R_X], lambda e: e.tensor_copy(out=Mrot[:, :, 0, 0], in_=a8r[:, :]))
    op("dve", [Rst], [R_X], lambda e: e.tensor_copy(out=Mrot[:, :, 1, 1], in_=a8r[:, :]))
    op("dve", [Rst], [R_X], lambda e: e.tensor_copy(out=Mrot[:, :, 0, 1], in_=a8in[:, :]))
    op("dve", [Rst], [R_X], lambda e: e.tensor_copy(out=Mrot[:, :, 1, 0], in_=a8i[:, :]))
    if nseq == 1:
        op("act", [R_st], [R_S], lambda e: e.copy(out=Sinb[:, :, :, 0], in_=st[:, :, :, 0]))
    else:
        op("act", [R_st], [R_S], lambda e: e.copy(out=Sinb[:, :, :, :], in_=st[:, :, :, :]))
    nsteps = NC if nseq == 1 else nseq
    for c in range(nsteps):
        if nseq == 1:
            prev = st[:, :, :, 0] if c == 0 else Xs[:, :, :, c - 1]
        else:
            prev = st[:, :, :, c]
        cur = Xs[:, :, :, c]
        rd = [R_X, R_st, Rst]
        pb_ = prev.unsqueeze(2).broadcast_to([128, 32, 2, 2])
        op("dve", rd, [R_X], lambda e, pb_=pb_: e.tensor_tensor(out=prod[:], in0=pb_, in1=Mrot[:], op=ALU.mult))
        op("dve", rd, [R_X], lambda e: e.tensor_tensor(out=t1[:], in0=prod[:, :, :, 0], in1=prod[:, :, :, 1], op=ALU.add))
        op("dve", rd, [R_X], lambda e, cur=cur: e.tensor_tensor(out=cur, in0=cur, in1=t1[:], op=ALU.add))
    if nseq == 1:
        op("act", [R_X], [R_S], lambda e: e.copy(out=Sinb[:, :, :, 1:NC], in_=Xs[:, :, :, 0:NC - 1]))
        op("dve", [R_X], [R_st], lambda e: e.tensor_copy(out=st[:, :, :, 0], in_=Xs[:, :, :, NC - 1]))
    else:
        op("dve", [R_X], [R_st], lambda e: e.tensor_copy(out=st[:, :, :, :], in_=Xs[:, :, :, :]))
    if pn != "A":
        stg = sb("ostage", [16, 2048])
        for ri, nm in enumerate(("s5_re", "s5_im")):
            store_T(ctx, lambda b, ri=ri: st[:, b, ri, :], 32, nseq, O[nm + ("_p" if nseq == 1 else "_s")][j], stg, R_st, "s5o%d" % ri)
    kb.barrier()
    ies.close()
    sb = sbo
    KIs = [sb("KI%d" % i, [128, 8, 128], BF16) for i in range(2)]
    WYs = [sb("WY%d" % i, [128, 4, 8, 2, 32], BF16) for i in range(2)]
    R_KI = [Res() for _ in range(2)]
    R_WY = [Res() for _ in range(2)]
    cg = colgroups(NC, 64)
    for c in range(NB):
        wi = c % 2
        dma("sp", KIs[wi][:], ctx["KId"][j][:, c], [], [R_KI[wi]], ds_kw[wi])
        dma("sp", WYs[wi][:], ctx["WYd"][j][:, 4 * c:4 * c + 4], [], [R_WY[wi]], ds_kw[wi])
        kb.group_final([R_KI[wi], R_WY[wi]], ds_kw[wi])
        banks = banksets[c % 2]
        for gi, (n0, nn) in enumerate(cg):
            bank = banks[gi]
            uv = uT[:, c, n0 * 8:(n0 + nn) * 8].rearrange("p (n t) -> p t n", t=8)
            for lag in range(8):
                op("pe", [R_KI[wi], R_u[c]], [RP[bank]], lambda e, lag=lag, bank=bank, nn=nn, uv=uv, wi=wi: e.matmul(
                    PS[bank][:, lag * nn:8 * nn], lhsT=KIs[wi][:, lag, :], rhs=uv[:, 0:8 - lag, :], start=(lag == 0), stop=False))
            for tau in range(8):
                for ri in range(2):
                    for pr in range(4):
                        lastmm = (tau == 7 and ri == 1)
                        op("pe", [R_WY[wi], R_S], [RP[bank]], lambda e, pr=pr, tau=tau, ri=ri, bank=bank, wi=wi, n0=n0, nn=nn, c=c, lastmm=lastmm: e.matmul(
                            PS[bank][32 * pr:32 * pr + 32, tau * nn:(tau + 1) * nn], lhsT=WYs[wi][:, pr, tau, ri, :], rhs=Sinb[:, 4 * c + pr, ri, n0:n0 + nn], start=False, stop=lastmm, tile_position=(0, 32 * pr)))
            op("act", [RP[bank]], [R_gy[c]], lambda e, bank=bank, n0=n0, nn=nn, c=c: e.activation(out=gyT[:, c, n0 * 8:(n0 + nn) * 8].rearrange("p (n t) -> p n t", t=8), in_=PS[bank][:, 0:nn * 8].rearrange("p (t n) -> p n t", t=8), func=AF.Gelu_apprx_tanh))
    if pn == "A":
        dma("sp", ctx["S5WOd"][j], wo[:].rearrange("p a b -> p (a b)"), [R_wo], [], ctx["ds_wb"])
    sgs = [sb("sg%d" % i, [128, D]) for i in range(2)]
    R_sgs = [Res("sg%d" % i) for i in range(2)]
    pending = None
    for t in range(NT):
        sg = sgs[t % 2]
        R_sg = R_sgs[t % 2]
        r = rows[t]
        pb = [2, 3, 0, 1]
        for qq in (2, 3, 0, 1):
            for kc in range(NB):
                op("pe", [R_gy[kc], R_wo], [RP[pb[qq]]], lambda e, kc=kc, qq=qq: e.matmul(PS[pb[qq]][0:r, :], lhsT=gyT[:, kc, t * 128:t * 128 + r], rhs=wo[:, kc, qq * 512:(qq + 1) * 512], start=(kc == 0), stop=(kc == NB - 1)))
        zi = t % 4
        z = env["zt"][zi]
        for hh in range(2):
            op("act", [RP[pb[2 + hh]]], [R_sg], lambda e, hh=hh: e.activation(out=sg[0:r, hh * 512:(hh + 1) * 512], in_=PS[pb[2 + hh]][0:r, :], func=AF.Sigmoid))
            op("dve", [R_sg, RP[pb[hh]]], [R_sg], lambda e, hh=hh: e.tensor_tensor(out=sg[0:r, hh * 512:(hh + 1) * 512], in0=sg[0:r, hh * 512:(hh + 1) * 512], in1=PS[pb[hh]][0:r, :], op=ALU.mult))
            op("dve", [env["R_h"][t], R_sg], [env["R_zt"][zi]], lambda e, hh=hh: e.scalar_tensor_tensor(out=z[0:r, hh * 512:(hh + 1) * 512], in0=h_tok[0:r, t, hh * 512:(hh + 1) * 512], scalar=DN_ALPHA, in1=sg[0:r, hh * 512:(hh + 1) * 512], op0=ALU.mult, op1=ALU.add))
        if pending:
            pending()
        pending = env["layer_norm_tile"](t, zi, li_, 0, False)
    pending()


_NC_CACHE = {}


def _get_nc(dbg=None):
    key = repr(dbg)
    if key not in _NC_CACHE:
        nc = bass.Bass("TRN2", target_bir_lowering=False)
        build(nc, dbg)
        _NC_CACHE[key] = nc
    return _NC_CACHE[key]


def kernel(dbg=None, **inp):
    f = lambda a: np.ascontiguousarray(np.asarray(a, dtype=np.float32))
    ident = np.eye(128, dtype=np.float32)
    bmask = np.kron(np.eye(4, dtype=np.float32), np.ones((32, 32), np.float32))
    wnames = ["meta_tokens", "s5_w_in", "s5_lam_re", "s5_lam_im", "s5_log_step", "s5_b_re", "s5_b_im", "s5_c_re",
              "s5_c_im", "s5_d", "s5_w_out", "rg_w_in", "rg_conv_w", "rg_conv_b", "rg_w_gates", "rg_b_gates",
              "rg_lam", "rg_w_out", "ffn_w_up", "ffn_conv_w", "ffn_conv_b", "ffn_w_down", "ln_g", "ln_b"]
    shared = {n: f(inp[n]) for n in wnames}
    shared["ident"] = ident
    shared["bmask"] = bmask
    shared["sel2"] = np.kron(np.eye(2, dtype=np.float32), np.ones((1, 64), np.float32))
    in_maps = []
    for c in range(8):
        m = dict(shared)
        sl = slice(16 * c, 16 * c + 16)
        m["x_prompt"] = f(inp["x_prompt"][c])
        m["x_sample"] = f(inp["x_sample"][sl]).reshape(128, D)
        m["state_s5_re"] = f(inp["state_s5_re"][:, sl]).reshape(2, 16, 4096)
        m["state_s5_im"] = f(inp["state_s5_im"][:, sl]).reshape(2, 16, 4096)
        m["state_rg_h"] = f(inp["state_rg_h"][:, sl])
        m["state_rg_conv"] = f(inp["state_rg_conv"][:, sl]).reshape(2, 48, D)
        m["state_ffn_conv"] = f(inp["state_ffn_conv"][:, sl]).reshape(4, 32, FF2)
        in_maps.append(m)
    nc = _get_nc(dbg)
    res = run_bass_kernel_spmd(nc, in_maps, core_ids=list(range(8)))
    R = res.results
    cat = lambda k, ax: np.concatenate([np.asarray(R[c][k]) for c in range(8)], axis=ax)
    y_prompt = np.stack([np.asarray(R[c]["y_prompt"]) for c in range(8)], 0)
    y_sample = cat("y_sample", 0).reshape(128, 8, D)
    s5_re_p = cat("s5_re_p", 1).reshape(2, 8, 64, 64)
    s5_im_p = cat("s5_im_p", 1).reshape(2, 8, 64, 64)
    rg_h_p = cat("rg_h_p", 1).reshape(2, 8, D)
    rg_conv_p = np.stack([np.asarray(R[c]["rg_conv_p"]) for c in range(8)], 1).reshape(2, 8, 3, D)
    ffn_conv_p = np.stack([np.asarray(R[c]["ffn_conv_p"]) for c in range(8)], 1).reshape(4, 8, 2, FF2)
    s5_re_s = cat("s5_re_s", 1).reshape(2, 128, 64, 64)
    s5_im_s = cat("s5_im_s", 1).reshape(2, 128, 64, 64)
    rg_h_s = cat("rg_h_s", 1).reshape(2, 128, D)
    rg_conv_s = cat("rg_conv_s", 1).reshape(2, 128, 3, D)
    ffn_conv_s = cat("ffn_conv_s", 1).reshape(4, 128, 2, FF2)
    outs = (y_prompt, y_sample, s5_re_p, s5_im_p, rg_h_p, rg_conv_p, ffn_conv_p,
            s5_re_s, s5_im_s, rg_h_s, rg_conv_s, ffn_conv_s)
    return tuple(np.ascontiguousarray(o, dtype=np.float32) for o in outs)
```

```python
import math
from contextlib import ExitStack
import numpy as np
import concourse.bass as bass
import concourse.mybir as mybir
from concourse.bass_utils import run_bass_kernel_spmd

F32 = mybir.dt.float32
BF16 = mybir.dt.bfloat16
I32 = mybir.dt.int32
AF = mybir.ActivationFunctionType
ALU = mybir.AluOpType

D = 1024
NB = 8
FF = 2816
FF2 = 5632
FB = 44
FBH = 22
DEPTH = 4
SEQ = 2048
NMETA = 16
DN_ALPHA = (2 * DEPTH) ** 0.25
LN_EPS = 1e-5
RG_C = 8.0
TWO_PI = 2.0 * math.pi
ATTACH_WAITS = True
STATS = {}


class Res:
    __slots__ = ("name", "w", "r")

    def __init__(self, name=""):
        self.name = name
        self.w = None
        self.r = {}


class DSem:
    def __init__(self, sem):
        self.sem = sem
        self.val = 0


class Q:
    def __init__(self, name, eng, sem, eager):
        self.name = name
        self.eng = eng
        self.sem = sem
        self.eager = eager
        self.n = 0
        self.last = None
        self.last_ms = True
        self.ms = []
        self.semval = 0
        self.known = {}


class KB:
    def __init__(self, nc, es):
        self.nc = nc
        self.es = es
        self.q = {}
        for name, eng, eager in (("pe", nc.tensor, False), ("act", nc.scalar, False),
                                 ("dve", nc.vector, False), ("pool", nc.gpsimd, True),
                                 ("sp", nc.sync, True)):
            self.q[name] = Q(name, eng, self.sem("q_" + name), eager)
        self.dsems = []
        self.nsem = 0

    def sem(self, name):
        return self.es.enter_context(self.nc.semaphore(name))

    def dsem(self, name):
        d = DSem(self.sem("d_" + name))
        self.dsems.append(d)
        return d

    def sb(self, name, shape, dt, es=None):
        return (es or self.es).enter_context(self.nc.sbuf_tensor("sb_" + name, list(shape), dt))

    def ps(self, name, shape, dt=F32, es=None):
        return (es or self.es).enter_context(self.nc.psum_tensor("pt_" + name, list(shape), dt))

    def _milestone(self, A, k):
        lo, hi = 0, len(A.ms)
        while lo < hi:
            mid = (lo + hi) // 2
            if A.ms[mid][0] >= k:
                hi = mid
            else:
                lo = mid + 1
        if lo < len(A.ms):
            return A.ms[lo][1]
        assert A.last is not None and not A.last_ms and A.n >= k
        A.semval += 1
        A.last.then_inc(A.sem, 1)
        A.last_ms = True
        A.ms.append((A.n, A.semval))
        return A.semval

    def _wait(self, q, deps):
        need = {}
        for d in deps:
            if d is None:
                continue
            if d[0] == "q":
                A, k = d[1], d[2]
                if A is q and q.name == "pe":
                    continue
                v = self._milestone(A, k)
                key = A
                sem = A.sem
            else:
                key = d[1]
                sem = d[1].sem
                v = d[2]
            if q.known.get(key, 0) >= v:
                continue
            if need.get(key, (None, 0))[1] < v:
                need[key] = (sem, v)
        items = list(need.items())
        attach = None
        if ATTACH_WAITS and items:
            key, (sem, v) = items.pop()
            q.known[key] = v
            attach = (sem, v)
        for key, (sem, v) in items:
            q.eng.wait_ge(sem, v)
            q.known[key] = v
            STATS[q.name] = STATS.get(q.name, 0) + 1
        if attach:
            STATS["att_" + q.name] = STATS.get("att_" + q.name, 0) + 1
        return attach

    def _deps(self, reads, writes):
        deps = []
        for r in reads:
            if r.w is not None:
                deps.append(r.w)
        for w in writes:
            if w.w is not None:
                deps.append(w.w)
            deps.extend(w.r.values())
        return deps

    def op(self, qn, reads, writes, fn):
        q = self.q[qn]
        att = self._wait(q, self._deps(reads, writes))
        ins = fn(q.eng)
        if att is not None:
            ins._wait_ge(att[0], att[1])
        q.n += 1
        q.last = ins
        q.last_ms = False
        if q.eager:
            q.semval += 1
            ins.then_inc(q.sem, 1)
            q.last_ms = True
            q.ms.append((q.n, q.semval))
        me = ("q", q, q.n)
        for r in reads:
            r.r[q] = me
        for w in writes:
            w.w = me
            w.r = {}
        return ins

    def dma(self, qn, out, in_, reads, writes, ds, **kw):
        q = self.q[qn]
        att = self._wait(q, self._deps(reads, writes))
        ins = q.eng.dma_start(out=out, in_=in_, **kw)
        if att is not None:
            ins._wait_ge(att[0], att[1])
        ds.val += 16
        ins.then_inc(ds.sem, 16)
        me = ("d", ds, ds.val)
        for r in reads:
            r.r[ds] = me
        for w in writes:
            w.w = me
            w.r = {}
        return ins

    def group_final(self, ress, ds):
        for r in ress:
            r.w = ("d", ds, ds.val)

    def barrier(self, exclude=(), full=False):
        marks = []
        for A in self.q.values():
            if A.n > 0:
                marks.append(("q", A, A.n))
        for d in self.dsems:
            if d in exclude:
                continue
            if d.val > 0:
                marks.append(("d", d, d.val))
        global ATTACH_WAITS
        sv = ATTACH_WAITS
        ATTACH_WAITS = False
        for q in self.q.values():
            if q.name == "pe" and not full:
                continue
            self._wait(q, [m for m in marks if not (m[0] == "q" and m[1] is q)])
        ATTACH_WAITS = sv

    def finish(self):
        self.barrier(full=True)


def build(nc, dbg=None):
    es = ExitStack()
    kb = KB(nc, es)
    with es:
        _build(nc, kb, dbg)
    return nc


def _dram_in(nc, name, shape, dt=F32):
    return nc.dram_tensor(name, list(shape), dt, kind="ExternalInput").ap()


def _dram_out(nc, name, shape, dt=F32):
    return nc.dram_tensor(name, list(shape), dt, kind="ExternalOutput").ap()


def _build(nc, kb, dbg):
    op, dma = kb.op, kb.dma
    I = {}
    I["x_prompt"] = _dram_in(nc, "x_prompt", [SEQ, D])
    I["x_sample"] = _dram_in(nc, "x_sample", [128, D])
    I["state_s5_re"] = _dram_in(nc, "state_s5_re", [2, 16, 4096])
    I["state_s5_im"] = _dram_in(nc, "state_s5_im", [2, 16, 4096])
    I["state_rg_h"] = _dram_in(nc, "state_rg_h", [2, 16, D])
    I["state_rg_conv"] = _dram_in(nc, "state_rg_conv", [2, 48, D])
    I["state_ffn_conv"] = _dram_in(nc, "state_ffn_conv", [4, 32, FF2])
    I["meta_tokens"] = _dram_in(nc, "meta_tokens", [NMETA, D])
    I["s5_w_in"] = _dram_in(nc, "s5_w_in", [2, D, D])
    I["s5_lam_re"] = _dram_in(nc, "s5_lam_re", [2, 64, 64])
    I["s5_lam_im"] = _dram_in(nc, "s5_lam_im", [2, 64, 64])
    I["s5_log_step"] = _dram_in(nc, "s5_log_step", [2, 64])
    I["s5_b_re"] = _dram_in(nc, "s5_b_re", [2, 64, 64, 16])
    I["s5_b_im"] = _dram_in(nc, "s5_b_im", [2, 64, 64, 16])
    I["s5_c_re"] = _dram_in(nc, "s5_c_re", [2, 64, 16, 64])
    I["s5_c_im"] = _dram_in(nc, "s5_c_im", [2, 64, 16, 64])
    I["s5_d"] = _dram_in(nc, "s5_d", [2, D])
    I["s5_w_out"] = _dram_in(nc, "s5_w_out", [2, D, 2 * D])
    I["rg_w_in"] = _dram_in(nc, "rg_w_in", [2, D, 2 * D])
    I["rg_conv_w"] = _dram_in(nc, "rg_conv_w", [2, 4, D])
    I["rg_conv_b"] = _dram_in(nc, "rg_conv_b", [2, D])
    I["rg_w_gates"] = _dram_in(nc, "rg_w_gates", [2, 4, 256, 512])
    I["rg_b_gates"] = _dram_in(nc, "rg_b_gates", [2, 2 * D])
    I["rg_lam"] = _dram_in(nc, "rg_lam", [2, D])
    I["rg_w_out"] = _dram_in(nc, "rg_w_out", [2, D, D])
    I["ffn_w_up"] = _dram_in(nc, "ffn_w_up", [4, D, FF2])
    I["ffn_conv_w"] = _dram_in(nc, "ffn_conv_w", [4, 3, FF2])
    I["ffn_conv_b"] = _dram_in(nc, "ffn_conv_b", [4, FF2])
    I["ffn_w_down"] = _dram_in(nc, "ffn_w_down", [4, FF, D])
    I["ln_g"] = _dram_in(nc, "ln_g", [4, 2, D])
    I["ln_b"] = _dram_in(nc, "ln_b", [4, 2, D])
    I["ident"] = _dram_in(nc, "ident", [128, 128])
    I["bmask"] = _dram_in(nc, "bmask", [128, 128])
    I["sel2"] = _dram_in(nc, "sel2", [2, 128])

    O = {}
    O["y_prompt"] = _dram_out(nc, "y_prompt", [SEQ, D])
    O["y_sample"] = _dram_out(nc, "y_sample", [128, D])
    O["s5_re_p"] = _dram_out(nc, "s5_re_p", [2, 1, 4096])
    O["s5_im_p"] = _dram_out(nc, "s5_im_p", [2, 1, 4096])
    O["rg_h_p"] = _dram_out(nc, "rg_h_p", [2, 1, D])
    O["rg_conv_p"] = _dram_out(nc, "rg_conv_p", [2, 3, D])
    O["ffn_conv_p"] = _dram_out(nc, "ffn_conv_p", [4, 2, FF2])
    O["s5_re_s"] = _dram_out(nc, "s5_re_s", [2, 16, 4096])
    O["s5_im_s"] = _dram_out(nc, "s5_im_s", [2, 16, 4096])
    O["rg_h_s"] = _dram_out(nc, "rg_h_s", [2, 16, D])
    O["rg_conv_s"] = _dram_out(nc, "rg_conv_s", [2, 48, D])
    O["ffn_conv_s"] = _dram_out(nc, "ffn_conv_s", [4, 32, FF2])

    WXd = [nc.dram_tensor("WXd%d" % j, [128, 8, 8, 2, 128], BF16, kind="Internal").ap() for j in range(2)]
    WYd = [nc.dram_tensor("WYd%d" % j, [128, 32, 8, 2, 32], BF16, kind="Internal").ap() for j in range(2)]
    KId = [nc.dram_tensor("KId%d" % j, [128, 8, 8, 128], BF16, kind="Internal").ap() for j in range(2)]

    NBLK = 2 * NB + 2 * 2 * NB + 4 * FB
    ctx_w = dict(
        WBLK=nc.dram_tensor("WBLK", [NBLK, 128, NB * 128], BF16, kind="Internal").ap(),
        WDd=[nc.dram_tensor("WDd%d" % l, [128, FBH * D], BF16, kind="Internal").ap() for l in range(4)],
        S5WOd=[nc.dram_tensor("S5WOd%d" % j, [128, NB * 2 * D], BF16, kind="Internal").ap() for j in range(2)],
        RGWOd=[nc.dram_tensor("RGWOd%d" % j, [128, NB * D], BF16, kind="Internal").ap() for j in range(2)],
        RGWGd=[nc.dram_tensor("RGWGd%d" % j, [128, 4 * 2 * 512], BF16, kind="Internal").ap() for j in range(2)],
    )
    sb = kb.sb
    ident = sb("ident", [128, 128], F32)
    bmask = sb("bmask", [128, 128], F32)
    R_const = Res("const")
    ds_c = kb.dsem("const")
    dma("sp", ident[:], I["ident"][:, :], [], [R_const], ds_c)
    dma("sp", bmask[:], I["bmask"][:, :], [], [R_const], ds_c)
    sel2 = sb("sel2", [2, 128], F32)
    dma("sp", sel2[:], I["sel2"][:, :], [], [R_const], ds_c)

    R_par = Res("par")
    ds_p = kb.dsem("par")

    par_loads = []

    def load_cols(name, src_1d, nblk, ds):
        t = sb(name, [128, nblk], F32)
        par_loads.append((t, src_1d))
        return t

    def issue_par_loads():
        with nc.allow_non_contiguous_dma(reason="small param"):
            for t, src_1d in par_loads:
                dma("sp", t[:], src_1d.rearrange("(b p) -> p b", p=128), [], [R_par], ds_p)

    ffn_cw = [[load_cols("fcw%d_%d" % (l, k), I["ffn_conv_w"][l, k], FB, ds_c) for k in range(3)] for l in range(4)]
    ffn_cb = [load_cols("fcb%d" % l, I["ffn_conv_b"][l], FB, ds_c) for l in range(4)]
    rg_cw = [[load_cols("rcw%d_%d" % (j, k), I["rg_conv_w"][j, k], NB, ds_c) for k in range(4)] for j in range(2)]
    rg_cb = [load_cols("rcb%d" % j, I["rg_conv_b"][j], NB, ds_c) for j in range(2)]
    rg_bg = [load_cols("rbg%d" % j, I["rg_b_gates"][j], 16, ds_c) for j in range(2)]
    rg_lm = [load_cols("rlm%d" % j, I["rg_lam"][j], NB, ds_c) for j in range(2)]
    s5_dd = [sb("s5d%d" % j, [128, NB], F32) for j in range(2)]
    with nc.allow_non_contiguous_dma(reason="small param"):
        for j in range(2):
            dma("sp", s5_dd[j][:], I["s5_d"][j].rearrange("(b p) -> p b", p=128), [], [R_const], ds_c)
    rg_m8sp = [sb("m8sp%d" % j, [128, NB], F32) for j in range(2)]
    A8re = [sb("A8re%d" % j, [128, 32], F32) for j in range(2)]
    A8im = [sb("A8im%d" % j, [128, 32], F32) for j in range(2)]
    A8imn = [sb("A8imn%d" % j, [128, 32], F32) for j in range(2)]
    ffn_hist_p = [sb("fhp%d" % l, [128, FB, 1, 2], F32) for l in range(4)]
    rg_hist_p = [sb("rhp%d" % j, [128, NB, 1, 3], F32) for j in range(2)]
    rg_h_p = [sb("rgh%d" % j, [128, NB, 1], F32) for j in range(2)]
    s5st_p = [sb("s5p%d" % j, [128, 32, 2, 1], F32) for j in range(2)]
    R_state = Res("state")

    PSall = kb.ps("psall", [128, 8 * 512])
    PS = [PSall[:, i * 512:(i + 1) * 512] for i in range(8)]
    RP = [Res("ps%d" % i) for i in range(8)]

    for j in range(2):
        pass
    for t in ffn_hist_p + rg_hist_p + rg_h_p + s5st_p:
        op("dve", [], [R_state], lambda e, t=t: e.memset(t[:], 0.0))

    ctx = dict(nc=nc, kb=kb, I=I, O=O, sel2=sel2, R_par=R_par, WXd=WXd, WYd=WYd, KId=KId, ident=ident, bmask=bmask,
               R_const=R_const, PS=PS, PSall=PSall, RP=RP, ffn_cw=ffn_cw, ffn_cb=ffn_cb, rg_cw=rg_cw, rg_cb=rg_cb,
               rg_bg=rg_bg, rg_m8sp=rg_m8sp, s5_dd=s5_dd, A8re=A8re, A8im=A8im, A8imn=A8imn,
               ffn_hist_p=ffn_hist_p, rg_hist_p=rg_hist_p, rg_h_p=rg_h_p, s5st_p=s5st_p,
               R_state=R_state, dbg=dbg, ds_out=kb.dsem("out"), ds_w=None, ds_wb=kb.dsem("wb"), **ctx_w)

    with ExitStack() as lds:
        with ExitStack() as nat:
            gens = [s5_prologue_loads(ctx, j, lds, nat) for j in range(2)]
            for g in gens:
                next(g)
            lds_ = [next(g) for g in gens]
            kb.barrier()
        for j in range(2):
            with ExitStack() as pes:
                ctx["issue_par_loads"] = issue_par_loads
                s5_prologue(ctx, j, pes, lds_[j])
                kb.barrier()
    kb.group_final([R_par], ds_p)
    for j in range(2):
        op("act", [R_par], [R_par], lambda e, j=j: e.activation(out=rg_m8sp[j][:], in_=rg_lm[j][:], func=AF.Exp, scale=-1.0))
        op("act", [R_par], [R_par], lambda e, j=j: e.activation(out=rg_m8sp[j][:], in_=rg_m8sp[j][:], func=AF.Ln, bias=1.0))
        op("dve", [R_par], [R_par], lambda e, j=j: e.tensor_scalar(out=rg_m8sp[j][:], in0=rg_m8sp[j][:], scalar1=-RG_C, scalar2=None, op0=ALU.mult))
    kb.barrier()
    passes = [("A", 1, 1024), ("B", 1, 1040), ("S", 16, 8)]
    for (pn, nseq, L) in passes:
        with ExitStack() as pes:
            run_pass(ctx, pn, nseq, L, pes)
            kb.barrier()
    kb.finish()


def colgroups(N, g=512):
    return [(c0, min(g, N - c0)) for c0 in range(0, N, g)]


def store_T(ctx, src_fn, nblk, m, dram_rows, stage, R_src, tag):
    kb, PS, RP, ident = ctx["kb"], ctx["PS"], ctx["RP"], ctx["ident"]
    R_stage = ctx.setdefault("stage_res", {}).setdefault(id(stage), Res("stage" + tag))
    GB = stage.shape[1] // 128
    for g0 in range(0, nblk, GB):
        ng = min(GB, nblk - g0)
        for b0 in range(g0, g0 + ng, 4):
            nb_ = min(4, g0 + ng - b0)
            bank = 6 + ((b0 // 4) % 2)
            for b in range(nb_):
                kb.op("pe", [R_src, ctx["R_const"]], [RP[bank]],
                      lambda e, b=b, b0=b0, bank=bank: e.transpose(out=PS[bank][0:m, b * 128:(b + 1) * 128], in_=src_fn(b0 + b), identity=ident[:, :]))
            kb.op("act", [RP[bank]], [R_stage],
                  lambda e, b0=b0, nb_=nb_, bank=bank, g0=g0: e.copy(out=stage[0:m, (b0 - g0) * 128:(b0 - g0 + nb_) * 128], in_=PS[bank][0:m, 0:nb_ * 128]))
        kb.dma("sp", dram_rows[:, g0 * 128:(g0 + ng) * 128], stage[0:m, 0:ng * 128], [R_stage], [], ctx["ds_out"])


def load_T(ctx, dram_rows, nblk, m, dst_fn, stage, R_dst, tag, ds):
    kb, PS, RP, ident = ctx["kb"], ctx["PS"], ctx["RP"], ctx["ident"]
    R_stage = ctx.setdefault("stage_res", {}).setdefault(id(stage), Res("lstage" + tag))
    ctx["uid"] = ctx.get("uid", 0) + 1
    ds = kb.dsem("lt%d" % ctx["uid"])
    kb.dma("sp", stage[0:m, 0:nblk * 128], dram_rows, [], [R_stage], ds)
    per = 512 // m
    gi = 0
    for b0 in range(0, nblk, per):
        nb_ = min(per, nblk - b0)
        bank = 6 + (gi % 2)
        gi += 1
        for b in range(nb_):
            kb.op("pe", [R_stage, ctx["R_const"]], [RP[bank]],
                  lambda e, b=b: e.transpose(out=PS[bank][:, b * m:(b + 1) * m], in_=stage[0:m, (b0 + b) * 128:(b0 + b + 1) * 128], identity=ident[0:m, 0:m]))
        kb.op("act", [RP[bank]], [R_dst],
              lambda e: e.copy(out=dst_fn(b0, nb_), in_=PS[bank][:, 0:nb_ * m].rearrange("p (b m) -> p b m", m=m)))


def s5_prologue_loads(ctx, j, pes, nat):
    nc, kb, I = ctx["nc"], ctx["kb"], ctx["I"]
    dma, op = kb.dma, kb.op
    PS, RP, ident, Rc = ctx["PS"], ctx["RP"], ctx["ident"], ctx["R_const"]
    sb = lambda n, s, d=F32: kb.sb("pl%d_%s" % (j, n), s, d, es=pes)
    ds = kb.dsem("pl%d" % j)
    R = Res("pl")
    Rn = Res("plnat")
    lr = sb("lr", [128, 32]); li = sb("li", [128, 32]); stp = sb("stp", [128, 32])
    Bre = sb("Bre", [128, 32, 16]); Bim = sb("Bim", [128, 32, 16])
    yield None
    sbn = lambda n, s, d=F32: kb.sb("pl%d_%s" % (j, n), s, d, es=nat)
    lrn = sbn("lrn", [32, 128]); lin = sbn("lin", [32, 128]); stn = sbn("stn", [2, 32])
    Bn = [sbn("Bn%d" % i, [32, 128, 16]) for i in range(2)]
    dma("sp", lrn[:], I["s5_lam_re"][j].rearrange("(pr g2) p -> pr (g2 p)", g2=2), [], [Rn], ds)
    dma("sp", lin[:], I["s5_lam_im"][j].rearrange("(pr g2) p -> pr (g2 p)", g2=2), [], [Rn], ds)
    with nc.allow_non_contiguous_dma(reason="tiny"):
        dma("sp", stn[:], I["s5_log_step"][j].rearrange("(pr g2) -> g2 pr", g2=2), [], [Rn], ds)
    dma("sp", Bn[0][:], I["s5_b_re"][j].rearrange("(pr g2) p h -> pr (g2 p) h", g2=2), [], [Rn], ds)
    dma("sp", Bn[1][:], I["s5_b_im"][j].rearrange("(pr g2) p h -> pr (g2 p) h", g2=2), [], [Rn], ds)
    kb.group_final([Rn], ds)
    bank = 4 + j
    op("pe", [Rn, Rc], [RP[bank]], lambda e: e.transpose(out=PS[bank][:, 0:32], in_=lrn[:, :], identity=ident[0:32, 0:32]))
    op("pe", [Rn, Rc], [RP[bank]], lambda e: e.transpose(out=PS[bank][:, 32:64], in_=lin[:, :], identity=ident[0:32, 0:32]))
    op("pe", [Rn, Rc], [RP[bank]], lambda e: e.matmul(PS[bank][:, 64:96], lhsT=ctx["sel2"][:, :], rhs=stn[:, :], start=True, stop=True))
    op("act", [RP[bank]], [R], lambda e: e.copy(out=lr[:], in_=PS[bank][:, 0:32]))
    op("act", [RP[bank]], [R], lambda e: e.copy(out=li[:], in_=PS[bank][:, 32:64]))
    op("act", [RP[bank]], [R], lambda e: e.copy(out=stp[:], in_=PS[bank][:, 64:96]))
    for i, Bd in enumerate((Bre, Bim)):
        for h in range(16):
            op("pe", [Rn, Rc], [RP[bank]], lambda e, h=h, i=i: e.transpose(out=PS[bank][:, h * 32:(h + 1) * 32], in_=Bn[i][:, :, h], identity=ident[0:32, 0:32]))
        op("act", [RP[bank]], [R], lambda e, Bd=Bd: e.copy(out=Bd[:, :, :].rearrange("p r h -> p h r"), in_=PS[bank][:, :].rearrange("p (h r) -> p h r", r=32)))
    yield dict(R=R, ds=ds, lr=lr, li=li, stp=stp, Bre=Bre, Bim=Bim)


def s5_prologue(ctx, j, pes, ld):
    nc, kb, I = ctx["nc"], ctx["kb"], ctx["I"]
    op, dma = kb.op, kb.dma
    PS, RP, ident, bmask = ctx["PS"], ctx["RP"], ctx["ident"], ctx["bmask"]
    Rc = ctx["R_const"]
    sb = lambda n, s, d=F32: kb.sb("pl%d_%s" % (j, n), s, d, es=pes)
    R, ds = ld["R"], ld["ds"]
    lr, li, stp, Bre, Bim = (ld[k] for k in ("lr", "li", "stp", "Bre", "Bim"))
    Cre = sb("Cre", [128, 32, 16]); Cim = sb("Cim", [128, 32, 16])
    Ch_re = sb("Chre", [16, 64, 64]); Ch_im = sb("Chim", [16, 64, 64])
    R_ch = Res("ch")
    ds_ch = kb.dsem("plc%d" % j)
    with nc.allow_non_contiguous_dma(reason="param layout"):
        dma("sp", Ch_re[:], I["s5_c_re"][j].rearrange("g h p -> h g p"), [], [R_ch], ds_ch)
        dma("sp", Ch_im[:], I["s5_c_im"][j].rearrange("g h p -> h g p"), [], [R_ch], ds_ch)
    kb.group_final([R_ch], ds_ch)
    if j == 1:
        ctx["issue_par_loads"]()
    for (Ch, Cd) in ((Ch_re, Cre), (Ch_im, Cim)):
        for g0 in range(0, 32, 16):
            bank = 4 + (g0 // 16)
            for pr in range(g0, g0 + 16):
                op("pe", [R_ch, Rc], [RP[bank]],
                   lambda e, pr=pr, Ch=Ch: e.transpose(out=PS[bank][:, (pr - g0) * 16:(pr - g0 + 1) * 16],
                                                       in_=Ch[0:16, 2 * pr:2 * pr + 2, :].rearrange("h g p -> h (g p)"),
                                                       identity=ident[0:16, 0:16]))
            op("act", [RP[bank]], [R], lambda e, Cd=Cd: e.copy(out=Cd[:, g0:g0 + 16, :], in_=PS[bank][:, 0:256].rearrange("p (a h) -> p a h", h=16)))

    t = lambda n: sb(n, [128, 32])
    ang = t("ang"); mag = t("mag"); sn = t("sn"); cs = t("cs"); tmp = t("tmp"); tmp2 = t("tmp2")
    abre = t("abre"); abim = t("abim"); qre = t("qre"); qim = t("qim"); ki = sb("ki", [128, 32], I32)
    V = lambda f: op("dve", [R], [R], f)
    A = lambda f: op("act", [R], [R], f)
    A(lambda e: e.activation(out=stp[:], in_=stp[:], func=AF.Exp))
    V(lambda e: e.tensor_tensor(out=ang[:], in0=li[:], in1=stp[:], op=ALU.mult))
    V(lambda e: e.tensor_tensor(out=mag[:], in0=lr[:], in1=stp[:], op=ALU.mult))
    A(lambda e: e.activation(out=mag[:], in_=mag[:], func=AF.Exp))

    def sin_of(dst, shift):
        V(lambda e: e.tensor_scalar(out=tmp[:], in0=ang[:], scalar1=shift, scalar2=1.0 / TWO_PI, op0=ALU.add, op1=ALU.mult))
        V(lambda e: e.tensor_copy(out=ki[:], in_=tmp[:]))
        V(lambda e: e.tensor_copy(out=tmp2[:], in_=ki[:]))
        V(lambda e: e.tensor_tensor(out=tmp[:], in0=tmp[:], in1=tmp2[:], op=ALU.subtract))
        V(lambda e: e.tensor_scalar(out=tmp2[:], in0=tmp[:], scalar1=0.5, scalar2=None, op0=ALU.is_gt))
        V(lambda e: e.tensor_tensor(out=tmp[:], in0=tmp[:], in1=tmp2[:], op=ALU.subtract))
        V(lambda e: e.tensor_scalar(out=tmp2[:], in0=tmp[:], scalar1=-0.5, scalar2=None, op0=ALU.is_lt))
        V(lambda e: e.tensor_tensor(out=tmp[:], in0=tmp[:], in1=tmp2[:], op=ALU.add))
        V(lambda e: e.tensor_scalar(out=tmp[:], in0=tmp[:], scalar1=TWO_PI, scalar2=math.pi, op0=ALU.mult, op1=ALU.min))
        V(lambda e: e.tensor_scalar(out=tmp[:], in0=tmp[:], scalar1=-math.pi, scalar2=None, op0=ALU.max))
        A(lambda e: e.activation(out=dst[:], in_=tmp[:], func=AF.Sin))

    sin_of(sn, 0.0)
    sin_of(cs, math.pi / 2)
    V(lambda e: e.tensor_tensor(out=abre[:], in0=mag[:], in1=cs[:], op=ALU.mult))
    V(lambda e: e.tensor_tensor(out=abim[:], in0=mag[:], in1=sn[:], op=ALU.mult))
    den = t("den"); nr = t("nr")
    V(lambda e: e.tensor_tensor(out=den[:], in0=lr[:], in1=lr[:], op=ALU.mult))
    V(lambda e: e.tensor_tensor(out=tmp[:], in0=li[:], in1=li[:], op=ALU.mult))
    V(lambda e: e.tensor_tensor(out=den[:], in0=den[:], in1=tmp[:], op=ALU.add))
    V(lambda e: e.reciprocal(out=den[:], in_=den[:]))
    V(lambda e: e.tensor_scalar(out=nr[:], in0=abre[:], scalar1=-1.0, scalar2=None, op0=ALU.add))
    V(lambda e: e.tensor_tensor(out=qre[:], in0=nr[:], in1=lr[:], op=ALU.mult))
    V(lambda e: e.tensor_tensor(out=tmp[:], in0=abim[:], in1=li[:], op=ALU.mult))
    V(lambda e: e.tensor_tensor(out=qre[:], in0=qre[:], in1=tmp[:], op=ALU.add))
    V(lambda e: e.tensor_tensor(out=qre[:], in0=qre[:], in1=den[:], op=ALU.mult))
    V(lambda e: e.tensor_tensor(out=qim[:], in0=abim[:], in1=lr[:], op=ALU.mult))
    V(lambda e: e.tensor_tensor(out=tmp[:], in0=nr[:], in1=li[:], op=ALU.mult))
    V(lambda e: e.tensor_tensor(out=qim[:], in0=qim[:], in1=tmp[:], op=ALU.subtract))
    V(lambda e: e.tensor_tensor(out=qim[:], in0=qim[:], in1=den[:], op=ALU.mult))
    pwr = sb("pwr", [128, 9, 32]); pwi = sb("pwi", [128, 9, 32])
    V(lambda e: e.memset(pwr[:, 0, :], 1.0))
    V(lambda e: e.memset(pwi[:, 0, :], 0.0))
    for k in range(1, 9):
        V(lambda e, k=k: e.tensor_tensor(out=pwr[:, k, :], in0=pwr[:, k - 1, :], in1=abre[:], op=ALU.mult))
        V(lambda e, k=k: e.tensor_tensor(out=tmp[:], in0=pwi[:, k - 1, :], in1=abim[:], op=ALU.mult))
        V(lambda e, k=k: e.tensor_tensor(out=pwr[:, k, :], in0=pwr[:, k, :], in1=tmp[:], op=ALU.subtract))
        V(lambda e, k=k: e.tensor_tensor(out=pwi[:, k, :], in0=pwr[:, k - 1, :], in1=abim[:], op=ALU.mult))
        V(lambda e, k=k: e.tensor_tensor(out=tmp[:], in0=pwi[:, k - 1, :], in1=abre[:], op=ALU.mult))
        V(lambda e, k=k: e.tensor_tensor(out=pwi[:, k, :], in0=pwi[:, k, :], in1=tmp[:], op=ALU.add))
    Rst = ctx["R_state"]
    op("dve", [R], [Rst], lambda e: e.tensor_copy(out=ctx["A8re"][j][:], in_=pwr[:, 8, :]))
    op("dve", [R], [Rst], lambda e: e.tensor_copy(out=ctx["A8im"][j][:], in_=pwi[:, 8, :]))
    op("dve", [R], [Rst], lambda e: e.tensor_scalar(out=ctx["A8imn"][j][:], in0=pwi[:, 8, :], scalar1=-1.0, scalar2=None, op0=ALU.mult))

    def bc(x2d):
        return x2d.unsqueeze(2).broadcast_to([128, 32, 16])

    T3 = lambda n: sb(n, [128, 32, 16])
    w1 = T3("w1"); w2 = T3("w2")

    def cmul(dre, dim, xre, xim, sre, sim_):
        V(lambda e: e.tensor_tensor(out=w1[:], in0=xre, in1=bc(sre), op=ALU.mult))
        V(lambda e: e.tensor_tensor(out=w2[:], in0=xim, in1=bc(sim_), op=ALU.mult))
        V(lambda e: e.tensor_tensor(out=dre, in0=w1[:], in1=w2[:], op=ALU.subtract))
        V(lambda e: e.tensor_tensor(out=w1[:], in0=xre, in1=bc(sim_), op=ALU.mult))
        V(lambda e: e.tensor_tensor(out=w2[:], in0=xim, in1=bc(sre), op=ALU.mult))
        V(lambda e: e.tensor_tensor(out=dim, in0=w1[:], in1=w2[:], op=ALU.add))

    Bbre = T3("Bbre"); Bbim = T3("Bbim")
    cmul(Bbre[:], Bbim[:], Bre[:], Bim[:], qre[:], qim[:])
    Bpre = sb("Bpre", [128, 32, 32]); Bpim = sb("Bpim", [128, 32, 32])
    V(lambda e: e.memset(Bpre[:], 0.0))
    V(lambda e: e.memset(Bpim[:], 0.0))
    for g2 in range(2):
        hp = slice(g2 * 64, (g2 + 1) * 64); hc = slice(g2 * 16, (g2 + 1) * 16)
        V(lambda e, hp=hp, hc=hc: e.tensor_copy(out=Bpre[hp, :, hc], in_=Bbre[hp, :, :]))
        V(lambda e, hp=hp, hc=hc: e.tensor_copy(out=Bpim[hp, :, hc], in_=Bbim[hp, :, :]))

    WY = sb("WY", [128, 32, 8, 2, 32], BF16)
    KI = sb("KI", [128, 8, 8, 128], BF16)
    WX = sb("WX", [128, 8, 8, 2, 128], BF16)
    V(lambda e: e.memset(WY[:].rearrange("p a b c d -> p (a b c d)"), 0.0))
    CAre = T3("CAre"); CAim = T3("CAim")
    CApre = sb("CApre", [128, 32, 32]); CApimn = sb("CApimn", [128, 32, 32])
    V(lambda e: e.memset(CApre[:], 0.0))
    V(lambda e: e.memset(CApimn[:], 0.0))
    kit = sb("kit", [128, 128])
    dcol = ctx["s5_dd"][j]
    R_o = Res("pl_out")
    R_kit = Res("kit")
    for k in range(9):
        cmul(CAre[:], CAim[:], Cre[:], Cim[:], pwr[:, k, :], pwi[:, k, :])
        for g2 in range(2):
            hp = slice(g2 * 64, (g2 + 1) * 64); hc = slice(g2 * 16, (g2 + 1) * 16)
            V(lambda e, hp=hp, hc=hc: e.tensor_copy(out=CApre[hp, :, hc], in_=CAre[hp, :, :]))
            V(lambda e, hp=hp, hc=hc: e.tensor_scalar(out=CApimn[hp, :, hc], in0=CAim[hp, :, :], scalar1=-1.0, scalar2=None, op0=ALU.mult))
            if k >= 1:
                V(lambda e, hp=hp, hc=hc, k=k: e.tensor_copy(out=WY[hp, :, k - 1, 0, hc], in_=CAre[hp, :, :]))
                V(lambda e, hp=hp, hc=hc, k=k: e.tensor_scalar(out=WY[hp, :, k - 1, 1, hc], in0=CAim[hp, :, :], scalar1=-1.0, scalar2=None, op0=ALU.mult))
        if k <= 7:
            for c in range(8):
                bank = 4 + (c % 2)
                cs4 = slice(4 * c, 4 * c + 4)
                op("pe", [R], [RP[bank]], lambda e, cs4=cs4: e.matmul(PS[bank][:, 0:128], lhsT=Bpre[:, cs4, :].rearrange("p a b -> p (a b)"),
                                                                 rhs=CApre[:, cs4, :].rearrange("p a b -> p (a b)"), start=True, stop=False))
                op("pe", [R], [RP[bank]], lambda e, cs4=cs4: e.matmul(PS[bank][:, 0:128], lhsT=Bpim[:, cs4, :].rearrange("p a b -> p (a b)"),
                                                                 rhs=CApimn[:, cs4, :].rearrange("p a b -> p (a b)"), start=False, stop=True))
                if k == 0:
                    op("dve", [RP[bank], Rc], [R_kit], lambda e, bank=bank: e.tensor_tensor(out=kit[:], in0=PS[bank][:, 0:128], in1=bmask[:], op=ALU.mult))
                    op("dve", [R_kit, Rc, ctx["R_par"]], [R_o], lambda e, c=c: e.scalar_tensor_tensor(out=KI[:, c, 0, :], in0=ident[:], scalar=dcol[:, c:c + 1], in1=kit[:], op0=ALU.mult, op1=ALU.add))
                else:
                    op("dve", [RP[bank], Rc], [R_o], lambda e, c=c, k=k, bank=bank: e.tensor_tensor(out=KI[:, c, k, :], in0=PS[bank][:, 0:128], in1=bmask[:], op=ALU.mult))
    XBre = sb("XBre", [128, 32, 32]); XBim = sb("XBim", [128, 32, 32])
    x1 = sb("x1", [128, 32, 32]); x2 = sb("x2", [128, 32, 32])
    bc32 = lambda x2d: x2d.unsqueeze(2).broadcast_to([128, 32, 32])
    for tau in range(8):
        k = 7 - tau
        V(lambda e, k=k: e.tensor_tensor(out=x1[:], in0=Bpre[:], in1=bc32(pwr[:, k, :]), op=ALU.mult))
        V(lambda e, k=k: e.tensor_tensor(out=x2[:], in0=Bpim[:], in1=bc32(pwi[:, k, :]), op=ALU.mult))
        V(lambda e: e.tensor_tensor(out=XBre[:], in0=x1[:], in1=x2[:], op=ALU.subtract))
        V(lambda e, k=k: e.tensor_tensor(out=x1[:], in0=Bpre[:], in1=bc32(pwi[:, k, :]), op=ALU.mult))
        V(lambda e, k=k: e.tensor_tensor(out=x2[:], in0=Bpim[:], in1=bc32(pwr[:, k, :]), op=ALU.mult))
        V(lambda e: e.tensor_tensor(out=XBim[:], in0=x1[:], in1=x2[:], op=ALU.add))
        for ri, XB in enumerate((XBre, XBim)):
            for c in range(8):
                bank = 4 + (c % 2)
                op("pe", [R, Rc], [RP[bank]], lambda e, c=c, XB=XB: e.transpose(out=PS[bank][:, 0:128], in_=XB[:, 4 * c:4 * c + 4, :].rearrange("p a b -> p (a b)"), identity=ident[:, :]))
                op("act", [RP[bank]], [R_o], lambda e, c=c, ri=ri, tau=tau, bank=bank: e.copy(out=WX[:, c, tau, ri, :], in_=PS[bank][:, 0:128]))
    dma("sp", ctx["WXd"][j].rearrange("p a b c d -> p (a b c d)"), WX[:].rearrange("p a b c d -> p (a b c d)"), [R, R_o], [], ds)
    dma("sp", ctx["WYd"][j].rearrange("p a b c d -> p (a b c d)"), WY[:].rearrange("p a b c d -> p (a b c d)"), [R], [], ds)
    dma("sp", ctx["KId"][j].rearrange("p a b c -> p (a b c)"), KI[:].rearrange("p a b c -> p (a b c)"), [R, R_o], [], ds)


def run_pass(ctx, pn, nseq, L, pes):
    nc, kb, I, O = ctx["nc"], ctx["kb"], ctx["I"], ctx["O"]
    op, dma = kb.op, kb.dma
    PS, RP, ident = ctx["PS"], ctx["RP"], ctx["ident"]
    Rc = ctx["R_const"]
    N = nseq * L
    NT = (N + 127) // 128
    rows = [min(128, N - t * 128) for t in range(NT)]
    NC = N // 8
    sb = lambda n, s, d=F32, es=None: kb.sb("p%s_%s" % (pn, n), s, d, es=es or pes)

    h_tok = sb("htok", [128, NT, D])
    hT = sb("hT", [128, NB, N], BF16)
    R_h = [Res("h%d" % t) for t in range(NT)]
    R_hT = [Res("hT%d" % t) for t in range(NT)]
    ds_in = kb.dsem("in" + pn)
    ds_w = kb.dsem("w" + pn)
    ds_ln = kb.dsem("ln" + pn)
    gbc = sb("gbc", [128, D]); bbc = sb("bbc", [128, D])
    R_ln = Res("ln")
    NZ = 4
    zt = [sb("zt%d" % i, [128, D]) for i in range(NZ)]
    R_zt = [Res("zt%d" % i) for i in range(NZ)]
    stat = [sb("stat%d" % i, [128, 2, 6]) for i in range(NZ)]
    mv = [sb("mv%d" % i, [128, 2]) for i in range(NZ)]
    rstd = [sb("rstd%d" % i, [128, 1]) for i in range(NZ)]
    NWS = 8 if pn == "S" else 4
    wslot = [sb("wslot%d" % i, [128, NB, 128], BF16) for i in range(NWS)]
    R_ws = [Res("ws%d" % i) for i in range(NWS)]
    ws_i = [0]
    ds_ws = [kb.dsem("ws%s%d" % (pn, i)) for i in range(NWS)]

    if pn == "A":
        dma("sp", h_tok[0:16, 0, :], I["meta_tokens"][:, :], [], [R_h[0]], ds_in)
        dma("sp", h_tok[16:128, 0, :], I["x_prompt"][0:112, :], [], [R_h[0]], ds_in)
        for t in range(1, NT):
            dma("sp", h_tok[:, t, :], I["x_prompt"][112 + 128 * (t - 1):112 + 128 * t, :], [], [R_h[t]], ds_in)
    elif pn == "B":
        for t in range(NT):
            dma("sp", h_tok[0:rows[t], t, :], I["x_prompt"][1008 + 128 * t:1008 + 128 * t + rows[t], :], [], [R_h[t]], ds_in)
    else:
        dma("sp", h_tok[:, 0, :], I["x_sample"][:, :], [], [R_h[0]], ds_in)

    kb.group_final(R_h, ds_in)

    def to_hT(t):
        r = rows[t]
        for half in range(2):
            bank = 4 + half
            for b in range(4):
                blk = half * 4 + b
                op("pe", [R_h[t], Rc], [RP[bank]], lambda e, b=b, blk=blk: e.transpose(out=PS[bank][:, b * 128:b * 128 + r], in_=h_tok[0:r, t, blk * 128:(blk + 1) * 128], identity=ident[0:r, 0:r]))
            op("act", [RP[bank]], [R_hT[t]], lambda e, half=half: e.copy(out=hT[:, half * 4:half * 4 + 4, t * 128:t * 128 + r],
                                                                     in_=PS[bank][:, :].rearrange("p (b n) -> p b n", n=128)[:, :, 0:r]))

    for t in range(NT):
        to_hT(t)

    nlayers = ctx["dbg"].get("nlayers", DEPTH) if ctx["dbg"] else DEPTH
    glist = []
    for l_ in range(nlayers):
        j_ = l_ // 2
        if l_ % 2 == 0:
            glist += [(I["s5_w_in"][j_], c * 128) for c in range(NB)]
        else:
            for c in range(NB):
                glist += [(I["rg_w_in"][j_], c * 128), (I["rg_w_in"][j_], D + c * 128)]
        for v in range(FBH):
            glist += [(I["ffn_w_up"][l_], v * 128), (I["ffn_w_up"][l_], (v + FBH) * 128)]
    wplan = {"list": glist, "issued": 0, "hooks": {}, "off": 0, "next_off": 0}

    def plan_w(blocks, hooks=None):
        wplan["off"] = wplan["next_off"]
        wplan["next_off"] = wplan["off"] + len(blocks)
        for k, f in (hooks or {}).items():
            gi = wplan["off"] + k
            if gi < wplan["issued"]:
                f()
            else:
                wplan["hooks"][gi] = f

    def write_back(k):
        s = k % NWS
        dma("sp", ctx["WBLK"][k], wslot[s][:].rearrange("p a b -> p (a b)"), [R_ws[s]], [], ctx["ds_wb"])

    def load_w_block(i, pf=NWS - 1):
        gi = wplan["off"] + i
        while wplan["issued"] < min(len(wplan["list"]), gi + pf + 1):
            k = wplan["issued"]
            src2d, c0 = wplan["list"][k]
            s = k % NWS
            if pn == "A":
                dma("pool", wslot[s][:], src2d[:, c0:c0 + 128].rearrange("(kc p) f -> p kc f", p=128), [], [R_ws[s]], ds_ws[s])
                if k >= 2:
                    write_back(k - 2)
            else:
                dma("pool", wslot[s][:].rearrange("p a b -> p (a b)"), ctx["WBLK"][k], [], [R_ws[s]], ds_ws[s])
            wplan["issued"] += 1
            if k in wplan["hooks"]:
                wplan["hooks"].pop(k)()
        return gi % NWS

    def up_matmul(s, ps_banks, extra_reads):
        for gi, (c0, cn) in enumerate(colgroups(N)):
            bank = ps_banks[gi]
            for kc in range(NB):
                op("pe", [R_ws[s]] + R_hT + extra_reads, [RP[bank]],
                   lambda e, kc=kc, bank=bank, c0=c0, cn=cn: e.matmul(PS[bank][:, 0:cn], lhsT=wslot[s][:, kc, :], rhs=hT[:, kc, c0:c0 + cn], start=(kc == 0), stop=(kc == NB - 1)))

    def layer_norm_tile(t, zi, li_, k, last):
        r = rows[t]
        z = zt[zi]
        for hh in range(2):
            op("dve", [R_zt[zi]], [R_zt[zi]], lambda e, hh=hh: e.bn_stats(out=stat[zi][0:r, hh, :], in_=z[0:r, hh * 512:(hh + 1) * 512]))
        op("dve", [R_zt[zi]], [R_zt[zi]], lambda e: e.bn_aggr(out=mv[zi][0:r, :], in_=stat[zi][0:r, :, :].rearrange("p a b -> p (a b)")))
        op("act", [R_zt[zi]], [R_zt[zi]], lambda e: e.activation(out=rstd[zi][0:r, :], in_=mv[zi][0:r, 1:2], func=AF.Sqrt, bias=LN_EPS_AP[0:r, :], scale=1.0))
        op("dve", [R_zt[zi]], [R_zt[zi]], lambda e: e.reciprocal(out=rstd[zi][0:r, :], in_=rstd[zi][0:r, :]))
        op("dve", [R_zt[zi]], [R_zt[zi]], lambda e: e.tensor_scalar(out=z[0:r, :], in0=z[0:r, :], scalar1=mv[zi][0:r, 0:1], scalar2=rstd[zi][0:r, 0:1], op0=ALU.subtract, op1=ALU.mult))
        op("pool", [R_zt[zi], R_ln], [R_zt[zi]], lambda e: e.tensor_tensor(out=z[0:r, :], in0=z[0:r, :], in1=gbc[0:r, :], op=ALU.mult))
        op("pool", [R_zt[zi], R_ln], [R_h[t]], lambda e: e.tensor_tensor(out=h_tok[0:r, t, :], in0=z[0:r, :], in1=bbc[0:r, :], op=ALU.add))
        if not last:
            return lambda: to_hT(t)
        else:
            if pn == "A":
                if t == 0:
                    dma("sp", O["y_prompt"][0:112, :], h_tok[16:128, 0, :], [R_h[t]], [], ctx["ds_out"])
                else:
                    dma("sp", O["y_prompt"][112 + 128 * (t - 1):112 + 128 * t, :], h_tok[:, t, :], [R_h[t]], [], ctx["ds_out"])
            elif pn == "B":
                dma("sp", O["y_prompt"][1008 + 128 * t:1008 + 128 * t + r, :], h_tok[0:r, t, :], [R_h[t]], [], ctx["ds_out"])
            else:
                dma("sp", O["y_sample"][:, :], h_tok[:, 0, :], [R_h[t]], [], ctx["ds_out"])
            return lambda: None

    LN_EPS_AP = sb("lneps", [128, 1])
    op("dve", [], [Rc], lambda e: e.memset(LN_EPS_AP[:], LN_EPS))
    one_ap = sb("one", [128, 1])
    op("dve", [], [Rc], lambda e: e.memset(one_ap[:], 1.0))

    def load_ln(li_, k):
        dma("sp", gbc[:], I["ln_g"][li_, k].partition_broadcast(128), [], [R_ln], ds_ln)
        dma("sp", bbc[:], I["ln_b"][li_, k].partition_broadcast(128), [], [R_ln], ds_ln)

    env = dict(ctx=ctx, pn=pn, nseq=nseq, L=L, N=N, NT=NT, rows=rows, NC=NC, sb=sb, h_tok=h_tok, hT=hT,
               R_h=R_h, R_hT=R_hT, ds_w=ds_w, ds_in=ds_in, zt=zt, R_zt=R_zt, load_w_block=load_w_block, plan_w=plan_w,
               up_matmul=up_matmul, layer_norm_tile=layer_norm_tile, one_ap=one_ap, load_ln=load_ln, wslot=wslot, R_ws=R_ws)

    for li_ in range(nlayers):
        j = li_ // 2
        with ExitStack() as les:
            if li_ % 2 == 0:
                s5_layer(env, li_, j, les)
            else:
                rg_layer(env, li_, j, les)
            kb.barrier()
        with ExitStack() as les:
            ffn_layer(env, li_, les, last=(li_ == nlayers - 1))
            if pn == "A" and li_ == nlayers - 1:
                for k in range(max(0, len(glist) - 2), len(glist)):
                    write_back(k)
            kb.barrier()


def conv_taps(kb, xs, acc, nseq, L, K, wcols, bcol, R_main, R_halo, R_acc):
    kb.op("act", [R_main], [R_acc], lambda e: e.activation(out=acc[:, :, :], in_=xs[:, :, K - 1:K - 1 + L], func=AF.Identity, scale=wcols[K - 1], bias=bcol))
    for k in range(K - 1):
        kb.op("dve", [R_main, R_halo, R_acc], [R_acc], lambda e, k=k: e.scalar_tensor_tensor(out=acc[:, :, :], in0=xs[:, :, k:k + L], scalar=wcols[k], in1=acc[:, :, :], op0=ALU.mult, op1=ALU.add))


def ffn_layer(env, li_, les, last):
    ctx = env["ctx"]; kb = ctx["kb"]; nc = ctx["nc"]; I = ctx["I"]; O = ctx["O"]
    op, dma = kb.op, kb.dma
    PS, RP = ctx["PS"], ctx["RP"]
    pn, nseq, L, N, NT, rows = env["pn"], env["nseq"], env["L"], env["N"], env["NT"], env["rows"]
    sb = lambda n, s, d=F32: env["sb"]("f%d_%s" % (li_, n), s, d, es=les)
    h_tok, hT = env["h_tok"], env["hT"]
    actT = sb("actT", [128, FBH, N], BF16)
    R_act = [Res("act%d" % v) for v in range(FBH)]
    wd = sb("wd", [128, FBH, D], BF16)
    R_wd = [Res("wd%d" % v) for v in range(FBH)]
    xs = [sb("xs%d" % i, [128, nseq, 2 + L]) for i in range(2)]
    acc4 = [sb("acc%d" % i, [128, nseq, L]) for i in range(4)]
    R_xs = [Res("xs%d" % i) for i in range(2)]
    R_xh = [Res("xh%d" % i) for i in range(2)]
    R_acc4 = [Res("acc%d" % i) for i in range(4)]
    cw, cb = ctx["ffn_cw"][li_], ctx["ffn_cb"][li_]
    Rst = ctx["R_state"]
    if nseq == 1:
        hist = ctx["ffn_hist_p"][li_]
        R_hist = Rst
    else:
        hist = sb("hist", [128, FB, nseq, 2])
        R_hist = Res("hist")
        stage = sb("hstage", [32, FF2])
        load_T(ctx, I["state_ffn_conv"][li_], FB, 32, lambda b0, nb_: hist[:, b0:b0 + nb_, :, :].rearrange("p b s k -> p b (s k)"), stage, R_hist, "fh", env["ds_in"])
    env["load_ln"](li_, 1)
    def wd_hook(v):
        def f():
            if pn == "A":
                dma("pool", wd[:, v, :], I["ffn_w_down"][li_][v * 128:(v + 1) * 128, :], [], [R_wd[v]], env["ds_w"])
            else:
                dma("pool", wd[:, v, :], ctx["WDd"][li_][:, v * D:(v + 1) * D], [], [R_wd[v]], env["ds_w"])
            if v == FBH - 1:
                kb.group_final(R_wd, env["ds_w"])
        return f
    blocks = []
    for v in range(FBH):
        blocks += [(I["ffn_w_up"][li_], v * 128), (I["ffn_w_up"][li_], (v + FBH) * 128)]
    env["plan_w"](blocks, {2 * v + 1: wd_hook(v) for v in range(FBH)})
    ngrp = len(colgroups(N))
    banksets = [[0, 1, 2][:ngrp], [3, 4, 5][:ngrp]]
    bi = 0
    for v in range(FBH):
        acc = acc4[2 * (v % 2):2 * (v % 2) + 2]
        R_acc = R_acc4[2 * (v % 2):2 * (v % 2) + 2]
        for which in range(2):
            blk = v + which * FBH
            s = env["load_w_block"](2 * v + which)
            banks = banksets[bi % 2]
            bi += 1
            env["up_matmul"](s, banks, [])
            x = xs[which]
            op("act", [R_hist], [R_xh[which]], lambda e, x=x, blk=blk: e.copy(out=x[:, :, 0:2], in_=hist[:, blk, :, :]))
            if nseq == 1:
                op("act", [RP[b_] for b_ in banks], [R_xs[which]], lambda e, x=x, banks=banks: e.copy(out=x[:, 0, 2:2 + N], in_=ctx["PSall"][:, banks[0] * 512:banks[0] * 512 + N]))
            else:
                op("act", [RP[banks[0]]], [R_xs[which]], lambda e, x=x, banks=banks: e.copy(out=x[:, :, 2:2 + L], in_=PS[banks[0]][:, 0:N].rearrange("p (s t) -> p s t", t=L)))
            op("act", [R_xs[which]], [R_hist], lambda e, x=x, blk=blk: e.copy(out=hist[:, blk, :, :], in_=x[:, :, L:L + 2]))
            conv_taps(kb, x, acc[which], nseq, L, 3, [cw[k][:, blk:blk + 1] for k in range(3)], cb[:, blk:blk + 1], R_xs[which], R_xh[which], R_acc[which])
        op("act", [R_acc[1]], [R_acc[1]], lambda e, acc=acc: e.activation(out=acc[1][:, :, :], in_=acc[1][:, :, :], func=AF.Gelu_apprx_tanh))
        op("dve", [R_acc[0], R_acc[1]], [R_act[v]], lambda e, v=v, acc=acc: e.tensor_tensor(out=actT[:, v, :], in0=acc[0][:, :, :].rearrange("p s t -> p (s t)"), in1=acc[1][:, :, :].rearrange("p s t -> p (s t)"), op=ALU.mult))
    if pn == "A":
        dma("sp", ctx["WDd"][li_], wd[:].rearrange("p a b -> p (a b)"), R_wd, [], ctx["ds_wb"])
    if pn != "A":
        stg = sb("ostage", [32, 256 if nseq == 1 else 2048])
        m = 2 * nseq
        store_T(ctx, lambda b: hist[:, b, :, :].rearrange("p s k -> p (s k)"), FB, m,
                (O["ffn_conv_p"] if nseq == 1 else O["ffn_conv_s"])[li_], stg, R_hist, "fo")
    pending = None
    for t in range(NT):
        r = rows[t]
        pb = [0, 1] if t % 2 == 0 else [2, 3]
        for v in range(FBH):
            for hh in range(2):
                op("pe", [R_act[v], R_wd[v]], [RP[pb[hh]]], lambda e, v=v, hh=hh: e.matmul(PS[pb[hh]][0:r, :], lhsT=actT[:, v, t * 128:t * 128 + r], rhs=wd[:, v, hh * 512:(hh + 1) * 512], start=(v == 0), stop=(v == FBH - 1)))
        zi = t % 4
        z = env["zt"][zi]
        for hh in range(2):
            op("dve", [env["R_h"][t], RP[pb[hh]]], [env["R_zt"][zi]], lambda e, hh=hh: e.scalar_tensor_tensor(out=z[0:r, hh * 512:(hh + 1) * 512], in0=h_tok[0:r, t, hh * 512:(hh + 1) * 512], scalar=DN_ALPHA, in1=PS[pb[hh]][0:r, :], op0=ALU.mult, op1=ALU.add))
        if pending:
            pending()
        pending = env["layer_norm_tile"](t, zi, li_, 1, last)
    pending()


def rg_layer(env, li_, j, les):
    ctx = env["ctx"]; kb = ctx["kb"]; nc = ctx["nc"]; I = ctx["I"]; O = ctx["O"]
    op, dma = kb.op, kb.dma
    PS, RP = ctx["PS"], ctx["RP"]
    pn, nseq, L, N, NT, rows = env["pn"], env["nseq"], env["L"], env["N"], env["NT"], env["rows"]
    sb = lambda n, s, d=F32: env["sb"]("r%d_%s" % (li_, n), s, d, es=les)
    h_tok, hT = env["h_tok"], env["hT"]
    Rst = ctx["R_state"]
    hgT = sb("hgT", [128, NB, N], BF16)
    R_hg = [Res("hg%d" % c) for c in range(NB)]
    wg = sb("wg", [128, 4, 2, 512], BF16)
    R_wg = Res("wg")
    wo = sb("wo", [128, NB, D], BF16)
    R_wo = Res("wo")
    if pn == "A":
        for n in range(4):
            dma("pool", wg[:, n, :, :], I["rg_w_gates"][j, n].rearrange("(kc p) f -> p kc f", p=128), [], [R_wg], env["ds_w"])
    else:
        dma("pool", wg[:].rearrange("p a b c -> p (a b c)"), ctx["RGWGd"][j], [], [R_wg], env["ds_w"])
    kb.group_final([R_wg], env["ds_w"])
    env["load_ln"](li_, 0)
    if nseq == 1:
        hist = ctx["rg_hist_p"][j]; hst = ctx["rg_h_p"][j]; R_hist = Rst
    else:
        hist = sb("hist", [128, NB, nseq, 3]); hst = sb("hst", [128, NB, nseq]); R_hist = Res("rhist")
        stage = sb("stage", [48, D])
        load_T(ctx, I["state_rg_conv"][j], NB, 48, lambda b0, nb_: hist[:, b0:b0 + nb_, :, :].rearrange("p b s k -> p b (s k)"), stage, R_hist, "rc", env["ds_in"])
        stage2 = sb("stage2", [16, D])
        load_T(ctx, I["state_rg_h"][j], NB, 16, lambda b0, nb_: hst[:, b0:b0 + nb_, :], stage2, R_hist, "rh", env["ds_in"])
    xs = [sb("xs%d" % i, [128, nseq, 3 + L]) for i in range(2)]
    xc = [[sb("xc%d_%d" % (b, i), [128, nseq, L]) for i in range(2)] for b in range(2)]
    xcb = [[sb("xcb%d_%d" % (b, i), [128, N], BF16) for i in range(2)] for b in range(2)]
    gl = [[sb("gl%d_%d" % (b, i), [128, N]) for i in range(2)] for b in range(2)]
    work = [[sb("wk%d_%d" % (b, i), [128, nseq, L]) for i in range(3)] for b in range(2)]
    R_xs = [Res() for _ in range(2)]; R_xh = [Res() for _ in range(2)]
    R_xc = [[Res() for _ in range(2)] for _ in range(2)]; R_gl = [[Res() for _ in range(2)] for _ in range(2)]
    R_wk = [Res("rgwork0"), Res("rgwork1")]
    cw, cb, bg, m8 = ctx["rg_cw"][j], ctx["rg_cb"][j], ctx["rg_bg"][j], ctx["rg_m8sp"][j]
    one_ap = env["one_ap"]
    ngrp = len(colgroups(N))
    banksets = [[0, 1, 2][:ngrp], [3, 4, 5][:ngrp]]
    bi = [0]
    blocks = []
    for c in range(NB):
        blocks += [(I["rg_w_in"][j], c * 128), (I["rg_w_in"][j], D + c * 128)]
    env["plan_w"](blocks)
    flat = lambda t: t[:, :, :].rearrange("p s t -> p (s t)")

    def stage_proj(n):
        pb_ = n % 2
        for q in range(2):
            c = 2 * n + q
            s = env["load_w_block"](2 * c)
            banks = banksets[bi[0] % 2]; bi[0] += 1
            env["up_matmul"](s, banks, [])
            op("act", [RP[b_] for b_ in banks], [R_gl[pb_][q]], lambda e, q=q, banks=banks: e.activation(out=gl[pb_][q][:, 0:N], in_=ctx["PSall"][:, banks[0] * 512:banks[0] * 512 + N], func=AF.Gelu_apprx_tanh))
            s = env["load_w_block"](2 * c + 1)
            banks = banksets[bi[0] % 2]; bi[0] += 1
            env["up_matmul"](s, banks, [])
            x = xs[q]
            op("act", [R_hist], [R_xh[q]], lambda e, x=x, c=c: e.copy(out=x[:, :, 0:3], in_=hist[:, c, :, :]))
            if nseq == 1:
                op("act", [RP[b_] for b_ in banks], [R_xs[q]], lambda e, x=x, banks=banks: e.copy(out=x[:, 0, 3:3 + N], in_=ctx["PSall"][:, banks[0] * 512:banks[0] * 512 + N]))
            else:
                op("act", [RP[banks[0]]], [R_xs[q]], lambda e, x=x, banks=banks: e.copy(out=x[:, :, 3:3 + L], in_=PS[banks[0]][:, 0:N].rearrange("p (s t) -> p s t", t=L)))
            op("act", [R_xs[q]], [R_hist], lambda e, x=x, c=c: e.copy(out=hist[:, c, :, :], in_=x[:, :, L:L + 3]))
            conv_taps(kb, x, xc[pb_][q], nseq, L, 4, [cw[k][:, c:c + 1] for k in range(4)], cb[:, c:c + 1], R_xs[q], R_xh[q], R_xc[pb_][q])
            op("dve", [R_xc[pb_][q]], [R_xc[pb_][q]], lambda e, q=q: e.tensor_copy(out=xcb[pb_][q][:, :], in_=flat(xc[pb_][q])))

    def stage_gate(n):
        pb_ = n % 2
        for q in range(2):
            c = 2 * n + q
            T1, T2, T3 = work[c % 2]
            Rw = R_wk[c % 2]
            pr_ = banksets[0]; pi_ = banksets[1]
            for (dst_banks, off) in ((pr_, q * 128), (pi_, 256 + q * 128)):
                for gi, (c0, cn) in enumerate(colgroups(N)):
                    for kc in range(2):
                        op("pe", [R_wg, R_xc[pb_][kc]], [RP[dst_banks[gi]]], lambda e, kc=kc, gi=gi, c0=c0, cn=cn, off=off, dst_banks=dst_banks: e.matmul(PS[dst_banks[gi]][:, 0:cn], lhsT=wg[:, n, kc, off:off + 128], rhs=xcb[pb_][kc][:, c0:c0 + cn], start=(kc == 0), stop=(kc == 1)))
            op("act", [RP[b_] for b_ in pr_], [Rw], lambda e, c=c: e.activation(out=flat(T1)[:, 0:N], in_=ctx["PSall"][:, pr_[0] * 512:pr_[0] * 512 + N], func=AF.Sigmoid, bias=bg[:, c:c + 1], scale=1.0))
            op("act", [RP[b_] for b_ in pi_], [Rw], lambda e, c=c: e.activation(out=flat(T2)[:, 0:N], in_=ctx["PSall"][:, pi_[0] * 512:pi_[0] * 512 + N], func=AF.Sigmoid, bias=bg[:, 8 + c:8 + c + 1], scale=1.0))
            op("act", [Rw], [Rw], lambda e, c=c: e.activation(out=flat(T1), in_=flat(T1), func=AF.Exp, scale=m8[:, c:c + 1]))
            op("act", [Rw], [Rw], lambda e: e.activation(out=flat(T3), in_=flat(T1), func=AF.Square))
            op("act", [Rw], [Rw], lambda e: e.activation(out=flat(T3), in_=flat(T3), func=AF.Sqrt, scale=-1.0, bias=one_ap[:, :]))
            op("dve", [Rw, R_xc[pb_][q]], [Rw], lambda e, q=q: e.tensor_tensor(out=flat(T2), in0=flat(T2), in1=flat(xc[pb_][q]), op=ALU.mult))
            op("dve", [Rw], [Rw], lambda e: e.tensor_tensor(out=flat(T2), in0=flat(T2), in1=flat(T3), op=ALU.mult))
            for s_ in range(nseq):
                op("dve", [Rw, R_hist], [Rw], lambda e, s_=s_, c=c: e.tensor_tensor_scan(out=T3[:, s_, :], data0=T1[:, s_, :], data1=T2[:, s_, :], initial=hst[:, c, s_:s_ + 1], op0=ALU.mult, op1=ALU.add))
            op("dve", [Rw], [R_hist], lambda e, c=c: e.tensor_copy(out=hst[:, c, :], in_=T3[:, :, L - 1]))
            op("dve", [Rw, R_gl[pb_][q]], [R_hg[c]], lambda e, c=c, q=q: e.tensor_tensor(out=hgT[:, c, :], in0=flat(T3), in1=gl[pb_][q][:, :], op=ALU.mult))

    for n in range(4):
        stage_proj(n)
        if n == 1:
            if pn == "A":
                dma("pool", wo[:], I["rg_w_out"][j].rearrange("(kc p) f -> p kc f", p=128), [], [R_wo], env["ds_w"])
            else:
                dma("pool", wo[:].rearrange("p a b -> p (a b)"), ctx["RGWOd"][j], [], [R_wo], env["ds_w"])
            kb.group_final([R_wo], env["ds_w"])
        if n > 0:
            stage_gate(n - 1)
    stage_gate(3)
    if pn == "A":
        dma("sp", ctx["RGWGd"][j], wg[:].rearrange("p a b c -> p (a b c)"), [R_wg], [], ctx["ds_wb"])
        dma("sp", ctx["RGWOd"][j], wo[:].rearrange("p a b -> p (a b)"), [R_wo], [], ctx["ds_wb"])
    if pn != "A":
        stg = sb("ostage", [48, 256])
        store_T(ctx, lambda b: hist[:, b, :, :].rearrange("p s k -> p (s k)"), NB, 3 * nseq,
                (O["rg_conv_p"] if nseq == 1 else O["rg_conv_s"])[j], stg, R_hist, "ro")
        store_T(ctx, lambda b: hst[:, b, :], NB, nseq, (O["rg_h_p"] if nseq == 1 else O["rg_h_s"])[j], stg, R_hist, "rho")
    pending = None
    for t in range(NT):
        r = rows[t]
        pb = [0, 1] if t % 2 == 0 else [2, 3]
        for kc in range(NB):
            for hh in range(2):
                op("pe", [R_hg[kc], R_wo], [RP[pb[hh]]], lambda e, kc=kc, hh=hh: e.matmul(PS[pb[hh]][0:r, :], lhsT=hgT[:, kc, t * 128:t * 128 + r], rhs=wo[:, kc, hh * 512:(hh + 1) * 512], start=(kc == 0), stop=(kc == NB - 1)))
        zi = t % 4
        z = env["zt"][zi]
        for hh in range(2):
            op("dve", [env["R_h"][t], RP[pb[hh]]], [env["R_zt"][zi]], lambda e, hh=hh: e.scalar_tensor_tensor(out=z[0:r, hh * 512:(hh + 1) * 512], in0=h_tok[0:r, t, hh * 512:(hh + 1) * 512], scalar=DN_ALPHA, in1=PS[pb[hh]][0:r, :], op0=ALU.mult, op1=ALU.add))
        if pending:
            pending()
        pending = env["layer_norm_tile"](t, zi, li_, 0, False)
    pending()


def ONE_AP(env):
    if "one_ap" not in env:
        kb = env["ctx"]["kb"]
        t = env["sb"]("one", [128, 1])
        kb.op("dve", [], [env["ctx"]["R_const"]], lambda e: e.memset(t[:], 1.0))
        env["one_ap"] = t
    return env["one_ap"]


def s5_layer(env, li_, j, les):
    ctx = env["ctx"]; kb = ctx["kb"]; nc = ctx["nc"]; I = ctx["I"]; O = ctx["O"]
    op, dma = kb.op, kb.dma
    PS, RP = ctx["PS"], ctx["RP"]
    pn, nseq, L, N, NT, rows, NC = env["pn"], env["nseq"], env["L"], env["N"], env["NT"], env["rows"], env["NC"]
    sb = lambda n, s, d=F32: env["sb"]("s%d_%s" % (li_, n), s, d, es=les)
    h_tok, hT = env["h_tok"], env["hT"]
    Rst = ctx["R_state"]
    uT = sb("uT", [128, NB, N], BF16)
    gyT = uT
    R_u = [Res() for _ in range(NB)]
    R_gy = R_u
    R_X = Res("Xs")
    Sinb = sb("Sinb", [128, 32, 2, NC], BF16)
    R_S = Res("Sinb")
    wo = sb("wo", [128, NB, 2 * D], BF16)
    R_wo = Res("wo")
    env["load_ln"](li_, 0)
    if nseq == 1:
        st = ctx["s5st_p"][j]; R_st = Rst
    else:
        st = sb("st", [128, 32, 2, nseq]); R_st = Res("s5st")
        stage = sb("stage", [16, 4096])
        for ri, nm in enumerate(("state_s5_re", "state_s5_im")):
            load_T(ctx, I[nm][j], 32, 16, lambda b0, nb_, ri=ri: st[:, b0:b0 + nb_, ri, :], stage, R_st, "s5" + str(ri), env["ds_in"])
    ngrp = len(colgroups(N))
    banksets = [[0, 1, 2][:ngrp], [3, 4, 5][:ngrp]]
    env["plan_w"]([(I["s5_w_in"][j], c * 128) for c in range(NB)])
    for c in range(NB):
        s = env["load_w_block"](c)
        banks = banksets[c % 2]
        env["up_matmul"](s, banks, [])
        op("act", [RP[b_] for b_ in banks], [R_u[c]], lambda e, c=c, banks=banks: e.copy(out=uT[:, c, 0:N], in_=ctx["PSall"][:, banks[0] * 512:banks[0] * 512 + N]))
    if pn == "A":
        dma("pool", wo[:, 0:4, :], I["s5_w_out"][j][0:512, :].rearrange("(kc p) f -> p kc f", p=128), [], [R_wo], env["ds_w"])
        dma("pool", wo[:, 4:8, :], I["s5_w_out"][j][512:1024, :].rearrange("(kc p) f -> p kc f", p=128), [], [R_wo], env["ds_w"])
        kb.group_final([R_wo], env["ds_w"])
    else:
        dma("pool", wo[:].rearrange("p a b -> p (a b)"), ctx["S5WOd"][j], [], [R_wo], env["ds_w"])
        kb.group_final([R_wo], env["ds_w"])
    ies = ExitStack()
    sbo = sb
    sb = lambda n, s_, d=F32: env["sb"]("s%d_%s" % (li_, n), s_, d, es=ies)
    Xs = sb("Xs", [128, 32, 2, NC])
    WXs = [sb("WX%d" % i, [128, 8, 2, 128], BF16) for i in range(2)]
    R_WX = [Res() for _ in range(2)]
    ds_wx = [kb.dsem("s5x%s%d_%d" % (pn, li_, i)) for i in range(2)]
    ds_kw = [kb.dsem("s5k%s%d_%d" % (pn, li_, i)) for i in range(2)]
    for c in range(NB):
        wi = c % 2
        dma("sp", WXs[wi][:], ctx["WXd"][j][:, c], [], [R_WX[wi]], ds_wx[wi])
        for ri in range(2):
            for tau in range(8):
                for pr in range(4):
                    bank = pr
                    rs = slice(32 * pr, 32 * pr + 32)
                    op("pe", [R_WX[wi], R_u[c]], [RP[bank]], lambda e, ri=ri, tau=tau, rs=rs, bank=bank, wi=wi, c=c, pr=pr: e.matmul(
                        PS[bank][:, ri * 256:ri * 256 + NC], lhsT=WXs[wi][rs, tau, ri, :],
                        rhs=uT[rs, c, :].rearrange("p (n t) -> p n t", t=8)[:, :, tau], start=(tau == 0), stop=(tau == 7), tile_position=(32 * pr, 0)))
        for pr in range(4):
            bank = pr
            op("act", [RP[bank]], [R_X], lambda e, bank=bank, c=c, pr=pr: e.copy(out=Xs[:, 4 * c + pr, :, :], in_=PS[bank][:, :].rearrange("p (r n) -> p r n", n=256)[:, :, 0:NC]))
    a8r, a8i, a8in = ctx["A8re"][j], ctx["A8im"][j], ctx["A8imn"][j]
    Mrot = sb("Mrot", [128, 32, 2, 2]); prod = sb("prod", [128, 32, 2, 2]); t1 = sb("t1", [128, 32, 2])
    op("dve", [Rst], [R_X], lambda e: e.tensor_copy(out=Mrot[:, :, 0, 0], in_=a8r[:, :]))
    op("dve", [Rst], [R_X], lambda e: e.tensor_copy(out=Mrot[:, :, 1, 1], in_=a8r[:, :]))
    op("dve", [Rst], [R_X], lambda e: e.tensor_copy(out=Mrot[:, :, 0, 1], in_=a8in[:, :]))
    op("dve", [Rst], [R_X], lambda e: e.tensor_copy(out=Mrot[:, :, 1, 0], in_=a8i[:, :]))
    if nseq == 1:
        op("act", [R_st], [R_S], lambda e: e.copy(out=Sinb[:, :, :, 0], in_=st[:, :, :, 0]))
    else:
        op("act", [R_st], [R_S], lambda e: e.copy(out=Sinb[:, :, :, :], in_=st[:, :, :, :]))
    nsteps = NC if nseq == 1 else nseq
    for c in range(nsteps):
        if nseq == 1:
            prev = st[:, :, :, 0] if c == 0 else Xs[:, :, :, c - 1]
        else:
            prev = st[:, :, :, c]
        cur = Xs[:, :, :, c]
        rd = [R_X, R_st, Rst]
        pb_ = prev.unsqueeze(2).broadcast_to([128, 32, 2, 2])
        op("dve", rd, [R_X], lambda e, pb_=pb_: e.tensor_tensor(out=prod[:], in0=pb_, in1=Mrot[:], op=ALU.mult))
        op("dve", rd, [R_X], lambda e: e.tensor_tensor(out=t1[:], in0=prod[:, :, :, 0], in1=prod[:, :, :, 1], op=ALU.add))
        op("dve", rd, [R_X], lambda e, cur=cur: e.tensor_tensor(out=cur, in0=cur, in1=t1[:], op=ALU.add))
    if nseq == 1:
        op("act", [R_X], [R_S], lambda e: e.copy(out=Sinb[:, :, :, 1:NC], in_=Xs[:, :, :, 0:NC - 1]))
        op("dve", [R_X], [R_st], lambda e: e.tensor_copy(out=st[:, :, :, 0], in_=Xs[:, :, :, NC - 1]))
    else:
        op("dve", [R_X], [R_st], lambda e: e.tensor_copy(out=st[:, :, :, :], in_=Xs[:, :, :, :]))
    if pn != "A":
        stg = sb("ostage", [16, 2048])
        for ri, nm in enumerate(("s5_re", "s5_im")):
            store_T(ctx, lambda b, ri=ri: st[:, b, ri, :], 32, nseq, O[nm + ("_p" if nseq == 1 else "_s")][j], stg, R_st, "s5o%d" % ri)
    kb.barrier()
    ies.close()
    sb = sbo
    KIs = [sb("KI%d" % i, [128, 8, 128], BF16) for i in range(2)]
    WYs = [sb("WY%d" % i, [128, 4, 8, 2, 32], BF16) for i in range(2)]
    R_KI = [Res() for _ in range(2)]
    R_WY = [Res() for _ in range(2)]
    cg = colgroups(NC, 64)
    for c in range(NB):
        wi = c % 2
        dma("sp", KIs[wi][:], ctx["KId"][j][:, c], [], [R_KI[wi]], ds_kw[wi])
        dma("sp", WYs[wi][:], ctx["WYd"][j][:, 4 * c:4 * c + 4], [], [R_WY[wi]], ds_kw[wi])
        kb.group_final([R_KI[wi], R_WY[wi]], ds_kw[wi])
        banks = banksets[c % 2]
        for gi, (n0, nn) in enumerate(cg):
            bank = banks[gi]
            uv = uT[:, c, n0 * 8:(n0 + nn) * 8].rearrange("p (n t) -> p t n", t=8)
            for lag in range(8):
                op("pe", [R_KI[wi], R_u[c]], [RP[bank]], lambda e, lag=lag, bank=bank, nn=nn, uv=uv, wi=wi: e.matmul(
                    PS[bank][:, lag * nn:8 * nn], lhsT=KIs[wi][:, lag, :], rhs=uv[:, 0:8 - lag, :], start=(lag == 0), stop=False))
            for tau in range(8):
                for ri in range(2):
                    for pr in range(4):
                        lastmm = (tau == 7 and ri == 1)
                        op("pe", [R_WY[wi], R_S], [RP[bank]], lambda e, pr=pr, tau=tau, ri=ri, bank=bank, wi=wi, n0=n0, nn=nn, c=c, lastmm=lastmm: e.matmul(
                            PS[bank][32 * pr:32 * pr + 32, tau * nn:(tau + 1) * nn], lhsT=WYs[wi][:, pr, tau, ri, :], rhs=Sinb[:, 4 * c + pr, ri, n0:n0 + nn], start=False, stop=lastmm, tile_position=(0, 32 * pr)))
            op("act", [RP[bank]], [R_gy[c]], lambda e, bank=bank, n0=n0, nn=nn, c=c: e.activation(out=gyT[:, c, n0 * 8:(n0 + nn) * 8].rearrange("p (n t) -> p n t", t=8), in_=PS[bank][:, 0:nn * 8].rearrange("p (t n) -> p n t", t=8), func=AF.Gelu_apprx_tanh))
    if pn == "A":
        dma("sp", ctx["S5WOd"][j], wo[:].rearrange("p a b -> p (a b)"), [R_wo], [], ctx["ds_wb"])
    sgs = [sb("sg%d" % i, [128, D]) for i in range(2)]
    R_sgs = [Res("sg%d" % i) for i in range(2)]
    pending = None
    for t in range(NT):
        sg = sgs[t % 2]
        R_sg = R_sgs[t % 2]
        r = rows[t]
        pb = [2, 3, 0, 1]
        for qq in (2, 3, 0, 1):
            for kc in range(NB):
                op("pe", [R_gy[kc], R_wo], [RP[pb[qq]]], lambda e, kc=kc, qq=qq: e.matmul(PS[pb[qq]][0:r, :], lhsT=gyT[:, kc, t * 128:t * 128 + r], rhs=wo[:, kc, qq * 512:(qq + 1) * 512], start=(kc == 0), stop=(kc == NB - 1)))
        zi = t % 4
        z = env["zt"][zi]
        for hh in range(2):
            op("act", [RP[pb[2 + hh]]], [R_sg], lambda e, hh=hh: e.activation(out=sg[0:r, hh * 512:(hh + 1) * 512], in_=PS[pb[2 + hh]][0:r, :], func=AF.Sigmoid))
            op("dve", [R_sg, RP[pb[hh]]], [R_sg], lambda e, hh=hh: e.tensor_tensor(out=sg[0:r, hh * 512:(hh + 1) * 512], in0=sg[0:r, hh * 512:(hh + 1) * 512], in1=PS[pb[hh]][0:r, :], op=ALU.mult))
            op("dve", [env["R_h"][t], R_sg], [env["R_zt"][zi]], lambda e, hh=hh: e.scalar_tensor_tensor(out=z[0:r, hh * 512:(hh + 1) * 512], in0=h_tok[0:r, t, hh * 512:(hh + 1) * 512], scalar=DN_ALPHA, in1=sg[0:r, hh * 512:(hh + 1) * 512], op0=ALU.mult, op1=ALU.add))
        if pending:
            pending()
        pending = env["layer_norm_tile"](t, zi, li_, 0, False)
    pending()


_NC_CACHE = {}


def _get_nc(dbg=None):
    key = repr(dbg)
    if key not in _NC_CACHE:
        nc = bass.Bass("TRN2", target_bir_lowering=False)
        build(nc, dbg)
        _NC_CACHE[key] = nc
    return _NC_CACHE[key]


def kernel(dbg=None, **inp):
    f = lambda a: np.ascontiguousarray(np.asarray(a, dtype=np.float32))
    ident = np.eye(128, dtype=np.float32)
    bmask = np.kron(np.eye(4, dtype=np.float32), np.ones((32, 32), np.float32))
    wnames = ["meta_tokens", "s5_w_in", "s5_lam_re", "s5_lam_im", "s5_log_step", "s5_b_re", "s5_b_im", "s5_c_re",
              "s5_c_im", "s5_d", "s5_w_out", "rg_w_in", "rg_conv_w", "rg_conv_b", "rg_w_gates", "rg_b_gates",
              "rg_lam", "rg_w_out", "ffn_w_up", "ffn_conv_w", "ffn_conv_b", "ffn_w_down", "ln_g", "ln_b"]
    shared = {n: f(inp[n]) for n in wnames}
    shared["ident"] = ident
    shared["bmask"] = bmask
    shared["sel2"] = np.kron(np.eye(2, dtype=np.float32), np.ones((1, 64), np.float32))
    in_maps = []
    for c in range(8):
        m = dict(shared)
        sl = slice(16 * c, 16 * c + 16)
        m["x_prompt"] = f(inp["x_prompt"][c])
        m["x_sample"] = f(inp["x_sample"][sl]).reshape(128, D)
        m["state_s5_re"] = f(inp["state_s5_re"][:, sl]).reshape(2, 16, 4096)
        m["state_s5_im"] = f(inp["state_s5_im"][:, sl]).reshape(2, 16, 4096)
        m["state_rg_h"] = f(inp["state_rg_h"][:, sl])
        m["state_rg_conv"] = f(inp["state_rg_conv"][:, sl]).reshape(2, 48, D)
        m["state_ffn_conv"] = f(inp["state_ffn_conv"][:, sl]).reshape(4, 32, FF2)
        in_maps.append(m)
    nc = _get_nc(dbg)
    res = run_bass_kernel_spmd(nc, in_maps, core_ids=list(range(8)))
    R = res.results
    cat = lambda k, ax: np.concatenate([np.asarray(R[c][k]) for c in range(8)], axis=ax)
    y_prompt = np.stack([np.asarray(R[c]["y_prompt"]) for c in range(8)], 0)
    y_sample = cat("y_sample", 0).reshape(128, 8, D)
    s5_re_p = cat("s5_re_p", 1).reshape(2, 8, 64, 64)
    s5_im_p = cat("s5_im_p", 1).reshape(2, 8, 64, 64)
    rg_h_p = cat("rg_h_p", 1).reshape(2, 8, D)
    rg_conv_p = np.stack([np.asarray(R[c]["rg_conv_p"]) for c in range(8)], 1).reshape(2, 8, 3, D)
    ffn_conv_p = np.stack([np.asarray(R[c]["ffn_conv_p"]) for c in range(8)], 1).reshape(4, 8, 2, FF2)
    s5_re_s = cat("s5_re_s", 1).reshape(2, 128, 64, 64)
    s5_im_s = cat("s5_im_s", 1).reshape(2, 128, 64, 64)
    rg_h_s = cat("rg_h_s", 1).reshape(2, 128, D)
    rg_conv_s = cat("rg_conv_s", 1).reshape(2, 128, 3, D)
    ffn_conv_s = cat("ffn_conv_s", 1).reshape(4, 128, 2, FF2)
    outs = (y_prompt, y_sample, s5_re_p, s5_im_p, rg_h_p, rg_conv_p, ffn_conv_p,
            s5_re_s, s5_im_s, rg_h_s, rg_conv_s, ffn_conv_s)
    return tuple(np.ascontiguousarray(o, dtype=np.float32) for o in outs)
```

```python
import math
from contextlib import ExitStack
import numpy as np
import concourse.bass as bass
import concourse.mybir as mybir
from concourse.bass_utils import run_bass_kernel_spmd

F32 = mybir.dt.float32
BF16 = mybir.dt.bfloat16
I32 = mybir.dt.int32
AF = mybir.ActivationFunctionType
ALU = mybir.AluOpType

D = 1024
NB = 8
FF = 2816
FF2 = 5632
FB = 44
FBH = 22
DEPTH = 4
SEQ = 2048
NMETA = 16
DN_ALPHA = (2 * DEPTH) ** 0.25
LN_EPS = 1e-5
RG_C = 8.0
TWO_PI = 2.0 * math.pi
ATTACH_WAITS = True
STATS = {}


class Res:
    __slots__ = ("name", "w", "r")

    def __init__(self, name=""):
        self.name = name
        self.w = None
        self.r = {}


class DSem:
    def __init__(self, sem):
        self.sem = sem
        self.val = 0


class Q:
    def __init__(self, name, eng, sem, eager):
        self.name = name
        self.eng = eng
        self.sem = sem
        self.eager = eager
        self.n = 0
        self.last = None
        self.last_ms = True
        self.ms = []
        self.semval = 0
        self.known = {}


class KB:
    def __init__(self, nc, es):
        self.nc = nc
        self.es = es
        self.q = {}
        for name, eng, eager in (("pe", nc.tensor, False), ("act", nc.scalar, False),
                                 ("dve", nc.vector, False), ("pool", nc.gpsimd, True),
                                 ("sp", nc.sync, True)):
            self.q[name] = Q(name, eng, self.sem("q_" + name), eager)
        self.dsems = []
        self.nsem = 0

    def sem(self, name):
        return self.es.enter_context(self.nc.semaphore(name))

    def dsem(self, name):
        d = DSem(self.sem("d_" + name))
        self.dsems.append(d)
        return d

    def sb(self, name, shape, dt, es=None):
        return (es or self.es).enter_context(self.nc.sbuf_tensor("sb_" + name, list(shape), dt))

    def ps(self, name, shape, dt=F32, es=None):
        return (es or self.es).enter_context(self.nc.psum_tensor("pt_" + name, list(shape), dt))

    def _milestone(self, A, k):
        lo, hi = 0, len(A.ms)
        while lo < hi:
            mid = (lo + hi) // 2
            if A.ms[mid][0] >= k:
                hi = mid
            else:
                lo = mid + 1
        if lo < len(A.ms):
            return A.ms[lo][1]
        assert A.last is not None and not A.last_ms and A.n >= k
        A.semval += 1
        A.last.then_inc(A.sem, 1)
        A.last_ms = True
        A.ms.append((A.n, A.semval))
        return A.semval

    def _wait(self, q, deps):
        need = {}
        for d in deps:
            if d is None:
                continue
            if d[0] == "q":
                A, k = d[1], d[2]
                if A is q and q.name == "pe":
                    continue
                v = self._milestone(A, k)
                key = A
                sem = A.sem
            else:
                key = d[1]
                sem = d[1].sem
                v = d[2]
            if q.known.get(key, 0) >= v:
                continue
            if need.get(key, (None, 0))[1] < v:
                need[key] = (sem, v)
        items = list(need.items())
        attach = None
        if ATTACH_WAITS and items:
            key, (sem, v) = items.pop()
            q.known[key] = v
            attach = (sem, v)
        for key, (sem, v) in items:
            q.eng.wait_ge(sem, v)
            q.known[key] = v
            STATS[q.name] = STATS.get(q.name, 0) + 1
        if attach:
            STATS["att_" + q.name] = STATS.get("att_" + q.name, 0) + 1
        return attach

    def _deps(self, reads, writes):
        deps = []
        for r in reads:
            if r.w is not None:
                deps.append(r.w)
        for w in writes:
            if w.w is not None:
                deps.append(w.w)
            deps.extend(w.r.values())
        return deps

    def op(self, qn, reads, writes, fn):
        q = self.q[qn]
        att = self._wait(q, self._deps(reads, writes))
        ins = fn(q.eng)
        if att is not None:
            ins._wait_ge(att[0], att[1])
        q.n += 1
        q.last = ins
        q.last_ms = False
        if q.eager:
            q.semval += 1
            ins.then_inc(q.sem, 1)
            q.last_ms = True
            q.ms.append((q.n, q.semval))
        me = ("q", q, q.n)
        for r in reads:
            r.r[q] = me
        for w in writes:
            w.w = me
            w.r = {}
        return ins

    def dma(self, qn, out, in_, reads, writes, ds, **kw):
        q = self.q[qn]
        att = self._wait(q, self._deps(reads, writes))
        ins = q.eng.dma_start(out=out, in_=in_, **kw)
        if att is not None:
            ins._wait_ge(att[0], att[1])
        ds.val += 16
        ins.then_inc(ds.sem, 16)
        me = ("d", ds, ds.val)
        for r in reads:
            r.r[ds] = me
        for w in writes:
            w.w = me
            w.r = {}
        return ins

    def group_final(self, ress, ds):
        for r in ress:
            r.w = ("d", ds, ds.val)

    def barrier(self, exclude=(), full=False):
        marks = []
        for A in self.q.values():
            if A.n > 0:
                marks.append(("q", A, A.n))
        for d in self.dsems:
            if d in exclude:
                continue
            if d.val > 0:
                marks.append(("d", d, d.val))
        global ATTACH_WAITS
        sv = ATTACH_WAITS
        ATTACH_WAITS = False
        for q in self.q.values():
            if q.name == "pe" and not full:
                continue
            self._wait(q, [m for m in marks if not (m[0] == "q" and m[1] is q)])
        ATTACH_WAITS = sv

    def finish(self):
        self.barrier(full=True)


def build(nc, dbg=None):
    es = ExitStack()
    kb = KB(nc, es)
    with es:
        _build(nc, kb, dbg)
    return nc


def _dram_in(nc, name, shape, dt=F32):
    return nc.dram_tensor(name, list(shape), dt, kind="ExternalInput").ap()


def _dram_out(nc, name, shape, dt=F32):
    return nc.dram_tensor(name, list(shape), dt, kind="ExternalOutput").ap()


def _build(nc, kb, dbg):
    op, dma = kb.op, kb.dma
    I = {}
    I["x_prompt"] = _dram_in(nc, "x_prompt", [SEQ, D])
    I["x_sample"] = _dram_in(nc, "x_sample", [128, D])
    I["state_s5_re"] = _dram_in(nc, "state_s5_re", [2, 16, 4096])
    I["state_s5_im"] = _dram_in(nc, "state_s5_im", [2, 16, 4096])
    I["state_rg_h"] = _dram_in(nc, "state_rg_h", [2, 16, D])
    I["state_rg_conv"] = _dram_in(nc, "state_rg_conv", [2, 48, D])
    I["state_ffn_conv"] = _dram_in(nc, "state_ffn_conv", [4, 32, FF2])
    I["meta_tokens"] = _dram_in(nc, "meta_tokens", [NMETA, D])
    I["s5_w_in"] = _dram_in(nc, "s5_w_in", [2, D, D])
    I["s5_lam_re"] = _dram_in(nc, "s5_lam_re", [2, 64, 64])
    I["s5_lam_im"] = _dram_in(nc, "s5_lam_im", [2, 64, 64])
    I["s5_log_step"] = _dram_in(nc, "s5_log_step", [2, 64])
    I["s5_b_re"] = _dram_in(nc, "s5_b_re", [2, 64, 64, 16])
    I["s5_b_im"] = _dram_in(nc, "s5_b_im", [2, 64, 64, 16])
    I["s5_c_re"] = _dram_in(nc, "s5_c_re", [2, 64, 16, 64])
    I["s5_c_im"] = _dram_in(nc, "s5_c_im", [2, 64, 16, 64])
    I["s5_d"] = _dram_in(nc, "s5_d", [2, D])
    I["s5_w_out"] = _dram_in(nc, "s5_w_out", [2, D, 2 * D])
    I["rg_w_in"] = _dram_in(nc, "rg_w_in", [2, D, 2 * D])
    I["rg_conv_w"] = _dram_in(nc, "rg_conv_w", [2, 4, D])
    I["rg_conv_b"] = _dram_in(nc, "rg_conv_b", [2, D])
    I["rg_w_gates"] = _dram_in(nc, "rg_w_gates", [2, 4, 256, 512])
    I["rg_b_gates"] = _dram_in(nc, "rg_b_gates", [2, 2 * D])
    I["rg_lam"] = _dram_in(nc, "rg_lam", [2, D])
    I["rg_w_out"] = _dram_in(nc, "rg_w_out", [2, D, D])
    I["ffn_w_up"] = _dram_in(nc, "ffn_w_up", [4, D, FF2])
    I["ffn_conv_w"] = _dram_in(nc, "ffn_conv_w", [4, 3, FF2])
    I["ffn_conv_b"] = _dram_in(nc, "ffn_conv_b", [4, FF2])
    I["ffn_w_down"] = _dram_in(nc, "ffn_w_down", [4, FF, D])
    I["ln_g"] = _dram_in(nc, "ln_g", [4, 2, D])
    I["ln_b"] = _dram_in(nc, "ln_b", [4, 2, D])
    I["ident"] = _dram_in(nc, "ident", [128, 128])
    I["bmask"] = _dram_in(nc, "bmask", [128, 128])
    I["sel2"] = _dram_in(nc, "sel2", [2, 128])

    O = {}
    O["y_prompt"] = _dram_out(nc, "y_prompt", [SEQ, D])
    O["y_sample"] = _dram_out(nc, "y_sample", [128, D])
    O["s5_re_p"] = _dram_out(nc, "s5_re_p", [2, 1, 4096])
    O["s5_im_p"] = _dram_out(nc, "s5_im_p", [2, 1, 4096])
    O["rg_h_p"] = _dram_out(nc, "rg_h_p", [2, 1, D])
    O["rg_conv_p"] = _dram_out(nc, "rg_conv_p", [2, 3, D])
    O["ffn_conv_p"] = _dram_out(nc, "ffn_conv_p", [4, 2, FF2])
    O["s5_re_s"] = _dram_out(nc, "s5_re_s", [2, 16, 4096])
    O["s5_im_s"] = _dram_out(nc, "s5_im_s", [2, 16, 4096])
    O["rg_h_s"] = _dram_out(nc, "rg_h_s", [2, 16, D])
    O["rg_conv_s"] = _dram_out(nc, "rg_conv_s", [2, 48, D])
    O["ffn_conv_s"] = _dram_out(nc, "ffn_conv_s", [4, 32, FF2])

    WXd = [nc.dram_tensor("WXd%d" % j, [128, 8, 8, 2, 128], BF16, kind="Internal").ap() for j in range(2)]
    WYd = [nc.dram_tensor("WYd%d" % j, [128, 32, 8, 2, 32], BF16, kind="Internal").ap() for j in range(2)]
    KId = [nc.dram_tensor("KId%d" % j, [128, 8, 8, 128], BF16, kind="Internal").ap() for j in range(2)]

    NBLK = 2 * NB + 2 * 2 * NB + 4 * FB
    ctx_w = dict(
        WBLK=nc.dram_tensor("WBLK", [NBLK, 128, NB * 128], BF16, kind="Internal").ap(),
        WDd=[nc.dram_tensor("WDd%d" % l, [128, FBH * D], BF16, kind="Internal").ap() for l in range(4)],
        S5WOd=[nc.dram_tensor("S5WOd%d" % j, [128, NB * 2 * D], BF16, kind="Internal").ap() for j in range(2)],
        RGWOd=[nc.dram_tensor("RGWOd%d" % j, [128, NB * D], BF16, kind="Internal").ap() for j in range(2)],
        RGWGd=[nc.dram_tensor("RGWGd%d" % j, [128, 4 * 2 * 512], BF16, kind="Internal").ap() for j in range(2)],
    )
    sb = kb.sb
    ident = sb("ident", [128, 128], F32)
    bmask = sb("bmask", [128, 128], F32)
    R_const = Res("const")
    ds_c = kb.dsem("const")
    dma("sp", ident[:], I["ident"][:, :], [], [R_const], ds_c)
    dma("sp", bmask[:], I["bmask"][:, :], [], [R_const], ds_c)
    sel2 = sb("sel2", [2, 128], F32)
    dma("sp", sel2[:], I["sel2"][:, :], [], [R_const], ds_c)

    R_par = Res("par")
    ds_p = kb.dsem("par")

    par_loads = []

    def load_cols(name, src_1d, nblk, ds):
        t = sb(name, [128, nblk], F32)
        par_loads.append((t, src_1d))
        return t

    def issue_par_loads():
        with nc.allow_non_contiguous_dma(reason="small param"):
            for t, src_1d in par_loads:
                dma("sp", t[:], src_1d.rearrange("(b p) -> p b", p=128), [], [R_par], ds_p)

    ffn_cw = [[load_cols("fcw%d_%d" % (l, k), I["ffn_conv_w"][l, k], FB, ds_c) for k in range(3)] for l in range(4)]
    ffn_cb = [load_cols("fcb%d" % l, I["ffn_conv_b"][l], FB, ds_c) for l in range(4)]
    rg_cw = [[load_cols("rcw%d_%d" % (j, k), I["rg_conv_w"][j, k], NB, ds_c) for k in range(4)] for j in range(2)]
    rg_cb = [load_cols("rcb%d" % j, I["rg_conv_b"][j], NB, ds_c) for j in range(2)]
    rg_bg = [load_cols("rbg%d" % j, I["rg_b_gates"][j], 16, ds_c) for j in range(2)]
    rg_lm = [load_cols("rlm%d" % j, I["rg_lam"][j], NB, ds_c) for j in range(2)]
    s5_dd = [sb("s5d%d" % j, [128, NB], F32) for j in range(2)]
    with nc.allow_non_contiguous_dma(reason="small param"):
        for j in range(2):
            dma("sp", s5_dd[j][:], I["s5_d"][j].rearrange("(b p) -> p b", p=128), [], [R_const], ds_c)
    rg_m8sp = [sb("m8sp%d" % j, [128, NB], F32) for j in range(2)]
    A8re = [sb("A8re%d" % j, [128, 32], F32) for j in range(2)]
    A8im = [sb("A8im%d" % j, [128, 32], F32) for j in range(2)]
    A8imn = [sb("A8imn%d" % j, [128, 32], F32) for j in range(2)]
    ffn_hist_p = [sb("fhp%d" % l, [128, FB, 1, 2], F32) for l in range(4)]
    rg_hist_p = [sb("rhp%d" % j, [128, NB, 1, 3], F32) for j in range(2)]
    rg_h_p = [sb("rgh%d" % j, [128, NB, 1], F32) for j in range(2)]
    s5st_p = [sb("s5p%d" % j, [128, 32, 2, 1], F32) for j in range(2)]
    R_state = Res("state")

    PSall = kb.ps("psall", [128, 8 * 512])
    PS = [PSall[:, i * 512:(i + 1) * 512] for i in range(8)]
    RP = [Res("ps%d" % i) for i in range(8)]

    for j in range(2):
        pass
    for t in ffn_hist_p + rg_hist_p + rg_h_p + s5st_p:
        op("dve", [], [R_state], lambda e, t=t: e.memset(t[:], 0.0))

    ctx = dict(nc=nc, kb=kb, I=I, O=O, sel2=sel2, R_par=R_par, WXd=WXd, WYd=WYd, KId=KId, ident=ident, bmask=bmask,
               R_const=R_const, PS=PS, PSall=PSall, RP=RP, ffn_cw=ffn_cw, ffn_cb=ffn_cb, rg_cw=rg_cw, rg_cb=rg_cb,
               rg_bg=rg_bg, rg_m8sp=rg_m8sp, s5_dd=s5_dd, A8re=A8re, A8im=A8im, A8imn=A8imn,
               ffn_hist_p=ffn_hist_p, rg_hist_p=rg_hist_p, rg_h_p=rg_h_p, s5st_p=s5st_p,
               R_state=R_state, dbg=dbg, ds_out=kb.dsem("out"), ds_w=None, ds_wb=kb.dsem("wb"), **ctx_w)

    with ExitStack() as lds:
        with ExitStack() as nat:
            gens = [s5_prologue_loads(ctx, j, lds, nat) for j in range(2)]
            for g in gens:
                next(g)
            lds_ = [next(g) for g in gens]
            kb.barrier()
        for j in range(2):
            with ExitStack() as pes:
                ctx["issue_par_loads"] = issue_par_loads
                s5_prologue(ctx, j, pes, lds_[j])
                kb.barrier()
    kb.group_final([R_par], ds_p)
    for j in range(2):
        op("act", [R_par], [R_par], lambda e, j=j: e.activation(out=rg_m8sp[j][:], in_=rg_lm[j][:], func=AF.Exp, scale=-1.0))
        op("act", [R_par], [R_par], lambda e, j=j: e.activation(out=rg_m8sp[j][:], in_=rg_m8sp[j][:], func=AF.Ln, bias=1.0))
        op("dve", [R_par], [R_par], lambda e, j=j: e.tensor_scalar(out=rg_m8sp[j][:], in0=rg_m8sp[j][:], scalar1=-RG_C, scalar2=None, op0=ALU.mult))
    kb.barrier()
    passes = [("A", 1, 1024), ("B", 1, 1040), ("S", 16, 8)]
    for (pn, nseq, L) in passes:
        with ExitStack() as pes:
            run_pass(ctx, pn, nseq, L, pes)
            kb.barrier()
    kb.finish()


def colgroups(N, g=512):
    return [(c0, min(g, N - c0)) for c0 in range(0, N, g)]


def store_T(ctx, src_fn, nblk, m, dram_rows, stage, R_src, tag):
    kb, PS, RP, ident = ctx["kb"], ctx["PS"], ctx["RP"], ctx["ident"]
    R_stage = ctx.setdefault("stage_res", {}).setdefault(id(stage), Res("stage" + tag))
    GB = stage.shape[1] // 128
    for g0 in range(0, nblk, GB):
        ng = min(GB, nblk - g0)
        for b0 in range(g0, g0 + ng, 4):
            nb_ = min(4, g0 + ng - b0)
            bank = 6 + ((b0 // 4) % 2)
            for b in range(nb_):
                kb.op("pe", [R_src, ctx["R_const"]], [RP[bank]],
                      lambda e, b=b, b0=b0, bank=bank: e.transpose(out=PS[bank][0:m, b * 128:(b + 1) * 128], in_=src_fn(b0 + b), identity=ident[:, :]))
            kb.op("act", [RP[bank]], [R_stage],
                  lambda e, b0=b0, nb_=nb_, bank=bank, g0=g0: e.copy(out=stage[0:m, (b0 - g0) * 128:(b0 - g0 + nb_) * 128], in_=PS[bank][0:m, 0:nb_ * 128]))
        kb.dma("sp", dram_rows[:, g0 * 128:(g0 + ng) * 128], stage[0:m, 0:ng * 128], [R_stage], [], ctx["ds_out"])


def load_T(ctx, dram_rows, nblk, m, dst_fn, stage, R_dst, tag, ds):
    kb, PS, RP, ident = ctx["kb"], ctx["PS"], ctx["RP"], ctx["ident"]
    R_stage = ctx.setdefault("stage_res", {}).setdefault(id(stage), Res("lstage" + tag))
    ctx["uid"] = ctx.get("uid", 0) + 1
    ds = kb.dsem("lt%d" % ctx["uid"])
    kb.dma("sp", stage[0:m, 0:nblk * 128], dram_rows, [], [R_stage], ds)
    per = 512 // m
    gi = 0
    for b0 in range(0, nblk, per):
        nb_ = min(per, nblk - b0)
        bank = 6 + (gi % 2)
        gi += 1
        for b in range(nb_):
            kb.op("pe", [R_stage, ctx["R_const"]], [RP[bank]],
                  lambda e, b=b: e.transpose(out=PS[bank][:, b * m:(b + 1) * m], in_=stage[0:m, (b0 + b) * 128:(b0 + b + 1) * 128], identity=ident[0:m, 0:m]))
        kb.op("act", [RP[bank]], [R_dst],
              lambda e: e.copy(out=dst_fn(b0, nb_), in_=PS[bank][:, 0:nb_ * m].rearrange("p (b m) -> p b m", m=m)))


def s5_prologue_loads(ctx, j, pes, nat):
    nc, kb, I = ctx["nc"], ctx["kb"], ctx["I"]
    dma, op = kb.dma, kb.op
    PS, RP, ident, Rc = ctx["PS"], ctx["RP"], ctx["ident"], ctx["R_const"]
    sb = lambda n, s, d=F32: kb.sb("pl%d_%s" % (j, n), s, d, es=pes)
    ds = kb.dsem("pl%d" % j)
    R = Res("pl")
    Rn = Res("plnat")
    lr = sb("lr", [128, 32]); li = sb("li", [128, 32]); stp = sb("stp", [128, 32])
    Bre = sb("Bre", [128, 32, 16]); Bim = sb("Bim", [128, 32, 16])
    yield None
    sbn = lambda n, s, d=F32: kb.sb("pl%d_%s" % (j, n), s, d, es=nat)
    lrn = sbn("lrn", [32, 128]); lin = sbn("lin", [32, 128]); stn = sbn("stn", [2, 32])
    Bn = [sbn("Bn%d" % i, [32, 128, 16]) for i in range(2)]
    dma("sp", lrn[:], I["s5_lam_re"][j].rearrange("(pr g2) p -> pr (g2 p)", g2=2), [], [Rn], ds)
    dma("sp", lin[:], I["s5_lam_im"][j].rearrange("(pr g2) p -> pr (g2 p)", g2=2), [], [Rn], ds)
    with nc.allow_non_contiguous_dma(reason="tiny"):
        dma("sp", stn[:], I["s5_log_step"][j].rearrange("(pr g2) -> g2 pr", g2=2), [], [Rn], ds)
    dma("sp", Bn[0][:], I["s5_b_re"][j].rearrange("(pr g2) p h -> pr (g2 p) h", g2=2), [], [Rn], ds)
    dma("sp", Bn[1][:], I["s5_b_im"][j].rearrange("(pr g2) p h -> pr (g2 p) h", g2=2), [], [Rn], ds)
    kb.group_final([Rn], ds)
    bank = 4 + j
    op("pe", [Rn, Rc], [RP[bank]], lambda e: e.transpose(out=PS[bank][:, 0:32], in_=lrn[:, :], identity=ident[0:32, 0:32]))
    op("pe", [Rn, Rc], [RP[bank]], lambda e: e.transpose(out=PS[bank][:, 32:64], in_=lin[:, :], identity=ident[0:32, 0:32]))
    op("pe", [Rn, Rc], [RP[bank]], lambda e: e.matmul(PS[bank][:, 64:96], lhsT=ctx["sel2"][:, :], rhs=stn[:, :], start=True, stop=True))
    op("act", [RP[bank]], [R], lambda e: e.copy(out=lr[:], in_=PS[bank][:, 0:32]))
    op("act", [RP[bank]], [R], lambda e: e.copy(out=li[:], in_=PS[bank][:, 32:64]))
    op("act", [RP[bank]], [R], lambda e: e.copy(out=stp[:], in_=PS[bank][:, 64:96]))
    for i, Bd in enumerate((Bre, Bim)):
        for h in range(16):
            op("pe", [Rn, Rc], [RP[bank]], lambda e, h=h, i=i: e.transpose(out=PS[bank][:, h * 32:(h + 1) * 32], in_=Bn[i][:, :, h], identity=ident[0:32, 0:32]))
        op("act", [RP[bank]], [R], lambda e, Bd=Bd: e.copy(out=Bd[:, :, :].rearrange("p r h -> p h r"), in_=PS[bank][:, :].rearrange("p (h r) -> p h r", r=32)))
    yield dict(R=R, ds=ds, lr=lr, li=li, stp=stp, Bre=Bre, Bim=Bim)


def s5_prologue(ctx, j, pes, ld):
    nc, kb, I = ctx["nc"], ctx["kb"], ctx["I"]
    op, dma = kb.op, kb.dma
    PS, RP, ident, bmask = ctx["PS"], ctx["RP"], ctx["ident"], ctx["bmask"]
    Rc = ctx["R_const"]
    sb = lambda n, s, d=F32: kb.sb("pl%d_%s" % (j, n), s, d, es=pes)
    R, ds = ld["R"], ld["ds"]
    lr, li, stp, Bre, Bim = (ld[k] for k in ("lr", "li", "stp", "Bre", "Bim"))
    Cre = sb("Cre", [128, 32, 16]); Cim = sb("Cim", [128, 32, 16])
    Ch_re = sb("Chre", [16, 64, 64]); Ch_im = sb("Chim", [16, 64, 64])
    R_ch = Res("ch")
    ds_ch = kb.dsem("plc%d" % j)
    with nc.allow_non_contiguous_dma(reason="param layout"):
        dma("sp", Ch_re[:], I["s5_c_re"][j].rearrange("g h p -> h g p"), [], [R_ch], ds_ch)
        dma("sp", Ch_im[:], I["s5_c_im"][j].rearrange("g h p -> h g p"), [], [R_ch], ds_ch)
    kb.group_final([R_ch], ds_ch)
    if j == 1:
        ctx["issue_par_loads"]()
    for (Ch, Cd) in ((Ch_re, Cre), (Ch_im, Cim)):
        for g0 in range(0, 32, 16):
            bank = 4 + (g0 // 16)
            for pr in range(g0, g0 + 16):
                op("pe", [R_ch, Rc], [RP[bank]],
                   lambda e, pr=pr, Ch=Ch: e.transpose(out=PS[bank][:, (pr - g0) * 16:(pr - g0 + 1) * 16],
                                                       in_=Ch[0:16, 2 * pr:2 * pr + 2, :].rearrange("h g p -> h (g p)"),
                                                       identity=ident[0:16, 0:16]))
            op("act", [RP[bank]], [R], lambda e, Cd=Cd: e.copy(out=Cd[:, g0:g0 + 16, :], in_=PS[bank][:, 0:256].rearrange("p (a h) -> p a h", h=16)))

    t = lambda n: sb(n, [128, 32])
    ang = t("ang"); mag = t("mag"); sn = t("sn"); cs = t("cs"); tmp = t("tmp"); tmp2 = t("tmp2")
    abre = t("abre"); abim = t("abim"); qre = t("qre"); qim = t("qim"); ki = sb("ki", [128, 32], I32)
    V = lambda f: op("dve", [R], [R], f)
    A = lambda f: op("act", [R], [R], f)
    A(lambda e: e.activation(out=stp[:], in_=stp[:], func=AF.Exp))
    V(lambda e: e.tensor_tensor(out=ang[:], in0=li[:], in1=stp[:], op=ALU.mult))
    V(lambda e: e.tensor_tensor(out=mag[:], in0=lr[:], in1=stp[:], op=ALU.mult))
    A(lambda e: e.activation(out=mag[:], in_=mag[:], func=AF.Exp))

    def sin_of(dst, shift):
        V(lambda e: e.tensor_scalar(out=tmp[:], in0=ang[:], scalar1=shift, scalar2=1.0 / TWO_PI, op0=ALU.add, op1=ALU.mult))
        V(lambda e: e.tensor_copy(out=ki[:], in_=tmp[:]))
        V(lambda e: e.tensor_copy(out=tmp2[:], in_=ki[:]))
        V(lambda e: e.tensor_tensor(out=tmp[:], in0=tmp[:], in1=tmp2[:], op=ALU.subtract))
        V(lambda e: e.tensor_scalar(out=tmp2[:], in0=tmp[:], scalar1=0.5, scalar2=None, op0=ALU.is_gt))
        V(lambda e: e.tensor_tensor(out=tmp[:], in0=tmp[:], in1=tmp2[:], op=ALU.subtract))
        V(lambda e: e.tensor_scalar(out=tmp2[:], in0=tmp[:], scalar1=-0.5, scalar2=None, op0=ALU.is_lt))
        V(lambda e: e.tensor_tensor(out=tmp[:], in0=tmp[:], in1=tmp2[:], op=ALU.add))
        V(lambda e: e.tensor_scalar(out=tmp[:], in0=tmp[:], scalar1=TWO_PI, scalar2=math.pi, op0=ALU.mult, op1=ALU.min))
        V(lambda e: e.tensor_scalar(out=tmp[:], in0=tmp[:], scalar1=-math.pi, scalar2=None, op0=ALU.max))
        A(lambda e: e.activation(out=dst[:], in_=tmp[:], func=AF.Sin))

    sin_of(sn, 0.0)
    sin_of(cs, math.pi / 2)
    V(lambda e: e.tensor_tensor(out=abre[:], in0=mag[:], in1=cs[:], op=ALU.mult))
    V(lambda e: e.tensor_tensor(out=abim[:], in0=mag[:], in1=sn[:], op=ALU.mult))
    den = t("den"); nr = t("nr")
    V(lambda e: e.tensor_tensor(out=den[:], in0=lr[:], in1=lr[:], op=ALU.mult))
    V(lambda e: e.tensor_tensor(out=tmp[:], in0=li[:], in1=li[:], op=ALU.mult))
    V(lambda e: e.tensor_tensor(out=den[:], in0=den[:], in1=tmp[:], op=ALU.add))
    V(lambda e: e.reciprocal(out=den[:], in_=den[:]))
    V(lambda e: e.tensor_scalar(out=nr[:], in0=abre[:], scalar1=-1.0, scalar2=None, op0=ALU.add))
    V(lambda e: e.tensor_tensor(out=qre[:], in0=nr[:], in1=lr[:], op=ALU.mult))
    V(lambda e: e.tensor_tensor(out=tmp[:], in0=abim[:], in1=li[:], op=ALU.mult))
    V(lambda e: e.tensor_tensor(out=qre[:], in0=qre[:], in1=tmp[:], op=ALU.add))
    V(lambda e: e.tensor_tensor(out=qre[:], in0=qre[:], in1=den[:], op=ALU.mult))
    V(lambda e: e.tensor_tensor(out=qim[:], in0=abim[:], in1=lr[:], op=ALU.mult))
    V(lambda e: e.tensor_tensor(out=tmp[:], in0=nr[:], in1=li[:], op=ALU.mult))
    V(lambda e: e.tensor_tensor(out=qim[:], in0=qim[:], in1=tmp[:], op=ALU.subtract))
    V(lambda e: e.tensor_tensor(out=qim[:], in0=qim[:], in1=den[:], op=ALU.mult))
    pwr = sb("pwr", [128, 9, 32]); pwi = sb("pwi", [128, 9, 32])
    V(lambda e: e.memset(pwr[:, 0, :], 1.0))
    V(lambda e: e.memset(pwi[:, 0, :], 0.0))
    for k in range(1, 9):
        V(lambda e, k=k: e.tensor_tensor(out=pwr[:, k, :], in0=pwr[:, k - 1, :], in1=abre[:], op=ALU.mult))
        V(lambda e, k=k: e.tensor_tensor(out=tmp[:], in0=pwi[:, k - 1, :], in1=abim[:], op=ALU.mult))
        V(lambda e, k=k: e.tensor_tensor(out=pwr[:, k, :], in0=pwr[:, k, :], in1=tmp[:], op=ALU.subtract))
        V(lambda e, k=k: e.tensor_tensor(out=pwi[:, k, :], in0=pwr[:, k - 1, :], in1=abim[:], op=ALU.mult))
        V(lambda e, k=k: e.tensor_tensor(out=tmp[:], in0=pwi[:, k - 1, :], in1=abre[:], op=ALU.mult))
        V(lambda e, k=k: e.tensor_tensor(out=pwi[:, k, :], in0=pwi[:, k, :], in1=tmp[:], op=ALU.add))
    Rst = ctx["R_state"]
    op("dve", [R], [Rst], lambda e: e.tensor_copy(out=ctx["A8re"][j][:], in_=pwr[:, 8, :]))
    op("dve", [R], [Rst], lambda e: e.tensor_copy(out=ctx["A8im"][j][:], in_=pwi[:, 8, :]))
    op("dve", [R], [Rst], lambda e: e.tensor_scalar(out=ctx["A8imn"][j][:], in0=pwi[:, 8, :], scalar1=-1.0, scalar2=None, op0=ALU.mult))

    def bc(x2d):
        return x2d.unsqueeze(2).broadcast_to([128, 32, 16])

    T3 = lambda n: sb(n, [128, 32, 16])
    w1 = T3("w1"); w2 = T3("w2")

    def cmul(dre, dim, xre, xim, sre, sim_):
        V(lambda e: e.tensor_tensor(out=w1[:], in0=xre, in1=bc(sre), op=ALU.mult))
        V(lambda e: e.tensor_tensor(out=w2[:], in0=xim, in1=bc(sim_), op=ALU.mult))
        V(lambda e: e.tensor_tensor(out=dre, in0=w1[:], in1=w2[:], op=ALU.subtract))
        V(lambda e: e.tensor_tensor(out=w1[:], in0=xre, in1=bc(sim_), op=ALU.mult))
        V(lambda e: e.tensor_tensor(out=w2[:], in0=xim, in1=bc(sre), op=ALU.mult))
        V(lambda e: e.tensor_tensor(out=dim, in0=w1[:], in1=w2[:], op=ALU.add))

    Bbre = T3("Bbre"); Bbim = T3("Bbim")
    cmul(Bbre[:], Bbim[:], Bre[:], Bim[:], qre[:], qim[:])
    Bpre = sb("Bpre", [128, 32, 32]); Bpim = sb("Bpim", [128, 32, 32])
    V(lambda e: e.memset(Bpre[:], 0.0))
    V(lambda e: e.memset(Bpim[:], 0.0))
    for g2 in range(2):
        hp = slice(g2 * 64, (g2 + 1) * 64); hc = slice(g2 * 16, (g2 + 1) * 16)
        V(lambda e, hp=hp, hc=hc: e.tensor_copy(out=Bpre[hp, :, hc], in_=Bbre[hp, :, :]))
        V(lambda e, hp=hp, hc=hc: e.tensor_copy(out=Bpim[hp, :, hc], in_=Bbim[hp, :, :]))

    WY = sb("WY", [128, 32, 8, 2, 32], BF16)
    KI = sb("KI", [128, 8, 8, 128], BF16)
    WX = sb("WX", [128, 8, 8, 2, 128], BF16)
    V(lambda e: e.memset(WY[:].rearrange("p a b c d -> p (a b c d)"), 0.0))
    CAre = T3("CAre"); CAim = T3("CAim")
    CApre = sb("CApre", [128, 32, 32]); CApimn = sb("CApimn", [128, 32, 32])
    V(lambda e: e.memset(CApre[:], 0.0))
    V(lambda e: e.memset(CApimn[:], 0.0))
    kit = sb("kit", [128, 128])
    dcol = ctx["s5_dd"][j]
    R_o = Res("pl_out")
    R_kit = Res("kit")
    for k in range(9):
        cmul(CAre[:], CAim[:], Cre[:], Cim[:], pwr[:, k, :], pwi[:, k, :])
        for g2 in range(2):
            hp = slice(g2 * 64, (g2 + 1) * 64); hc = slice(g2 * 16, (g2 + 1) * 16)
            V(lambda e, hp=hp, hc=hc: e.tensor_copy(out=CApre[hp, :, hc], in_=CAre[hp, :, :]))
            V(lambda e, hp=hp, hc=hc: e.tensor_scalar(out=CApimn[hp, :, hc], in0=CAim[hp, :, :], scalar1=-1.0, scalar2=None, op0=ALU.mult))
            if k >= 1:
                V(lambda e, hp=hp, hc=hc, k=k: e.tensor_copy(out=WY[hp, :, k - 1, 0, hc], in_=CAre[hp, :, :]))
                V(lambda e, hp=hp, hc=hc, k=k: e.tensor_scalar(out=WY[hp, :, k - 1, 1, hc], in0=CAim[hp, :, :], scalar1=-1.0, scalar2=None, op0=ALU.mult))
        if k <= 7:
            for c in range(8):
                bank = 4 + (c % 2)
                cs4 = slice(4 * c, 4 * c + 4)
                op("pe", [R], [RP[bank]], lambda e, cs4=cs4: e.matmul(PS[bank][:, 0:128], lhsT=Bpre[:, cs4, :].rearrange("p a b -> p (a b)"),
                                                                 rhs=CApre[:, cs4, :].rearrange("p a b -> p (a b)"), start=True, stop=False))
                op("pe", [R], [RP[bank]], lambda e, cs4=cs4: e.matmul(PS[bank][:, 0:128], lhsT=Bpim[:, cs4, :].rearrange("p a b -> p (a b)"),
                                                                 rhs=CApimn[:, cs4, :].rearrange("p a b -> p (a b)"), start=False, stop=True))
                if k == 0:
                    op("dve", [RP[bank], Rc], [R_kit], lambda e, bank=bank: e.tensor_tensor(out=kit[:], in0=PS[bank][:, 0:128], in1=bmask[:], op=ALU.mult))
                    op("dve", [R_kit, Rc, ctx["R_par"]], [R_o], lambda e, c=c: e.scalar_tensor_tensor(out=KI[:, c, 0, :], in0=ident[:], scalar=dcol[:, c:c + 1], in1=kit[:], op0=ALU.mult, op1=ALU.add))
                else:
                    op("dve", [RP[bank], Rc], [R_o], lambda e, c=c, k=k, bank=bank: e.tensor_tensor(out=KI[:, c, k, :], in0=PS[bank][:, 0:128], in1=bmask[:], op=ALU.mult))
    XBre = sb("XBre", [128, 32, 32]); XBim = sb("XBim", [128, 32, 32])
    x1 = sb("x1", [128, 32, 32]); x2 = sb("x2", [128, 32, 32])
    bc32 = lambda x2d: x2d.unsqueeze(2).broadcast_to([128, 32, 32])
    for tau in range(8):
        k = 7 - tau
        V(lambda e, k=k: e.tensor_tensor(out=x1[:], in0=Bpre[:], in1=bc32(pwr[:, k, :]), op=ALU.mult))
        V(lambda e, k=k: e.tensor_tensor(out=x2[:], in0=Bpim[:], in1=bc32(pwi[:, k, :]), op=ALU.mult))
        V(lambda e: e.tensor_tensor(out=XBre[:], in0=x1[:], in1=x2[:], op=ALU.subtract))
        V(lambda e, k=k: e.tensor_tensor(out=x1[:], in0=Bpre[:], in1=bc32(pwi[:, k, :]), op=ALU.mult))
        V(lambda e, k=k: e.tensor_tensor(out=x2[:], in0=Bpim[:], in1=bc32(pwr[:, k, :]), op=ALU.mult))
        V(lambda e: e.tensor_tensor(out=XBim[:], in0=x1[:], in1=x2[:], op=ALU.add))
        for ri, XB in enumerate((XBre, XBim)):
            for c in range(8):
                bank = 4 + (c % 2)
                op("pe", [R, Rc], [RP[bank]], lambda e, c=c, XB=XB: e.transpose(out=PS[bank][:, 0:128], in_=XB[:, 4 * c:4 * c + 4, :].rearrange("p a b -> p (a b)"), identity=ident[:, :]))
                op("act", [RP[bank]], [R_o], lambda e, c=c, ri=ri, tau=tau, bank=bank: e.copy(out=WX[:, c, tau, ri, :], in_=PS[bank][:, 0:128]))
    dma("sp", ctx["WXd"][j].rearrange("p a b c d -> p (a b c d)"), WX[:].rearrange("p a b c d -> p (a b c d)"), [R, R_o], [], ds)
    dma("sp", ctx["WYd"][j].rearrange("p a b c d -> p (a b c d)"), WY[:].rearrange("p a b c d -> p (a b c d)"), [R], [], ds)
    dma("sp", ctx["KId"][j].rearrange("p a b c -> p (a b c)"), KI[:].rearrange("p a b c -> p (a b c)"), [R, R_o], [], ds)


def run_pass(ctx, pn, nseq, L, pes):
    nc, kb, I, O = ctx["nc"], ctx["kb"], ctx["I"], ctx["O"]
    op, dma = kb.op, kb.dma
    PS, RP, ident = ctx["PS"], ctx["RP"], ctx["ident"]
    Rc = ctx["R_const"]
    N = nseq * L
    NT = (N + 127) // 128
    rows = [min(128, N - t * 128) for t in range(NT)]
    NC = N // 8
    sb = lambda n, s, d=F32, es=None: kb.sb("p%s_%s" % (pn, n), s, d, es=es or pes)

    h_tok = sb("htok", [128, NT, D])
    hT = sb("hT", [128, NB, N], BF16)
    R_h = [Res("h%d" % t) for t in range(NT)]
    R_hT = [Res("hT%d" % t) for t in range(NT)]
    ds_in = kb.dsem("in" + pn)
    ds_w = kb.dsem("w" + pn)
    ds_ln = kb.dsem("ln" + pn)
    gbc = sb("gbc", [128, D]); bbc = sb("bbc", [128, D])
    R_ln = Res("ln")
    NZ = 4
    zt = [sb("zt%d" % i, [128, D]) for i in range(NZ)]
    R_zt = [Res("zt%d" % i) for i in range(NZ)]
    stat = [sb("stat%d" % i, [128, 2, 6]) for i in range(NZ)]
    mv = [sb("mv%d" % i, [128, 2]) for i in range(NZ)]
    rstd = [sb("rstd%d" % i, [128, 1]) for i in range(NZ)]
    NWS = 8 if pn == "S" else 4
    wslot = [sb("wslot%d" % i, [128, NB, 128], BF16) for i in range(NWS)]
    R_ws = [Res("ws%d" % i) for i in range(NWS)]
    ws_i = [0]
    ds_ws = [kb.dsem("ws%s%d" % (pn, i)) for i in range(NWS)]

    if pn == "A":
        dma("sp", h_tok[0:16, 0, :], I["meta_tokens"][:, :], [], [R_h[0]], ds_in)
        dma("sp", h_tok[16:128, 0, :], I["x_prompt"][0:112, :], [], [R_h[0]], ds_in)
        for t in range(1, NT):
            dma("sp", h_tok[:, t, :], I["x_prompt"][112 + 128 * (t - 1):112 + 128 * t, :], [], [R_h[t]], ds_in)
    elif pn == "B":
        for t in range(NT):
            dma("sp", h_tok[0:rows[t], t, :], I["x_prompt"][1008 + 128 * t:1008 + 128 * t + rows[t], :], [], [R_h[t]], ds_in)
    else:
        dma("sp", h_tok[:, 0, :], I["x_sample"][:, :], [], [R_h[0]], ds_in)

    kb.group_final(R_h, ds_in)

    def to_hT(t):
        r = rows[t]
        for half in range(2):
            bank = 4 + half
            for b in range(4):
                blk = half * 4 + b
                op("pe", [R_h[t], Rc], [RP[bank]], lambda e, b=b, blk=blk: e.transpose(out=PS[bank][:, b * 128:b * 128 + r], in_=h_tok[0:r, t, blk * 128:(blk + 1) * 128], identity=ident[0:r, 0:r]))
            op("act", [RP[bank]], [R_hT[t]], lambda e, half=half: e.copy(out=hT[:, half * 4:half * 4 + 4, t * 128:t * 128 + r],
                                                                     in_=PS[bank][:, :].rearrange("p (b n) -> p b n", n=128)[:, :, 0:r]))

    for t in range(NT):
        to_hT(t)

    nlayers = ctx["dbg"].get("nlayers", DEPTH) if ctx["dbg"] else DEPTH
    glist = []
    for l_ in range(nlayers):
        j_ = l_ // 2
        if l_ % 2 == 0:
            glist += [(I["s5_w_in"][j_], c * 128) for c in range(NB)]
        else:
            for c in range(NB):
                glist += [(I["rg_w_in"][j_], c * 128), (I["rg_w_in"][j_], D + c * 128)]
        for v in range(FBH):
            glist += [(I["ffn_w_up"][l_], v * 128), (I["ffn_w_up"][l_], (v + FBH) * 128)]
    wplan = {"list": glist, "issued": 0, "hooks": {}, "off": 0, "next_off": 0}

    def plan_w(blocks, hooks=None):
        wplan["off"] = wplan["next_off"]
        wplan["next_off"] = wplan["off"] + len(blocks)
        for k, f in (hooks or {}).items():
            gi = wplan["off"] + k
            if gi < wplan["issued"]:
                f()
            else:
                wplan["hooks"][gi] = f

    def write_back(k):
        s = k % NWS
        dma("sp", ctx["WBLK"][k], wslot[s][:].rearrange("p a b -> p (a b)"), [R_ws[s]], [], ctx["ds_wb"])

    def load_w_block(i, pf=NWS - 1):
        gi = wplan["off"] + i
        while wplan["issued"] < min(len(wplan["list"]), gi + pf + 1):
            k = wplan["issued"]
            src2d, c0 = wplan["list"][k]
            s = k % NWS
            if pn == "A":
                dma("pool", wslot[s][:], src2d[:, c0:c0 + 128].rearrange("(kc p) f -> p kc f", p=128), [], [R_ws[s]], ds_ws[s])
                if k >= 2:
                    write_back(k - 2)
            else:
                dma("sp" if pn == "S" else "pool", wslot[s][:].rearrange("p a b -> p (a b)"), ctx["WBLK"][k], [], [R_ws[s]], ds_ws[s])
            wplan["issued"] += 1
            if k in wplan["hooks"]:
                wplan["hooks"].pop(k)()
        return gi % NWS

    def up_matmul(s, ps_banks, extra_reads):
        for gi, (c0, cn) in enumerate(colgroups(N)):
            bank = ps_banks[gi]
            for kc in range(NB):
                op("pe", [R_ws[s]] + R_hT + extra_reads, [RP[bank]],
                   lambda e, kc=kc, bank=bank, c0=c0, cn=cn: e.matmul(PS[bank][:, 0:cn], lhsT=wslot[s][:, kc, :], rhs=hT[:, kc, c0:c0 + cn], start=(kc == 0), stop=(kc == NB - 1)))

    def layer_norm_tile(t, zi, li_, k, last):
        r = rows[t]
        z = zt[zi]
        for hh in range(2):
            op("dve", [R_zt[zi]], [R_zt[zi]], lambda e, hh=hh: e.bn_stats(out=stat[zi][0:r, hh, :], in_=z[0:r, hh * 512:(hh + 1) * 512]))
        op("dve", [R_zt[zi]], [R_zt[zi]], lambda e: e.bn_aggr(out=mv[zi][0:r, :], in_=stat[zi][0:r, :, :].rearrange("p a b -> p (a b)")))
        op("act", [R_zt[zi]], [R_zt[zi]], lambda e: e.activation(out=rstd[zi][0:r, :], in_=mv[zi][0:r, 1:2], func=AF.Sqrt, bias=LN_EPS_AP[0:r, :], scale=1.0))
        op("dve", [R_zt[zi]], [R_zt[zi]], lambda e: e.reciprocal(out=rstd[zi][0:r, :], in_=rstd[zi][0:r, :]))
        op("dve", [R_zt[zi]], [R_zt[zi]], lambda e: e.tensor_scalar(out=z[0:r, :], in0=z[0:r, :], scalar1=mv[zi][0:r, 0:1], scalar2=rstd[zi][0:r, 0:1], op0=ALU.subtract, op1=ALU.mult))
        op("pool", [R_zt[zi], R_ln], [R_zt[zi]], lambda e: e.tensor_tensor(out=z[0:r, :], in0=z[0:r, :], in1=gbc[0:r, :], op=ALU.mult))
        op("pool", [R_zt[zi], R_ln], [R_h[t]], lambda e: e.tensor_tensor(out=h_tok[0:r, t, :], in0=z[0:r, :], in1=bbc[0:r, :], op=ALU.add))
        if not last:
            return lambda: to_hT(t)
        else:
            if pn == "A":
                if t == 0:
                    dma("sp", O["y_prompt"][0:112, :], h_tok[16:128, 0, :], [R_h[t]], [], ctx["ds_out"])
                else:
                    dma("sp", O["y_prompt"][112 + 128 * (t - 1):112 + 128 * t, :], h_tok[:, t, :], [R_h[t]], [], ctx["ds_out"])
            elif pn == "B":
                dma("sp", O["y_prompt"][1008 + 128 * t:1008 + 128 * t + r, :], h_tok[0:r, t, :], [R_h[t]], [], ctx["ds_out"])
            else:
                dma("sp", O["y_sample"][:, :], h_tok[:, 0, :], [R_h[t]], [], ctx["ds_out"])
            return lambda: None

    LN_EPS_AP = sb("lneps", [128, 1])
    op("dve", [], [Rc], lambda e: e.memset(LN_EPS_AP[:], LN_EPS))
    one_ap = sb("one", [128, 1])
    op("dve", [], [Rc], lambda e: e.memset(one_ap[:], 1.0))

    def load_ln(li_, k):
        dma("sp", gbc[:], I["ln_g"][li_, k].partition_broadcast(128), [], [R_ln], ds_ln)
        dma("sp", bbc[:], I["ln_b"][li_, k].partition_broadcast(128), [], [R_ln], ds_ln)

    env = dict(ctx=ctx, pn=pn, nseq=nseq, L=L, N=N, NT=NT, rows=rows, NC=NC, sb=sb, h_tok=h_tok, hT=hT,
               R_h=R_h, R_hT=R_hT, ds_w=ds_w, ds_in=ds_in, zt=zt, R_zt=R_zt, load_w_block=load_w_block, plan_w=plan_w,
               up_matmul=up_matmul, layer_norm_tile=layer_norm_tile, one_ap=one_ap, load_ln=load_ln, wslot=wslot, R_ws=R_ws)

    for li_ in range(nlayers):
        j = li_ // 2
        with ExitStack() as les:
            if li_ % 2 == 0:
                s5_layer(env, li_, j, les)
            else:
                rg_layer(env, li_, j, les)
            kb.barrier()
        with ExitStack() as les:
            ffn_layer(env, li_, les, last=(li_ == nlayers - 1))
            if pn == "A" and li_ == nlayers - 1:
                for k in range(max(0, len(glist) - 2), len(glist)):
                    write_back(k)
            kb.barrier()


def conv_taps(kb, xs, acc, nseq, L, K, wcols, bcol, R_main, R_halo, R_acc):
    kb.op("act", [R_main], [R_acc], lambda e: e.activation(out=acc[:, :, :], in_=xs[:, :, K - 1:K - 1 + L], func=AF.Identity, scale=wcols[K - 1], bias=bcol))
    for k in range(K - 1):
        kb.op("dve", [R_main, R_halo, R_acc], [R_acc], lambda e, k=k: e.scalar_tensor_tensor(out=acc[:, :, :], in0=xs[:, :, k:k + L], scalar=wcols[k], in1=acc[:, :, :], op0=ALU.mult, op1=ALU.add))


def ffn_layer(env, li_, les, last):
    ctx = env["ctx"]; kb = ctx["kb"]; nc = ctx["nc"]; I = ctx["I"]; O = ctx["O"]
    op, dma = kb.op, kb.dma
    PS, RP = ctx["PS"], ctx["RP"]
    pn, nseq, L, N, NT, rows = env["pn"], env["nseq"], env["L"], env["N"], env["NT"], env["rows"]
    sb = lambda n, s, d=F32: env["sb"]("f%d_%s" % (li_, n), s, d, es=les)
    h_tok, hT = env["h_tok"], env["hT"]
    actT = sb("actT", [128, FBH, N], BF16)
    R_act = [Res("act%d" % v) for v in range(FBH)]
    wd = sb("wd", [128, FBH, D], BF16)
    R_wd = [Res("wd%d" % v) for v in range(FBH)]
    xs = [sb("xs%d" % i, [128, nseq, 2 + L]) for i in range(2)]
    acc4 = [sb("acc%d" % i, [128, nseq, L]) for i in range(4)]
    R_xs = [Res("xs%d" % i) for i in range(2)]
    R_xh = [Res("xh%d" % i) for i in range(2)]
    R_acc4 = [Res("acc%d" % i) for i in range(4)]
    cw, cb = ctx["ffn_cw"][li_], ctx["ffn_cb"][li_]
    Rst = ctx["R_state"]
    if nseq == 1:
        hist = ctx["ffn_hist_p"][li_]
        R_hist = Rst
    else:
        hist = sb("hist", [128, FB, nseq, 2])
        R_hist = Res("hist")
        stage = sb("hstage", [32, FF2])
        load_T(ctx, I["state_ffn_conv"][li_], FB, 32, lambda b0, nb_: hist[:, b0:b0 + nb_, :, :].rearrange("p b s k -> p b (s k)"), stage, R_hist, "fh", env["ds_in"])
    env["load_ln"](li_, 1)
    def wd_hook(v):
        def f():
            if pn == "A":
                dma("pool", wd[:, v, :], I["ffn_w_down"][li_][v * 128:(v + 1) * 128, :], [], [R_wd[v]], env["ds_w"])
            else:
                dma("sp" if pn == "S" else "pool", wd[:, v, :], ctx["WDd"][li_][:, v * D:(v + 1) * D], [], [R_wd[v]], env["ds_w"])
            if v == FBH - 1:
                kb.group_final(R_wd, env["ds_w"])
        return f
    blocks = []
    for v in range(FBH):
        blocks += [(I["ffn_w_up"][li_], v * 128), (I["ffn_w_up"][li_], (v + FBH) * 128)]
    env["plan_w"](blocks, {2 * v + 1: wd_hook(v) for v in range(FBH)})
    ngrp = len(colgroups(N))
    banksets = [[0, 1, 2][:ngrp], [3, 4, 5][:ngrp]]
    bi = 0
    for v in range(FBH):
        acc = acc4[2 * (v % 2):2 * (v % 2) + 2]
        R_acc = R_acc4[2 * (v % 2):2 * (v % 2) + 2]
        for which in range(2):
            blk = v + which * FBH
            s = env["load_w_block"](2 * v + which)
            banks = banksets[bi % 2]
            bi += 1
            env["up_matmul"](s, banks, [])
            x = xs[which]
            op("act", [R_hist], [R_xh[which]], lambda e, x=x, blk=blk: e.copy(out=x[:, :, 0:2], in_=hist[:, blk, :, :]))
            if nseq == 1:
                op("act", [RP[b_] for b_ in banks], [R_xs[which]], lambda e, x=x, banks=banks: e.copy(out=x[:, 0, 2:2 + N], in_=ctx["PSall"][:, banks[0] * 512:banks[0] * 512 + N]))
            else:
                op("act", [RP[banks[0]]], [R_xs[which]], lambda e, x=x, banks=banks: e.copy(out=x[:, :, 2:2 + L], in_=PS[banks[0]][:, 0:N].rearrange("p (s t) -> p s t", t=L)))
            op("act", [R_xs[which]], [R_hist], lambda e, x=x, blk=blk: e.copy(out=hist[:, blk, :, :], in_=x[:, :, L:L + 2]))
            conv_taps(kb, x, acc[which], nseq, L, 3, [cw[k][:, blk:blk + 1] for k in range(3)], cb[:, blk:blk + 1], R_xs[which], R_xh[which], R_acc[which])
        op("act", [R_acc[1]], [R_acc[1]], lambda e, acc=acc: e.activation(out=acc[1][:, :, :], in_=acc[1][:, :, :], func=AF.Gelu_apprx_tanh))
        op("dve", [R_acc[0], R_acc[1]], [R_act[v]], lambda e, v=v, acc=acc: e.tensor_tensor(out=actT[:, v, :], in0=acc[0][:, :, :].rearrange("p s t -> p (s t)"), in1=acc[1][:, :, :].rearrange("p s t -> p (s t)"), op=ALU.mult))
    if pn == "A":
        dma("sp", ctx["WDd"][li_], wd[:].rearrange("p a b -> p (a b)"), R_wd, [], ctx["ds_wb"])
    if pn != "A":
        stg = sb("ostage", [32, 256 if nseq == 1 else 2048])
        m = 2 * nseq
        store_T(ctx, lambda b: hist[:, b, :, :].rearrange("p s k -> p (s k)"), FB, m,
                (O["ffn_conv_p"] if nseq == 1 else O["ffn_conv_s"])[li_], stg, R_hist, "fo")
    pending = None
    for t in range(NT):
        r = rows[t]
        pb = [0, 1] if t % 2 == 0 else [2, 3]
        for v in range(FBH):
            for hh in range(2):
                op("pe", [R_act[v], R_wd[v]], [RP[pb[hh]]], lambda e, v=v, hh=hh: e.matmul(PS[pb[hh]][0:r, :], lhsT=actT[:, v, t * 128:t * 128 + r], rhs=wd[:, v, hh * 512:(hh + 1) * 512], start=(v == 0), stop=(v == FBH - 1)))
        zi = t % 4
        z = env["zt"][zi]
        for hh in range(2):
            op("dve", [env["R_h"][t], RP[pb[hh]]], [env["R_zt"][zi]], lambda e, hh=hh: e.scalar_tensor_tensor(out=z[0:r, hh * 512:(hh + 1) * 512], in0=h_tok[0:r, t, hh * 512:(hh + 1) * 512], scalar=DN_ALPHA, in1=PS[pb[hh]][0:r, :], op0=ALU.mult, op1=ALU.add))
        if pending:
            pending()
        pending = env["layer_norm_tile"](t, zi, li_, 1, last)
    pending()


def rg_layer(env, li_, j, les):
    ctx = env["ctx"]; kb = ctx["kb"]; nc = ctx["nc"]; I = ctx["I"]; O = ctx["O"]
    op, dma = kb.op, kb.dma
    PS, RP = ctx["PS"], ctx["RP"]
    pn, nseq, L, N, NT, rows = env["pn"], env["nseq"], env["L"], env["N"], env["NT"], env["rows"]
    sb = lambda n, s, d=F32: env["sb"]("r%d_%s" % (li_, n), s, d, es=les)
    h_tok, hT = env["h_tok"], env["hT"]
    Rst = ctx["R_state"]
    hgT = sb("hgT", [128, NB, N], BF16)
    R_hg = [Res("hg%d" % c) for c in range(NB)]
    wg = sb("wg", [128, 4, 2, 512], BF16)
    R_wg = Res("wg")
    wo = sb("wo", [128, NB, D], BF16)
    R_wo = Res("wo")
    if pn == "A":
        for n in range(4):
            dma("pool", wg[:, n, :, :], I["rg_w_gates"][j, n].rearrange("(kc p) f -> p kc f", p=128), [], [R_wg], env["ds_w"])
    else:
        dma("pool", wg[:].rearrange("p a b c -> p (a b c)"), ctx["RGWGd"][j], [], [R_wg], env["ds_w"])
    kb.group_final([R_wg], env["ds_w"])
    env["load_ln"](li_, 0)
    if nseq == 1:
        hist = ctx["rg_hist_p"][j]; hst = ctx["rg_h_p"][j]; R_hist = Rst
    else:
        hist = sb("hist", [128, NB, nseq, 3]); hst = sb("hst", [128, NB, nseq]); R_hist = Res("rhist")
        stage = sb("stage", [48, D])
        load_T(ctx, I["state_rg_conv"][j], NB, 48, lambda b0, nb_: hist[:, b0:b0 + nb_, :, :].rearrange("p b s k -> p b (s k)"), stage, R_hist, "rc", env["ds_in"])
        stage2 = sb("stage2", [16, D])
        load_T(ctx, I["state_rg_h"][j], NB, 16, lambda b0, nb_: hst[:, b0:b0 + nb_, :], stage2, R_hist, "rh", env["ds_in"])
    xs = [sb("xs%d" % i, [128, nseq, 3 + L]) for i in range(2)]
    xc = [[sb("xc%d_%d" % (b, i), [128, nseq, L]) for i in range(2)] for b in range(2)]
    xcb = [[sb("xcb%d_%d" % (b, i), [128, N], BF16) for i in range(2)] for b in range(2)]
    gl = [[sb("gl%d_%d" % (b, i), [128, N]) for i in range(2)] for b in range(2)]
    work = [[sb("wk%d_%d" % (b, i), [128, nseq, L]) for i in range(3)] for b in range(2)]
    R_xs = [Res() for _ in range(2)]; R_xh = [Res() for _ in range(2)]
    R_xc = [[Res() for _ in range(2)] for _ in range(2)]; R_gl = [[Res() for _ in range(2)] for _ in range(2)]
    R_wk = [Res("rgwork0"), Res("rgwork1")]
    cw, cb, bg, m8 = ctx["rg_cw"][j], ctx["rg_cb"][j], ctx["rg_bg"][j], ctx["rg_m8sp"][j]
    one_ap = env["one_ap"]
    ngrp = len(colgroups(N))
    banksets = [[0, 1, 2][:ngrp], [3, 4, 5][:ngrp]]
    bi = [0]
    blocks = []
    for c in range(NB):
        blocks += [(I["rg_w_in"][j], c * 128), (I["rg_w_in"][j], D + c * 128)]
    env["plan_w"](blocks)
    flat = lambda t: t[:, :, :].rearrange("p s t -> p (s t)")

    def stage_proj(n):
        pb_ = n % 2
        for q in range(2):
            c = 2 * n + q
            s = env["load_w_block"](2 * c)
            banks = banksets[bi[0] % 2]; bi[0] += 1
            env["up_matmul"](s, banks, [])
            op("act", [RP[b_] for b_ in banks], [R_gl[pb_][q]], lambda e, q=q, banks=banks: e.activation(out=gl[pb_][q][:, 0:N], in_=ctx["PSall"][:, banks[0] * 512:banks[0] * 512 + N], func=AF.Gelu_apprx_tanh))
            s = env["load_w_block"](2 * c + 1)
            banks = banksets[bi[0] % 2]; bi[0] += 1
            env["up_matmul"](s, banks, [])
            x = xs[q]
            op("act", [R_hist], [R_xh[q]], lambda e, x=x, c=c: e.copy(out=x[:, :, 0:3], in_=hist[:, c, :, :]))
            if nseq == 1:
                op("act", [RP[b_] for b_ in banks], [R_xs[q]], lambda e, x=x, banks=banks: e.copy(out=x[:, 0, 3:3 + N], in_=ctx["PSall"][:, banks[0] * 512:banks[0] * 512 + N]))
            else:
                op("act", [RP[banks[0]]], [R_xs[q]], lambda e, x=x, banks=banks: e.copy(out=x[:, :, 3:3 + L], in_=PS[banks[0]][:, 0:N].rearrange("p (s t) -> p s t", t=L)))
            op("act", [R_xs[q]], [R_hist], lambda e, x=x, c=c: e.copy(out=hist[:, c, :, :], in_=x[:, :, L:L + 3]))
            conv_taps(kb, x, xc[pb_][q], nseq, L, 4, [cw[k][:, c:c + 1] for k in range(4)], cb[:, c:c + 1], R_xs[q], R_xh[q], R_xc[pb_][q])
            op("dve", [R_xc[pb_][q]], [R_xc[pb_][q]], lambda e, q=q: e.tensor_copy(out=xcb[pb_][q][:, :], in_=flat(xc[pb_][q])))

    def stage_gate(n):
        pb_ = n % 2
        for q in range(2):
            c = 2 * n + q
            T1, T2, T3 = work[c % 2]
            Rw = R_wk[c % 2]
            pr_ = banksets[0]; pi_ = banksets[1]
            for (dst_banks, off) in ((pr_, q * 128), (pi_, 256 + q * 128)):
                for gi, (c0, cn) in enumerate(colgroups(N)):
                    for kc in range(2):
                        op("pe", [R_wg, R_xc[pb_][kc]], [RP[dst_banks[gi]]], lambda e, kc=kc, gi=gi, c0=c0, cn=cn, off=off, dst_banks=dst_banks: e.matmul(PS[dst_banks[gi]][:, 0:cn], lhsT=wg[:, n, kc, off:off + 128], rhs=xcb[pb_][kc][:, c0:c0 + cn], start=(kc == 0), stop=(kc == 1)))
            op("act", [RP[b_] for b_ in pr_], [Rw], lambda e, c=c: e.activation(out=flat(T1)[:, 0:N], in_=ctx["PSall"][:, pr_[0] * 512:pr_[0] * 512 + N], func=AF.Sigmoid, bias=bg[:, c:c + 1], scale=1.0))
            op("act", [RP[b_] for b_ in pi_], [Rw], lambda e, c=c: e.activation(out=flat(T2)[:, 0:N], in_=ctx["PSall"][:, pi_[0] * 512:pi_[0] * 512 + N], func=AF.Sigmoid, bias=bg[:, 8 + c:8 + c + 1], scale=1.0))
            op("act", [Rw], [Rw], lambda e, c=c: e.activation(out=flat(T1), in_=flat(T1), func=AF.Exp, scale=m8[:, c:c + 1]))
            op("act", [Rw], [Rw], lambda e: e.activation(out=flat(T3), in_=flat(T1), func=AF.Square))
            op("act", [Rw], [Rw], lambda e: e.activation(out=flat(T3), in_=flat(T3), func=AF.Sqrt, scale=-1.0, bias=one_ap[:, :]))
            op("dve", [Rw, R_xc[pb_][q]], [Rw], lambda e, q=q: e.tensor_tensor(out=flat(T2), in0=flat(T2), in1=flat(xc[pb_][q]), op=ALU.mult))
            op("dve", [Rw], [Rw], lambda e: e.tensor_tensor(out=flat(T2), in0=flat(T2), in1=flat(T3), op=ALU.mult))
            for s_ in range(nseq):
                op("dve", [Rw, R_hist], [Rw], lambda e, s_=s_, c=c: e.tensor_tensor_scan(out=T3[:, s_, :], data0=T1[:, s_, :], data1=T2[:, s_, :], initial=hst[:, c, s_:s_ + 1], op0=ALU.mult, op1=ALU.add))
            op("dve", [Rw], [R_hist], lambda e, c=c: e.tensor_copy(out=hst[:, c, :], in_=T3[:, :, L - 1]))
            op("dve", [Rw, R_gl[pb_][q]], [R_hg[c]], lambda e, c=c, q=q: e.tensor_tensor(out=hgT[:, c, :], in0=flat(T3), in1=gl[pb_][q][:, :], op=ALU.mult))

    for n in range(4):
        stage_proj(n)
        if n == 1:
            if pn == "A":
                dma("pool", wo[:], I["rg_w_out"][j].rearrange("(kc p) f -> p kc f", p=128), [], [R_wo], env["ds_w"])
            else:
                dma("pool", wo[:].rearrange("p a b -> p (a b)"), ctx["RGWOd"][j], [], [R_wo], env["ds_w"])
            kb.group_final([R_wo], env["ds_w"])
        if n > 0:
            stage_gate(n - 1)
    stage_gate(3)
    if pn == "A":
        dma("sp", ctx["RGWGd"][j], wg[:].rearrange("p a b c -> p (a b c)"), [R_wg], [], ctx["ds_wb"])
        dma("sp", ctx["RGWOd"][j], wo[:].rearrange("p a b -> p (a b)"), [R_wo], [], ctx["ds_wb"])
    if pn != "A":
        stg = sb("ostage", [48, 256])
        store_T(ctx, lambda b: hist[:, b, :, :].rearrange("p s k -> p (s k)"), NB, 3 * nseq,
                (O["rg_conv_p"] if nseq == 1 else O["rg_conv_s"])[j], stg, R_hist, "ro")
        store_T(ctx, lambda b: hst[:, b, :], NB, nseq, (O["rg_h_p"] if nseq == 1 else O["rg_h_s"])[j], stg, R_hist, "rho")
    pending = None
    for t in range(NT):
        r = rows[t]
        pb = [0, 1] if t % 2 == 0 else [2, 3]
        for kc in range(NB):
            for hh in range(2):
                op("pe", [R_hg[kc], R_wo], [RP[pb[hh]]], lambda e, kc=kc, hh=hh: e.matmul(PS[pb[hh]][0:r, :], lhsT=hgT[:, kc, t * 128:t * 128 + r], rhs=wo[:, kc, hh * 512:(hh + 1) * 512], start=(kc == 0), stop=(kc == NB - 1)))
        zi = t % 4
        z = env["zt"][zi]
        for hh in range(2):
            op("dve", [env["R_h"][t], RP[pb[hh]]], [env["R_zt"][zi]], lambda e, hh=hh: e.scalar_tensor_tensor(out=z[0:r, hh * 512:(hh + 1) * 512], in0=h_tok[0:r, t, hh * 512:(hh + 1) * 512], scalar=DN_ALPHA, in1=PS[pb[hh]][0:r, :], op0=ALU.mult, op1=ALU.add))
        if pending:
            pending()
        pending = env["layer_norm_tile"](t, zi, li_, 0, False)
    pending()


def ONE_AP(env):
    if "one_ap" not in env:
        kb = env["ctx"]["kb"]
        t = env["sb"]("one", [128, 1])
        kb.op("dve", [], [env["ctx"]["R_const"]], lambda e: e.memset(t[:], 1.0))
        env["one_ap"] = t
    return env["one_ap"]


def s5_layer(env, li_, j, les):
    ctx = env["ctx"]; kb = ctx["kb"]; nc = ctx["nc"]; I = ctx["I"]; O = ctx["O"]
    op, dma = kb.op, kb.dma
    PS, RP = ctx["PS"], ctx["RP"]
    pn, nseq, L, N, NT, rows, NC = env["pn"], env["nseq"], env["L"], env["N"], env["NT"], env["rows"], env["NC"]
    sb = lambda n, s, d=F32: env["sb"]("s%d_%s" % (li_, n), s, d, es=les)
    h_tok, hT = env["h_tok"], env["hT"]
    Rst = ctx["R_state"]
    uT = sb("uT", [128, NB, N], BF16)
    gyT = uT
    R_u = [Res() for _ in range(NB)]
    R_gy = R_u
    R_X = Res("Xs")
    Sinb = sb("Sinb", [128, 32, 2, NC], BF16)
    R_S = Res("Sinb")
    wo = sb("wo", [128, NB, 2 * D], BF16)
    R_wo = Res("wo")
    env["load_ln"](li_, 0)
    if nseq == 1:
        st = ctx["s5st_p"][j]; R_st = Rst
    else:
        st = sb("st", [128, 32, 2, nseq]); R_st = Res("s5st")
        stage = sb("stage", [16, 4096])
        for ri, nm in enumerate(("state_s5_re", "state_s5_im")):
            load_T(ctx, I[nm][j], 32, 16, lambda b0, nb_, ri=ri: st[:, b0:b0 + nb_, ri, :], stage, R_st, "s5" + str(ri), env["ds_in"])
    ngrp = len(colgroups(N))
    banksets = [[0, 1, 2][:ngrp], [3, 4, 5][:ngrp]]
    env["plan_w"]([(I["s5_w_in"][j], c * 128) for c in range(NB)])
    for c in range(NB):
        s = env["load_w_block"](c)
        banks = banksets[c % 2]
        env["up_matmul"](s, banks, [])
        op("act", [RP[b_] for b_ in banks], [R_u[c]], lambda e, c=c, banks=banks: e.copy(out=uT[:, c, 0:N], in_=ctx["PSall"][:, banks[0] * 512:banks[0] * 512 + N]))
    if pn == "A":
        dma("pool", wo[:, 0:4, :], I["s5_w_out"][j][0:512, :].rearrange("(kc p) f -> p kc f", p=128), [], [R_wo], env["ds_w"])
        dma("pool", wo[:, 4:8, :], I["s5_w_out"][j][512:1024, :].rearrange("(kc p) f -> p kc f", p=128), [], [R_wo], env["ds_w"])
        kb.group_final([R_wo], env["ds_w"])
    else:
        dma("pool", wo[:].rearrange("p a b -> p (a b)"), ctx["S5WOd"][j], [], [R_wo], env["ds_w"])
        kb.group_final([R_wo], env["ds_w"])
    ies = ExitStack()
    sbo = sb
    sb = lambda n, s_, d=F32: env["sb"]("s%d_%s" % (li_, n), s_, d, es=ies)
    Xs = sb("Xs", [128, 32, 2, NC])
    WXs = [sb("WX%d" % i, [128, 8, 2, 128], BF16) for i in range(2)]
    R_WX = [Res() for _ in range(2)]
    ds_wx = [kb.dsem("s5x%s%d_%d" % (pn, li_, i)) for i in range(2)]
    ds_kw = [kb.dsem("s5k%s%d_%d" % (pn, li_, i)) for i in range(2)]
    for c in range(NB):
        wi = c % 2
        dma("sp", WXs[wi][:], ctx["WXd"][j][:, c], [], [R_WX[wi]], ds_wx[wi])
        for ri in range(2):
            for tau in range(8):
                for pr in range(4):
                    bank = pr
                    rs = slice(32 * pr, 32 * pr + 32)
                    op("pe", [R_WX[wi], R_u[c]], [RP[bank]], lambda e, ri=ri, tau=tau, rs=rs, bank=bank, wi=wi, c=c, pr=pr: e.matmul(
                        PS[bank][:, ri * 256:ri * 256 + NC], lhsT=WXs[wi][rs, tau, ri, :],
                        rhs=uT[rs, c, :].rearrange("p (n t) -> p n t", t=8)[:, :, tau], start=(tau == 0), stop=(tau == 7), tile_position=(32 * pr, 0)))
        for pr in range(4):
            bank = pr
            op("act", [RP[bank]], [R_X], lambda e, bank=bank, c=c, pr=pr: e.copy(out=Xs[:, 4 * c + pr, :, :], in_=PS[bank][:, :].rearrange("p (r n) -> p r n", n=256)[:, :, 0:NC]))
    a8r, a8i, a8in = ctx["A8re"][j], ctx["A8im"][j], ctx["A8imn"][j]
    Mrot = sb("Mrot", [128, 32, 2, 2]); prod = sb("prod", [128, 32, 2, 2]); t1 = sb("t1", [128, 32, 2])
    op("dve", [Rst], [R_X], lambda e: e.tensor_copy(out=Mrot[:, :, 0, 0], in_=a8r[:, :]))
    op("dve", [Rst], [R_X], lambda e: e.tensor_copy(out=Mrot[:, :, 1, 1], in_=a8r[:, :]))
    op("dve", [Rst], [R_X], lambda e: e.tensor_copy(out=Mrot[:, :, 0, 1], in_=a8in[:, :]))
    op("dve", [Rst], [R_X], lambda e: e.tensor_copy(out=Mrot[:, :, 1, 0], in_=a8i[:, :]))
    if nseq == 1:
        op("act", [R_st], [R_S], lambda e: e.copy(out=Sinb[:, :, :, 0], in_=st[:, :, :, 0]))
    else:
        op("act", [R_st], [R_S], lambda e: e.copy(out=Sinb[:, :, :, :], in_=st[:, :, :, :]))
    nsteps = NC if nseq == 1 else nseq
    for c in range(nsteps):
        if nseq == 1:
            prev = st[:, :, :, 0] if c == 0 else Xs[:, :, :, c - 1]
        else:
            prev = st[:, :, :, c]
        cur = Xs[:, :, :, c]
        rd = [R_X, R_st, Rst]
        pb_ = prev.unsqueeze(2).broadcast_to([128, 32, 2, 2])
        op("dve", rd, [R_X], lambda e, pb_=pb_: e.tensor_tensor(out=prod[:], in0=pb_, in1=Mrot[:], op=ALU.mult))
        op("dve", rd, [R_X], lambda e: e.tensor_tensor(out=t1[:], in0=prod[:, :, :, 0], in1=prod[:, :, :, 1], op=ALU.add))
        op("dve", rd, [R_X], lambda e, cur=cur: e.tensor_tensor(out=cur, in0=cur, in1=t1[:], op=ALU.add))
    if nseq == 1:
        op("act", [R_X], [R_S], lambda e: e.copy(out=Sinb[:, :, :, 1:NC], in_=Xs[:, :, :, 0:NC - 1]))
        op("dve", [R_X], [R_st], lambda e: e.tensor_copy(out=st[:, :, :, 0], in_=Xs[:, :, :, NC - 1]))
    else:
        op("dve", [R_X], [R_st], lambda e: e.tensor_copy(out=st[:, :, :, :], in_=Xs[:, :, :, :]))
    if pn != "A":
        stg = sb("ostage", [16, 2048])
        for ri, nm in enumerate(("s5_re", "s5_im")):
            store_T(ctx, lambda b, ri=ri: st[:, b, ri, :], 32, nseq, O[nm + ("_p" if nseq == 1 else "_s")][j], stg, R_st, "s5o%d" % ri)
    kb.barrier()
    ies.close()
    sb = sbo
    KIs = [sb("KI%d" % i, [128, 8, 128], BF16) for i in range(2)]
    WYs = [sb("WY%d" % i, [128, 4, 8, 2, 32], BF16) for i in range(2)]
    R_KI = [Res() for _ in range(2)]
    R_WY = [Res() for _ in range(2)]
    cg = colgroups(NC, 64)
    for c in range(NB):
        wi = c % 2
        dma("sp", KIs[wi][:], ctx["KId"][j][:, c], [], [R_KI[wi]], ds_kw[wi])
        dma("sp", WYs[wi][:], ctx["WYd"][j][:, 4 * c:4 * c + 4], [], [R_WY[wi]], ds_kw[wi])
        kb.group_final([R_KI[wi], R_WY[wi]], ds_kw[wi])
        banks = banksets[c % 2]
        for gi, (n0, nn) in enumerate(cg):
            bank = banks[gi]
            uv = uT[:, c, n0 * 8:(n0 + nn) * 8].rearrange("p (n t) -> p t n", t=8)
            for lag in range(8):
                op("pe", [R_KI[wi], R_u[c]], [RP[bank]], lambda e, lag=lag, bank=bank, nn=nn, uv=uv, wi=wi: e.matmul(
                    PS[bank][:, lag * nn:8 * nn], lhsT=KIs[wi][:, lag, :], rhs=uv[:, 0:8 - lag, :], start=(lag == 0), stop=False))
            for tau in range(8):
                for ri in range(2):
                    for pr in range(4):
                        lastmm = (tau == 7 and ri == 1)
                        op("pe", [R_WY[wi], R_S], [RP[bank]], lambda e, pr=pr, tau=tau, ri=ri, bank=bank, wi=wi, n0=n0, nn=nn, c=c, lastmm=lastmm: e.matmul(
                            PS[bank][32 * pr:32 * pr + 32, tau * nn:(tau + 1) * nn], lhsT=WYs[wi][:, pr, tau, ri, :], rhs=Sinb[:, 4 * c + pr, ri, n0:n0 + nn], start=False, stop=lastmm, tile_position=(0, 32 * pr)))
            op("act", [RP[bank]], [R_gy[c]], lambda e, bank=bank, n0=n0, nn=nn, c=c: e.activation(out=gyT[:, c, n0 * 8:(n0 + nn) * 8].rearrange("p (n t) -> p n t", t=8), in_=PS[bank][:, 0:nn * 8].rearrange("p (t n) -> p n t", t=8), func=AF.Gelu_apprx_tanh))
    if pn == "A":
        dma("sp", ctx["S5WOd"][j], wo[:].rearrange("p a b -> p (a b)"), [R_wo], [], ctx["ds_wb"])
    sgs = [sb("sg%d" % i, [128, D]) for i in range(2)]
    R_sgs = [Res("sg%d" % i) for i in range(2)]
    pending = None
    for t in range(NT):
        sg = sgs[t % 2]
        R_sg = R_sgs[t % 2]
        r = rows[t]
        pb = [2, 3, 0, 1]
        for qq in (2, 3, 0, 1):
            for kc in range(NB):
                op("pe", [R_gy[kc], R_wo], [RP[pb[qq]]], lambda e, kc=kc, qq=qq: e.matmul(PS[pb[qq]][0:r, :], lhsT=gyT[:, kc, t * 128:t * 128 + r], rhs=wo[:, kc, qq * 512:(qq + 1) * 512], start=(kc == 0), stop=(kc == NB - 1)))
        zi = t % 4
        z = env["zt"][zi]
        for hh in range(2):
            op("act", [RP[pb[2 + hh]]], [R_sg], lambda e, hh=hh: e.activation(out=sg[0:r, hh * 512:(hh + 1) * 512], in_=PS[pb[2 + hh]][0:r, :], func=AF.Sigmoid))
            op("dve", [R_sg, RP[pb[hh]]], [R_sg], lambda e, hh=hh: e.tensor_tensor(out=sg[0:r, hh * 512:(hh + 1) * 512], in0=sg[0:r, hh * 512:(hh + 1) * 512], in1=PS[pb[hh]][0:r, :], op=ALU.mult))
            op("dve", [env["R_h"][t], R_sg], [env["R_zt"][zi]], lambda e, hh=hh: e.scalar_tensor_tensor(out=z[0:r, hh * 512:(hh + 1) * 512], in0=h_tok[0:r, t, hh * 512:(hh + 1) * 512], scalar=DN_ALPHA, in1=sg[0:r, hh * 512:(hh + 1) * 512], op0=ALU.mult, op1=ALU.add))
        if pending:
            pending()
        pending = env["layer_norm_tile"](t, zi, li_, 0, False)
    pending()


_NC_CACHE = {}


def _get_nc(dbg=None):
    key = repr(dbg)
    if key not in _NC_CACHE:
        nc = bass.Bass("TRN2", target_bir_lowering=False)
        build(nc, dbg)
        _NC_CACHE[key] = nc
    return _NC_CACHE[key]


def kernel(dbg=None, **inp):
    f = lambda a: np.ascontiguousarray(np.asarray(a, dtype=np.float32))
    ident = np.eye(128, dtype=np.float32)
    bmask = np.kron(np.eye(4, dtype=np.float32), np.ones((32, 32), np.float32))
    wnames = ["meta_tokens", "s5_w_in", "s5_lam_re", "s5_lam_im", "s5_log_step", "s5_b_re", "s5_b_im", "s5_c_re",
              "s5_c_im", "s5_d", "s5_w_out", "rg_w_in", "rg_conv_w", "rg_conv_b", "rg_w_gates", "rg_b_gates",
              "rg_lam", "rg_w_out", "ffn_w_up", "ffn_conv_w", "ffn_conv_b", "ffn_w_down", "ln_g", "ln_b"]
    shared = {n: f(inp[n]) for n in wnames}
    shared["ident"] = ident
    shared["bmask"] = bmask
    shared["sel2"] = np.kron(np.eye(2, dtype=np.float32), np.ones((1, 64), np.float32))
    in_maps = []
    for c in range(8):
        m = dict(shared)
        sl = slice(16 * c, 16 * c + 16)
        m["x_prompt"] = f(inp["x_prompt"][c])
        m["x_sample"] = f(inp["x_sample"][sl]).reshape(128, D)
        m["state_s5_re"] = f(inp["state_s5_re"][:, sl]).reshape(2, 16, 4096)
        m["state_s5_im"] = f(inp["state_s5_im"][:, sl]).reshape(2, 16, 4096)
        m["state_rg_h"] = f(inp["state_rg_h"][:, sl])
        m["state_rg_conv"] = f(inp["state_rg_conv"][:, sl]).reshape(2, 48, D)
        m["state_ffn_conv"] = f(inp["state_ffn_conv"][:, sl]).reshape(4, 32, FF2)
        in_maps.append(m)
    nc = _get_nc(dbg)
    res = run_bass_kernel_spmd(nc, in_maps, core_ids=list(range(8)))
    R = res.results
    cat = lambda k, ax: np.concatenate([np.asarray(R[c][k]) for c in range(8)], axis=ax)
    y_prompt = np.stack([np.asarray(R[c]["y_prompt"]) for c in range(8)], 0)
    y_sample = cat("y_sample", 0).reshape(128, 8, D)
    s5_re_p = cat("s5_re_p", 1).reshape(2, 8, 64, 64)
    s5_im_p = cat("s5_im_p", 1).reshape(2, 8, 64, 64)
    rg_h_p = cat("rg_h_p", 1).reshape(2, 8, D)
    rg_conv_p = np.stack([np.asarray(R[c]["rg_conv_p"]) for c in range(8)], 1).reshape(2, 8, 3, D)
    ffn_conv_p = np.stack([np.asarray(R[c]["ffn_conv_p"]) for c in range(8)], 1).reshape(4, 8, 2, FF2)
    s5_re_s = cat("s5_re_s", 1).reshape(2, 128, 64, 64)
    s5_im_s = cat("s5_im_s", 1).reshape(2, 128, 64, 64)
    rg_h_s = cat("rg_h_s", 1).reshape(2, 128, D)
    rg_conv_s = cat("rg_conv_s", 1).reshape(2, 128, 3, D)
    ffn_conv_s = cat("ffn_conv_s", 1).reshape(4, 128, 2, FF2)
    outs = (y_prompt, y_sample, s5_re_p, s5_im_p, rg_h_p, rg_conv_p, ffn_conv_p,
            s5_re_s, s5_im_s, rg_h_s, rg_conv_s, ffn_conv_s)
    return tuple(np.ascontiguousarray(o, dtype=np.float32) for o in outs)
```

```python
import math
from contextlib import ExitStack
import numpy as np
import concourse.bass as bass
import concourse.mybir as mybir
from concourse.bass_utils import run_bass_kernel_spmd

F32 = mybir.dt.float32
BF16 = mybir.dt.bfloat16
I32 = mybir.dt.int32
AF = mybir.ActivationFunctionType
ALU = mybir.AluOpType

D = 1024
NB = 8
FF = 2816
FF2 = 5632
FB = 44
FBH = 22
DEPTH = 4
SEQ = 2048
NMETA = 16
DN_ALPHA = (2 * DEPTH) ** 0.25
LN_EPS = 1e-5
RG_C = 8.0
TWO_PI = 2.0 * math.pi
ATTACH_WAITS = True
STATS = {}


class Res:
    __slots__ = ("name", "w", "r")

    def __init__(self, name=""):
        self.name = name
        self.w = None
        self.r = {}


class DSem:
    def __init__(self, sem):
        self.sem = sem
        self.val = 0


class Q:
    def __init__(self, name, eng, sem, eager):
        self.name = name
        self.eng = eng
        self.sem = sem
        self.eager = eager
        self.n = 0
        self.last = None
        self.last_ms = True
        self.ms = []
        self.semval = 0
        self.known = {}


class KB:
    def __init__(self, nc, es):
        self.nc = nc
        self.es = es
        self.q = {}
        for name, eng, eager in (("pe", nc.tensor, False), ("act", nc.scalar, False),
                                 ("dve", nc.vector, False), ("pool", nc.gpsimd, True),
                                 ("sp", nc.sync, True)):
            self.q[name] = Q(name, eng, self.sem("q_" + name), eager)
        self.dsems = []
        self.nsem = 0

    def sem(self, name):
        return self.es.enter_context(self.nc.semaphore(name))

    def dsem(self, name):
        d = DSem(self.sem("d_" + name))
        self.dsems.append(d)
        return d

    def sb(self, name, shape, dt, es=None):
        return (es or self.es).enter_context(self.nc.sbuf_tensor("sb_" + name, list(shape), dt))

    def ps(self, name, shape, dt=F32, es=None):
        return (es or self.es).enter_context(self.nc.psum_tensor("pt_" + name, list(shape), dt))

    def _milestone(self, A, k):
        lo, hi = 0, len(A.ms)
        while lo < hi:
            mid = (lo + hi) // 2
            if A.ms[mid][0] >= k:
                hi = mid
            else:
                lo = mid + 1
        if lo < len(A.ms):
            return A.ms[lo][1]
        assert A.last is not None and not A.last_ms and A.n >= k
        A.semval += 1
        A.last.then_inc(A.sem, 1)
        A.last_ms = True
        A.ms.append((A.n, A.semval))
        return A.semval

    def _wait(self, q, deps):
        need = {}
        for d in deps:
            if d is None:
                continue
            if d[0] == "q":
                A, k = d[1], d[2]
                if A is q and q.name == "pe":
                    continue
                v = self._milestone(A, k)
                key = A
                sem = A.sem
            else:
                key = d[1]
                sem = d[1].sem
                v = d[2]
            if q.known.get(key, 0) >= v:
                continue
            if need.get(key, (None, 0))[1] < v:
                need[key] = (sem, v)
        items = list(need.items())
        attach = None
        if ATTACH_WAITS and items:
            key, (sem, v) = items.pop()
            q.known[key] = v
            attach = (sem, v)
        for key, (sem, v) in items:
            q.eng.wait_ge(sem, v)
            q.known[key] = v
            STATS[q.name] = STATS.get(q.name, 0) + 1
        if attach:
            STATS["att_" + q.name] = STATS.get("att_" + q.name, 0) + 1
        return attach

    def _deps(self, reads, writes):
        deps = []
        for r in reads:
            if r.w is not None:
                deps.append(r.w)
        for w in writes:
            if w.w is not None:
                deps.append(w.w)
            deps.extend(w.r.values())
        return deps

    def op(self, qn, reads, writes, fn):
        q = self.q[qn]
        att = self._wait(q, self._deps(reads, writes))
        ins = fn(q.eng)
        if att is not None:
            ins._wait_ge(att[0], att[1])
        q.n += 1
        q.last = ins
        q.last_ms = False
        if q.eager:
            q.semval += 1
            ins.then_inc(q.sem, 1)
            q.last_ms = True
            q.ms.append((q.n, q.semval))
        me = ("q", q, q.n)
        for r in reads:
            r.r[q] = me
        for w in writes:
            w.w = me
            w.r = {}
        return ins

    def dma(self, qn, out, in_, reads, writes, ds, **kw):
        q = self.q[qn]
        att = self._wait(q, self._deps(reads, writes))
        ins = q.eng.dma_start(out=out, in_=in_, **kw)
        if att is not None:
            ins._wait_ge(att[0], att[1])
        ds.val += 16
        ins.then_inc(ds.sem, 16)
        me = ("d", ds, ds.val)
        for r in reads:
            r.r[ds] = me
        for w in writes:
            w.w = me
            w.r = {}
        return ins

    def group_final(self, ress, ds):
        for r in ress:
            r.w = ("d", ds, ds.val)

    def barrier(self, exclude=(), full=False):
        marks = []
        for A in self.q.values():
            if A.n > 0:
                marks.append(("q", A, A.n))
        for d in self.dsems:
            if d in exclude:
                continue
            if d.val > 0:
                marks.append(("d", d, d.val))
        global ATTACH_WAITS
        sv = ATTACH_WAITS
        ATTACH_WAITS = False
        for q in self.q.values():
            if q.name == "pe" and not full:
                continue
            self._wait(q, [m for m in marks if not (m[0] == "q" and m[1] is q)])
        ATTACH_WAITS = sv

    def finish(self):
        self.barrier(full=True)


def build(nc, dbg=None):
    es = ExitStack()
    kb = KB(nc, es)
    with es:
        _build(nc, kb, dbg)
    return nc


def _dram_in(nc, name, shape, dt=F32):
    return nc.dram_tensor(name, list(shape), dt, kind="ExternalInput").ap()


def _dram_out(nc, name, shape, dt=F32):
    return nc.dram_tensor(name, list(shape), dt, kind="ExternalOutput").ap()


def _build(nc, kb, dbg):
    op, dma = kb.op, kb.dma
    I = {}
    I["x_prompt"] = _dram_in(nc, "x_prompt", [SEQ, D])
    I["x_sample"] = _dram_in(nc, "x_sample", [128, D])
    I["state_s5_re"] = _dram_in(nc, "state_s5_re", [2, 16, 4096])
    I["state_s5_im"] = _dram_in(nc, "state_s5_im", [2, 16, 4096])
    I["state_rg_h"] = _dram_in(nc, "state_rg_h", [2, 16, D])
    I["state_rg_conv"] = _dram_in(nc, "state_rg_conv", [2, 48, D])
    I["state_ffn_conv"] = _dram_in(nc, "state_ffn_conv", [4, 32, FF2])
    I["meta_tokens"] = _dram_in(nc, "meta_tokens", [NMETA, D])
    I["s5_w_in"] = _dram_in(nc, "s5_w_in", [2, D, D])
    I["s5_lam_re"] = _dram_in(nc, "s5_lam_re", [2, 64, 64])
    I["s5_lam_im"] = _dram_in(nc, "s5_lam_im", [2, 64, 64])
    I["s5_log_step"] = _dram_in(nc, "s5_log_step", [2, 64])
    I["s5_b_re"] = _dram_in(nc, "s5_b_re", [2, 64, 64, 16])
    I["s5_b_im"] = _dram_in(nc, "s5_b_im", [2, 64, 64, 16])
    I["s5_c_re"] = _dram_in(nc, "s5_c_re", [2, 64, 16, 64])
    I["s5_c_im"] = _dram_in(nc, "s5_c_im", [2, 64, 16, 64])
    I["s5_d"] = _dram_in(nc, "s5_d", [2, D])
    I["s5_w_out"] = _dram_in(nc, "s5_w_out", [2, D, 2 * D])
    I["rg_w_in"] = _dram_in(nc, "rg_w_in", [2, D, 2 * D])
    I["rg_conv_w"] = _dram_in(nc, "rg_conv_w", [2, 4, D])
    I["rg_conv_b"] = _dram_in(nc, "rg_conv_b", [2, D])
    I["rg_w_gates"] = _dram_in(nc, "rg_w_gates", [2, 4, 256, 512])
    I["rg_b_gates"] = _dram_in(nc, "rg_b_gates", [2, 2 * D])
    I["rg_lam"] = _dram_in(nc, "rg_lam", [2, D])
    I["rg_w_out"] = _dram_in(nc, "rg_w_out", [2, D, D])
    I["ffn_w_up"] = _dram_in(nc, "ffn_w_up", [4, D, FF2])
    I["ffn_conv_w"] = _dram_in(nc, "ffn_conv_w", [4, 3, FF2])
    I["ffn_conv_b"] = _dram_in(nc, "ffn_conv_b", [4, FF2])
    I["ffn_w_down"] = _dram_in(nc, "ffn_w_down", [4, FF, D])
    I["ln_g"] = _dram_in(nc, "ln_g", [4, 2, D])
    I["ln_b"] = _dram_in(nc, "ln_b", [4, 2, D])
    I["ident"] = _dram_in(nc, "ident", [128, 128])
    I["bmask"] = _dram_in(nc, "bmask", [128, 128])
    I["sel2"] = _dram_in(nc, "sel2", [2, 128])

    O = {}
    O["y_prompt"] = _dram_out(nc, "y_prompt", [SEQ, D])
    O["y_sample"] = _dram_out(nc, "y_sample", [128, D])
    O["s5_re_p"] = _dram_out(nc, "s5_re_p", [2, 1, 4096])
    O["s5_im_p"] = _dram_out(nc, "s5_im_p", [2, 1, 4096])
    O["rg_h_p"] = _dram_out(nc, "rg_h_p", [2, 1, D])
    O["rg_conv_p"] = _dram_out(nc, "rg_conv_p", [2, 3, D])
    O["ffn_conv_p"] = _dram_out(nc, "ffn_conv_p", [4, 2, FF2])
    O["s5_re_s"] = _dram_out(nc, "s5_re_s", [2, 16, 4096])
    O["s5_im_s"] = _dram_out(nc, "s5_im_s", [2, 16, 4096])
    O["rg_h_s"] = _dram_out(nc, "rg_h_s", [2, 16, D])
    O["rg_conv_s"] = _dram_out(nc, "rg_conv_s", [2, 48, D])
    O["ffn_conv_s"] = _dram_out(nc, "ffn_conv_s", [4, 32, FF2])

    WXd = [nc.dram_tensor("WXd%d" % j, [128, 8, 8, 2, 128], BF16, kind="Internal").ap() for j in range(2)]
    WYd = [nc.dram_tensor("WYd%d" % j, [128, 32, 8, 2, 32], BF16, kind="Internal").ap() for j in range(2)]
    KId = [nc.dram_tensor("KId%d" % j, [128, 8, 8, 128], BF16, kind="Internal").ap() for j in range(2)]

    NBLK = 2 * NB + 2 * 2 * NB + 4 * FB
    ctx_w = dict(
        WBLK=nc.dram_tensor("WBLK", [NBLK, 128, NB * 128], BF16, kind="Internal").ap(),
        WDd=[nc.dram_tensor("WDd%d" % l, [128, FBH * D], BF16, kind="Internal").ap() for l in range(4)],
        S5WOd=[nc.dram_tensor("S5WOd%d" % j, [128, NB * 2 * D], BF16, kind="Internal").ap() for j in range(2)],
        RGWOd=[nc.dram_tensor("RGWOd%d" % j, [128, NB * D], BF16, kind="Internal").ap() for j in range(2)],
        RGWGd=[nc.dram_tensor("RGWGd%d" % j, [128, 4 * 2 * 512], BF16, kind="Internal").ap() for j in range(2)],
    )
    sb = kb.sb
    ident = sb("ident", [128, 128], F32)
    bmask = sb("bmask", [128, 128], F32)
    R_const = Res("const")
    ds_c = kb.dsem("const")
    dma("sp", ident[:], I["ident"][:, :], [], [R_const], ds_c)
    dma("sp", bmask[:], I["bmask"][:, :], [], [R_const], ds_c)
    sel2 = sb("sel2", [2, 128], F32)
    dma("sp", sel2[:], I["sel2"][:, :], [], [R_const], ds_c)

    R_par = Res("par")
    ds_p = kb.dsem("par")

    par_loads = []

    def load_cols(name, src_1d, nblk, ds):
        t = sb(name, [128, nblk], F32)
        par_loads.append((t, src_1d))
        return t

    def issue_par_loads():
        with nc.allow_non_contiguous_dma(reason="small param"):
            for t, src_1d in par_loads:
                dma("sp", t[:], src_1d.rearrange("(b p) -> p b", p=128), [], [R_par], ds_p)

    ffn_cw = [[load_cols("fcw%d_%d" % (l, k), I["ffn_conv_w"][l, k], FB, ds_c) for k in range(3)] for l in range(4)]
    ffn_cb = [load_cols("fcb%d" % l, I["ffn_conv_b"][l], FB, ds_c) for l in range(4)]
    rg_cw = [[load_cols("rcw%d_%d" % (j, k), I["rg_conv_w"][j, k], NB, ds_c) for k in range(4)] for j in range(2)]
    rg_cb = [load_cols("rcb%d" % j, I["rg_conv_b"][j], NB, ds_c) for j in range(2)]
    rg_bg = [load_cols("rbg%d" % j, I["rg_b_gates"][j], 16, ds_c) for j in range(2)]
    rg_lm = [load_cols("rlm%d" % j, I["rg_lam"][j], NB, ds_c) for j in range(2)]
    s5_dd = [sb("s5d%d" % j, [128, NB], F32) for j in range(2)]
    with nc.allow_non_contiguous_dma(reason="small param"):
        for j in range(2):
            dma("sp", s5_dd[j][:], I["s5_d"][j].rearrange("(b p) -> p b", p=128), [], [R_const], ds_c)
    rg_m8sp = [sb("m8sp%d" % j, [128, NB], F32) for j in range(2)]
    A8re = [sb("A8re%d" % j, [128, 32], F32) for j in range(2)]
    A8im = [sb("A8im%d" % j, [128, 32], F32) for j in range(2)]
    A8imn = [sb("A8imn%d" % j, [128, 32], F32) for j in range(2)]
    ffn_hist_p = [sb("fhp%d" % l, [128, FB, 1, 2], F32) for l in range(4)]
    rg_hist_p = [sb("rhp%d" % j, [128, NB, 1, 3], F32) for j in range(2)]
    rg_h_p = [sb("rgh%d" % j, [128, NB, 1], F32) for j in range(2)]
    s5st_p = [sb("s5p%d" % j, [128, 32, 2, 1], F32) for j in range(2)]
    R_state = Res("state")

    PSall = kb.ps("psall", [128, 8 * 512])
    PS = [PSall[:, i * 512:(i + 1) * 512] for i in range(8)]
    RP = [Res("ps%d" % i) for i in range(8)]

    for j in range(2):
        pass
    for t in ffn_hist_p + rg_hist_p + rg_h_p + s5st_p:
        op("dve", [], [R_state], lambda e, t=t: e.memset(t[:], 0.0))

    ctx = dict(nc=nc, kb=kb, I=I, O=O, sel2=sel2, R_par=R_par, WXd=WXd, WYd=WYd, KId=KId, ident=ident, bmask=bmask,
               R_const=R_const, PS=PS, PSall=PSall, RP=RP, ffn_cw=ffn_cw, ffn_cb=ffn_cb, rg_cw=rg_cw, rg_cb=rg_cb,
               rg_bg=rg_bg, rg_m8sp=rg_m8sp, s5_dd=s5_dd, A8re=A8re, A8im=A8im, A8imn=A8imn,
               ffn_hist_p=ffn_hist_p, rg_hist_p=rg_hist_p, rg_h_p=rg_h_p, s5st_p=s5st_p,
               R_state=R_state, dbg=dbg, ds_out=kb.dsem("out"), ds_w=None, ds_wb=kb.dsem("wb"), **ctx_w)

    with ExitStack() as lds:
        with ExitStack() as nat:
            gens = [s5_prologue_loads(ctx, j, lds, nat) for j in range(2)]
            for g in gens:
                next(g)
            lds_ = [next(g) for g in gens]
            kb.barrier()
        for j in range(2):
            with ExitStack() as pes:
                ctx["issue_par_loads"] = issue_par_loads
                s5_prologue(ctx, j, pes, lds_[j])
                kb.barrier()
    kb.group_final([R_par], ds_p)
    for j in range(2):
        op("act", [R_par], [R_par], lambda e, j=j: e.activation(out=rg_m8sp[j][:], in_=rg_lm[j][:], func=AF.Exp, scale=-1.0))
        op("act", [R_par], [R_par], lambda e, j=j: e.activation(out=rg_m8sp[j][:], in_=rg_m8sp[j][:], func=AF.Ln, bias=1.0))
        op("dve", [R_par], [R_par], lambda e, j=j: e.tensor_scalar(out=rg_m8sp[j][:], in0=rg_m8sp[j][:], scalar1=-RG_C, scalar2=None, op0=ALU.mult))
    kb.barrier()
    passes = [("A", 1, 1024), ("B", 1, 1040), ("S", 16, 8)]
    for (pn, nseq, L) in passes:
        with ExitStack() as pes:
            run_pass(ctx, pn, nseq, L, pes)
            kb.barrier()
    kb.finish()


def colgroups(N, g=512):
    return [(c0, min(g, N - c0)) for c0 in range(0, N, g)]


def store_T(ctx, src_fn, nblk, m, dram_rows, stage, R_src, tag):
    kb, PS, RP, ident = ctx["kb"], ctx["PS"], ctx["RP"], ctx["ident"]
    R_stage = ctx.setdefault("stage_res", {}).setdefault(id(stage), Res("stage" + tag))
    GB = stage.shape[1] // 128
    for g0 in range(0, nblk, GB):
        ng = min(GB, nblk - g0)
        for b0 in range(g0, g0 + ng, 4):
            nb_ = min(4, g0 + ng - b0)
            bank = 6 + ((b0 // 4) % 2)
            for b in range(nb_):
                kb.op("pe", [R_src, ctx["R_const"]], [RP[bank]],
                      lambda e, b=b, b0=b0, bank=bank: e.transpose(out=PS[bank][0:m, b * 128:(b + 1) * 128], in_=src_fn(b0 + b), identity=ident[:, :]))
            kb.op("act", [RP[bank]], [R_stage],
                  lambda e, b0=b0, nb_=nb_, bank=bank, g0=g0: e.copy(out=stage[0:m, (b0 - g0) * 128:(b0 - g0 + nb_) * 128], in_=PS[bank][0:m, 0:nb_ * 128]))
        kb.dma("sp", dram_rows[:, g0 * 128:(g0 + ng) * 128], stage[0:m, 0:ng * 128], [R_stage], [], ctx["ds_out"])


def load_T(ctx, dram_rows, nblk, m, dst_fn, stage, R_dst, tag, ds):
    kb, PS, RP, ident = ctx["kb"], ctx["PS"], ctx["RP"], ctx["ident"]
    R_stage = ctx.setdefault("stage_res", {}).setdefault(id(stage), Res("lstage" + tag))
    ctx["uid"] = ctx.get("uid", 0) + 1
    ds = kb.dsem("lt%d" % ctx["uid"])
    kb.dma("sp", stage[0:m, 0:nblk * 128], dram_rows, [], [R_stage], ds)
    per = 512 // m
    gi = 0
    for b0 in range(0, nblk, per):
        nb_ = min(per, nblk - b0)
        bank = 6 + (gi % 2)
        gi += 1
        for b in range(nb_):
            kb.op("pe", [R_stage, ctx["R_const"]], [RP[bank]],
                  lambda e, b=b: e.transpose(out=PS[bank][:, b * m:(b + 1) * m], in_=stage[0:m, (b0 + b) * 128:(b0 + b + 1) * 128], identity=ident[0:m, 0:m]))
        kb.op("act", [RP[bank]], [R_dst],
              lambda e: e.copy(out=dst_fn(b0, nb_), in_=PS[bank][:, 0:nb_ * m].rearrange("p (b m) -> p b m", m=m)))


def s5_prologue_loads(ctx, j, pes, nat):
    nc, kb, I = ctx["nc"], ctx["kb"], ctx["I"]
    dma, op = kb.dma, kb.op
    PS, RP, ident, Rc = ctx["PS"], ctx["RP"], ctx["ident"], ctx["R_const"]
    sb = lambda n, s, d=F32: kb.sb("pl%d_%s" % (j, n), s, d, es=pes)
    ds = kb.dsem("pl%d" % j)
    R = Res("pl")
    Rn = Res("plnat")
    lr = sb("lr", [128, 32]); li = sb("li", [128, 32]); stp = sb("stp", [128, 32])
    Bre = sb("Bre", [128, 32, 16]); Bim = sb("Bim", [128, 32, 16])
    yield None
    sbn = lambda n, s, d=F32: kb.sb("pl%d_%s" % (j, n), s, d, es=nat)
    lrn = sbn("lrn", [32, 128]); lin = sbn("lin", [32, 128]); stn = sbn("stn", [2, 32])
    Bn = [sbn("Bn%d" % i, [32, 128, 16]) for i in range(2)]
    dma("sp", lrn[:], I["s5_lam_re"][j].rearrange("(pr g2) p -> pr (g2 p)", g2=2), [], [Rn], ds)
    dma("sp", lin[:], I["s5_lam_im"][j].rearrange("(pr g2) p -> pr (g2 p)", g2=2), [], [Rn], ds)
    with nc.allow_non_contiguous_dma(reason="tiny"):
        dma("sp", stn[:], I["s5_log_step"][j].rearrange("(pr g2) -> g2 pr", g2=2), [], [Rn], ds)
    dma("sp", Bn[0][:], I["s5_b_re"][j].rearrange("(pr g2) p h -> pr (g2 p) h", g2=2), [], [Rn], ds)
    dma("sp", Bn[1][:], I["s5_b_im"][j].rearrange("(pr g2) p h -> pr (g2 p) h", g2=2), [], [Rn], ds)
    kb.group_final([Rn], ds)
    bank = 4 + j
    op("pe", [Rn, Rc], [RP[bank]], lambda e: e.transpose(out=PS[bank][:, 0:32], in_=lrn[:, :], identity=ident[0:32, 0:32]))
    op("pe", [Rn, Rc], [RP[bank]], lambda e: e.transpose(out=PS[bank][:, 32:64], in_=lin[:, :], identity=ident[0:32, 0:32]))
    op("pe", [Rn, Rc], [RP[bank]], lambda e: e.matmul(PS[bank][:, 64:96], lhsT=ctx["sel2"][:, :], rhs=stn[:, :], start=True, stop=True))
    op("act", [RP[bank]], [R], lambda e: e.copy(out=lr[:], in_=PS[bank][:, 0:32]))
    op("act", [RP[bank]], [R], lambda e: e.copy(out=li[:], in_=PS[bank][:, 32:64]))
    op("act", [RP[bank]], [R], lambda e: e.copy(out=stp[:], in_=PS[bank][:, 64:96]))
    for i, Bd in enumerate((Bre, Bim)):
        for h in range(16):
            op("pe", [Rn, Rc], [RP[bank]], lambda e, h=h, i=i: e.transpose(out=PS[bank][:, h * 32:(h + 1) * 32], in_=Bn[i][:, :, h], identity=ident[0:32, 0:32]))
        op("act", [RP[bank]], [R], lambda e, Bd=Bd: e.copy(out=Bd[:, :, :].rearrange("p r h -> p h r"), in_=PS[bank][:, :].rearrange("p (h r) -> p h r", r=32)))
    yield dict(R=R, ds=ds, lr=lr, li=li, stp=stp, Bre=Bre, Bim=Bim)


def s5_prologue(ctx, j, pes, ld):
    nc, kb, I = ctx["nc"], ctx["kb"], ctx["I"]
    op, dma = kb.op, kb.dma
    PS, RP, ident, bmask = ctx["PS"], ctx["RP"], ctx["ident"], ctx["bmask"]
    Rc = ctx["R_const"]
    sb = lambda n, s, d=F32: kb.sb("pl%d_%s" % (j, n), s, d, es=pes)
    R, ds = ld["R"], ld["ds"]
    lr, li, stp, Bre, Bim = (ld[k] for k in ("lr", "li", "stp", "Bre", "Bim"))
    Cre = sb("Cre", [128, 32, 16]); Cim = sb("Cim", [128, 32, 16])
    Ch_re = sb("Chre", [16, 64, 64]); Ch_im = sb("Chim", [16, 64, 64])
    R_ch = Res("ch")
    ds_ch = kb.dsem("plc%d" % j)
    with nc.allow_non_contiguous_dma(reason="param layout"):
        dma("sp", Ch_re[:], I["s5_c_re"][j].rearrange("g h p -> h g p"), [], [R_ch], ds_ch)
        dma("sp", Ch_im[:], I["s5_c_im"][j].rearrange("g h p -> h g p"), [], [R_ch], ds_ch)
    kb.group_final([R_ch], ds_ch)
    if j == 1:
        ctx["issue_par_loads"]()
    for (Ch, Cd) in ((Ch_re, Cre), (Ch_im, Cim)):
        for g0 in range(0, 32, 16):
            bank = 4 + (g0 // 16)
            for pr in range(g0, g0 + 16):
                op("pe", [R_ch, Rc], [RP[bank]],
                   lambda e, pr=pr, Ch=Ch: e.transpose(out=PS[bank][:, (pr - g0) * 16:(pr - g0 + 1) * 16],
                                                       in_=Ch[0:16, 2 * pr:2 * pr + 2, :].rearrange("h g p -> h (g p)"),
                                                       identity=ident[0:16, 0:16]))
            op("act", [RP[bank]], [R], lambda e, Cd=Cd: e.copy(out=Cd[:, g0:g0 + 16, :], in_=PS[bank][:, 0:256].rearrange("p (a h) -> p a h", h=16)))

    t = lambda n: sb(n, [128, 32])
    ang = t("ang"); mag = t("mag"); sn = t("sn"); cs = t("cs"); tmp = t("tmp"); tmp2 = t("tmp2")
    abre = t("abre"); abim = t("abim"); qre = t("qre"); qim = t("qim"); ki = sb("ki", [128, 32], I32)
    V = lambda f: op("dve", [R], [R], f)
    A = lambda f: op("act", [R], [R], f)
    A(lambda e: e.activation(out=stp[:], in_=stp[:], func=AF.Exp))
    V(lambda e: e.tensor_tensor(out=ang[:], in0=li[:], in1=stp[:], op=ALU.mult))
    V(lambda e: e.tensor_tensor(out=mag[:], in0=lr[:], in1=stp[:], op=ALU.mult))
    A(lambda e: e.activation(out=mag[:], in_=mag[:], func=AF.Exp))

    def sin_of(dst, shift):
        V(lambda e: e.tensor_scalar(out=tmp[:], in0=ang[:], scalar1=shift, scalar2=1.0 / TWO_PI, op0=ALU.add, op1=ALU.mult))
        V(lambda e: e.tensor_copy(out=ki[:], in_=tmp[:]))
        V(lambda e: e.tensor_copy(out=tmp2[:], in_=ki[:]))
        V(lambda e: e.tensor_tensor(out=tmp[:], in0=tmp[:], in1=tmp2[:], op=ALU.subtract))
        V(lambda e: e.tensor_scalar(out=tmp2[:], in0=tmp[:], scalar1=0.5, scalar2=None, op0=ALU.is_gt))
        V(lambda e: e.tensor_tensor(out=tmp[:], in0=tmp[:], in1=tmp2[:], op=ALU.subtract))
        V(lambda e: e.tensor_scalar(out=tmp2[:], in0=tmp[:], scalar1=-0.5, scalar2=None, op0=ALU.is_lt))
        V(lambda e: e.tensor_tensor(out=tmp[:], in0=tmp[:], in1=tmp2[:], op=ALU.add))
        V(lambda e: e.tensor_scalar(out=tmp[:], in0=tmp[:], scalar1=TWO_PI, scalar2=math.pi, op0=ALU.mult, op1=ALU.min))
        V(lambda e: e.tensor_scalar(out=tmp[:], in0=tmp[:], scalar1=-math.pi, scalar2=None, op0=ALU.max))
        A(lambda e: e.activation(out=dst[:], in_=tmp[:], func=AF.Sin))

    sin_of(sn, 0.0)
    sin_of(cs, math.pi / 2)
    V(lambda e: e.tensor_tensor(out=abre[:], in0=mag[:], in1=cs[:], op=ALU.mult))
    V(lambda e: e.tensor_tensor(out=abim[:], in0=mag[:], in1=sn[:], op=ALU.mult))
    den = t("den"); nr = t("nr")
    V(lambda e: e.tensor_tensor(out=den[:], in0=lr[:], in1=lr[:], op=ALU.mult))
    V(lambda e: e.tensor_tensor(out=tmp[:], in0=li[:], in1=li[:], op=ALU.mult))
    V(lambda e: e.tensor_tensor(out=den[:], in0=den[:], in1=tmp[:], op=ALU.add))
    V(lambda e: e.reciprocal(out=den[:], in_=den[:]))
    V(lambda e: e.tensor_scalar(out=nr[:], in0=abre[:], scalar1=-1.0, scalar2=None, op0=ALU.add))
    V(lambda e: e.tensor_tensor(out=qre[:], in0=nr[:], in1=lr[:], op=ALU.mult))
    V(lambda e: e.tensor_tensor(out=tmp[:], in0=abim[:], in1=li[:], op=ALU.mult))
    V(lambda e: e.tensor_tensor(out=qre[:], in0=qre[:], in1=tmp[:], op=ALU.add))
    V(lambda e: e.tensor_tensor(out=qre[:], in0=qre[:], in1=den[:], op=ALU.mult))
    V(lambda e: e.tensor_tensor(out=qim[:], in0=abim[:], in1=lr[:], op=ALU.mult))
    V(lambda e: e.tensor_tensor(out=tmp[:], in0=nr[:], in1=li[:], op=ALU.mult))
    V(lambda e: e.tensor_tensor(out=qim[:], in0=qim[:], in1=tmp[:], op=ALU.subtract))
    V(lambda e: e.tensor_tensor(out=qim[:], in0=qim[:], in1=den[:], op=ALU.mult))
    pwr = sb("pwr", [128, 9, 32]); pwi = sb("pwi", [128, 9, 32])
    V(lambda e: e.memset(pwr[:, 0, :], 1.0))
    V(lambda e: e.memset(pwi[:, 0, :], 0.0))
    for k in range(1, 9):
        V(lambda e, k=k: e.tensor_tensor(out=pwr[:, k, :], in0=pwr[:, k - 1, :], in1=abre[:], op=ALU.mult))
        V(lambda e, k=k: e.tensor_tensor(out=tmp[:], in0=pwi[:, k - 1, :], in1=abim[:], op=ALU.mult))
        V(lambda e, k=k: e.tensor_tensor(out=pwr[:, k, :], in0=pwr[:, k, :], in1=tmp[:], op=ALU.subtract))
        V(lambda e, k=k: e.tensor_tensor(out=pwi[:, k, :], in0=pwr[:, k - 1, :], in1=abim[:], op=ALU.mult))
        V(lambda e, k=k: e.tensor_tensor(out=tmp[:], in0=pwi[:, k - 1, :], in1=abre[:], op=ALU.mult))
        V(lambda e, k=k: e.tensor_tensor(out=pwi[:, k, :], in0=pwi[:, k, :], in1=tmp[:], op=ALU.add))
    Rst = ctx["R_state"]
    op("dve", [R], [Rst], lambda e: e.tensor_copy(out=ctx["A8re"][j][:], in_=pwr[:, 8, :]))
    op("dve", [R], [Rst], lambda e: e.tensor_copy(out=ctx["A8im"][j][:], in_=pwi[:, 8, :]))
    op("dve", [R], [Rst], lambda e: e.tensor_scalar(out=ctx["A8imn"][j][:], in0=pwi[:, 8, :], scalar1=-1.0, scalar2=None, op0=ALU.mult))

    def bc(x2d):
        return x2d.unsqueeze(2).broadcast_to([128, 32, 16])

    T3 = lambda n: sb(n, [128, 32, 16])
    w1 = T3("w1"); w2 = T3("w2")

    def cmul(dre, dim, xre, xim, sre, sim_):
        V(lambda e: e.tensor_tensor(out=w1[:], in0=xre, in1=bc(sre), op=ALU.mult))
        V(lambda e: e.tensor_tensor(out=w2[:], in0=xim, in1=bc(sim_), op=ALU.mult))
        V(lambda e: e.tensor_tensor(out=dre, in0=w1[:], in1=w2[:], op=ALU.subtract))
        V(lambda e: e.tensor_tensor(out=w1[:], in0=xre, in1=bc(sim_), op=ALU.mult))
        V(lambda e: e.tensor_tensor(out=w2[:], in0=xim, in1=bc(sre), op=ALU.mult))
        V(lambda e: e.tensor_tensor(out=dim, in0=w1[:], in1=w2[:], op=ALU.add))

    Bbre = T3("Bbre"); Bbim = T3("Bbim")
    cmul(Bbre[:], Bbim[:], Bre[:], Bim[:], qre[:], qim[:])
    Bpre = sb("Bpre", [128, 32, 32]); Bpim = sb("Bpim", [128, 32, 32])
    V(lambda e: e.memset(Bpre[:], 0.0))
    V(lambda e: e.memset(Bpim[:], 0.0))
    for g2 in range(2):
        hp = slice(g2 * 64, (g2 + 1) * 64); hc = slice(g2 * 16, (g2 + 1) * 16)
        V(lambda e, hp=hp, hc=hc: e.tensor_copy(out=Bpre[hp, :, hc], in_=Bbre[hp, :, :]))
        V(lambda e, hp=hp, hc=hc: e.tensor_copy(out=Bpim[hp, :, hc], in_=Bbim[hp, :, :]))

    WY = sb("WY", [128, 32, 8, 2, 32], BF16)
    KI = sb("KI", [128, 8, 8, 128], BF16)
    WX = sb("WX", [128, 8, 8, 2, 128], BF16)
    R_wy = Res("wy"); R_tmp = Res("pltmp")
    R_ca = [Res("ca0"), Res("ca1")]; R_cap = [Res("cap0"), Res("cap1")]; R_xb = [Res("xb0"), Res("xb1")]
    op("pool", [], [R_wy], lambda e: e.memset(WY[:].rearrange("p a b c d -> p (a b c d)"), 0.0))
    CAre2 = [T3("CAre%d" % b) for b in range(2)]; CAim2 = [T3("CAim%d" % b) for b in range(2)]
    CApre2 = [sb("CApre%d" % b, [128, 32, 32]) for b in range(2)]; CApimn2 = [sb("CApimn%d" % b, [128, 32, 32]) for b in range(2)]
    for b in range(2):
        op("pool", [], [R_cap[b]], lambda e, b=b: e.memset(CApre2[b][:], 0.0))
        op("pool", [], [R_cap[b]], lambda e, b=b: e.memset(CApimn2[b][:], 0.0))
    kit = sb("kit", [128, 128])
    dcol = ctx["s5_dd"][j]
    R_o = Res("pl_out")
    R_kit = Res("kit")

    def cmul2(dre, dim, xre, xim, sre, sim_, Rw):
        D = lambda f, rd, wr: op("dve", rd, wr, f)
        D(lambda e: e.tensor_tensor(out=w1[:], in0=xre, in1=bc(sre), op=ALU.mult), [R, R_tmp], [R_tmp])
        D(lambda e: e.tensor_tensor(out=w2[:], in0=xim, in1=bc(sim_), op=ALU.mult), [R, R_tmp], [R_tmp])
        D(lambda e: e.tensor_tensor(out=dre, in0=w1[:], in1=w2[:], op=ALU.subtract), [R_tmp], [Rw])
        D(lambda e: e.tensor_tensor(out=w1[:], in0=xre, in1=bc(sim_), op=ALU.mult), [R, R_tmp], [R_tmp])
        D(lambda e: e.tensor_tensor(out=w2[:], in0=xim, in1=bc(sre), op=ALU.mult), [R, R_tmp], [R_tmp])
        D(lambda e: e.tensor_tensor(out=dim, in0=w1[:], in1=w2[:], op=ALU.add), [R_tmp], [Rw])

    for k in range(9):
        b = k % 2
        CAre, CAim, CApre, CApimn = CAre2[b], CAim2[b], CApre2[b], CApimn2[b]
        cmul2(CAre[:], CAim[:], Cre[:], Cim[:], pwr[:, k, :], pwi[:, k, :], R_ca[b])
        for g2 in range(2):
            hp = slice(g2 * 64, (g2 + 1) * 64); hc = slice(g2 * 16, (g2 + 1) * 16)
            op("pool", [R_ca[b]], [R_cap[b]], lambda e, hp=hp, hc=hc, CAre=CAre, CApre=CApre: e.tensor_copy(out=CApre[hp, :, hc], in_=CAre[hp, :, :]))
            op("pool", [R_ca[b]], [R_cap[b]], lambda e, hp=hp, hc=hc, CAim=CAim, CApimn=CApimn: e.tensor_scalar(out=CApimn[hp, :, hc], in0=CAim[hp, :, :], scalar1=-1.0, scalar2=0.0, op0=ALU.mult, op1=ALU.add))
            if k >= 1:
                op("pool", [R_ca[b]], [R_wy], lambda e, hp=hp, hc=hc, k=k, CAre=CAre: e.tensor_copy(out=WY[hp, :, k - 1, 0, hc], in_=CAre[hp, :, :]))
                op("pool", [R_ca[b]], [R_wy], lambda e, hp=hp, hc=hc, k=k, CAim=CAim: e.tensor_scalar(out=WY[hp, :, k - 1, 1, hc], in0=CAim[hp, :, :], scalar1=-1.0, scalar2=0.0, op0=ALU.mult, op1=ALU.add))
        if k <= 7:
            for c in range(8):
                bank = 4 + (c % 2)
                cs4 = slice(4 * c, 4 * c + 4)
                op("pe", [R, R_cap[b]], [RP[bank]], lambda e, cs4=cs4, CApre=CApre: e.matmul(PS[bank][:, 0:128], lhsT=Bpre[:, cs4, :].rearrange("p a b -> p (a b)"),
                                                                 rhs=CApre[:, cs4, :].rearrange("p a b -> p (a b)"), start=True, stop=False))
                op("pe", [R, R_cap[b]], [RP[bank]], lambda e, cs4=cs4, CApimn=CApimn: e.matmul(PS[bank][:, 0:128], lhsT=Bpim[:, cs4, :].rearrange("p a b -> p (a b)"),
                                                                 rhs=CApimn[:, cs4, :].rearrange("p a b -> p (a b)"), start=False, stop=True))
                if k == 0:
                    op("dve", [RP[bank], Rc], [R_kit], lambda e, bank=bank: e.tensor_tensor(out=kit[:], in0=PS[bank][:, 0:128], in1=bmask[:], op=ALU.mult))
                    op("dve", [R_kit, Rc, ctx["R_par"]], [R_o], lambda e, c=c: e.scalar_tensor_tensor(out=KI[:, c, 0, :], in0=ident[:], scalar=dcol[:, c:c + 1], in1=kit[:], op0=ALU.mult, op1=ALU.add))
                else:
                    op("dve", [RP[bank], Rc], [R_o], lambda e, c=c, k=k, bank=bank: e.tensor_tensor(out=KI[:, c, k, :], in0=PS[bank][:, 0:128], in1=bmask[:], op=ALU.mult))
    XBre2 = [sb("XBre%d" % b, [128, 32, 32]) for b in range(2)]; XBim2 = [sb("XBim%d" % b, [128, 32, 32]) for b in range(2)]
    x1 = sb("x1", [128, 32, 32]); x2 = sb("x2", [128, 32, 32])
    bc32 = lambda x2d: x2d.unsqueeze(2).broadcast_to([128, 32, 32])
    D = lambda f, rd, wr: op("dve", rd, wr, f)
    for tau in range(8):
        k = 7 - tau
        b = tau % 2
        XBre, XBim = XBre2[b], XBim2[b]
        D(lambda e, k=k: e.tensor_tensor(out=x1[:], in0=Bpre[:], in1=bc32(pwr[:, k, :]), op=ALU.mult), [R, R_tmp], [R_tmp])
        D(lambda e, k=k: e.tensor_tensor(out=x2[:], in0=Bpim[:], in1=bc32(pwi[:, k, :]), op=ALU.mult), [R, R_tmp], [R_tmp])
        D(lambda e, XBre=XBre: e.tensor_tensor(out=XBre[:], in0=x1[:], in1=x2[:], op=ALU.subtract), [R_tmp], [R_xb[b]])
        D(lambda e, k=k: e.tensor_tensor(out=x1[:], in0=Bpre[:], in1=bc32(pwi[:, k, :]), op=ALU.mult), [R, R_tmp], [R_tmp])
        D(lambda e, k=k: e.tensor_tensor(out=x2[:], in0=Bpim[:], in1=bc32(pwr[:, k, :]), op=ALU.mult), [R, R_tmp], [R_tmp])
        D(lambda e, XBim=XBim: e.tensor_tensor(out=XBim[:], in0=x1[:], in1=x2[:], op=ALU.add), [R_tmp], [R_xb[b]])
        for ri, XB in enumerate((XBre, XBim)):
            for c in range(8):
                bank = 4 + (c % 2)
                op("pe", [R_xb[b], Rc], [RP[bank]], lambda e, c=c, XB=XB: e.transpose(out=PS[bank][:, 0:128], in_=XB[:, 4 * c:4 * c + 4, :].rearrange("p a b -> p (a b)"), identity=ident[:, :]))
                op("act", [RP[bank]], [R_o], lambda e, c=c, ri=ri, tau=tau, bank=bank: e.copy(out=WX[:, c, tau, ri, :], in_=PS[bank][:, 0:128]))
    dma("sp", ctx["WXd"][j].rearrange("p a b c d -> p (a b c d)"), WX[:].rearrange("p a b c d -> p (a b c d)"), [R, R_o], [], ds)
    dma("sp", ctx["WYd"][j].rearrange("p a b c d -> p (a b c d)"), WY[:].rearrange("p a b c d -> p (a b c d)"), [R, R_wy], [], ds)
    dma("sp", ctx["KId"][j].rearrange("p a b c -> p (a b c)"), KI[:].rearrange("p a b c -> p (a b c)"), [R, R_o], [], ds)


def run_pass(ctx, pn, nseq, L, pes):
    nc, kb, I, O = ctx["nc"], ctx["kb"], ctx["I"], ctx["O"]
    op, dma = kb.op, kb.dma
    PS, RP, ident = ctx["PS"], ctx["RP"], ctx["ident"]
    Rc = ctx["R_const"]
    N = nseq * L
    NT = (N + 127) // 128
    rows = [min(128, N - t * 128) for t in range(NT)]
    NC = N // 8
    sb = lambda n, s, d=F32, es=None: kb.sb("p%s_%s" % (pn, n), s, d, es=es or pes)

    h_tok = sb("htok", [128, NT, D])
    hT = sb("hT", [128, NB, N], BF16)
    R_h = [Res("h%d" % t) for t in range(NT)]
    R_hT = [Res("hT%d" % t) for t in range(NT)]
    ds_in = kb.dsem("in" + pn)
    ds_w = kb.dsem("w" + pn)
    ds_ln = kb.dsem("ln" + pn)
    gbc = sb("gbc", [128, D]); bbc = sb("bbc", [128, D])
    R_ln = Res("ln")
    NZ = 4
    zt = [sb("zt%d" % i, [128, D]) for i in range(NZ)]
    R_zt = [Res("zt%d" % i) for i in range(NZ)]
    stat = [sb("stat%d" % i, [128, 2, 6]) for i in range(NZ)]
    mv = [sb("mv%d" % i, [128, 2]) for i in range(NZ)]
    rstd = [sb("rstd%d" % i, [128, 1]) for i in range(NZ)]
    NWS = 8 if pn == "S" else 4
    wslot = [sb("wslot%d" % i, [128, NB, 128], BF16) for i in range(NWS)]
    R_ws = [Res("ws%d" % i) for i in range(NWS)]
    ws_i = [0]
    ds_ws = [kb.dsem("ws%s%d" % (pn, i)) for i in range(NWS)]

    if pn == "A":
        dma("sp", h_tok[0:16, 0, :], I["meta_tokens"][:, :], [], [R_h[0]], ds_in)
        dma("sp", h_tok[16:128, 0, :], I["x_prompt"][0:112, :], [], [R_h[0]], ds_in)
        for t in range(1, NT):
            dma("sp", h_tok[:, t, :], I["x_prompt"][112 + 128 * (t - 1):112 + 128 * t, :], [], [R_h[t]], ds_in)
    elif pn == "B":
        for t in range(NT):
            dma("sp", h_tok[0:rows[t], t, :], I["x_prompt"][1008 + 128 * t:1008 + 128 * t + rows[t], :], [], [R_h[t]], ds_in)
    else:
        dma("sp", h_tok[:, 0, :], I["x_sample"][:, :], [], [R_h[0]], ds_in)

    kb.group_final(R_h, ds_in)

    def to_hT(t):
        r = rows[t]
        for half in range(2):
            bank = 4 + half
            for b in range(4):
                blk = half * 4 + b
                op("pe", [R_h[t], Rc], [RP[bank]], lambda e, b=b, blk=blk: e.transpose(out=PS[bank][:, b * 128:b * 128 + r], in_=h_tok[0:r, t, blk * 128:(blk + 1) * 128], identity=ident[0:r, 0:r]))
            op("act", [RP[bank]], [R_hT[t]], lambda e, half=half: e.copy(out=hT[:, half * 4:half * 4 + 4, t * 128:t * 128 + r],
                                                                     in_=PS[bank][:, :].rearrange("p (b n) -> p b n", n=128)[:, :, 0:r]))

    for t in range(NT):
        to_hT(t)

    nlayers = ctx["dbg"].get("nlayers", DEPTH) if ctx["dbg"] else DEPTH
    glist = []
    for l_ in range(nlayers):
        j_ = l_ // 2
        if l_ % 2 == 0:
            glist += [(I["s5_w_in"][j_], c * 128) for c in range(NB)]
        else:
            for c in range(NB):
                glist += [(I["rg_w_in"][j_], c * 128), (I["rg_w_in"][j_], D + c * 128)]
        for v in range(FBH):
            glist += [(I["ffn_w_up"][l_], v * 128), (I["ffn_w_up"][l_], (v + FBH) * 128)]
    wplan = {"list": glist, "issued": 0, "hooks": {}, "off": 0, "next_off": 0}

    def plan_w(blocks, hooks=None):
        wplan["off"] = wplan["next_off"]
        wplan["next_off"] = wplan["off"] + len(blocks)
        for k, f in (hooks or {}).items():
            gi = wplan["off"] + k
            if gi < wplan["issued"]:
                f()
            else:
                wplan["hooks"][gi] = f

    def write_back(k):
        s = k % NWS
        dma("sp", ctx["WBLK"][k], wslot[s][:].rearrange("p a b -> p (a b)"), [R_ws[s]], [], ctx["ds_wb"])

    def load_w_block(i, pf=NWS - 1):
        gi = wplan["off"] + i
        while wplan["issued"] < min(len(wplan["list"]), gi + pf + 1):
            k = wplan["issued"]
            src2d, c0 = wplan["list"][k]
            s = k % NWS
            if pn == "A":
                dma("pool", wslot[s][:], src2d[:, c0:c0 + 128].rearrange("(kc p) f -> p kc f", p=128), [], [R_ws[s]], ds_ws[s])
                if k >= 2:
                    write_back(k - 2)
            else:
                dma("pool", wslot[s][:].rearrange("p a b -> p (a b)"), ctx["WBLK"][k], [], [R_ws[s]], ds_ws[s])
            wplan["issued"] += 1
            if k in wplan["hooks"]:
                wplan["hooks"].pop(k)()
        return gi % NWS

    def up_matmul(s, ps_banks, extra_reads):
        for gi, (c0, cn) in enumerate(colgroups(N)):
            bank = ps_banks[gi]
            for kc in range(NB):
                op("pe", [R_ws[s]] + R_hT + extra_reads, [RP[bank]],
                   lambda e, kc=kc, bank=bank, c0=c0, cn=cn: e.matmul(PS[bank][:, 0:cn], lhsT=wslot[s][:, kc, :], rhs=hT[:, kc, c0:c0 + cn], start=(kc == 0), stop=(kc == NB - 1)))

    def layer_norm_tile(t, zi, li_, k, last):
        r = rows[t]
        z = zt[zi]
        for hh in range(2):
            op("dve", [R_zt[zi]], [R_zt[zi]], lambda e, hh=hh: e.bn_stats(out=stat[zi][0:r, hh, :], in_=z[0:r, hh * 512:(hh + 1) * 512]))
        op("dve", [R_zt[zi]], [R_zt[zi]], lambda e: e.bn_aggr(out=mv[zi][0:r, :], in_=stat[zi][0:r, :, :].rearrange("p a b -> p (a b)")))
        op("act", [R_zt[zi]], [R_zt[zi]], lambda e: e.activation(out=rstd[zi][0:r, :], in_=mv[zi][0:r, 1:2], func=AF.Sqrt, bias=LN_EPS_AP[0:r, :], scale=1.0))
        op("dve", [R_zt[zi]], [R_zt[zi]], lambda e: e.reciprocal(out=rstd[zi][0:r, :], in_=rstd[zi][0:r, :]))
        op("dve", [R_zt[zi]], [R_zt[zi]], lambda e: e.tensor_scalar(out=z[0:r, :], in0=z[0:r, :], scalar1=mv[zi][0:r, 0:1], scalar2=rstd[zi][0:r, 0:1], op0=ALU.subtract, op1=ALU.mult))
        op("pool", [R_zt[zi], R_ln], [R_zt[zi]], lambda e: e.tensor_tensor(out=z[0:r, :], in0=z[0:r, :], in1=gbc[0:r, :], op=ALU.mult))
        op("pool", [R_zt[zi], R_ln], [R_h[t]], lambda e: e.tensor_tensor(out=h_tok[0:r, t, :], in0=z[0:r, :], in1=bbc[0:r, :], op=ALU.add))
        if not last:
            return lambda: to_hT(t)
        else:
            if pn == "A":
                if t == 0:
                    dma("sp", O["y_prompt"][0:112, :], h_tok[16:128, 0, :], [R_h[t]], [], ctx["ds_out"])
                else:
                    dma("sp", O["y_prompt"][112 + 128 * (t - 1):112 + 128 * t, :], h_tok[:, t, :], [R_h[t]], [], ctx["ds_out"])
            elif pn == "B":
                dma("sp", O["y_prompt"][1008 + 128 * t:1008 + 128 * t + r, :], h_tok[0:r, t, :], [R_h[t]], [], ctx["ds_out"])
            else:
                dma("sp", O["y_sample"][:, :], h_tok[:, 0, :], [R_h[t]], [], ctx["ds_out"])
            return lambda: None

    LN_EPS_AP = sb("lneps", [128, 1])
    op("dve", [], [Rc], lambda e: e.memset(LN_EPS_AP[:], LN_EPS))
    one_ap = sb("one", [128, 1])
    op("dve", [], [Rc], lambda e: e.memset(one_ap[:], 1.0))

    def load_ln(li_, k):
        dma("sp", gbc[:], I["ln_g"][li_, k].partition_broadcast(128), [], [R_ln], ds_ln)
        dma("sp", bbc[:], I["ln_b"][li_, k].partition_broadcast(128), [], [R_ln], ds_ln)

    env = dict(ctx=ctx, pn=pn, nseq=nseq, L=L, N=N, NT=NT, rows=rows, NC=NC, sb=sb, h_tok=h_tok, hT=hT,
               R_h=R_h, R_hT=R_hT, ds_w=ds_w, ds_in=ds_in, zt=zt, R_zt=R_zt, load_w_block=load_w_block, plan_w=plan_w,
               up_matmul=up_matmul, layer_norm_tile=layer_norm_tile, one_ap=one_ap, load_ln=load_ln, wslot=wslot, R_ws=R_ws)

    for li_ in range(nlayers):
        j = li_ // 2
        with ExitStack() as les:
            if li_ % 2 == 0:
                s5_layer(env, li_, j, les)
            else:
                rg_layer(env, li_, j, les)
            kb.barrier()
        with ExitStack() as les:
            ffn_layer(env, li_, les, last=(li_ == nlayers - 1))
            if pn == "A" and li_ == nlayers - 1:
                for k in range(max(0, len(glist) - 2), len(glist)):
                    write_back(k)
            kb.barrier()


def conv_taps(kb, xs, acc, nseq, L, K, wcols, bcol, R_main, R_halo, R_acc):
    kb.op("act", [R_main], [R_acc], lambda e: e.activation(out=acc[:, :, :], in_=xs[:, :, K - 1:K - 1 + L], func=AF.Identity, scale=wcols[K - 1], bias=bcol))
    for k in range(K - 1):
        kb.op("dve", [R_main, R_halo, R_acc], [R_acc], lambda e, k=k: e.scalar_tensor_tensor(out=acc[:, :, :], in0=xs[:, :, k:k + L], scalar=wcols[k], in1=acc[:, :, :], op0=ALU.mult, op1=ALU.add))


def ffn_layer(env, li_, les, last):
    ctx = env["ctx"]; kb = ctx["kb"]; nc = ctx["nc"]; I = ctx["I"]; O = ctx["O"]
    op, dma = kb.op, kb.dma
    PS, RP = ctx["PS"], ctx["RP"]
    pn, nseq, L, N, NT, rows = env["pn"], env["nseq"], env["L"], env["N"], env["NT"], env["rows"]
    sb = lambda n, s, d=F32: env["sb"]("f%d_%s" % (li_, n), s, d, es=les)
    h_tok, hT = env["h_tok"], env["hT"]
    actT = sb("actT", [128, FBH, N], BF16)
    R_act = [Res("act%d" % v) for v in range(FBH)]
    wd = sb("wd", [128, FBH, D], BF16)
    R_wd = [Res("wd%d" % v) for v in range(FBH)]
    xs = [sb("xs%d" % i, [128, nseq, 2 + L]) for i in range(2)]
    acc4 = [sb("acc%d" % i, [128, nseq, L]) for i in range(4)]
    R_xs = [Res("xs%d" % i) for i in range(2)]
    R_xh = [Res("xh%d" % i) for i in range(2)]
    R_acc4 = [Res("acc%d" % i) for i in range(4)]
    cw, cb = ctx["ffn_cw"][li_], ctx["ffn_cb"][li_]
    Rst = ctx["R_state"]
    if nseq == 1:
        hist = ctx["ffn_hist_p"][li_]
        R_hist = Rst
    else:
        hist = sb("hist", [128, FB, nseq, 2])
        R_hist = Res("hist")
        stage = sb("hstage", [32, FF2])
        load_T(ctx, I["state_ffn_conv"][li_], FB, 32, lambda b0, nb_: hist[:, b0:b0 + nb_, :, :].rearrange("p b s k -> p b (s k)"), stage, R_hist, "fh", env["ds_in"])
    env["load_ln"](li_, 1)
    def wd_hook(v):
        def f():
            if pn == "A":
                dma("pool", wd[:, v, :], I["ffn_w_down"][li_][v * 128:(v + 1) * 128, :], [], [R_wd[v]], env["ds_w"])
            else:
                dma("pool", wd[:, v, :], ctx["WDd"][li_][:, v * D:(v + 1) * D], [], [R_wd[v]], env["ds_w"])
            if v == FBH - 1:
                kb.group_final(R_wd, env["ds_w"])
        return f
    blocks = []
    for v in range(FBH):
        blocks += [(I["ffn_w_up"][li_], v * 128), (I["ffn_w_up"][li_], (v + FBH) * 128)]
    env["plan_w"](blocks, {2 * v + 1: wd_hook(v) for v in range(FBH)})
    ngrp = len(colgroups(N))
    banksets = [[0, 1, 2][:ngrp], [3, 4, 5][:ngrp]]
    bi = 0
    for v in range(FBH):
        acc = acc4[2 * (v % 2):2 * (v % 2) + 2]
        R_acc = R_acc4[2 * (v % 2):2 * (v % 2) + 2]
        for which in range(2):
            blk = v + which * FBH
            s = env["load_w_block"](2 * v + which)
            banks = banksets[bi % 2]
            bi += 1
            env["up_matmul"](s, banks, [])
            x = xs[which]
            op("act", [R_hist], [R_xh[which]], lambda e, x=x, blk=blk: e.copy(out=x[:, :, 0:2], in_=hist[:, blk, :, :]))
            if nseq == 1:
                op("act", [RP[b_] for b_ in banks], [R_xs[which]], lambda e, x=x, banks=banks: e.copy(out=x[:, 0, 2:2 + N], in_=ctx["PSall"][:, banks[0] * 512:banks[0] * 512 + N]))
            else:
                op("act", [RP[banks[0]]], [R_xs[which]], lambda e, x=x, banks=banks: e.copy(out=x[:, :, 2:2 + L], in_=PS[banks[0]][:, 0:N].rearrange("p (s t) -> p s t", t=L)))
            op("act", [R_xs[which]], [R_hist], lambda e, x=x, blk=blk: e.copy(out=hist[:, blk, :, :], in_=x[:, :, L:L + 2]))
            conv_taps(kb, x, acc[which], nseq, L, 3, [cw[k][:, blk:blk + 1] for k in range(3)], cb[:, blk:blk + 1], R_xs[which], R_xh[which], R_acc[which])
        op("act", [R_acc[1]], [R_acc[1]], lambda e, acc=acc: e.activation(out=acc[1][:, :, :], in_=acc[1][:, :, :], func=AF.Gelu_apprx_tanh))
        op("dve", [R_acc[0], R_acc[1]], [R_act[v]], lambda e, v=v, acc=acc: e.tensor_tensor(out=actT[:, v, :], in0=acc[0][:, :, :].rearrange("p s t -> p (s t)"), in1=acc[1][:, :, :].rearrange("p s t -> p (s t)"), op=ALU.mult))
    if pn == "A":
        dma("sp", ctx["WDd"][li_], wd[:].rearrange("p a b -> p (a b)"), R_wd, [], ctx["ds_wb"])
    if pn != "A":
        stg = sb("ostage", [32, 256 if nseq == 1 else 2048])
        m = 2 * nseq
        store_T(ctx, lambda b: hist[:, b, :, :].rearrange("p s k -> p (s k)"), FB, m,
                (O["ffn_conv_p"] if nseq == 1 else O["ffn_conv_s"])[li_], stg, R_hist, "fo")
    pending = None
    for t in range(NT):
        r = rows[t]
        pb = [0, 1] if t % 2 == 0 else [2, 3]
        for v in range(FBH):
            for hh in range(2):
                op("pe", [R_act[v], R_wd[v]], [RP[pb[hh]]], lambda e, v=v, hh=hh: e.matmul(PS[pb[hh]][0:r, :], lhsT=actT[:, v, t * 128:t * 128 + r], rhs=wd[:, v, hh * 512:(hh + 1) * 512], start=(v == 0), stop=(v == FBH - 1)))
        zi = t % 4
        z = env["zt"][zi]
        for hh in range(2):
            op("dve", [env["R_h"][t], RP[pb[hh]]], [env["R_zt"][zi]], lambda e, hh=hh: e.scalar_tensor_tensor(out=z[0:r, hh * 512:(hh + 1) * 512], in0=h_tok[0:r, t, hh * 512:(hh + 1) * 512], scalar=DN_ALPHA, in1=PS[pb[hh]][0:r, :], op0=ALU.mult, op1=ALU.add))
        if pending:
            pending()
        pending = env["layer_norm_tile"](t, zi, li_, 1, last)
    pending()


def rg_layer(env, li_, j, les):
    ctx = env["ctx"]; kb = ctx["kb"]; nc = ctx["nc"]; I = ctx["I"]; O = ctx["O"]
    op, dma = kb.op, kb.dma
    PS, RP = ctx["PS"], ctx["RP"]
    pn, nseq, L, N, NT, rows = env["pn"], env["nseq"], env["L"], env["N"], env["NT"], env["rows"]
    sb = lambda n, s, d=F32: env["sb"]("r%d_%s" % (li_, n), s, d, es=les)
    h_tok, hT = env["h_tok"], env["hT"]
    Rst = ctx["R_state"]
    hgT = sb("hgT", [128, NB, N], BF16)
    R_hg = [Res("hg%d" % c) for c in range(NB)]
    wg = sb("wg", [128, 4, 2, 512], BF16)
    R_wg = Res("wg")
    wo = sb("wo", [128, NB, D], BF16)
    R_wo = Res("wo")
    if pn == "A":
        for n in range(4):
            dma("pool", wg[:, n, :, :], I["rg_w_gates"][j, n].rearrange("(kc p) f -> p kc f", p=128), [], [R_wg], env["ds_w"])
    else:
        dma("pool", wg[:].rearrange("p a b c -> p (a b c)"), ctx["RGWGd"][j], [], [R_wg], env["ds_w"])
    kb.group_final([R_wg], env["ds_w"])
    env["load_ln"](li_, 0)
    if nseq == 1:
        hist = ctx["rg_hist_p"][j]; hst = ctx["rg_h_p"][j]; R_hist = Rst
    else:
        hist = sb("hist", [128, NB, nseq, 3]); hst = sb("hst", [128, NB, nseq]); R_hist = Res("rhist")
        stage = sb("stage", [48, D])
        load_T(ctx, I["state_rg_conv"][j], NB, 48, lambda b0, nb_: hist[:, b0:b0 + nb_, :, :].rearrange("p b s k -> p b (s k)"), stage, R_hist, "rc", env["ds_in"])
        stage2 = sb("stage2", [16, D])
        load_T(ctx, I["state_rg_h"][j], NB, 16, lambda b0, nb_: hst[:, b0:b0 + nb_, :], stage2, R_hist, "rh", env["ds_in"])
    xs = [sb("xs%d" % i, [128, nseq, 3 + L]) for i in range(2)]
    xc = [[sb("xc%d_%d" % (b, i), [128, nseq, L]) for i in range(2)] for b in range(2)]
    xcb = [[sb("xcb%d_%d" % (b, i), [128, N], BF16) for i in range(2)] for b in range(2)]
    gl = [[sb("gl%d_%d" % (b, i), [128, N]) for i in range(2)] for b in range(2)]
    work = [[sb("wk%d_%d" % (b, i), [128, nseq, L]) for i in range(3)] for b in range(2)]
    R_xs = [Res() for _ in range(2)]; R_xh = [Res() for _ in range(2)]
    R_xc = [[Res() for _ in range(2)] for _ in range(2)]; R_gl = [[Res() for _ in range(2)] for _ in range(2)]
    R_wk = [Res("rgwork0"), Res("rgwork1")]
    cw, cb, bg, m8 = ctx["rg_cw"][j], ctx["rg_cb"][j], ctx["rg_bg"][j], ctx["rg_m8sp"][j]
    one_ap = env["one_ap"]
    ngrp = len(colgroups(N))
    banksets = [[0, 1, 2][:ngrp], [3, 4, 5][:ngrp]]
    bi = [0]
    blocks = []
    for c in range(NB):
        blocks += [(I["rg_w_in"][j], c * 128), (I["rg_w_in"][j], D + c * 128)]
    env["plan_w"](blocks)
    flat = lambda t: t[:, :, :].rearrange("p s t -> p (s t)")

    def stage_proj(n):
        pb_ = n % 2
        for q in range(2):
            c = 2 * n + q
            s = env["load_w_block"](2 * c)
            banks = banksets[bi[0] % 2]; bi[0] += 1
            env["up_matmul"](s, banks, [])
            op("act", [RP[b_] for b_ in banks], [R_gl[pb_][q]], lambda e, q=q, banks=banks: e.activation(out=gl[pb_][q][:, 0:N], in_=ctx["PSall"][:, banks[0] * 512:banks[0] * 512 + N], func=AF.Gelu_apprx_tanh))
            s = env["load_w_block"](2 * c + 1)
            banks = banksets[bi[0] % 2]; bi[0] += 1
            env["up_matmul"](s, banks, [])
            x = xs[q]
            op("act", [R_hist], [R_xh[q]], lambda e, x=x, c=c: e.copy(out=x[:, :, 0:3], in_=hist[:, c, :, :]))
            if nseq == 1:
                op("act", [RP[b_] for b_ in banks], [R_xs[q]], lambda e, x=x, banks=banks: e.copy(out=x[:, 0, 3:3 + N], in_=ctx["PSall"][:, banks[0] * 512:banks[0] * 512 + N]))
            else:
                op("act", [RP[banks[0]]], [R_xs[q]], lambda e, x=x, banks=banks: e.copy(out=x[:, :, 3:3 + L], in_=PS[banks[0]][:, 0:N].rearrange("p (s t) -> p s t", t=L)))
            op("act", [R_xs[q]], [R_hist], lambda e, x=x, c=c: e.copy(out=hist[:, c, :, :], in_=x[:, :, L:L + 3]))
            conv_taps(kb, x, xc[pb_][q], nseq, L, 4, [cw[k][:, c:c + 1] for k in range(4)], cb[:, c:c + 1], R_xs[q], R_xh[q], R_xc[pb_][q])
            op("dve", [R_xc[pb_][q]], [R_xc[pb_][q]], lambda e, q=q: e.tensor_copy(out=xcb[pb_][q][:, :], in_=flat(xc[pb_][q])))

    def stage_gate(n):
        pb_ = n % 2
        for q in range(2):
            c = 2 * n + q
            T1, T2, T3 = work[c % 2]
            Rw = R_wk[c % 2]
            pr_ = banksets[0]; pi_ = banksets[1]
            for (dst_banks, off) in ((pr_, q * 128), (pi_, 256 + q * 128)):
                for gi, (c0, cn) in enumerate(colgroups(N)):
                    for kc in range(2):
                        op("pe", [R_wg, R_xc[pb_][kc]], [RP[dst_banks[gi]]], lambda e, kc=kc, gi=gi, c0=c0, cn=cn, off=off, dst_banks=dst_banks: e.matmul(PS[dst_banks[gi]][:, 0:cn], lhsT=wg[:, n, kc, off:off + 128], rhs=xcb[pb_][kc][:, c0:c0 + cn], start=(kc == 0), stop=(kc == 1)))
            op("act", [RP[b_] for b_ in pr_], [Rw], lambda e, c=c: e.activation(out=flat(T1)[:, 0:N], in_=ctx["PSall"][:, pr_[0] * 512:pr_[0] * 512 + N], func=AF.Sigmoid, bias=bg[:, c:c + 1], scale=1.0))
            op("act", [RP[b_] for b_ in pi_], [Rw], lambda e, c=c: e.activation(out=flat(T2)[:, 0:N], in_=ctx["PSall"][:, pi_[0] * 512:pi_[0] * 512 + N], func=AF.Sigmoid, bias=bg[:, 8 + c:8 + c + 1], scale=1.0))
            op("act", [Rw], [Rw], lambda e, c=c: e.activation(out=flat(T1), in_=flat(T1), func=AF.Exp, scale=m8[:, c:c + 1]))
            op("act", [Rw], [Rw], lambda e: e.activation(out=flat(T3), in_=flat(T1), func=AF.Square))
            op("act", [Rw], [Rw], lambda e: e.activation(out=flat(T3), in_=flat(T3), func=AF.Sqrt, scale=-1.0, bias=one_ap[:, :]))
            op("dve", [Rw, R_xc[pb_][q]], [Rw], lambda e, q=q: e.tensor_tensor(out=flat(T2), in0=flat(T2), in1=flat(xc[pb_][q]), op=ALU.mult))
            op("dve", [Rw], [Rw], lambda e: e.tensor_tensor(out=flat(T2), in0=flat(T2), in1=flat(T3), op=ALU.mult))
            for s_ in range(nseq):
                op("dve", [Rw, R_hist], [Rw], lambda e, s_=s_, c=c: e.tensor_tensor_scan(out=T3[:, s_, :], data0=T1[:, s_, :], data1=T2[:, s_, :], initial=hst[:, c, s_:s_ + 1], op0=ALU.mult, op1=ALU.add))
            op("dve", [Rw], [R_hist], lambda e, c=c: e.tensor_copy(out=hst[:, c, :], in_=T3[:, :, L - 1]))
            op("dve", [Rw, R_gl[pb_][q]], [R_hg[c]], lambda e, c=c, q=q: e.tensor_tensor(out=hgT[:, c, :], in0=flat(T3), in1=gl[pb_][q][:, :], op=ALU.mult))

    for n in range(4):
        stage_proj(n)
        if n == 1:
            if pn == "A":
                dma("pool", wo[:], I["rg_w_out"][j].rearrange("(kc p) f -> p kc f", p=128), [], [R_wo], env["ds_w"])
            else:
                dma("pool", wo[:].rearrange("p a b -> p (a b)"), ctx["RGWOd"][j], [], [R_wo], env["ds_w"])
            kb.group_final([R_wo], env["ds_w"])
        if n > 0:
            stage_gate(n - 1)
    stage_gate(3)
    if pn == "A":
        dma("sp", ctx["RGWGd"][j], wg[:].rearrange("p a b c -> p (a b c)"), [R_wg], [], ctx["ds_wb"])
        dma("sp", ctx["RGWOd"][j], wo[:].rearrange("p a b -> p (a b)"), [R_wo], [], ctx["ds_wb"])
    if pn != "A":
        stg = sb("ostage", [48, 256])
        store_T(ctx, lambda b: hist[:, b, :, :].rearrange("p s k -> p (s k)"), NB, 3 * nseq,
                (O["rg_conv_p"] if nseq == 1 else O["rg_conv_s"])[j], stg, R_hist, "ro")
        store_T(ctx, lambda b: hst[:, b, :], NB, nseq, (O["rg_h_p"] if nseq == 1 else O["rg_h_s"])[j], stg, R_hist, "rho")
    pending = None
    for t in range(NT):
        r = rows[t]
        pb = [0, 1] if t % 2 == 0 else [2, 3]
        for kc in range(NB):
            for hh in range(2):
                op("pe", [R_hg[kc], R_wo], [RP[pb[hh]]], lambda e, kc=kc, hh=hh: e.matmul(PS[pb[hh]][0:r, :], lhsT=hgT[:, kc, t * 128:t * 128 + r], rhs=wo[:, kc, hh * 512:(hh + 1) * 512], start=(kc == 0), stop=(kc == NB - 1)))
        zi = t % 4
        z = env["zt"][zi]
        for hh in range(2):
            op("dve", [env["R_h"][t], RP[pb[hh]]], [env["R_zt"][zi]], lambda e, hh=hh: e.scalar_tensor_tensor(out=z[0:r, hh * 512:(hh + 1) * 512], in0=h_tok[0:r, t, hh * 512:(hh + 1) * 512], scalar=DN_ALPHA, in1=PS[pb[hh]][0:r, :], op0=ALU.mult, op1=ALU.add))
        if pending:
            pending()
        pending = env["layer_norm_tile"](t, zi, li_, 0, False)
    pending()


def ONE_AP(env):
    if "one_ap" not in env:
        kb = env["ctx"]["kb"]
        t = env["sb"]("one", [128, 1])
        kb.op("dve", [], [env["ctx"]["R_const"]], lambda e: e.memset(t[:], 1.0))
        env["one_ap"] = t
    return env["one_ap"]


def s5_layer(env, li_, j, les):
    ctx = env["ctx"]; kb = ctx["kb"]; nc = ctx["nc"]; I = ctx["I"]; O = ctx["O"]
    op, dma = kb.op, kb.dma
    PS, RP = ctx["PS"], ctx["RP"]
    pn, nseq, L, N, NT, rows, NC = env["pn"], env["nseq"], env["L"], env["N"], env["NT"], env["rows"], env["NC"]
    sb = lambda n, s, d=F32: env["sb"]("s%d_%s" % (li_, n), s, d, es=les)
    h_tok, hT = env["h_tok"], env["hT"]
    Rst = ctx["R_state"]
    uT = sb("uT", [128, NB, N], BF16)
    gyT = uT
    R_u = [Res() for _ in range(NB)]
    R_gy = R_u
    R_X = Res("Xs")
    Sinb = sb("Sinb", [128, 32, 2, NC], BF16)
    R_S = Res("Sinb")
    wo = sb("wo", [128, NB, 2 * D], BF16)
    R_wo = Res("wo")
    env["load_ln"](li_, 0)
    if nseq == 1:
        st = ctx["s5st_p"][j]; R_st = Rst
    else:
        st = sb("st", [128, 32, 2, nseq]); R_st = Res("s5st")
        stage = sb("stage", [16, 4096])
        for ri, nm in enumerate(("state_s5_re", "state_s5_im")):
            load_T(ctx, I[nm][j], 32, 16, lambda b0, nb_, ri=ri: st[:, b0:b0 + nb_, ri, :], stage, R_st, "s5" + str(ri), env["ds_in"])
    ngrp = len(colgroups(N))
    banksets = [[0, 1, 2][:ngrp], [3, 4, 5][:ngrp]]
    env["plan_w"]([(I["s5_w_in"][j], c * 128) for c in range(NB)])
    for c in range(NB):
        s = env["load_w_block"](c)
        banks = banksets[c % 2]
        env["up_matmul"](s, banks, [])
        op("act", [RP[b_] for b_ in banks], [R_u[c]], lambda e, c=c, banks=banks: e.copy(out=uT[:, c, 0:N], in_=ctx["PSall"][:, banks[0] * 512:banks[0] * 512 + N]))
    if pn == "A":
        dma("pool", wo[:, 0:4, :], I["s5_w_out"][j][0:512, :].rearrange("(kc p) f -> p kc f", p=128), [], [R_wo], env["ds_w"])
        dma("pool", wo[:, 4:8, :], I["s5_w_out"][j][512:1024, :].rearrange("(kc p) f -> p kc f", p=128), [], [R_wo], env["ds_w"])
        kb.group_final([R_wo], env["ds_w"])
    else:
        dma("pool", wo[:].rearrange("p a b -> p (a b)"), ctx["S5WOd"][j], [], [R_wo], env["ds_w"])
        kb.group_final([R_wo], env["ds_w"])
    ies = ExitStack()
    sbo = sb
    sb = lambda n, s_, d=F32: env["sb"]("s%d_%s" % (li_, n), s_, d, es=ies)
    Xs = sb("Xs", [128, 32, 2, NC])
    WXs = [sb("WX%d" % i, [128, 8, 2, 128], BF16) for i in range(2)]
    R_WX = [Res() for _ in range(2)]
    ds_wx = [kb.dsem("s5x%s%d_%d" % (pn, li_, i)) for i in range(2)]
    ds_kw = [kb.dsem("s5k%s%d_%d" % (pn, li_, i)) for i in range(2)]
    for c in range(NB):
        wi = c % 2
        dma("sp", WXs[wi][:], ctx["WXd"][j][:, c], [], [R_WX[wi]], ds_wx[wi])
        for ri in range(2):
            for tau in range(8):
                for pr in range(4):
                    bank = pr
                    rs = slice(32 * pr, 32 * pr + 32)
                    op("pe", [R_WX[wi], R_u[c]], [RP[bank]], lambda e, ri=ri, tau=tau, rs=rs, bank=bank, wi=wi, c=c, pr=pr: e.matmul(
                        PS[bank][:, ri * 256:ri * 256 + NC], lhsT=WXs[wi][rs, tau, ri, :],
                        rhs=uT[rs, c, :].rearrange("p (n t) -> p n t", t=8)[:, :, tau], start=(tau == 0), stop=(tau == 7), tile_position=(32 * pr, 0)))
        for pr in range(4):
            bank = pr
            op("act", [RP[bank]], [R_X], lambda e, bank=bank, c=c, pr=pr: e.copy(out=Xs[:, 4 * c + pr, :, :], in_=PS[bank][:, :].rearrange("p (r n) -> p r n", n=256)[:, :, 0:NC]))
    a8r, a8i, a8in = ctx["A8re"][j], ctx["A8im"][j], ctx["A8imn"][j]
    Mrot = sb("Mrot", [128, 32, 2, 2]); prod = sb("prod", [128, 32, 2, 2]); t1 = sb("t1", [128, 32, 2])
    op("dve", [Rst], [R_X], lambda e: e.tensor_copy(out=Mrot[:, :, 0, 0], in_=a8r[:, :]))
    op("dve", [Rst], [R_X], lambda e: e.tensor_copy(out=Mrot[:, :, 1, 1], in_=a8r[:, :]))
    op("dve", [Rst], [R_X], lambda e: e.tensor_copy(out=Mrot[:, :, 0, 1], in_=a8in[:, :]))
    op("dve", [Rst], [R_X], lambda e: e.tensor_copy(out=Mrot[:, :, 1, 0], in_=a8i[:, :]))
    if nseq == 1:
        op("act", [R_st], [R_S], lambda e: e.copy(out=Sinb[:, :, :, 0], in_=st[:, :, :, 0]))
    else:
        op("act", [R_st], [R_S], lambda e: e.copy(out=Sinb[:, :, :, :], in_=st[:, :, :, :]))
    nsteps = NC if nseq == 1 else nseq
    for c in range(nsteps):
        if nseq == 1:
            prev = st[:, :, :, 0] if c == 0 else Xs[:, :, :, c - 1]
        else:
            prev = st[:, :, :, c]
        cur = Xs[:, :, :, c]
        rd = [R_X, R_st, Rst]
        pb_ = prev.unsqueeze(2).broadcast_to([128, 32, 2, 2])
        op("dve", rd, [R_X], lambda e, pb_=pb_: e.tensor_tensor(out=prod[:], in0=pb_, in1=Mrot[:], op=ALU.mult))
        op("dve", rd, [R_X], lambda e: e.tensor_tensor(out=t1[:], in0=prod[:, :, :, 0], in1=prod[:, :, :, 1], op=ALU.add))
        op("dve", rd, [R_X], lambda e, cur=cur: e.tensor_tensor(out=cur, in0=cur, in1=t1[:], op=ALU.add))
    if nseq == 1:
        op("act", [R_X], [R_S], lambda e: e.copy(out=Sinb[:, :, :, 1:NC], in_=Xs[:, :, :, 0:NC - 1]))
        op("dve", [R_X], [R_st], lambda e: e.tensor_copy(out=st[:, :, :, 0], in_=Xs[:, :, :, NC - 1]))
    else:
        op("dve", [R_X], [R_st], lambda e: e.tensor_copy(out=st[:, :, :, :], in_=Xs[:, :, :, :]))
    if pn != "A":
        stg = sb("ostage", [16, 2048])
        for ri, nm in enumerate(("s5_re", "s5_im")):
            store_T(ctx, lambda b, ri=ri: st[:, b, ri, :], 32, nseq, O[nm + ("_p" if nseq == 1 else "_s")][j], stg, R_st, "s5o%d" % ri)
    kb.barrier()
    ies.close()
    sb = sbo
    KIs = [sb("KI%d" % i, [128, 8, 128], BF16) for i in range(2)]
    WYs = [sb("WY%d" % i, [128, 4, 8, 2, 32], BF16) for i in range(2)]
    R_KI = [Res() for _ in range(2)]
    R_WY = [Res() for _ in range(2)]
    cg = colgroups(NC, 64)
    for c in range(NB):
        wi = c % 2
        dma("sp", KIs[wi][:], ctx["KId"][j][:, c], [], [R_KI[wi]], ds_kw[wi])
        dma("sp", WYs[wi][:], ctx["WYd"][j][:, 4 * c:4 * c + 4], [], [R_WY[wi]], ds_kw[wi])
        kb.group_final([R_KI[wi], R_WY[wi]], ds_kw[wi])
        banks = banksets[c % 2]
        for gi, (n0, nn) in enumerate(cg):
            bank = banks[gi]
            uv = uT[:, c, n0 * 8:(n0 + nn) * 8].rearrange("p (n t) -> p t n", t=8)
            for lag in range(8):
                op("pe", [R_KI[wi], R_u[c]], [RP[bank]], lambda e, lag=lag, bank=bank, nn=nn, uv=uv, wi=wi: e.matmul(
                    PS[bank][:, lag * nn:8 * nn], lhsT=KIs[wi][:, lag, :], rhs=uv[:, 0:8 - lag, :], start=(lag == 0), stop=False))
            for tau in range(8):
                for ri in range(2):
                    for pr in range(4):
                        lastmm = (tau == 7 and ri == 1)
                        op("pe", [R_WY[wi], R_S], [RP[bank]], lambda e, pr=pr, tau=tau, ri=ri, bank=bank, wi=wi, n0=n0, nn=nn, c=c, lastmm=lastmm: e.matmul(
                            PS[bank][32 * pr:32 * pr + 32, tau * nn:(tau + 1) * nn], lhsT=WYs[wi][:, pr, tau, ri, :], rhs=Sinb[:, 4 * c + pr, ri, n0:n0 + nn], start=False, stop=lastmm, tile_position=(0, 32 * pr)))
            op("act", [RP[bank]], [R_gy[c]], lambda e, bank=bank, n0=n0, nn=nn, c=c: e.activation(out=gyT[:, c, n0 * 8:(n0 + nn) * 8].rearrange("p (n t) -> p n t", t=8), in_=PS[bank][:, 0:nn * 8].rearrange("p (t n) -> p n t", t=8), func=AF.Gelu_apprx_tanh))
    if pn == "A":
        dma("sp", ctx["S5WOd"][j], wo[:].rearrange("p a b -> p (a b)"), [R_wo], [], ctx["ds_wb"])
    sgs = [sb("sg%d" % i, [128, D]) for i in range(2)]
    R_sgs = [Res("sg%d" % i) for i in range(2)]
    pending = None
    for t in range(NT):
        sg = sgs[t % 2]
        R_sg = R_sgs[t % 2]
        r = rows[t]
        pb = [2, 3, 0, 1]
        for qq in (2, 3, 0, 1):
            for kc in range(NB):
                op("pe", [R_gy[kc], R_wo], [RP[pb[qq]]], lambda e, kc=kc, qq=qq: e.matmul(PS[pb[qq]][0:r, :], lhsT=gyT[:, kc, t * 128:t * 128 + r], rhs=wo[:, kc, qq * 512:(qq + 1) * 512], start=(kc == 0), stop=(kc == NB - 1)))
        zi = t % 4
        z = env["zt"][zi]
        for hh in range(2):
            op("act", [RP[pb[2 + hh]]], [R_sg], lambda e, hh=hh: e.activation(out=sg[0:r, hh * 512:(hh + 1) * 512], in_=PS[pb[2 + hh]][0:r, :], func=AF.Sigmoid))
            op("dve", [R_sg, RP[pb[hh]]], [R_sg], lambda e, hh=hh: e.tensor_tensor(out=sg[0:r, hh * 512:(hh + 1) * 512], in0=sg[0:r, hh * 512:(hh + 1) * 512], in1=PS[pb[hh]][0:r, :], op=ALU.mult))
            op("dve", [env["R_h"][t], R_sg], [env["R_zt"][zi]], lambda e, hh=hh: e.scalar_tensor_tensor(out=z[0:r, hh * 512:(hh + 1) * 512], in0=h_tok[0:r, t, hh * 512:(hh + 1) * 512], scalar=DN_ALPHA, in1=sg[0:r, hh * 512:(hh + 1) * 512], op0=ALU.mult, op1=ALU.add))
        if pending:
            pending()
        pending = env["layer_norm_tile"](t, zi, li_, 0, False)
    pending()


_NC_CACHE = {}


def _get_nc(dbg=None):
    key = repr(dbg)
    if key not in _NC_CACHE:
        nc = bass.Bass("TRN2", target_bir_lowering=False)
        build(nc, dbg)
        _NC_CACHE[key] = nc
    return _NC_CACHE[key]


def kernel(dbg=None, **inp):
    f = lambda a: np.ascontiguousarray(np.asarray(a, dtype=np.float32))
    ident = np.eye(128, dtype=np.float32)
    bmask = np.kron(np.eye(4, dtype=np.float32), np.ones((32, 32), np.float32))
    wnames = ["meta_tokens", "s5_w_in", "s5_lam_re", "s5_lam_im", "s5_log_step", "s5_b_re", "s5_b_im", "s5_c_re",
              "s5_c_im", "s5_d", "s5_w_out", "rg_w_in", "rg_conv_w", "rg_conv_b", "rg_w_gates", "rg_b_gates",
              "rg_lam", "rg_w_out", "ffn_w_up", "ffn_conv_w", "ffn_conv_b", "ffn_w_down", "ln_g", "ln_b"]
    shared = {n: f(inp[n]) for n in wnames}
    shared["ident"] = ident
    shared["bmask"] = bmask
    shared["sel2"] = np.kron(np.eye(2, dtype=np.float32), np.ones((1, 64), np.float32))
    in_maps = []
    for c in range(8):
        m = dict(shared)
        sl = slice(16 * c, 16 * c + 16)
        m["x_prompt"] = f(inp["x_prompt"][c])
        m["x_sample"] = f(inp["x_sample"][sl]).reshape(128, D)
        m["state_s5_re"] = f(inp["state_s5_re"][:, sl]).reshape(2, 16, 4096)
        m["state_s5_im"] = f(inp["state_s5_im"][:, sl]).reshape(2, 16, 4096)
        m["state_rg_h"] = f(inp["state_rg_h"][:, sl])
        m["state_rg_conv"] = f(inp["state_rg_conv"][:, sl]).reshape(2, 48, D)
        m["state_ffn_conv"] = f(inp["state_ffn_conv"][:, sl]).reshape(4, 32, FF2)
        in_maps.append(m)
    nc = _get_nc(dbg)
    res = run_bass_kernel_spmd(nc, in_maps, core_ids=list(range(8)))
    R = res.results
    cat = lambda k, ax: np.concatenate([np.asarray(R[c][k]) for c in range(8)], axis=ax)
    y_prompt = np.stack([np.asarray(R[c]["y_prompt"]) for c in range(8)], 0)
    y_sample = cat("y_sample", 0).reshape(128, 8, D)
    s5_re_p = cat("s5_re_p", 1).reshape(2, 8, 64, 64)
    s5_im_p = cat("s5_im_p", 1).reshape(2, 8, 64, 64)
    rg_h_p = cat("rg_h_p", 1).reshape(2, 8, D)
    rg_conv_p = np.stack([np.asarray(R[c]["rg_conv_p"]) for c in range(8)], 1).reshape(2, 8, 3, D)
    ffn_conv_p = np.stack([np.asarray(R[c]["ffn_conv_p"]) for c in range(8)], 1).reshape(4, 8, 2, FF2)
    s5_re_s = cat("s5_re_s", 1).reshape(2, 128, 64, 64)
    s5_im_s = cat("s5_im_s", 1).reshape(2, 128, 64, 64)
    rg_h_s = cat("rg_h_s", 1).reshape(2, 128, D)
    rg_conv_s = cat("rg_conv_s", 1).reshape(2, 128, 3, D)
    ffn_conv_s = cat("ffn_conv_s", 1).reshape(4, 128, 2, FF2)
    outs = (y_prompt, y_sample, s5_re_p, s5_im_p, rg_h_p, rg_conv_p, ffn_conv_p,
            s5_re_s, s5_im_s, rg_h_s, rg_conv_s, ffn_conv_s)
    return tuple(np.ascontiguousarray(o, dtype=np.float32) for o in outs)
```

```python
import math
from contextlib import ExitStack
import numpy as np
import concourse.bass as bass
import concourse.mybir as mybir
from concourse.bass_utils import run_bass_kernel_spmd

F32 = mybir.dt.float32
BF16 = mybir.dt.bfloat16
I32 = mybir.dt.int32
AF = mybir.ActivationFunctionType
ALU = mybir.AluOpType

D = 1024
NB = 8
FF = 2816
FF2 = 5632
FB = 44
FBH = 22
DEPTH = 4
SEQ = 2048
NMETA = 16
DN_ALPHA = (2 * DEPTH) ** 0.25
LN_EPS = 1e-5
RG_C = 8.0
TWO_PI = 2.0 * math.pi
ATTACH_WAITS = True
STATS = {}


class Res:
    __slots__ = ("name", "w", "r")

    def __init__(self, name=""):
        self.name = name
        self.w = None
        self.r = {}


class DSem:
    def __init__(self, sem):
        self.sem = sem
        self.val = 0


class Q:
    def __init__(self, name, eng, sem, eager):
        self.name = name
        self.eng = eng
        self.sem = sem
        self.eager = eager
        self.n = 0
        self.last = None
        self.last_ms = True
        self.ms = []
        self.semval = 0
        self.known = {}


class KB:
    def __init__(self, nc, es):
        self.nc = nc
        self.es = es
        self.q = {}
        for name, eng, eager in (("pe", nc.tensor, False), ("act", nc.scalar, False),
                                 ("dve", nc.vector, False), ("pool", nc.gpsimd, True),
                                 ("sp", nc.sync, True)):
            self.q[name] = Q(name, eng, self.sem("q_" + name), eager)
        self.dsems = []
        self.nsem = 0

    def sem(self, name):
        return self.es.enter_context(self.nc.semaphore(name))

    def dsem(self, name):
        d = DSem(self.sem("d_" + name))
        self.dsems.append(d)
        return d

    def sb(self, name, shape, dt, es=None):
        return (es or self.es).enter_context(self.nc.sbuf_tensor("sb_" + name, list(shape), dt))

    def ps(self, name, shape, dt=F32, es=None):
        return (es or self.es).enter_context(self.nc.psum_tensor("pt_" + name, list(shape), dt))

    def _milestone(self, A, k):
        lo, hi = 0, len(A.ms)
        while lo < hi:
            mid = (lo + hi) // 2
            if A.ms[mid][0] >= k:
                hi = mid
            else:
                lo = mid + 1
        if lo < len(A.ms):
            return A.ms[lo][1]
        assert A.last is not None and not A.last_ms and A.n >= k
        A.semval += 1
        A.last.then_inc(A.sem, 1)
        A.last_ms = True
        A.ms.append((A.n, A.semval))
        return A.semval

    def _wait(self, q, deps):
        need = {}
        for d in deps:
            if d is None:
                continue
            if d[0] == "q":
                A, k = d[1], d[2]
                if A is q and q.name == "pe":
                    continue
                v = self._milestone(A, k)
                key = A
                sem = A.sem
            else:
                key = d[1]
                sem = d[1].sem
                v = d[2]
            if q.known.get(key, 0) >= v:
                continue
            if need.get(key, (None, 0))[1] < v:
                need[key] = (sem, v)
        items = list(need.items())
        attach = None
        if ATTACH_WAITS and items:
            key, (sem, v) = items.pop()
            q.known[key] = v
            attach = (sem, v)
        for key, (sem, v) in items:
            q.eng.wait_ge(sem, v)
            q.known[key] = v
            STATS[q.name] = STATS.get(q.name, 0) + 1
        if attach:
            STATS["att_" + q.name] = STATS.get("att_" + q.name, 0) + 1
        return attach

    def _deps(self, reads, writes):
        deps = []
        for r in reads:
            if r.w is not None:
                deps.append(r.w)
        for w in writes:
            if w.w is not None:
                deps.append(w.w)
            deps.extend(w.r.values())
        return deps

    def op(self, qn, reads, writes, fn):
        q = self.q[qn]
        att = self._wait(q, self._deps(reads, writes))
        ins = fn(q.eng)
        if att is not None:
            ins._wait_ge(att[0], att[1])
        q.n += 1
        q.last = ins
        q.last_ms = False
        if q.eager:
            q.semval += 1
            ins.then_inc(q.sem, 1)
            q.last_ms = True
            q.ms.append((q.n, q.semval))
        me = ("q", q, q.n)
        for r in reads:
            r.r[q] = me
        for w in writes:
            w.w = me
            w.r = {}
        return ins

    def dma(self, qn, out, in_, reads, writes, ds, **kw):
        q = self.q[qn]
        att = self._wait(q, self._deps(reads, writes))
        ins = q.eng.dma_start(out=out, in_=in_, **kw)
        if att is not None:
            ins._wait_ge(att[0], att[1])
        ds.val += 16
        ins.then_inc(ds.sem, 16)
        me = ("d", ds, ds.val)
        for r in reads:
            r.r[ds] = me
        for w in writes:
            w.w = me
            w.r = {}
        return ins

    def group_final(self, ress, ds):
        for r in ress:
            r.w = ("d", ds, ds.val)

    def barrier(self, exclude=(), full=False):
        marks = []
        for A in self.q.values():
            if A.n > 0:
                marks.append(("q", A, A.n))
        for d in self.dsems:
            if d in exclude:
                continue
            if d.val > 0:
                marks.append(("d", d, d.val))
        global ATTACH_WAITS
        sv = ATTACH_WAITS
        ATTACH_WAITS = False
        for q in self.q.values():
            if q.name == "pe" and not full:
                continue
            self._wait(q, [m for m in marks if not (m[0] == "q" and m[1] is q)])
        ATTACH_WAITS = sv

    def finish(self):
        self.barrier(full=True)


def build(nc, dbg=None):
    es = ExitStack()
    kb = KB(nc, es)
    with es:
        _build(nc, kb, dbg)
    return nc


def _dram_in(nc, name, shape, dt=F32):
    return nc.dram_tensor(name, list(shape), dt, kind="ExternalInput").ap()


def _dram_out(nc, name, shape, dt=F32):
    return nc.dram_tensor(name, list(shape), dt, kind="ExternalOutput").ap()


def _build(nc, kb, dbg):
    op, dma = kb.op, kb.dma
    I = {}
    I["x_prompt"] = _dram_in(nc, "x_prompt", [SEQ, D])
    I["x_sample"] = _dram_in(nc, "x_sample", [128, D])
    I["state_s5_re"] = _dram_in(nc, "state_s5_re", [2, 16, 4096])
    I["state_s5_im"] = _dram_in(nc, "state_s5_im", [2, 16, 4096])
    I["state_rg_h"] = _dram_in(nc, "state_rg_h", [2, 16, D])
    I["state_rg_conv"] = _dram_in(nc, "state_rg_conv", [2, 48, D])
    I["state_ffn_conv"] = _dram_in(nc, "state_ffn_conv", [4, 32, FF2])
    I["meta_tokens"] = _dram_in(nc, "meta_tokens", [NMETA, D])
    I["s5_w_in"] = _dram_in(nc, "s5_w_in", [2, D, D])
    I["s5_lam_re"] = _dram_in(nc, "s5_lam_re", [2, 64, 64])
    I["s5_lam_im"] = _dram_in(nc, "s5_lam_im", [2, 64, 64])
    I["s5_log_step"] = _dram_in(nc, "s5_log_step", [2, 64])
    I["s5_b_re"] = _dram_in(nc, "s5_b_re", [2, 64, 64, 16])
    I["s5_b_im"] = _dram_in(nc, "s5_b_im", [2, 64, 64, 16])
    I["s5_c_re"] = _dram_in(nc, "s5_c_re", [2, 64, 16, 64])
    I["s5_c_im"] = _dram_in(nc, "s5_c_im", [2, 64, 16, 64])
    I["s5_d"] = _dram_in(nc, "s5_d", [2, D])
    I["s5_w_out"] = _dram_in(nc, "s5_w_out", [2, D, 2 * D])
    I["rg_w_in"] = _dram_in(nc, "rg_w_in", [2, D, 2 * D])
    I["rg_conv_w"] = _dram_in(nc, "rg_conv_w", [2, 4, D])
    I["rg_conv_b"] = _dram_in(nc, "rg_conv_b", [2, D])
    I["rg_w_gates"] = _dram_in(nc, "rg_w_gates", [2, 4, 256, 512])
    I["rg_b_gates"] = _dram_in(nc, "rg_b_gates", [2, 2 * D])
    I["rg_lam"] = _dram_in(nc, "rg_lam", [2, D])
    I["rg_w_out"] = _dram_in(nc, "rg_w_out", [2, D, D])
    I["ffn_w_up"] = _dram_in(nc, "ffn_w_up", [4, D, FF2])
    I["ffn_conv_w"] = _dram_in(nc, "ffn_conv_w", [4, 3, FF2])
    I["ffn_conv_b"] = _dram_in(nc, "ffn_conv_b", [4, FF2])
    I["ffn_w_down"] = _dram_in(nc, "ffn_w_down", [4, FF, D])
    I["ln_g"] = _dram_in(nc, "ln_g", [4, 2, D])
    I["ln_b"] = _dram_in(nc, "ln_b", [4, 2, D])
    I["ident"] = _dram_in(nc, "ident", [128, 128])
    I["bmask"] = _dram_in(nc, "bmask", [128, 128])
    I["sel2"] = _dram_in(nc, "sel2", [2, 128])

    O = {}
    O["y_prompt"] = _dram_out(nc, "y_prompt", [SEQ, D])
    O["y_sample"] = _dram_out(nc, "y_sample", [128, D])
    O["s5_re_p"] = _dram_out(nc, "s5_re_p", [2, 1, 4096])
    O["s5_im_p"] = _dram_out(nc, "s5_im_p", [2, 1, 4096])
    O["rg_h_p"] = _dram_out(nc, "rg_h_p", [2, 1, D])
    O["rg_conv_p"] = _dram_out(nc, "rg_conv_p", [2, 3, D])
    O["ffn_conv_p"] = _dram_out(nc, "ffn_conv_p", [4, 2, FF2])
    O["s5_re_s"] = _dram_out(nc, "s5_re_s", [2, 16, 4096])
    O["s5_im_s"] = _dram_out(nc, "s5_im_s", [2, 16, 4096])
    O["rg_h_s"] = _dram_out(nc, "rg_h_s", [2, 16, D])
    O["rg_conv_s"] = _dram_out(nc, "rg_conv_s", [2, 48, D])
    O["ffn_conv_s"] = _dram_out(nc, "ffn_conv_s", [4, 32, FF2])

    WXd = [nc.dram_tensor("WXd%d" % j, [128, 8, 8, 2, 128], BF16, kind="Internal").ap() for j in range(2)]
    WYd = [nc.dram_tensor("WYd%d" % j, [128, 32, 8, 2, 32], BF16, kind="Internal").ap() for j in range(2)]
    KId = [nc.dram_tensor("KId%d" % j, [128, 8, 8, 128], BF16, kind="Internal").ap() for j in range(2)]

    NBLK = 2 * NB + 2 * 2 * NB + 4 * FB
    ctx_w = dict(
        WBLK=nc.dram_tensor("WBLK", [NBLK, 128, NB * 128], BF16, kind="Internal").ap(),
        WDd=[nc.dram_tensor("WDd%d" % l, [128, FBH * D], BF16, kind="Internal").ap() for l in range(4)],
        S5WOd=[nc.dram_tensor("S5WOd%d" % j, [128, NB * 2 * D], BF16, kind="Internal").ap() for j in range(2)],
        RGWOd=[nc.dram_tensor("RGWOd%d" % j, [128, NB * D], BF16, kind="Internal").ap() for j in range(2)],
        RGWGd=[nc.dram_tensor("RGWGd%d" % j, [128, 4 * 2 * 512], BF16, kind="Internal").ap() for j in range(2)],
    )
    sb = kb.sb
    ident = sb("ident", [128, 128], F32)
    bmask = sb("bmask", [128, 128], F32)
    R_const = Res("const")
    ds_c = kb.dsem("const")
    dma("sp", ident[:], I["ident"][:, :], [], [R_const], ds_c)
    dma("sp", bmask[:], I["bmask"][:, :], [], [R_const], ds_c)
    sel2 = sb("sel2", [2, 128], F32)
    dma("sp", sel2[:], I["sel2"][:, :], [], [R_const], ds_c)

    R_par = Res("par")
    ds_p = kb.dsem("par")

    par_loads = []

    def load_cols(name, src_1d, nblk, ds):
        t = sb(name, [128, nblk], F32)
        par_loads.append((t, src_1d))
        return t

    def issue_par_loads():
        with nc.allow_non_contiguous_dma(reason="small param"):
            for t, src_1d in par_loads:
                dma("sp", t[:], src_1d.rearrange("(b p) -> p b", p=128), [], [R_par], ds_p)

    ffn_cw = [[load_cols("fcw%d_%d" % (l, k), I["ffn_conv_w"][l, k], FB, ds_c) for k in range(3)] for l in range(4)]
    ffn_cb = [load_cols("fcb%d" % l, I["ffn_conv_b"][l], FB, ds_c) for l in range(4)]
    rg_cw = [[load_cols("rcw%d_%d" % (j, k), I["rg_conv_w"][j, k], NB, ds_c) for k in range(4)] for j in range(2)]
    rg_cb = [load_cols("rcb%d" % j, I["rg_conv_b"][j], NB, ds_c) for j in range(2)]
    rg_bg = [load_cols("rbg%d" % j, I["rg_b_gates"][j], 16, ds_c) for j in range(2)]
    rg_lm = [load_cols("rlm%d" % j, I["rg_lam"][j], NB, ds_c) for j in range(2)]
    s5_dd = [sb("s5d%d" % j, [128, NB], F32) for j in range(2)]
    with nc.allow_non_contiguous_dma(reason="small param"):
        for j in range(2):
            dma("sp", s5_dd[j][:], I["s5_d"][j].rearrange("(b p) -> p b", p=128), [], [R_const], ds_c)
    rg_m8sp = [sb("m8sp%d" % j, [128, NB], F32) for j in range(2)]
    A8re = [sb("A8re%d" % j, [128, 32], F32) for j in range(2)]
    A8im = [sb("A8im%d" % j, [128, 32], F32) for j in range(2)]
    A8imn = [sb("A8imn%d" % j, [128, 32], F32) for j in range(2)]
    ffn_hist_p = [sb("fhp%d" % l, [128, FB, 1, 2], F32) for l in range(4)]
    rg_hist_p = [sb("rhp%d" % j, [128, NB, 1, 3], F32) for j in range(2)]
    rg_h_p = [sb("rgh%d" % j, [128, NB, 1], F32) for j in range(2)]
    s5st_p = [sb("s5p%d" % j, [128, 32, 2, 1], F32) for j in range(2)]
    R_state = Res("state")

    PSall = kb.ps("psall", [128, 8 * 512])
    PS = [PSall[:, i * 512:(i + 1) * 512] for i in range(8)]
    RP = [Res("ps%d" % i) for i in range(8)]

    for j in range(2):
        pass
    for t in ffn_hist_p + rg_hist_p + rg_h_p + s5st_p:
        op("dve", [], [R_state], lambda e, t=t: e.memset(t[:], 0.0))

    ctx = dict(nc=nc, kb=kb, I=I, O=O, sel2=sel2, R_par=R_par, WXd=WXd, WYd=WYd, KId=KId, ident=ident, bmask=bmask,
               R_const=R_const, PS=PS, PSall=PSall, RP=RP, ffn_cw=ffn_cw, ffn_cb=ffn_cb, rg_cw=rg_cw, rg_cb=rg_cb,
               rg_bg=rg_bg, rg_m8sp=rg_m8sp, s5_dd=s5_dd, A8re=A8re, A8im=A8im, A8imn=A8imn,
               ffn_hist_p=ffn_hist_p, rg_hist_p=rg_hist_p, rg_h_p=rg_h_p, s5st_p=s5st_p,
               R_state=R_state, dbg=dbg, ds_out=kb.dsem("out"), ds_w=None, ds_wb=kb.dsem("wb"), **ctx_w)

    with ExitStack() as lds:
        with ExitStack() as nat:
            gens = [s5_prologue_loads(ctx, j, lds, nat) for j in range(2)]
            for g in gens:
                next(g)
            lds_ = [next(g) for g in gens]
            kb.barrier()
        for j in range(2):
            with ExitStack() as pes:
                ctx["issue_par_loads"] = issue_par_loads
                s5_prologue(ctx, j, pes, lds_[j])
                kb.barrier()
    kb.group_final([R_par], ds_p)
    for j in range(2):
        op("act", [R_par], [R_par], lambda e, j=j: e.activation(out=rg_m8sp[j][:], in_=rg_lm[j][:], func=AF.Exp, scale=-1.0))
        op("act", [R_par], [R_par], lambda e, j=j: e.activation(out=rg_m8sp[j][:], in_=rg_m8sp[j][:], func=AF.Ln, bias=1.0))
        op("dve", [R_par], [R_par], lambda e, j=j: e.tensor_scalar(out=rg_m8sp[j][:], in0=rg_m8sp[j][:], scalar1=-RG_C, scalar2=None, op0=ALU.mult))
    kb.barrier()
    passes = [("A", 1, 1024), ("B", 1, 1040), ("S", 16, 8)]
    for (pn, nseq, L) in passes:
        with ExitStack() as pes:
            run_pass(ctx, pn, nseq, L, pes)
            kb.barrier()
    kb.finish()


def colgroups(N, g=512):
    return [(c0, min(g, N - c0)) for c0 in range(0, N, g)]


def store_T(ctx, src_fn, nblk, m, dram_rows, stage, R_src, tag):
    kb, PS, RP, ident = ctx["kb"], ctx["PS"], ctx["RP"], ctx["ident"]
    R_stage = ctx.setdefault("stage_res", {}).setdefault(id(stage), Res("stage" + tag))
    GB = stage.shape[1] // 128
    for g0 in range(0, nblk, GB):
        ng = min(GB, nblk - g0)
        for b0 in range(g0, g0 + ng, 4):
            nb_ = min(4, g0 + ng - b0)
            bank = 6 + ((b0 // 4) % 2)
            for b in range(nb_):
                kb.op("pe", [R_src, ctx["R_const"]], [RP[bank]],
                      lambda e, b=b, b0=b0, bank=bank: e.transpose(out=PS[bank][0:m, b * 128:(b + 1) * 128], in_=src_fn(b0 + b), identity=ident[:, :]))
            kb.op("act", [RP[bank]], [R_stage],
                  lambda e, b0=b0, nb_=nb_, bank=bank, g0=g0: e.copy(out=stage[0:m, (b0 - g0) * 128:(b0 - g0 + nb_) * 128], in_=PS[bank][0:m, 0:nb_ * 128]))
        kb.dma("sp", dram_rows[:, g0 * 128:(g0 + ng) * 128], stage[0:m, 0:ng * 128], [R_stage], [], ctx["ds_out"])


def load_T(ctx, dram_rows, nblk, m, dst_fn, stage, R_dst, tag, ds):
    kb, PS, RP, ident = ctx["kb"], ctx["PS"], ctx["RP"], ctx["ident"]
    R_stage = ctx.setdefault("stage_res", {}).setdefault(id(stage), Res("lstage" + tag))
    ctx["uid"] = ctx.get("uid", 0) + 1
    ds = kb.dsem("lt%d" % ctx["uid"])
    kb.dma("sp", stage[0:m, 0:nblk * 128], dram_rows, [], [R_stage], ds)
    per = 512 // m
    gi = 0
    for b0 in range(0, nblk, per):
        nb_ = min(per, nblk - b0)
        bank = 6 + (gi % 2)
        gi += 1
        for b in range(nb_):
            kb.op("pe", [R_stage, ctx["R_const"]], [RP[bank]],
                  lambda e, b=b: e.transpose(out=PS[bank][:, b * m:(b + 1) * m], in_=stage[0:m, (b0 + b) * 128:(b0 + b + 1) * 128], identity=ident[0:m, 0:m]))
        kb.op("act", [RP[bank]], [R_dst],
              lambda e: e.copy(out=dst_fn(b0, nb_), in_=PS[bank][:, 0:nb_ * m].rearrange("p (b m) -> p b m", m=m)))


def s5_prologue_loads(ctx, j, pes, nat):
    nc, kb, I = ctx["nc"], ctx["kb"], ctx["I"]
    dma, op = kb.dma, kb.op
    PS, RP, ident, Rc = ctx["PS"], ctx["RP"], ctx["ident"], ctx["R_const"]
    sb = lambda n, s, d=F32: kb.sb("pl%d_%s" % (j, n), s, d, es=pes)
    ds = kb.dsem("pl%d" % j)
    R = Res("pl")
    Rn = Res("plnat")
    lr = sb("lr", [128, 32]); li = sb("li", [128, 32]); stp = sb("stp", [128, 32])
    Bre = sb("Bre", [128, 32, 16]); Bim = sb("Bim", [128, 32, 16])
    yield None
    sbn = lambda n, s, d=F32: kb.sb("pl%d_%s" % (j, n), s, d, es=nat)
    lrn = sbn("lrn", [32, 128]); lin = sbn("lin", [32, 128]); stn = sbn("stn", [2, 32])
    Bn = [sbn("Bn%d" % i, [32, 128, 16]) for i in range(2)]
    dma("sp", lrn[:], I["s5_lam_re"][j].rearrange("(pr g2) p -> pr (g2 p)", g2=2), [], [Rn], ds)
    dma("sp", lin[:], I["s5_lam_im"][j].rearrange("(pr g2) p -> pr (g2 p)", g2=2), [], [Rn], ds)
    with nc.allow_non_contiguous_dma(reason="tiny"):
        dma("sp", stn[:], I["s5_log_step"][j].rearrange("(pr g2) -> g2 pr", g2=2), [], [Rn], ds)
    dma("sp", Bn[0][:], I["s5_b_re"][j].rearrange("(pr g2) p h -> pr (g2 p) h", g2=2), [], [Rn], ds)
    dma("sp", Bn[1][:], I["s5_b_im"][j].rearrange("(pr g2) p h -> pr (g2 p) h", g2=2), [], [Rn], ds)
    kb.group_final([Rn], ds)
    bank = 4 + j
    op("pe", [Rn, Rc], [RP[bank]], lambda e: e.transpose(out=PS[bank][:, 0:32], in_=lrn[:, :], identity=ident[0:32, 0:32]))
    op("pe", [Rn, Rc], [RP[bank]], lambda e: e.transpose(out=PS[bank][:, 32:64], in_=lin[:, :], identity=ident[0:32, 0:32]))
    op("pe", [Rn, Rc], [RP[bank]], lambda e: e.matmul(PS[bank][:, 64:96], lhsT=ctx["sel2"][:, :], rhs=stn[:, :], start=True, stop=True))
    op("act", [RP[bank]], [R], lambda e: e.copy(out=lr[:], in_=PS[bank][:, 0:32]))
    op("act", [RP[bank]], [R], lambda e: e.copy(out=li[:], in_=PS[bank][:, 32:64]))
    op("act", [RP[bank]], [R], lambda e: e.copy(out=stp[:], in_=PS[bank][:, 64:96]))
    for i, Bd in enumerate((Bre, Bim)):
        for h in range(16):
            op("pe", [Rn, Rc], [RP[bank]], lambda e, h=h, i=i: e.transpose(out=PS[bank][:, h * 32:(h + 1) * 32], in_=Bn[i][:, :, h], identity=ident[0:32, 0:32]))
        op("act", [RP[bank]], [R], lambda e, Bd=Bd: e.copy(out=Bd[:, :, :].rearrange("p r h -> p h r"), in_=PS[bank][:, :].rearrange("p (h r) -> p h r", r=32)))
    yield dict(R=R, ds=ds, lr=lr, li=li, stp=stp, Bre=Bre, Bim=Bim)


def s5_prologue(ctx, j, pes, ld):
    nc, kb, I = ctx["nc"], ctx["kb"], ctx["I"]
    op, dma = kb.op, kb.dma
    PS, RP, ident, bmask = ctx["PS"], ctx["RP"], ctx["ident"], ctx["bmask"]
    Rc = ctx["R_const"]
    sb = lambda n, s, d=F32: kb.sb("pl%d_%s" % (j, n), s, d, es=pes)
    R, ds = ld["R"], ld["ds"]
    lr, li, stp, Bre, Bim = (ld[k] for k in ("lr", "li", "stp", "Bre", "Bim"))
    Cre = sb("Cre", [128, 32, 16]); Cim = sb("Cim", [128, 32, 16])
    Ch_re = sb("Chre", [16, 64, 64]); Ch_im = sb("Chim", [16, 64, 64])
    R_ch = Res("ch")
    ds_ch = kb.dsem("plc%d" % j)
    with nc.allow_non_contiguous_dma(reason="param layout"):
        dma("sp", Ch_re[:], I["s5_c_re"][j].rearrange("g h p -> h g p"), [], [R_ch], ds_ch)
        dma("sp", Ch_im[:], I["s5_c_im"][j].rearrange("g h p -> h g p"), [], [R_ch], ds_ch)
    kb.group_final([R_ch], ds_ch)
    if j == 1:
        ctx["issue_par_loads"]()
    for (Ch, Cd) in ((Ch_re, Cre), (Ch_im, Cim)):
        for g0 in range(0, 32, 16):
            bank = 4 + (g0 // 16)
            for pr in range(g0, g0 + 16):
                op("pe", [R_ch, Rc], [RP[bank]],
                   lambda e, pr=pr, Ch=Ch: e.transpose(out=PS[bank][:, (pr - g0) * 16:(pr - g0 + 1) * 16],
                                                       in_=Ch[0:16, 2 * pr:2 * pr + 2, :].rearrange("h g p -> h (g p)"),
                                                       identity=ident[0:16, 0:16]))
            op("act", [RP[bank]], [R], lambda e, Cd=Cd: e.copy(out=Cd[:, g0:g0 + 16, :], in_=PS[bank][:, 0:256].rearrange("p (a h) -> p a h", h=16)))

    t = lambda n: sb(n, [128, 32])
    ang = t("ang"); mag = t("mag"); sn = t("sn"); cs = t("cs"); tmp = t("tmp"); tmp2 = t("tmp2")
    abre = t("abre"); abim = t("abim"); qre = t("qre"); qim = t("qim"); ki = sb("ki", [128, 32], I32)
    V = lambda f: op("dve", [R], [R], f)
    A = lambda f: op("act", [R], [R], f)
    A(lambda e: e.activation(out=stp[:], in_=stp[:], func=AF.Exp))
    V(lambda e: e.tensor_tensor(out=ang[:], in0=li[:], in1=stp[:], op=ALU.mult))
    V(lambda e: e.tensor_tensor(out=mag[:], in0=lr[:], in1=stp[:], op=ALU.mult))
    A(lambda e: e.activation(out=mag[:], in_=mag[:], func=AF.Exp))

    def sin_of(dst, shift):
        V(lambda e: e.tensor_scalar(out=tmp[:], in0=ang[:], scalar1=shift, scalar2=1.0 / TWO_PI, op0=ALU.add, op1=ALU.mult))
        V(lambda e: e.tensor_copy(out=ki[:], in_=tmp[:]))
        V(lambda e: e.tensor_copy(out=tmp2[:], in_=ki[:]))
        V(lambda e: e.tensor_tensor(out=tmp[:], in0=tmp[:], in1=tmp2[:], op=ALU.subtract))
        V(lambda e: e.tensor_scalar(out=tmp2[:], in0=tmp[:], scalar1=0.5, scalar2=None, op0=ALU.is_gt))
        V(lambda e: e.tensor_tensor(out=tmp[:], in0=tmp[:], in1=tmp2[:], op=ALU.subtract))
        V(lambda e: e.tensor_scalar(out=tmp2[:], in0=tmp[:], scalar1=-0.5, scalar2=None, op0=ALU.is_lt))
        V(lambda e: e.tensor_tensor(out=tmp[:], in0=tmp[:], in1=tmp2[:], op=ALU.add))
        V(lambda e: e.tensor_scalar(out=tmp[:], in0=tmp[:], scalar1=TWO_PI, scalar2=math.pi, op0=ALU.mult, op1=ALU.min))
        V(lambda e: e.tensor_scalar(out=tmp[:], in0=tmp[:], scalar1=-math.pi, scalar2=None, op0=ALU.max))
        A(lambda e: e.activation(out=dst[:], in_=tmp[:], func=AF.Sin))

    sin_of(sn, 0.0)
    sin_of(cs, math.pi / 2)
    V(lambda e: e.tensor_tensor(out=abre[:], in0=mag[:], in1=cs[:], op=ALU.mult))
    V(lambda e: e.tensor_tensor(out=abim[:], in0=mag[:], in1=sn[:], op=ALU.mult))
    den = t("den"); nr = t("nr")
    V(lambda e: e.tensor_tensor(out=den[:], in0=lr[:], in1=lr[:], op=ALU.mult))
    V(lambda e: e.tensor_tensor(out=tmp[:], in0=li[:], in1=li[:], op=ALU.mult))
    V(lambda e: e.tensor_tensor(out=den[:], in0=den[:], in1=tmp[:], op=ALU.add))
    V(lambda e: e.reciprocal(out=den[:], in_=den[:]))
    V(lambda e: e.tensor_scalar(out=nr[:], in0=abre[:], scalar1=-1.0, scalar2=None, op0=ALU.add))
    V(lambda e: e.tensor_tensor(out=qre[:], in0=nr[:], in1=lr[:], op=ALU.mult))
    V(lambda e: e.tensor_tensor(out=tmp[:], in0=abim[:], in1=li[:], op=ALU.mult))
    V(lambda e: e.tensor_tensor(out=qre[:], in0=qre[:], in1=tmp[:], op=ALU.add))
    V(lambda e: e.tensor_tensor(out=qre[:], in0=qre[:], in1=den[:], op=ALU.mult))
    V(lambda e: e.tensor_tensor(out=qim[:], in0=abim[:], in1=lr[:], op=ALU.mult))
    V(lambda e: e.tensor_tensor(out=tmp[:], in0=nr[:], in1=li[:], op=ALU.mult))
    V(lambda e: e.tensor_tensor(out=qim[:], in0=qim[:], in1=tmp[:], op=ALU.subtract))
    V(lambda e: e.tensor_tensor(out=qim[:], in0=qim[:], in1=den[:], op=ALU.mult))
    pwr = sb("pwr", [128, 9, 32]); pwi = sb("pwi", [128, 9, 32])
    V(lambda e: e.memset(pwr[:, 0, :], 1.0))
    V(lambda e: e.memset(pwi[:, 0, :], 0.0))
    for k in range(1, 9):
        V(lambda e, k=k: e.tensor_tensor(out=pwr[:, k, :], in0=pwr[:, k - 1, :], in1=abre[:], op=ALU.mult))
        V(lambda e, k=k: e.tensor_tensor(out=tmp[:], in0=pwi[:, k - 1, :], in1=abim[:], op=ALU.mult))
        V(lambda e, k=k: e.tensor_tensor(out=pwr[:, k, :], in0=pwr[:, k, :], in1=tmp[:], op=ALU.subtract))
        V(lambda e, k=k: e.tensor_tensor(out=pwi[:, k, :], in0=pwr[:, k - 1, :], in1=abim[:], op=ALU.mult))
        V(lambda e, k=k: e.tensor_tensor(out=tmp[:], in0=pwi[:, k - 1, :], in1=abre[:], op=ALU.mult))
        V(lambda e, k=k: e.tensor_tensor(out=pwi[:, k, :], in0=pwi[:, k, :], in1=tmp[:], op=ALU.add))
    Rst = ctx["R_state"]
    op("dve", [R], [Rst], lambda e: e.tensor_copy(out=ctx["A8re"][j][:], in_=pwr[:, 8, :]))
    op("dve", [R], [Rst], lambda e: e.tensor_copy(out=ctx["A8im"][j][:], in_=pwi[:, 8, :]))
    op("dve", [R], [Rst], lambda e: e.tensor_scalar(out=ctx["A8imn"][j][:], in0=pwi[:, 8, :], scalar1=-1.0, scalar2=None, op0=ALU.mult))

    def bc(x2d):
        return x2d.unsqueeze(2).broadcast_to([128, 32, 16])

    T3 = lambda n: sb(n, [128, 32, 16])
    w1 = T3("w1"); w2 = T3("w2")

    def cmul(dre, dim, xre, xim, sre, sim_):
        V(lambda e: e.tensor_tensor(out=w1[:], in0=xre, in1=bc(sre), op=ALU.mult))
        V(lambda e: e.tensor_tensor(out=w2[:], in0=xim, in1=bc(sim_), op=ALU.mult))
        V(lambda e: e.tensor_tensor(out=dre, in0=w1[:], in1=w2[:], op=ALU.subtract))
        V(lambda e: e.tensor_tensor(out=w1[:], in0=xre, in1=bc(sim_), op=ALU.mult))
        V(lambda e: e.tensor_tensor(out=w2[:], in0=xim, in1=bc(sre), op=ALU.mult))
        V(lambda e: e.tensor_tensor(out=dim, in0=w1[:], in1=w2[:], op=ALU.add))

    Bbre = T3("Bbre"); Bbim = T3("Bbim")
    cmul(Bbre[:], Bbim[:], Bre[:], Bim[:], qre[:], qim[:])
    Bpre = sb("Bpre", [128, 32, 32]); Bpim = sb("Bpim", [128, 32, 32])
    V(lambda e: e.memset(Bpre[:], 0.0))
    V(lambda e: e.memset(Bpim[:], 0.0))
    for g2 in range(2):
        hp = slice(g2 * 64, (g2 + 1) * 64); hc = slice(g2 * 16, (g2 + 1) * 16)
        V(lambda e, hp=hp, hc=hc: e.tensor_copy(out=Bpre[hp, :, hc], in_=Bbre[hp, :, :]))
        V(lambda e, hp=hp, hc=hc: e.tensor_copy(out=Bpim[hp, :, hc], in_=Bbim[hp, :, :]))

    WY = sb("WY", [128, 32, 8, 2, 32], BF16)
    KI = sb("KI", [128, 8, 8, 128], BF16)
    WX = sb("WX", [128, 8, 8, 2, 128], BF16)
    R_wy = Res("wy"); R_tmp = Res("pltmp")
    R_ca = [Res("ca0"), Res("ca1")]; R_cap = [Res("cap0"), Res("cap1")]; R_xb = [Res("xb0"), Res("xb1")]
    op("pool", [], [R_wy], lambda e: e.memset(WY[:].rearrange("p a b c d -> p (a b c d)"), 0.0))
    CAre2 = [T3("CAre%d" % b) for b in range(2)]; CAim2 = [T3("CAim%d" % b) for b in range(2)]
    CApre2 = [sb("CApre%d" % b, [128, 32, 32]) for b in range(2)]; CApimn2 = [sb("CApimn%d" % b, [128, 32, 32]) for b in range(2)]
    for b in range(2):
        op("pool", [], [R_cap[b]], lambda e, b=b: e.memset(CApre2[b][:], 0.0))
        op("pool", [], [R_cap[b]], lambda e, b=b: e.memset(CApimn2[b][:], 0.0))
    kit = sb("kit", [128, 128])
    dcol = ctx["s5_dd"][j]
    R_o = Res("pl_out")
    R_kit = Res("kit")

    def cmul2(dre, dim, xre, xim, sre, sim_, Rw):
        D = lambda f, rd, wr: op("dve", rd, wr, f)
        D(lambda e: e.tensor_tensor(out=w1[:], in0=xre, in1=bc(sre), op=ALU.mult), [R, R_tmp], [R_tmp])
        D(lambda e: e.tensor_tensor(out=w2[:], in0=xim, in1=bc(sim_), op=ALU.mult), [R, R_tmp], [R_tmp])
        D(lambda e: e.tensor_tensor(out=dre, in0=w1[:], in1=w2[:], op=ALU.subtract), [R_tmp], [Rw])
        D(lambda e: e.tensor_tensor(out=w1[:], in0=xre, in1=bc(sim_), op=ALU.mult), [R, R_tmp], [R_tmp])
        D(lambda e: e.tensor_tensor(out=w2[:], in0=xim, in1=bc(sre), op=ALU.mult), [R, R_tmp], [R_tmp])
        D(lambda e: e.tensor_tensor(out=dim, in0=w1[:], in1=w2[:], op=ALU.add), [R_tmp], [Rw])

    for k in range(9):
        b = k % 2
        CAre, CAim, CApre, CApimn = CAre2[b], CAim2[b], CApre2[b], CApimn2[b]
        cmul2(CAre[:], CAim[:], Cre[:], Cim[:], pwr[:, k, :], pwi[:, k, :], R_ca[b])
        for g2 in range(2):
            hp = slice(g2 * 64, (g2 + 1) * 64); hc = slice(g2 * 16, (g2 + 1) * 16)
            op("pool", [R_ca[b]], [R_cap[b]], lambda e, hp=hp, hc=hc, CAre=CAre, CApre=CApre: e.tensor_copy(out=CApre[hp, :, hc], in_=CAre[hp, :, :]))
            op("pool", [R_ca[b]], [R_cap[b]], lambda e, hp=hp, hc=hc, CAim=CAim, CApimn=CApimn: e.tensor_scalar(out=CApimn[hp, :, hc], in0=CAim[hp, :, :], scalar1=-1.0, scalar2=0.0, op0=ALU.mult, op1=ALU.add))
            if k >= 1:
                op("pool", [R_ca[b]], [R_wy], lambda e, hp=hp, hc=hc, k=k, CAre=CAre: e.tensor_copy(out=WY[hp, :, k - 1, 0, hc], in_=CAre[hp, :, :]))
                op("pool", [R_ca[b]], [R_wy], lambda e, hp=hp, hc=hc, k=k, CAim=CAim: e.tensor_scalar(out=WY[hp, :, k - 1, 1, hc], in0=CAim[hp, :, :], scalar1=-1.0, scalar2=0.0, op0=ALU.mult, op1=ALU.add))
        if k <= 7:
            for c in range(8):
                bank = 4 + (c % 2)
                cs4 = slice(4 * c, 4 * c + 4)
                op("pe", [R, R_cap[b]], [RP[bank]], lambda e, cs4=cs4, CApre=CApre: e.matmul(PS[bank][:, 0:128], lhsT=Bpre[:, cs4, :].rearrange("p a b -> p (a b)"),
                                                                 rhs=CApre[:, cs4, :].rearrange("p a b -> p (a b)"), start=True, stop=False))
                op("pe", [R, R_cap[b]], [RP[bank]], lambda e, cs4=cs4, CApimn=CApimn: e.matmul(PS[bank][:, 0:128], lhsT=Bpim[:, cs4, :].rearrange("p a b -> p (a b)"),
                                                                 rhs=CApimn[:, cs4, :].rearrange("p a b -> p (a b)"), start=False, stop=True))
                if k == 0:
                    op("dve", [RP[bank], Rc], [R_kit], lambda e, bank=bank: e.tensor_tensor(out=kit[:], in0=PS[bank][:, 0:128], in1=bmask[:], op=ALU.mult))
                    op("dve", [R_kit, Rc, ctx["R_par"]], [R_o], lambda e, c=c: e.scalar_tensor_tensor(out=KI[:, c, 0, :], in0=ident[:], scalar=dcol[:, c:c + 1], in1=kit[:], op0=ALU.mult, op1=ALU.add))
                else:
                    op("dve", [RP[bank], Rc], [R_o], lambda e, c=c, k=k, bank=bank: e.tensor_tensor(out=KI[:, c, k, :], in0=PS[bank][:, 0:128], in1=bmask[:], op=ALU.mult))
    XBre2 = [sb("XBre%d" % b, [128, 32, 32]) for b in range(2)]; XBim2 = [sb("XBim%d" % b, [128, 32, 32]) for b in range(2)]
    x1 = sb("x1", [128, 32, 32]); x2 = sb("x2", [128, 32, 32])
    bc32 = lambda x2d: x2d.unsqueeze(2).broadcast_to([128, 32, 32])
    D = lambda f, rd, wr: op("dve", rd, wr, f)
    for tau in range(8):
        k = 7 - tau
        b = tau % 2
        XBre, XBim = XBre2[b], XBim2[b]
        D(lambda e, k=k: e.tensor_tensor(out=x1[:], in0=Bpre[:], in1=bc32(pwr[:, k, :]), op=ALU.mult), [R, R_tmp], [R_tmp])
        D(lambda e, k=k: e.tensor_tensor(out=x2[:], in0=Bpim[:], in1=bc32(pwi[:, k, :]), op=ALU.mult), [R, R_tmp], [R_tmp])
        D(lambda e, XBre=XBre: e.tensor_tensor(out=XBre[:], in0=x1[:], in1=x2[:], op=ALU.subtract), [R_tmp], [R_xb[b]])
        D(lambda e, k=k: e.tensor_tensor(out=x1[:], in0=Bpre[:], in1=bc32(pwi[:, k, :]), op=ALU.mult), [R, R_tmp], [R_tmp])
        D(lambda e, k=k: e.tensor_tensor(out=x2[:], in0=Bpim[:], in1=bc32(pwr[:, k, :]), op=ALU.mult), [R, R_tmp], [R_tmp])
        D(lambda e, XBim=XBim: e.tensor_tensor(out=XBim[:], in0=x1[:], in1=x2[:], op=ALU.add), [R_tmp], [R_xb[b]])
        for ri, XB in enumerate((XBre, XBim)):
            for c in range(8):
                bank = 4 + (c % 2)
                op("pe", [R_xb[b], Rc], [RP[bank]], lambda e, c=c, XB=XB: e.transpose(out=PS[bank][:, 0:128], in_=XB[:, 4 * c:4 * c + 4, :].rearrange("p a b -> p (a b)"), identity=ident[:, :]))
                op("act", [RP[bank]], [R_o], lambda e, c=c, ri=ri, tau=tau, bank=bank: e.copy(out=WX[:, c, tau, ri, :], in_=PS[bank][:, 0:128]))
    dma("sp", ctx["WXd"][j].rearrange("p a b c d -> p (a b c d)"), WX[:].rearrange("p a b c d -> p (a b c d)"), [R, R_o], [], ds)
    dma("sp", ctx["WYd"][j].rearrange("p a b c d -> p (a b c d)"), WY[:].rearrange("p a b c d -> p (a b c d)"), [R, R_wy], [], ds)
    dma("sp", ctx["KId"][j].rearrange("p a b c -> p (a b c)"), KI[:].rearrange("p a b c -> p (a b c)"), [R, R_o], [], ds)


def run_pass(ctx, pn, nseq, L, pes):
    nc, kb, I, O = ctx["nc"], ctx["kb"], ctx["I"], ctx["O"]
    op, dma = kb.op, kb.dma
    PS, RP, ident = ctx["PS"], ctx["RP"], ctx["ident"]
    Rc = ctx["R_const"]
    N = nseq * L
    NT = (N + 127) // 128
    rows = [min(128, N - t * 128) for t in range(NT)]
    NC = N // 8
    sb = lambda n, s, d=F32, es=None: kb.sb("p%s_%s" % (pn, n), s, d, es=es or pes)

    h_tok = sb("htok", [128, NT, D])
    hT = sb("hT", [128, NB, N], BF16)
    R_h = [Res("h%d" % t) for t in range(NT)]
    R_hT = [Res("hT%d" % t) for t in range(NT)]
    ds_in = kb.dsem("in" + pn)
    ds_w = kb.dsem("w" + pn)
    ds_ln = kb.dsem("ln" + pn)
    gbc = sb("gbc", [128, D]); bbc = sb("bbc", [128, D])
    R_ln = Res("ln")
    NZ = 4
    zt = [sb("zt%d" % i, [128, D]) for i in range(NZ)]
    R_zt = [Res("zt%d" % i) for i in range(NZ)]
    stat = [sb("stat%d" % i, [128, 2, 6]) for i in range(NZ)]
    mv = [sb("mv%d" % i, [128, 2]) for i in range(NZ)]
    rstd = [sb("rstd%d" % i, [128, 1]) for i in range(NZ)]
    NWS = 8 if pn == "S" else 4
    wslot = [sb("wslot%d" % i, [128, NB, 128], BF16) for i in range(NWS)]
    R_ws = [Res("ws%d" % i) for i in range(NWS)]
    ws_i = [0]
    ds_ws = [kb.dsem("ws%s%d" % (pn, i)) for i in range(NWS)]

    if pn == "A":
        dma("sp", h_tok[0:16, 0, :], I["meta_tokens"][:, :], [], [R_h[0]], ds_in)
        dma("sp", h_tok[16:128, 0, :], I["x_prompt"][0:112, :], [], [R_h[0]], ds_in)
        for t in range(1, NT):
            dma("sp", h_tok[:, t, :], I["x_prompt"][112 + 128 * (t - 1):112 + 128 * t, :], [], [R_h[t]], ds_in)
    elif pn == "B":
        for t in range(NT):
            dma("sp", h_tok[0:rows[t], t, :], I["x_prompt"][1008 + 128 * t:1008 + 128 * t + rows[t], :], [], [R_h[t]], ds_in)
    else:
        dma("sp", h_tok[:, 0, :], I["x_sample"][:, :], [], [R_h[0]], ds_in)

    kb.group_final(R_h, ds_in)

    def to_hT(t):
        r = rows[t]
        for half in range(2):
            bank = 4 + half
            for b in range(4):
                blk = half * 4 + b
                op("pe", [R_h[t], Rc], [RP[bank]], lambda e, b=b, blk=blk: e.transpose(out=PS[bank][:, b * 128:b * 128 + r], in_=h_tok[0:r, t, blk * 128:(blk + 1) * 128], identity=ident[0:r, 0:r]))
            op("act", [RP[bank]], [R_hT[t]], lambda e, half=half: e.copy(out=hT[:, half * 4:half * 4 + 4, t * 128:t * 128 + r],
                                                                     in_=PS[bank][:, :].rearrange("p (b n) -> p b n", n=128)[:, :, 0:r]))

    for t in range(NT):
        to_hT(t)

    nlayers = ctx["dbg"].get("nlayers", DEPTH) if ctx["dbg"] else DEPTH
    glist = []
    for l_ in range(nlayers):
        j_ = l_ // 2
        if l_ % 2 == 0:
            glist += [(I["s5_w_in"][j_], c * 128) for c in range(NB)]
        else:
            for c in range(NB):
                glist += [(I["rg_w_in"][j_], c * 128), (I["rg_w_in"][j_], D + c * 128)]
        for v in range(FBH):
            glist += [(I["ffn_w_up"][l_], v * 128), (I["ffn_w_up"][l_], (v + FBH) * 128)]
    wplan = {"list": glist, "issued": 0, "hooks": {}, "off": 0, "next_off": 0}

    def plan_w(blocks, hooks=None):
        wplan["off"] = wplan["next_off"]
        wplan["next_off"] = wplan["off"] + len(blocks)
        for k, f in (hooks or {}).items():
            gi = wplan["off"] + k
            if gi < wplan["issued"]:
                f()
            else:
                wplan["hooks"][gi] = f

    ds_wbs = [kb.dsem("wb%s%d" % (pn, i)) for i in range(NWS)] if pn == "A" else None

    def write_back(k):
        s = k % NWS
        dma("sp", ctx["WBLK"][k], wslot[s][:].rearrange("p a b -> p (a b)"), [R_ws[s]], [], ds_wbs[s])

    def load_w_block(i, pf=NWS - 1):
        gi = wplan["off"] + i
        while wplan["issued"] < min(len(wplan["list"]), gi + pf + 1):
            k = wplan["issued"]
            src2d, c0 = wplan["list"][k]
            s = k % NWS
            if pn == "A":
                dma("pool", wslot[s][:], src2d[:, c0:c0 + 128].rearrange("(kc p) f -> p kc f", p=128), [], [R_ws[s]], ds_ws[s])
                if k >= 2:
                    write_back(k - 2)
            else:
                dma("pool", wslot[s][:].rearrange("p a b -> p (a b)"), ctx["WBLK"][k], [], [R_ws[s]], ds_ws[s])
            wplan["issued"] += 1
            if k in wplan["hooks"]:
                wplan["hooks"].pop(k)()
        return gi % NWS

    def up_matmul(s, ps_banks, extra_reads):
        for gi, (c0, cn) in enumerate(colgroups(N)):
            bank = ps_banks[gi]
            for kc in range(NB):
                op("pe", [R_ws[s]] + R_hT + extra_reads, [RP[bank]],
                   lambda e, kc=kc, bank=bank, c0=c0, cn=cn: e.matmul(PS[bank][:, 0:cn], lhsT=wslot[s][:, kc, :], rhs=hT[:, kc, c0:c0 + cn], start=(kc == 0), stop=(kc == NB - 1)))

    def layer_norm_tile(t, zi, li_, k, last):
        r = rows[t]
        z = zt[zi]
        for hh in range(2):
            op("dve", [R_zt[zi]], [R_zt[zi]], lambda e, hh=hh: e.bn_stats(out=stat[zi][0:r, hh, :], in_=z[0:r, hh * 512:(hh + 1) * 512]))
        op("dve", [R_zt[zi]], [R_zt[zi]], lambda e: e.bn_aggr(out=mv[zi][0:r, :], in_=stat[zi][0:r, :, :].rearrange("p a b -> p (a b)")))
        op("act", [R_zt[zi]], [R_zt[zi]], lambda e: e.activation(out=rstd[zi][0:r, :], in_=mv[zi][0:r, 1:2], func=AF.Sqrt, bias=LN_EPS_AP[0:r, :], scale=1.0))
        op("dve", [R_zt[zi]], [R_zt[zi]], lambda e: e.reciprocal(out=rstd[zi][0:r, :], in_=rstd[zi][0:r, :]))
        op("dve", [R_zt[zi]], [R_zt[zi]], lambda e: e.tensor_scalar(out=z[0:r, :], in0=z[0:r, :], scalar1=mv[zi][0:r, 0:1], scalar2=rstd[zi][0:r, 0:1], op0=ALU.subtract, op1=ALU.mult))
        op("pool", [R_zt[zi], R_ln], [R_zt[zi]], lambda e: e.tensor_tensor(out=z[0:r, :], in0=z[0:r, :], in1=gbc[0:r, :], op=ALU.mult))
        op("pool", [R_zt[zi], R_ln], [R_h[t]], lambda e: e.tensor_tensor(out=h_tok[0:r, t, :], in0=z[0:r, :], in1=bbc[0:r, :], op=ALU.add))
        if not last:
            return lambda: to_hT(t)
        else:
            if pn == "A":
                if t == 0:
                    dma("sp", O["y_prompt"][0:112, :], h_tok[16:128, 0, :], [R_h[t]], [], ctx["ds_out"])
                else:
                    dma("sp", O["y_prompt"][112 + 128 * (t - 1):112 + 128 * t, :], h_tok[:, t, :], [R_h[t]], [], ctx["ds_out"])
            elif pn == "B":
                dma("sp", O["y_prompt"][1008 + 128 * t:1008 + 128 * t + r, :], h_tok[0:r, t, :], [R_h[t]], [], ctx["ds_out"])
            else:
                dma("sp", O["y_sample"][:, :], h_tok[:, 0, :], [R_h[t]], [], ctx["ds_out"])
            return lambda: None

    LN_EPS_AP = sb("lneps", [128, 1])
    op("dve", [], [Rc], lambda e: e.memset(LN_EPS_AP[:], LN_EPS))
    one_ap = sb("one", [128, 1])
    op("dve", [], [Rc], lambda e: e.memset(one_ap[:], 1.0))

    def load_ln(li_, k):
        dma("sp", gbc[:], I["ln_g"][li_, k].partition_broadcast(128), [], [R_ln], ds_ln)
        dma("sp", bbc[:], I["ln_b"][li_, k].partition_broadcast(128), [], [R_ln], ds_ln)

    env = dict(ctx=ctx, pn=pn, nseq=nseq, L=L, N=N, NT=NT, rows=rows, NC=NC, sb=sb, h_tok=h_tok, hT=hT,
               R_h=R_h, R_hT=R_hT, ds_w=ds_w, ds_in=ds_in, zt=zt, R_zt=R_zt, load_w_block=load_w_block, plan_w=plan_w,
               up_matmul=up_matmul, layer_norm_tile=layer_norm_tile, one_ap=one_ap, load_ln=load_ln, wslot=wslot, R_ws=R_ws)

    for li_ in range(nlayers):
        j = li_ // 2
        with ExitStack() as les:
            if li_ % 2 == 0:
                s5_layer(env, li_, j, les)
            else:
                rg_layer(env, li_, j, les)
            kb.barrier()
        with ExitStack() as les:
            ffn_layer(env, li_, les, last=(li_ == nlayers - 1))
            if pn == "A" and li_ == nlayers - 1:
                for k in range(max(0, len(glist) - 2), len(glist)):
                    write_back(k)
            kb.barrier()


def conv_taps(kb, xs, acc, nseq, L, K, wcols, bcol, R_main, R_halo, R_acc):
    kb.op("act", [R_main], [R_acc], lambda e: e.activation(out=acc[:, :, :], in_=xs[:, :, K - 1:K - 1 + L], func=AF.Identity, scale=wcols[K - 1], bias=bcol))
    for k in range(K - 1):
        kb.op("dve", [R_main, R_halo, R_acc], [R_acc], lambda e, k=k: e.scalar_tensor_tensor(out=acc[:, :, :], in0=xs[:, :, k:k + L], scalar=wcols[k], in1=acc[:, :, :], op0=ALU.mult, op1=ALU.add))


def ffn_layer(env, li_, les, last):
    ctx = env["ctx"]; kb = ctx["kb"]; nc = ctx["nc"]; I = ctx["I"]; O = ctx["O"]
    op, dma = kb.op, kb.dma
    PS, RP = ctx["PS"], ctx["RP"]
    pn, nseq, L, N, NT, rows = env["pn"], env["nseq"], env["L"], env["N"], env["NT"], env["rows"]
    sb = lambda n, s, d=F32: env["sb"]("f%d_%s" % (li_, n), s, d, es=les)
    h_tok, hT = env["h_tok"], env["hT"]
    actT = sb("actT", [128, FBH, N], BF16)
    R_act = [Res("act%d" % v) for v in range(FBH)]
    wd = sb("wd", [128, FBH, D], BF16)
    R_wd = [Res("wd%d" % v) for v in range(FBH)]
    xs = [sb("xs%d" % i, [128, nseq, 2 + L]) for i in range(2)]
    acc4 = [sb("acc%d" % i, [128, nseq, L]) for i in range(4)]
    R_xs = [Res("xs%d" % i) for i in range(2)]
    R_xh = [Res("xh%d" % i) for i in range(2)]
    R_acc4 = [Res("acc%d" % i) for i in range(4)]
    cw, cb = ctx["ffn_cw"][li_], ctx["ffn_cb"][li_]
    Rst = ctx["R_state"]
    if nseq == 1:
        hist = ctx["ffn_hist_p"][li_]
        R_hist = Rst
    else:
        hist = sb("hist", [128, FB, nseq, 2])
        R_hist = Res("hist")
        stage = sb("hstage", [32, FF2])
        load_T(ctx, I["state_ffn_conv"][li_], FB, 32, lambda b0, nb_: hist[:, b0:b0 + nb_, :, :].rearrange("p b s k -> p b (s k)"), stage, R_hist, "fh", env["ds_in"])
    env["load_ln"](li_, 1)
    def wd_hook(v):
        def f():
            if pn == "A":
                dma("pool", wd[:, v, :], I["ffn_w_down"][li_][v * 128:(v + 1) * 128, :], [], [R_wd[v]], env["ds_w"])
            else:
                dma("pool", wd[:, v, :], ctx["WDd"][li_][:, v * D:(v + 1) * D], [], [R_wd[v]], env["ds_w"])
            if v == FBH - 1:
                kb.group_final(R_wd, env["ds_w"])
        return f
    blocks = []
    for v in range(FBH):
        blocks += [(I["ffn_w_up"][li_], v * 128), (I["ffn_w_up"][li_], (v + FBH) * 128)]
    env["plan_w"](blocks, {2 * v + 1: wd_hook(v) for v in range(FBH)})
    ngrp = len(colgroups(N))
    banksets = [[0, 1, 2][:ngrp], [3, 4, 5][:ngrp]]
    bi = 0
    for v in range(FBH):
        acc = acc4[2 * (v % 2):2 * (v % 2) + 2]
        R_acc = R_acc4[2 * (v % 2):2 * (v % 2) + 2]
        for which in range(2):
            blk = v + which * FBH
            s = env["load_w_block"](2 * v + which)
            banks = banksets[bi % 2]
            bi += 1
            env["up_matmul"](s, banks, [])
            x = xs[which]
            op("act", [R_hist], [R_xh[which]], lambda e, x=x, blk=blk: e.copy(out=x[:, :, 0:2], in_=hist[:, blk, :, :]))
            if nseq == 1:
                op("act", [RP[b_] for b_ in banks], [R_xs[which]], lambda e, x=x, banks=banks: e.copy(out=x[:, 0, 2:2 + N], in_=ctx["PSall"][:, banks[0] * 512:banks[0] * 512 + N]))
            else:
                op("act", [RP[banks[0]]], [R_xs[which]], lambda e, x=x, banks=banks: e.copy(out=x[:, :, 2:2 + L], in_=PS[banks[0]][:, 0:N].rearrange("p (s t) -> p s t", t=L)))
            op("act", [R_xs[which]], [R_hist], lambda e, x=x, blk=blk: e.copy(out=hist[:, blk, :, :], in_=x[:, :, L:L + 2]))
            conv_taps(kb, x, acc[which], nseq, L, 3, [cw[k][:, blk:blk + 1] for k in range(3)], cb[:, blk:blk + 1], R_xs[which], R_xh[which], R_acc[which])
        op("act", [R_acc[1]], [R_acc[1]], lambda e, acc=acc: e.activation(out=acc[1][:, :, :], in_=acc[1][:, :, :], func=AF.Gelu_apprx_tanh))
        op("dve", [R_acc[0], R_acc[1]], [R_act[v]], lambda e, v=v, acc=acc: e.tensor_tensor(out=actT[:, v, :], in0=acc[0][:, :, :].rearrange("p s t -> p (s t)"), in1=acc[1][:, :, :].rearrange("p s t -> p (s t)"), op=ALU.mult))
    if pn == "A":
        dma("sp", ctx["WDd"][li_], wd[:].rearrange("p a b -> p (a b)"), R_wd, [], ctx["ds_wb"])
    if pn != "A":
        stg = sb("ostage", [32, 256 if nseq == 1 else 2048])
        m = 2 * nseq
        store_T(ctx, lambda b: hist[:, b, :, :].rearrange("p s k -> p (s k)"), FB, m,
                (O["ffn_conv_p"] if nseq == 1 else O["ffn_conv_s"])[li_], stg, R_hist, "fo")
    pending = None
    for t in range(NT):
        r = rows[t]
        pb = [0, 1] if t % 2 == 0 else [2, 3]
        for v in range(FBH):
            for hh in range(2):
                op("pe", [R_act[v], R_wd[v]], [RP[pb[hh]]], lambda e, v=v, hh=hh: e.matmul(PS[pb[hh]][0:r, :], lhsT=actT[:, v, t * 128:t * 128 + r], rhs=wd[:, v, hh * 512:(hh + 1) * 512], start=(v == 0), stop=(v == FBH - 1)))
        zi = t % 4
        z = env["zt"][zi]
        for hh in range(2):
            op("dve", [env["R_h"][t], RP[pb[hh]]], [env["R_zt"][zi]], lambda e, hh=hh: e.scalar_tensor_tensor(out=z[0:r, hh * 512:(hh + 1) * 512], in0=h_tok[0:r, t, hh * 512:(hh + 1) * 512], scalar=DN_ALPHA, in1=PS[pb[hh]][0:r, :], op0=ALU.mult, op1=ALU.add))
        if pending:
            pending()
        pending = env["layer_norm_tile"](t, zi, li_, 1, last)
    pending()


def rg_layer(env, li_, j, les):
    ctx = env["ctx"]; kb = ctx["kb"]; nc = ctx["nc"]; I = ctx["I"]; O = ctx["O"]
    op, dma = kb.op, kb.dma
    PS, RP = ctx["PS"], ctx["RP"]
    pn, nseq, L, N, NT, rows = env["pn"], env["nseq"], env["L"], env["N"], env["NT"], env["rows"]
    sb = lambda n, s, d=F32: env["sb"]("r%d_%s" % (li_, n), s, d, es=les)
    h_tok, hT = env["h_tok"], env["hT"]
    Rst = ctx["R_state"]
    hgT = sb("hgT", [128, NB, N], BF16)
    R_hg = [Res("hg%d" % c) for c in range(NB)]
    wg = sb("wg", [128, 4, 2, 512], BF16)
    R_wg = Res("wg")
    wo = sb("wo", [128, NB, D], BF16)
    R_wo = Res("wo")
    if pn == "A":
        for n in range(4):
            dma("pool", wg[:, n, :, :], I["rg_w_gates"][j, n].rearrange("(kc p) f -> p kc f", p=128), [], [R_wg], env["ds_w"])
    else:
        dma("pool", wg[:].rearrange("p a b c -> p (a b c)"), ctx["RGWGd"][j], [], [R_wg], env["ds_w"])
    kb.group_final([R_wg], env["ds_w"])
    env["load_ln"](li_, 0)
    if nseq == 1:
        hist = ctx["rg_hist_p"][j]; hst = ctx["rg_h_p"][j]; R_hist = Rst
    else:
        hist = sb("hist", [128, NB, nseq, 3]); hst = sb("hst", [128, NB, nseq]); R_hist = Res("rhist")
        stage = sb("stage", [48, D])
        load_T(ctx, I["state_rg_conv"][j], NB, 48, lambda b0, nb_: hist[:, b0:b0 + nb_, :, :].rearrange("p b s k -> p b (s k)"), stage, R_hist, "rc", env["ds_in"])
        stage2 = sb("stage2", [16, D])
        load_T(ctx, I["state_rg_h"][j], NB, 16, lambda b0, nb_: hst[:, b0:b0 + nb_, :], stage2, R_hist, "rh", env["ds_in"])
    xs = [sb("xs%d" % i, [128, nseq, 3 + L]) for i in range(2)]
    xc = [[sb("xc%d_%d" % (b, i), [128, nseq, L]) for i in range(2)] for b in range(2)]
    xcb = [[sb("xcb%d_%d" % (b, i), [128, N], BF16) for i in range(2)] for b in range(2)]
    gl = [[sb("gl%d_%d" % (b, i), [128, N]) for i in range(2)] for b in range(2)]
    work = [[sb("wk%d_%d" % (b, i), [128, nseq, L]) for i in range(3)] for b in range(2)]
    R_xs = [Res() for _ in range(2)]; R_xh = [Res() for _ in range(2)]
    R_xc = [[Res() for _ in range(2)] for _ in range(2)]; R_gl = [[Res() for _ in range(2)] for _ in range(2)]
    R_wk = [Res("rgwork0"), Res("rgwork1")]
    cw, cb, bg, m8 = ctx["rg_cw"][j], ctx["rg_cb"][j], ctx["rg_bg"][j], ctx["rg_m8sp"][j]
    one_ap = env["one_ap"]
    ngrp = len(colgroups(N))
    banksets = [[0, 1, 2][:ngrp], [3, 4, 5][:ngrp]]
    bi = [0]
    blocks = []
    for c in range(NB):
        blocks += [(I["rg_w_in"][j], c * 128), (I["rg_w_in"][j], D + c * 128)]
    env["plan_w"](blocks)
    flat = lambda t: t[:, :, :].rearrange("p s t -> p (s t)")

    def stage_proj(n):
        pb_ = n % 2
        for q in range(2):
            c = 2 * n + q
            s = env["load_w_block"](2 * c)
            banks = banksets[bi[0] % 2]; bi[0] += 1
            env["up_matmul"](s, banks, [])
            op("act", [RP[b_] for b_ in banks], [R_gl[pb_][q]], lambda e, q=q, banks=banks: e.activation(out=gl[pb_][q][:, 0:N], in_=ctx["PSall"][:, banks[0] * 512:banks[0] * 512 + N], func=AF.Gelu_apprx_tanh))
            s = env["load_w_block"](2 * c + 1)
            banks = banksets[bi[0] % 2]; bi[0] += 1
            env["up_matmul"](s, banks, [])
            x = xs[q]
            op("act", [R_hist], [R_xh[q]], lambda e, x=x, c=c: e.copy(out=x[:, :, 0:3], in_=hist[:, c, :, :]))
            if nseq == 1:
                op("act", [RP[b_] for b_ in banks], [R_xs[q]], lambda e, x=x, banks=banks: e.copy(out=x[:, 0, 3:3 + N], in_=ctx["PSall"][:, banks[0] * 512:banks[0] * 512 + N]))
            else:
                op("act", [RP[banks[0]]], [R_xs[q]], lambda e, x=x, banks=banks: e.copy(out=x[:, :, 3:3 + L], in_=PS[banks[0]][:, 0:N].rearrange("p (s t) -> p s t", t=L)))
            op("act", [R_xs[q]], [R_hist], lambda e, x=x, c=c: e.copy(out=hist[:, c, :, :], in_=x[:, :, L:L + 3]))
            conv_taps(kb, x, xc[pb_][q], nseq, L, 4, [cw[k][:, c:c + 1] for k in range(4)], cb[:, c:c + 1], R_xs[q], R_xh[q], R_xc[pb_][q])
            op("dve", [R_xc[pb_][q]], [R_xc[pb_][q]], lambda e, q=q: e.tensor_copy(out=xcb[pb_][q][:, :], in_=flat(xc[pb_][q])))

    def stage_gate(n):
        pb_ = n % 2
        for q in range(2):
            c = 2 * n + q
            T1, T2, T3 = work[c % 2]
            Rw = R_wk[c % 2]
            pr_ = banksets[0]; pi_ = banksets[1]
            for (dst_banks, off) in ((pr_, q * 128), (pi_, 256 + q * 128)):
                for gi, (c0, cn) in enumerate(colgroups(N)):
                    for kc in range(2):
                        op("pe", [R_wg, R_xc[pb_][kc]], [RP[dst_banks[gi]]], lambda e, kc=kc, gi=gi, c0=c0, cn=cn, off=off, dst_banks=dst_banks: e.matmul(PS[dst_banks[gi]][:, 0:cn], lhsT=wg[:, n, kc, off:off + 128], rhs=xcb[pb_][kc][:, c0:c0 + cn], start=(kc == 0), stop=(kc == 1)))
            op("act", [RP[b_] for b_ in pr_], [Rw], lambda e, c=c: e.activation(out=flat(T1)[:, 0:N], in_=ctx["PSall"][:, pr_[0] * 512:pr_[0] * 512 + N], func=AF.Sigmoid, bias=bg[:, c:c + 1], scale=1.0))
            op("act", [RP[b_] for b_ in pi_], [Rw], lambda e, c=c: e.activation(out=flat(T2)[:, 0:N], in_=ctx["PSall"][:, pi_[0] * 512:pi_[0] * 512 + N], func=AF.Sigmoid, bias=bg[:, 8 + c:8 + c + 1], scale=1.0))
            op("act", [Rw], [Rw], lambda e, c=c: e.activation(out=flat(T1), in_=flat(T1), func=AF.Exp, scale=m8[:, c:c + 1]))
            op("act", [Rw], [Rw], lambda e: e.activation(out=flat(T3), in_=flat(T1), func=AF.Square))
            op("act", [Rw], [Rw], lambda e: e.activation(out=flat(T3), in_=flat(T3), func=AF.Sqrt, scale=-1.0, bias=one_ap[:, :]))
            op("dve", [Rw, R_xc[pb_][q]], [Rw], lambda e, q=q: e.tensor_tensor(out=flat(T2), in0=flat(T2), in1=flat(xc[pb_][q]), op=ALU.mult))
            op("dve", [Rw], [Rw], lambda e: e.tensor_tensor(out=flat(T2), in0=flat(T2), in1=flat(T3), op=ALU.mult))
            for s_ in range(nseq):
                op("dve", [Rw, R_hist], [Rw], lambda e, s_=s_, c=c: e.tensor_tensor_scan(out=T3[:, s_, :], data0=T1[:, s_, :], data1=T2[:, s_, :], initial=hst[:, c, s_:s_ + 1], op0=ALU.mult, op1=ALU.add))
            op("dve", [Rw], [R_hist], lambda e, c=c: e.tensor_copy(out=hst[:, c, :], in_=T3[:, :, L - 1]))
            op("dve", [Rw, R_gl[pb_][q]], [R_hg[c]], lambda e, c=c, q=q: e.tensor_tensor(out=hgT[:, c, :], in0=flat(T3), in1=gl[pb_][q][:, :], op=ALU.mult))

    for n in range(4):
        stage_proj(n)
        if n == 1:
            if pn == "A":
                dma("pool", wo[:], I["rg_w_out"][j].rearrange("(kc p) f -> p kc f", p=128), [], [R_wo], env["ds_w"])
            else:
                dma("pool", wo[:].rearrange("p a b -> p (a b)"), ctx["RGWOd"][j], [], [R_wo], env["ds_w"])
            kb.group_final([R_wo], env["ds_w"])
        if n > 0:
            stage_gate(n - 1)
    stage_gate(3)
    if pn == "A":
        dma("sp", ctx["RGWGd"][j], wg[:].rearrange("p a b c -> p (a b c)"), [R_wg], [], ctx["ds_wb"])
        dma("sp", ctx["RGWOd"][j], wo[:].rearrange("p a b -> p (a b)"), [R_wo], [], ctx["ds_wb"])
    if pn != "A":
        stg = sb("ostage", [48, 256])
        store_T(ctx, lambda b: hist[:, b, :, :].rearrange("p s k -> p (s k)"), NB, 3 * nseq,
                (O["rg_conv_p"] if nseq == 1 else O["rg_conv_s"])[j], stg, R_hist, "ro")
        store_T(ctx, lambda b: hst[:, b, :], NB, nseq, (O["rg_h_p"] if nseq == 1 else O["rg_h_s"])[j], stg, R_hist, "rho")
    pending = None
    for t in range(NT):
        r = rows[t]
        pb = [0, 1] if t % 2 == 0 else [2, 3]
        for kc in range(NB):
            for hh in range(2):
                op("pe", [R_hg[kc], R_wo], [RP[pb[hh]]], lambda e, kc=kc, hh=hh: e.matmul(PS[pb[hh]][0:r, :], lhsT=hgT[:, kc, t * 128:t * 128 + r], rhs=wo[:, kc, hh * 512:(hh + 1) * 512], start=(kc == 0), stop=(kc == NB - 1)))
        zi = t % 4
        z = env["zt"][zi]
        for hh in range(2):
            op("dve", [env["R_h"][t], RP[pb[hh]]], [env["R_zt"][zi]], lambda e, hh=hh: e.scalar_tensor_tensor(out=z[0:r, hh * 512:(hh + 1) * 512], in0=h_tok[0:r, t, hh * 512:(hh + 1) * 512], scalar=DN_ALPHA, in1=PS[pb[hh]][0:r, :], op0=ALU.mult, op1=ALU.add))
        if pending:
            pending()
        pending = env["layer_norm_tile"](t, zi, li_, 0, False)
    pending()


def ONE_AP(env):
    if "one_ap" not in env:
        kb = env["ctx"]["kb"]
        t = env["sb"]("one", [128, 1])
        kb.op("dve", [], [env["ctx"]["R_const"]], lambda e: e.memset(t[:], 1.0))
        env["one_ap"] = t
    return env["one_ap"]


def s5_layer(env, li_, j, les):
    ctx = env["ctx"]; kb = ctx["kb"]; nc = ctx["nc"]; I = ctx["I"]; O = ctx["O"]
    op, dma = kb.op, kb.dma
    PS, RP = ctx["PS"], ctx["RP"]
    pn, nseq, L, N, NT, rows, NC = env["pn"], env["nseq"], env["L"], env["N"], env["NT"], env["rows"], env["NC"]
    sb = lambda n, s, d=F32: env["sb"]("s%d_%s" % (li_, n), s, d, es=les)
    h_tok, hT = env["h_tok"], env["hT"]
    Rst = ctx["R_state"]
    uT = sb("uT", [128, NB, N], BF16)
    gyT = uT
    R_u = [Res() for _ in range(NB)]
    R_gy = R_u
    R_X = Res("Xs")
    Sinb = sb("Sinb", [128, 32, 2, NC], BF16)
    R_S = Res("Sinb")
    wo = sb("wo", [128, NB, 2 * D], BF16)
    R_wo = Res("wo")
    env["load_ln"](li_, 0)
    if nseq == 1:
        st = ctx["s5st_p"][j]; R_st = Rst
    else:
        st = sb("st", [128, 32, 2, nseq]); R_st = Res("s5st")
        stage = sb("stage", [16, 4096])
        for ri, nm in enumerate(("state_s5_re", "state_s5_im")):
            load_T(ctx, I[nm][j], 32, 16, lambda b0, nb_, ri=ri: st[:, b0:b0 + nb_, ri, :], stage, R_st, "s5" + str(ri), env["ds_in"])
    ngrp = len(colgroups(N))
    banksets = [[0, 1, 2][:ngrp], [3, 4, 5][:ngrp]]
    env["plan_w"]([(I["s5_w_in"][j], c * 128) for c in range(NB)])
    for c in range(NB):
        s = env["load_w_block"](c)
        banks = banksets[c % 2]
        env["up_matmul"](s, banks, [])
        op("act", [RP[b_] for b_ in banks], [R_u[c]], lambda e, c=c, banks=banks: e.copy(out=uT[:, c, 0:N], in_=ctx["PSall"][:, banks[0] * 512:banks[0] * 512 + N]))
    if pn == "A":
        dma("pool", wo[:, 0:4, :], I["s5_w_out"][j][0:512, :].rearrange("(kc p) f -> p kc f", p=128), [], [R_wo], env["ds_w"])
        dma("pool", wo[:, 4:8, :], I["s5_w_out"][j][512:1024, :].rearrange("(kc p) f -> p kc f", p=128), [], [R_wo], env["ds_w"])
        kb.group_final([R_wo], env["ds_w"])
    else:
        dma("pool", wo[:].rearrange("p a b -> p (a b)"), ctx["S5WOd"][j], [], [R_wo], env["ds_w"])
        kb.group_final([R_wo], env["ds_w"])
    ies = ExitStack()
    sbo = sb
    sb = lambda n, s_, d=F32: env["sb"]("s%d_%s" % (li_, n), s_, d, es=ies)
    Xs = sb("Xs", [128, 32, 2, NC])
    WXs = [sb("WX%d" % i, [128, 8, 2, 128], BF16) for i in range(2)]
    R_WX = [Res() for _ in range(2)]
    ds_wx = [kb.dsem("s5x%s%d_%d" % (pn, li_, i)) for i in range(2)]
    ds_kw = [kb.dsem("s5k%s%d_%d" % (pn, li_, i)) for i in range(2)]
    for c in range(NB):
        wi = c % 2
        dma("sp", WXs[wi][:], ctx["WXd"][j][:, c], [], [R_WX[wi]], ds_wx[wi])
        for ri in range(2):
            for tau in range(8):
                for pr in range(4):
                    bank = pr
                    rs = slice(32 * pr, 32 * pr + 32)
                    op("pe", [R_WX[wi], R_u[c]], [RP[bank]], lambda e, ri=ri, tau=tau, rs=rs, bank=bank, wi=wi, c=c, pr=pr: e.matmul(
                        PS[bank][:, ri * 256:ri * 256 + NC], lhsT=WXs[wi][rs, tau, ri, :],
                        rhs=uT[rs, c, :].rearrange("p (n t) -> p n t", t=8)[:, :, tau], start=(tau == 0), stop=(tau == 7), tile_position=(32 * pr, 0)))
        for pr in range(4):
            bank = pr
            op("act", [RP[bank]], [R_X], lambda e, bank=bank, c=c, pr=pr: e.copy(out=Xs[:, 4 * c + pr, :, :], in_=PS[bank][:, :].rearrange("p (r n) -> p r n", n=256)[:, :, 0:NC]))
    a8r, a8i, a8in = ctx["A8re"][j], ctx["A8im"][j], ctx["A8imn"][j]
    Mrot = sb("Mrot", [128, 32, 2, 2]); prod = sb("prod", [128, 32, 2, 2]); t1 = sb("t1", [128, 32, 2])
    op("dve", [Rst], [R_X], lambda e: e.tensor_copy(out=Mrot[:, :, 0, 0], in_=a8r[:, :]))
    op("dve", [Rst], [R_X], lambda e: e.tensor_copy(out=Mrot[:, :, 1, 1], in_=a8r[:, :]))
    op("dve", [Rst], [R_X], lambda e: e.tensor_copy(out=Mrot[:, :, 0, 1], in_=a8in[:, :]))
    op("dve", [Rst], [R_X], lambda e: e.tensor_copy(out=Mrot[:, :, 1, 0], in_=a8i[:, :]))
    if nseq == 1:
        op("act", [R_st], [R_S], lambda e: e.copy(out=Sinb[:, :, :, 0], in_=st[:, :, :, 0]))
    else:
        op("act", [R_st], [R_S], lambda e: e.copy(out=Sinb[:, :, :, :], in_=st[:, :, :, :]))
    nsteps = NC if nseq == 1 else nseq
    for c in range(nsteps):
        if nseq == 1:
            prev = st[:, :, :, 0] if c == 0 else Xs[:, :, :, c - 1]
        else:
            prev = st[:, :, :, c]
        cur = Xs[:, :, :, c]
        rd = [R_X, R_st, Rst]
        pb_ = prev.unsqueeze(2).broadcast_to([128, 32, 2, 2])
        op("dve", rd, [R_X], lambda e, pb_=pb_: e.tensor_tensor(out=prod[:], in0=pb_, in1=Mrot[:], op=ALU.mult))
        op("dve", rd, [R_X], lambda e: e.tensor_tensor(out=t1[:], in0=prod[:, :, :, 0], in1=prod[:, :, :, 1], op=ALU.add))
        op("dve", rd, [R_X], lambda e, cur=cur: e.tensor_tensor(out=cur, in0=cur, in1=t1[:], op=ALU.add))
    if nseq == 1:
        op("act", [R_X], [R_S], lambda e: e.copy(out=Sinb[:, :, :, 1:NC], in_=Xs[:, :, :, 0:NC - 1]))
        op("dve", [R_X], [R_st], lambda e: e.tensor_copy(out=st[:, :, :, 0], in_=Xs[:, :, :, NC - 1]))
    else:
        op("dve", [R_X], [R_st], lambda e: e.tensor_copy(out=st[:, :, :, :], in_=Xs[:, :, :, :]))
    if pn != "A":
        stg = sb("ostage", [16, 2048])
        for ri, nm in enumerate(("s5_re", "s5_im")):
            store_T(ctx, lambda b, ri=ri: st[:, b, ri, :], 32, nseq, O[nm + ("_p" if nseq == 1 else "_s")][j], stg, R_st, "s5o%d" % ri)
    kb.barrier()
    ies.close()
    sb = sbo
    KIs = [sb("KI%d" % i, [128, 8, 128], BF16) for i in range(2)]
    WYs = [sb("WY%d" % i, [128, 4, 8, 2, 32], BF16) for i in range(2)]
    R_KI = [Res() for _ in range(2)]
    R_WY = [Res() for _ in range(2)]
    cg = colgroups(NC, 64)
    for c in range(NB):
        wi = c % 2
        dma("sp", KIs[wi][:], ctx["KId"][j][:, c], [], [R_KI[wi]], ds_kw[wi])
        dma("sp", WYs[wi][:], ctx["WYd"][j][:, 4 * c:4 * c + 4], [], [R_WY[wi]], ds_kw[wi])
        kb.group_final([R_KI[wi], R_WY[wi]], ds_kw[wi])
        banks = banksets[c % 2]
        for gi, (n0, nn) in enumerate(cg):
            bank = banks[gi]
            uv = uT[:, c, n0 * 8:(n0 + nn) * 8].rearrange("p (n t) -> p t n", t=8)
            for lag in range(8):
                op("pe", [R_KI[wi], R_u[c]], [RP[bank]], lambda e, lag=lag, bank=bank, nn=nn, uv=uv, wi=wi: e.matmul(
                    PS[bank][:, lag * nn:8 * nn], lhsT=KIs[wi][:, lag, :], rhs=uv[:, 0:8 - lag, :], start=(lag == 0), stop=False))
            for tau in range(8):
                for ri in range(2):
                    for pr in range(4):
                        lastmm = (tau == 7 and ri == 1)
                        op("pe", [R_WY[wi], R_S], [RP[bank]], lambda e, pr=pr, tau=tau, ri=ri, bank=bank, wi=wi, n0=n0, nn=nn, c=c, lastmm=lastmm: e.matmul(
                            PS[bank][32 * pr:32 * pr + 32, tau * nn:(tau + 1) * nn], lhsT=WYs[wi][:, pr, tau, ri, :], rhs=Sinb[:, 4 * c + pr, ri, n0:n0 + nn], start=False, stop=lastmm, tile_position=(0, 32 * pr)))
            op("act", [RP[bank]], [R_gy[c]], lambda e, bank=bank, n0=n0, nn=nn, c=c: e.activation(out=gyT[:, c, n0 * 8:(n0 + nn) * 8].rearrange("p (n t) -> p n t", t=8), in_=PS[bank][:, 0:nn * 8].rearrange("p (t n) -> p n t", t=8), func=AF.Gelu_apprx_tanh))
    if pn == "A":
        dma("sp", ctx["S5WOd"][j], wo[:].rearrange("p a b -> p (a b)"), [R_wo], [], ctx["ds_wb"])
    sgs = [sb("sg%d" % i, [128, D]) for i in range(2)]
    R_sgs = [Res("sg%d" % i) for i in range(2)]
    pending = None
    for t in range(NT):
        sg = sgs[t % 2]
        R_sg = R_sgs[t % 2]
        r = rows[t]
        pb = [2, 3, 0, 1]
        for qq in (2, 3, 0, 1):
            for kc in range(NB):
                op("pe", [R_gy[kc], R_wo], [RP[pb[qq]]], lambda e, kc=kc, qq=qq: e.matmul(PS[pb[qq]][0:r, :], lhsT=gyT[:, kc, t * 128:t * 128 + r], rhs=wo[:, kc, qq * 512:(qq + 1) * 512], start=(kc == 0), stop=(kc == NB - 1)))
        zi = t % 4
        z = env["zt"][zi]
        for hh in range(2):
            op("act", [RP[pb[2 + hh]]], [R_sg], lambda e, hh=hh: e.activation(out=sg[0:r, hh * 512:(hh + 1) * 512], in_=PS[pb[2 + hh]][0:r, :], func=AF.Sigmoid))
            op("dve", [R_sg, RP[pb[hh]]], [R_sg], lambda e, hh=hh: e.tensor_tensor(out=sg[0:r, hh * 512:(hh + 1) * 512], in0=sg[0:r, hh * 512:(hh + 1) * 512], in1=PS[pb[hh]][0:r, :], op=ALU.mult))
            op("dve", [env["R_h"][t], R_sg], [env["R_zt"][zi]], lambda e, hh=hh: e.scalar_tensor_tensor(out=z[0:r, hh * 512:(hh + 1) * 512], in0=h_tok[0:r, t, hh * 512:(hh + 1) * 512], scalar=DN_ALPHA, in1=sg[0:r, hh * 512:(hh + 1) * 512], op0=ALU.mult, op1=ALU.add))
        if pending:
            pending()
        pending = env["layer_norm_tile"](t, zi, li_, 0, False)
    pending()


_NC_CACHE = {}


def _get_nc(dbg=None):
    key = repr(dbg)
    if key not in _NC_CACHE:
        nc = bass.Bass("TRN2", target_bir_lowering=False)
        build(nc, dbg)
        _NC_CACHE[key] = nc
    return _NC_CACHE[key]


def kernel(dbg=None, **inp):
    f = lambda a: np.ascontiguousarray(np.asarray(a, dtype=np.float32))
    ident = np.eye(128, dtype=np.float32)
    bmask = np.kron(np.eye(4, dtype=np.float32), np.ones((32, 32), np.float32))
    wnames = ["meta_tokens", "s5_w_in", "s5_lam_re", "s5_lam_im", "s5_log_step", "s5_b_re", "s5_b_im", "s5_c_re",
              "s5_c_im", "s5_d", "s5_w_out", "rg_w_in", "rg_conv_w", "rg_conv_b", "rg_w_gates", "rg_b_gates",
              "rg_lam", "rg_w_out", "ffn_w_up", "ffn_conv_w", "ffn_conv_b", "ffn_w_down", "ln_g", "ln_b"]
    shared = {n: f(inp[n]) for n in wnames}
    shared["ident"] = ident
    shared["bmask"] = bmask
    shared["sel2"] = np.kron(np.eye(2, dtype=np.float32), np.ones((1, 64), np.float32))
    in_maps = []
    for c in range(8):
        m = dict(shared)
        sl = slice(16 * c, 16 * c + 16)
        m["x_prompt"] = f(inp["x_prompt"][c])
        m["x_sample"] = f(inp["x_sample"][sl]).reshape(128, D)
        m["state_s5_re"] = f(inp["state_s5_re"][:, sl]).reshape(2, 16, 4096)
        m["state_s5_im"] = f(inp["state_s5_im"][:, sl]).reshape(2, 16, 4096)
        m["state_rg_h"] = f(inp["state_rg_h"][:, sl])
        m["state_rg_conv"] = f(inp["state_rg_conv"][:, sl]).reshape(2, 48, D)
        m["state_ffn_conv"] = f(inp["state_ffn_conv"][:, sl]).reshape(4, 32, FF2)
        in_maps.append(m)
    nc = _get_nc(dbg)
    res = run_bass_kernel_spmd(nc, in_maps, core_ids=list(range(8)))
    R = res.results
    cat = lambda k, ax: np.concatenate([np.asarray(R[c][k]) for c in range(8)], axis=ax)
    y_prompt = np.stack([np.asarray(R[c]["y_prompt"]) for c in range(8)], 0)
    y_sample = cat("y_sample", 0).reshape(128, 8, D)
    s5_re_p = cat("s5_re_p", 1).reshape(2, 8, 64, 64)
    s5_im_p = cat("s5_im_p", 1).reshape(2, 8, 64, 64)
    rg_h_p = cat("rg_h_p", 1).reshape(2, 8, D)
    rg_conv_p = np.stack([np.asarray(R[c]["rg_conv_p"]) for c in range(8)], 1).reshape(2, 8, 3, D)
    ffn_conv_p = np.stack([np.asarray(R[c]["ffn_conv_p"]) for c in range(8)], 1).reshape(4, 8, 2, FF2)
    s5_re_s = cat("s5_re_s", 1).reshape(2, 128, 64, 64)
    s5_im_s = cat("s5_im_s", 1).reshape(2, 128, 64, 64)
    rg_h_s = cat("rg_h_s", 1).reshape(2, 128, D)
    rg_conv_s = cat("rg_conv_s", 1).reshape(2, 128, 3, D)
    ffn_conv_s = cat("ffn_conv_s", 1).reshape(4, 128, 2, FF2)
    outs = (y_prompt, y_sample, s5_re_p, s5_im_p, rg_h_p, rg_conv_p, ffn_conv_p,
            s5_re_s, s5_im_s, rg_h_s, rg_conv_s, ffn_conv_s)
    return tuple(np.ascontiguousarray(o, dtype=np.float32) for o in outs)
```
